# Optimizing a Trainium2 kernel written in Bass

```python
import math
import jax, jax.numpy as jnp
from jax import lax
import numpy as np

D_MODEL = 1024
BATCH = 4
SEQ = 8192
DEPTH = 2

MIX_WIDTH = D_MODEL
GDN_WIDTH = MIX_WIDTH // 2
GDN_HEAD_DIM = 128
GDN_HEADS = GDN_WIDTH // GDN_HEAD_DIM
CONV_K = 4
CHUNK = 64
S5_WIDTH = MIX_WIDTH - GDN_WIDTH
S5_GROUP = 16
S5_GROUPS = S5_WIDTH // S5_GROUP
S5_STATE = 64
D_FF = ((8 * D_MODEL // 3 + 127) // 128) * 128
N_MOD = 9
PROJ_WIDTH = 4 * GDN_WIDTH + 2 * GDN_HEADS + S5_WIDTH
EPS = 1e-6

kernel_name = "hybrid_gdn_s5_macaron_block"


def rms_norm(x, w):
    xf = x.astype(jnp.float32)
    y = xf * lax.rsqrt(jnp.mean(xf * xf, axis=-1, keepdims=True) + EPS)
    return (y * w.astype(jnp.float32)).astype(x.dtype)


def l2_normalize(t):
    return t * lax.rsqrt(jnp.sum(t * t, axis=-1, keepdims=True) + EPS)


def modulate(h, shift, scale):
    return h * (1.0 + scale) + shift


def swiglu(h, w_in, w_out):
    gate, up = jnp.split(h @ w_in, 2, axis=-1)
    return (jax.nn.silu(gate) * up) @ w_out


def causal_dwconv(x, w):
    K = w.shape[0]
    L = x.shape[1]
    xp = jnp.pad(x, ((0, 0), (K - 1, 0), (0, 0)))
    return sum(xp[:, j:j + L] * w[j] for j in range(K))


def chunked_gated_delta_rule(q, k, v, beta, g):
    Bsz, H, L, dk = q.shape
    dv = v.shape[-1]
    nc = L // CHUNK
    q = q * (dk ** -0.5)
    rs = lambda t: t.reshape(t.shape[:2] + (nc, CHUNK) + t.shape[3:])
    q, k, v, beta, g = rs(q), rs(k), rs(v), rs(beta), rs(g)
    g = jnp.cumsum(g, axis=-1)
    idx = jnp.arange(CHUNK)
    lower_incl = idx[:, None] >= idx[None, :]
    strict = idx[:, None] > idx[None, :]
    decay = jnp.exp(jnp.where(lower_incl, g[..., :, None] - g[..., None, :], -jnp.inf))
    kb = k * beta[..., None]
    vb = v * beta[..., None]
    lmat = jnp.einsum('bhnik,bhnjk->bhnij', kb, k) * decay * strict
    system = lmat + jnp.eye(CHUNK, dtype=lmat.dtype)
    rhs = jnp.concatenate([vb, kb * jnp.exp(g)[..., None]], axis=-1)
    sol = lax.linalg.triangular_solve(system, rhs, left_side=True, lower=True, unit_diagonal=True)
    w_val, k_cum = sol[..., :dv], sol[..., dv:]
    attn_intra = jnp.einsum('bhnik,bhnjk->bhnij', q, k) * decay
    g_last = g[..., -1]
    k_state = k * jnp.exp(g_last[..., None] - g)[..., None]
    q_state = q * jnp.exp(g)[..., None]

    def step(S, inp):
        qs, ks, kc, wv, attn, gl = inp
        v_new = wv - jnp.einsum('bhck,bhkv->bhcv', kc, S)
        o = jnp.einsum('bhck,bhkv->bhcv', qs, S) + jnp.einsum('bhij,bhjv->bhiv', attn, v_new)
        S = S * jnp.exp(gl)[..., None, None] + jnp.einsum('bhck,bhcv->bhkv', ks, v_new)
        return S, o

    xs = tuple(jnp.moveaxis(t, 2, 0) for t in (q_state, k_state, k_cum, w_val, attn_intra, g_last))
    S0 = jnp.zeros((Bsz, H, dk, dv), dtype=q.dtype)
    _, o = lax.scan(step, S0, xs)
    return jnp.moveaxis(o, 0, 2).reshape(Bsz, H, L, dv)


def _complex_linear_combine(earlier, later):
    ar1, ai1, br1, bi1 = earlier
    ar2, ai2, br2, bi2 = later
    ar = ar2 * ar1 - ai2 * ai1
    ai = ar2 * ai1 + ai2 * ar1
    br = ar2 * br1 - ai2 * bi1 + br2
    bi = ar2 * bi1 + ai2 * br1 + bi2
    return (ar, ai, br, bi)


def s5_mixer(u, a_re, a_im, log_dt, b_re, b_im, c_re, c_im, d, w_glu):
    f32 = jnp.float32
    Bsz, L, _ = u.shape
    u = u.astype(f32).reshape(Bsz, L, S5_GROUPS, S5_GROUP)
    ar = jnp.minimum(a_re.astype(f32), -1e-4)
    ai = a_im.astype(f32)
    dt = jnp.exp(log_dt.astype(f32))[:, None]
    mag = jnp.exp(dt * ar)
    abar_re = mag * jnp.cos(dt * ai)
    abar_im = mag * jnp.sin(dt * ai)
    denom = ar * ar + ai * ai
    zr = abar_re - 1.0
    zi = abar_im
    fr = (zr * ar + zi * ai) / denom
    fi = (zi * ar - zr * ai) / denom
    br = b_re.astype(f32)
    bi = b_im.astype(f32)
    bbar_re = fr[..., None] * br - fi[..., None] * bi
    bbar_im = fr[..., None] * bi + fi[..., None] * br
    bu_re = jnp.einsum('gph,blgh->blgp', bbar_re, u)
    bu_im = jnp.einsum('gph,blgh->blgp', bbar_im, u)
    seq_shape = (1, L, S5_GROUPS, S5_STATE)
    a_seq_re = jnp.broadcast_to(abar_re, seq_shape)
    a_seq_im = jnp.broadcast_to(abar_im, seq_shape)
    _, _, xr, xi = lax.associative_scan(_complex_linear_combine, (a_seq_re, a_seq_im, bu_re, bu_im), axis=1)
    y = (jnp.einsum('ghp,blgp->blgh', c_re.astype(f32), xr)
         - jnp.einsum('ghp,blgp->blgh', c_im.astype(f32), xi)
         + d.astype(f32) * u)
    y = jax.nn.gelu(y.reshape(Bsz, L, S5_WIDTH))
    return y * jax.nn.sigmoid(y @ w_glu.astype(f32))


def hybrid_mixer(h, w_in, conv_w, a_log, dt_bias, gdn_norm_w,
                 s5_a_re, s5_a_im, s5_log_dt, s5_b_re, s5_b_im, s5_c_re, s5_c_im, s5_d, s5_w_glu, w_out):
    f32 = jnp.float32
    Bsz, L, _ = h.shape
    proj = h @ w_in
    qkv, z, beta_in, a_in, u = jnp.split(
        proj, [3 * GDN_WIDTH, 4 * GDN_WIDTH, 4 * GDN_WIDTH + GDN_HEADS, 4 * GDN_WIDTH + 2 * GDN_HEADS], axis=-1)
    qkv = jax.nn.silu(causal_dwconv(qkv, conv_w))
    q, k, v = jnp.split(qkv, 3, axis=-1)
    heads = lambda t: t.reshape(Bsz, L, GDN_HEADS, GDN_HEAD_DIM).transpose(0, 2, 1, 3).astype(f32)
    q = l2_normalize(heads(q))
    k = l2_normalize(heads(k))
    v = heads(v)
    beta = jax.nn.sigmoid(beta_in.astype(f32)).transpose(0, 2, 1)
    g = (-jnp.exp(a_log.astype(f32)) * jax.nn.softplus(a_in.astype(f32) + dt_bias.astype(f32))).transpose(0, 2, 1)
    o = chunked_gated_delta_rule(q, k, v, beta, g).transpose(0, 2, 1, 3)
    o = rms_norm(o, gdn_norm_w) * jax.nn.silu(z.astype(f32).reshape(Bsz, L, GDN_HEADS, GDN_HEAD_DIM))
    y_gdn = o.reshape(Bsz, L, GDN_WIDTH)
    y_s5 = s5_mixer(u, s5_a_re, s5_a_im, s5_log_dt, s5_b_re, s5_b_im, s5_c_re, s5_c_im, s5_d, s5_w_glu)
    y = jnp.concatenate([y_gdn.astype(h.dtype), y_s5.astype(h.dtype)], axis=-1)
    return y @ w_out


def setup_inputs(seed: int = 0) -> dict:
    key = jax.random.key(seed)
    ks = jax.random.split(key, 32)
    nrm = lambda k, shape, s: jax.random.normal(k, shape, jnp.float32) * s
    gain = lambda k, shape: 1.0 + 0.05 * jax.random.normal(k, shape, jnp.float32)
    L_ = DEPTH
    dt_gdn = jnp.exp(jax.random.uniform(ks[14], (L_, GDN_HEADS), jnp.float32, math.log(1e-3), math.log(1e-1)))
    return {
        "x": nrm(ks[0], (BATCH, SEQ, D_MODEL), 1.0),
        "c": nrm(ks[1], (BATCH, D_MODEL), 1.0),
        "w_mod": nrm(ks[2], (L_, D_MODEL, N_MOD * D_MODEL), 0.5 * D_MODEL ** -0.5),
        "b_mod": nrm(ks[3], (L_, N_MOD * D_MODEL), 0.01),
        "ff1_norm_pre": gain(ks[4], (L_, D_MODEL)),
        "ff1_norm_post": gain(ks[5], (L_, D_MODEL)),
        "ff1_w_in": nrm(ks[6], (L_, D_MODEL, 2 * D_FF), D_MODEL ** -0.5),
        "ff1_w_out": nrm(ks[7], (L_, D_FF, D_MODEL), D_FF ** -0.5),
        "mix_norm_pre": gain(ks[8], (L_, D_MODEL)),
        "mix_norm_post": gain(ks[9], (L_, D_MODEL)),
        "mix_w_in": nrm(ks[10], (L_, D_MODEL, PROJ_WIDTH), D_MODEL ** -0.5),
        "conv_w": nrm(ks[11], (L_, CONV_K, 3 * GDN_WIDTH), CONV_K ** -0.5),
        "a_log": jnp.log(jax.random.uniform(ks[12], (L_, GDN_HEADS), jnp.float32, 1.0, 16.0)),
        "dt_bias": dt_gdn + jnp.log(-jnp.expm1(-dt_gdn)),
        "gdn_norm_w": gain(ks[13], (L_, GDN_HEAD_DIM)),
        "s5_a_re": -0.5 + 0.01 * jax.random.normal(ks[15], (L_, S5_GROUPS, S5_STATE), jnp.float32),
        "s5_a_im": jnp.broadcast_to(math.pi * jnp.arange(S5_STATE, dtype=jnp.float32), (L_, S5_GROUPS, S5_STATE)),
        "s5_log_dt": jax.random.uniform(ks[16], (L_, S5_GROUPS), jnp.float32, math.log(1e-3), math.log(1e-1)),
        "s5_b_re": nrm(ks[17], (L_, S5_GROUPS, S5_STATE, S5_GROUP), (2 * S5_GROUP) ** -0.5),
        "s5_b_im": nrm(ks[18], (L_, S5_GROUPS, S5_STATE, S5_GROUP), (2 * S5_GROUP) ** -0.5),
        "s5_c_re": nrm(ks[19], (L_, S5_GROUPS, S5_GROUP, S5_STATE), (2 * S5_STATE) ** -0.5),
        "s5_c_im": nrm(ks[20], (L_, S5_GROUPS, S5_GROUP, S5_STATE), (2 * S5_STATE) ** -0.5),
        "s5_d": nrm(ks[21], (L_, S5_GROUPS, S5_GROUP), 1.0),
        "s5_w_glu": nrm(ks[22], (L_, S5_WIDTH, S5_WIDTH), S5_WIDTH ** -0.5),
        "mix_w_out": nrm(ks[23], (L_, MIX_WIDTH, D_MODEL), MIX_WIDTH ** -0.5),
        "ff2_norm_pre": gain(ks[24], (L_, D_MODEL)),
        "ff2_norm_post": gain(ks[25], (L_, D_MODEL)),
        "ff2_w_in": nrm(ks[26], (L_, D_MODEL, 2 * D_FF), D_MODEL ** -0.5),
        "ff2_w_out": nrm(ks[27], (L_, D_FF, D_MODEL), D_FF ** -0.5),
    }


def reference(x, c, w_mod, b_mod, ff1_norm_pre, ff1_norm_post, ff1_w_in, ff1_w_out,
              mix_norm_pre, mix_norm_post, mix_w_in, conv_w, a_log, dt_bias, gdn_norm_w,
              s5_a_re, s5_a_im, s5_log_dt, s5_b_re, s5_b_im, s5_c_re, s5_c_im, s5_d, s5_w_glu, mix_w_out,
              ff2_norm_pre, ff2_norm_post, ff2_w_in, ff2_w_out):
    c_act = jax.nn.silu(c)
    for l in range(DEPTH):
        mod = (c_act @ w_mod[l] + b_mod[l])[:, None, :]
        sh1, sc1, gt1, sh2, sc2, gt2, sh3, sc3, gt3 = jnp.split(mod, N_MOD, axis=-1)
        h = modulate(rms_norm(x, ff1_norm_pre[l]), sh1, sc1)
        x = x + 0.5 * gt1 * rms_norm(swiglu(h, ff1_w_in[l], ff1_w_out[l]), ff1_norm_post[l])
        h = modulate(rms_norm(x, mix_norm_pre[l]), sh2, sc2)
        y = hybrid_mixer(h, mix_w_in[l], conv_w[l], a_log[l], dt_bias[l], gdn_norm_w[l],
                         s5_a_re[l], s5_a_im[l], s5_log_dt[l], s5_b_re[l], s5_b_im[l],
                         s5_c_re[l], s5_c_im[l], s5_d[l], s5_w_glu[l], mix_w_out[l])
        x = x + gt2 * rms_norm(y, mix_norm_post[l])
        h = modulate(rms_norm(x, ff2_norm_pre[l]), sh3, sc3)
        x = x + 0.5 * gt3 * rms_norm(swiglu(h, ff2_w_in[l], ff2_w_out[l]), ff2_norm_post[l])
    return x
```

```python
import numpy as np
from contextlib import ExitStack
import concourse.bass as bass
import concourse.mybir as mybir
from concourse.bass_utils import run_bass_kernel_spmd

F32 = mybir.dt.float32
BF16 = mybir.dt.bfloat16
I32 = mybir.dt.int32
AF = mybir.ActivationFunctionType
ALU = mybir.AluOpType

ENGS = ("tensor", "vector", "scalar", "gpsimd", "sync")


class Buf:
    __slots__ = ("w", "r", "name")

    def __init__(self, name=""):
        self.w = []
        self.r = []
        self.name = name


class Prog:
    NDMA = 20

    def __init__(self, nc, stack):
        self.nc = nc
        self.q = {e: [] for e in ENGS}
        self.sems = {}
        self.cnt = {}
        for e in ("tensor", "vector", "scalar", "gpsimd"):
            self.sems[e] = stack.enter_context(nc.semaphore("pg_" + e))
            self.cnt[e] = 0
        self.dma_rr = {}
        for qn in ("sync", "gpsimd", "scalar"):
            self.dma_rr[qn] = 0
            for i in range(self.NDMA):
                k = ("dma", qn, i)
                self.sems[k] = stack.enter_context(nc.semaphore("pd_%s_%d" % (qn, i)))
                self.cnt[k] = 0
        self.seen = {e: {} for e in ENGS}
        self.pending_reads = {e: [] for e in ENGS}

    def _waits(self, eng, deps):
        out = []
        for tok in deps:
            if tok is None:
                continue
            key, val = tok
            if key == "tensor" and eng == "tensor":
                continue
            if self.seen[eng].get(key, 0) >= val:
                continue
            self.seen[eng][key] = val
            out.append((key, val))
        return out

    def op(self, eng, fn, reads=(), writes=(), inc=True):
        deps = []
        for b in reads:
            deps.extend(b.w)
        for b in writes:
            deps.extend(b.w)
            deps.extend(b.r)
        waits = self._waits(eng, deps)
        if inc:
            self.cnt[eng] += 1
            tok = (eng, self.cnt[eng])
        else:
            tok = (eng, self.cnt[eng] + 1)
        sem = self.sems[eng]
        self.q[eng].append((waits, fn, sem if inc else None, 1))
        for b in reads:
            b.r.append(tok)
        for b in writes:
            b.w = [tok]
            b.r = []
        return tok

    def dma(self, qn, fn, reads=(), writes=()):
        deps = []
        for b in reads:
            deps.extend(b.w)
        for b in writes:
            deps.extend(b.w)
            deps.extend(b.r)
        i = self.dma_rr[qn]
        self.dma_rr[qn] = (i + 1) % self.NDMA
        k = ("dma", qn, i)
        if self.cnt[k] > 0:
            deps.append((k, self.cnt[k]))
        waits = self._waits(qn, deps)
        self.cnt[k] += 16
        tok = (k, self.cnt[k])
        self.q[qn].append((waits, fn, self.sems[k], 16))
        for b in reads:
            b.r.append(tok)
        for b in writes:
            b.w = [tok]
            b.r = []
        return tok

    def barrier(self):
        toks = [(k, v) for k, v in self.cnt.items() if v > 0]
        for e in ENGS:
            waits = self._waits(e, toks)
            if waits:
                self.q[e].append((waits, None, None, 0))

    def final_wait(self, eng, bufs):
        deps = []
        for b in bufs:
            deps.extend(b.w)
        waits = self._waits(eng, deps)
        self.q[eng].append((waits, None, None, 0))

    def emit(self):
        nc = self.nc
        sems = self.sems
        with nc.Block() as block:
            def mk(name):
                def body(e):
                    for waits, fn, sem, inc in self.q[name]:
                        for key, val in waits:
                            e.wait_ge(sems[key], val)
                        if fn is not None:
                            ins = fn(e)
                            if sem is not None:
                                ins.then_inc(sem, inc)
                return body
            block.sync(mk("sync"))
            block.tensor(mk("tensor"))
            block.vector(mk("vector"))
            block.scalar(mk("scalar"))
            block.gpsimd(mk("gpsimd"))


def check_deadlock(P):
    pos = {e: 0 for e in ENGS}
    val = {}
    key_of = {id(s): k for k, s in P.sems.items()}
    progress = True
    while progress:
        progress = False
        for e in ENGS:
            q = P.q[e]
            while pos[e] < len(q):
                waits, fn, sem, inc = q[pos[e]]
                if all(val.get(k, 0) >= v for k, v in waits):
                    if sem is not None:
                        k = key_of[id(sem)]
                        val[k] = val.get(k, 0) + inc
                    pos[e] += 1
                    progress = True
                else:
                    break
    ok = all(pos[e] == len(P.q[e]) for e in ENGS)
    if not ok:
        for e in ENGS:
            if pos[e] < len(P.q[e]):
                waits = P.q[e][pos[e]][0]
                print("STUCK", e, pos[e], len(P.q[e]), [(k, v, val.get(k, 0)) for k, v in waits if val.get(k, 0) < v])
    return ok


D = 1024
DFF = 2816
NKC = 8
EPS = 1e-6
SLABS = [(0, 8), (8, 7), (15, 7)]
SLAB_MAX = 8
TT = 512


class FFNRes:
    def __init__(self, nc, stack, P, tag=""):
        self.nc, self.P = nc, P
        sb = lambda name, shape, dt: stack.enter_context(nc.sbuf_tensor(name + tag, shape, dt))
        ps = lambda name, shape, dt: stack.enter_context(nc.psum_tensor(name + tag, shape, dt))
        self.wi = [sb("wi%d" % i, [128, NKC, SLAB_MAX * 256], BF16) for i in range(2)]
        self.wo = [sb("wo%d" % i, [128, SLAB_MAX, D], BF16) for i in range(2)]
        self.slabB = [Buf("slab%d" % i) for i in range(2)]
        self.stg = [sb("stg%d" % i, [128, 1024], F32) for i in range(3)]
        self.stgB = [Buf("stg%d" % i) for i in range(3)]
        self.stg_i = 0
        self.hT = [sb("hT%d" % i, [128, NKC, TT], BF16) for i in range(2)]
        self.hTB = [[Buf("hT%d_%d" % (i, s)) for s in range(4)] for i in range(2)]
        self.aT = sb("aT", [128, SLAB_MAX, TT], BF16)
        self.aTB = [Buf("aT%d" % j) for j in range(SLAB_MAX)]
        self.xa = [sb("xa%d" % i, [128, D], F32) for i in range(2)]
        self.xaB = [Buf("xa%d" % i) for i in range(2)]
        self.xn = [sb("xn%d" % i, [128, D], BF16) for i in range(2)]
        self.xnB = [Buf("xn%d" % i) for i in range(2)]
        self.junk = sb("junk", [128, D], BF16)
        self.junkB = Buf("junk")
        self.ss = [sb("ss%d" % i, [128, 2], F32) for i in range(4)]
        self.ssB = [Buf("ss%d" % i) for i in range(4)]
        self.ss_i = 0
        self.sg = [sb("sg%d" % i, [128, TT], F32) for i in range(2)]
        self.sgB = [Buf("sg%d" % i) for i in range(2)]
        self.yb = [sb("yb%d" % i, [128, D], F32) for i in range(2)]
        self.ybB = [Buf("yb%d" % i) for i in range(2)]
        self.xr = [sb("xr%d" % i, [128, D], F32) for i in range(2)]
        self.xrB = [Buf("xr%d" % i) for i in range(2)]
        self.crow = sb("crow", [128, D], F32)
        self.crowB = Buf("crow")
        self.ctmp = sb("ctmp", [128, D], F32)
        self.ctmpB = Buf("ctmp")
        self.acol = sb("acol", [128, NKC], F32)
        self.bcol = sb("bcol", [128, NKC], F32)
        self.tcol = sb("tcol", [128, NKC], F32)
        self.colB = Buf("col")
        self.vrows = sb("vrows", [8, 3, 128], F32)
        self.vrowsB = Buf("vrows")
        self.identf = sb("identf", [128, 128], F32)
        self.identfB = Buf("identf")
        self.ident = sb("ident_sb", [128, 128], BF16)
        self.epsc = sb("epsc", [128, 1], F32)
        self.epscB = Buf("epsc")
        P.op("vector", lambda e: e.memset(self.epsc[:], D * EPS), writes=[self.epscB])
        self.identB = Buf("ident")
        self.pB = [ps("pB%d" % i, [128, TT], F32) for i in range(4)]
        self.pBB = [Buf("pB%d" % i) for i in range(4)]
        self.pC = [ps("pC%d" % i, [128, TT], F32) for i in range(2)]
        self.pCB = Buf("pC")
        self.pT = [ps("pT%d" % i, [128, NKC, 128], BF16) for i in range(2)]
        self.pTB = [Buf("pT%d" % i) for i in range(2)]

    def load_ident(self, ident_dram):
        P = self.P
        st = self.stg[0]
        P.dma("sync", lambda e: e.dma_start(out=st[:, 0:128], in_=ident_dram), writes=[self.stgB[0]])
        P.dma("sync", lambda e: e.dma_start(out=self.identf[:], in_=ident_dram), writes=[self.identfB])
        P.op("vector", lambda e: e.tensor_copy(out=self.ident[:], in_=st[:, 0:128]),
             reads=[self.stgB[0]], writes=[self.identB])


def load_slab(R, w_in, w_out, slab, buf, dff=DFF, gated=True):
    P = R.P
    j0, n = slab
    wi, wo, sB = R.wi[buf], R.wo[buf], R.slabB[buf]
    pieces = []
    ncol = n * 128
    for kc in range(NKC):
        for part in range(2 if gated else 1):
            c0 = part * dff + j0 * 128
            done = 0
            while done < ncol:
                w = min(1024, ncol - done)
                pieces.append(("in", kc, part * ncol + done, c0 + done, w))
                done += w
    if w_out is not None:
        for j in range(n):
            pieces.append(("out", j, 0, (j0 + j) * 128, 1024))
    for kind, a, dst0, src0, w in pieces:
        si = R.stg_i
        R.stg_i = (si + 1) % 3
        st, stB = R.stg[si], R.stgB[si]
        if kind == "in":
            src = w_in[a * 128:(a + 1) * 128, src0:src0 + w]
            dst = wi[:, a, dst0:dst0 + w]
        else:
            src = w_out[src0:src0 + 128, 0:1024]
            dst = wo[:, a, :]
        P.dma("sync", (lambda e, st=st, src=src, w=w: e.dma_start(out=st[:, 0:w], in_=src)), writes=[stB])
        P.op("gpsimd", (lambda e, st=st, dst=dst, w=w: e.tensor_copy(out=dst, in_=st[:, 0:w])),
             reads=[stB], writes=[sB])


def prep_vectors(R, w_pre, w_post, mod, ioff, gate_scale, modB=None):
    P = R.P
    colv = lambda v, off: v[off:off + D].rearrange("(kc p) -> p kc", p=128)
    rowv = lambda v, off: v[off:off + D].partition_broadcast(128)
    rows = lambda v, off: v[off:off + D].rearrange("(kc p) -> kc p", p=128)
    P.dma("sync", lambda e: e.dma_start(out=R.vrows[:, 0, :], in_=rows(w_pre, 0)), writes=[R.vrowsB])
    P.dma("sync", lambda e: e.dma_start(out=R.vrows[:, 1, :], in_=rows(mod, ioff * D)), reads=(list(modB) if modB else []), writes=[R.vrowsB])
    P.dma("sync", lambda e: e.dma_start(out=R.vrows[:, 2, :], in_=rows(mod, (ioff + 1) * D)), reads=(list(modB) if modB else []), writes=[R.vrowsB])
    pc = R.pC[0]

    def tr(i):
        P.op("tensor", lambda e: e.matmul(pc[:, i * 8:(i + 1) * 8], lhsT=R.vrows[:, i, :], rhs=R.identf[0:8, 0:8], start=True, stop=True),
             reads=[R.vrowsB, R.identfB], writes=[R.pCB], inc=(i == 2))
    for i in range(3):
        tr(i)
    P.op("vector", lambda e: e.tensor_copy(out=R.acol[:], in_=pc[:, 0:8]), reads=[R.pCB], writes=[R.colB])
    P.op("vector", lambda e: e.tensor_copy(out=R.bcol[:], in_=pc[:, 8:16]), reads=[R.pCB], writes=[R.colB])
    P.op("vector", lambda e: e.tensor_copy(out=R.tcol[:], in_=pc[:, 16:24]), reads=[R.pCB], writes=[R.colB])
    P.op("vector", lambda e: e.tensor_scalar(out=R.tcol[:], in0=R.tcol[:], scalar1=1.0, scalar2=32.0, op0=ALU.add, op1=ALU.mult),
         reads=[R.colB], writes=[R.colB])
    P.op("vector", lambda e: e.tensor_tensor(out=R.acol[:], in0=R.acol[:], in1=R.tcol[:], op=ALU.mult),
         reads=[R.colB], writes=[R.colB])
    P.dma("sync", lambda e: e.dma_start(out=R.crow[:], in_=rowv(w_post, 0)), writes=[R.crowB])
    P.dma("sync", lambda e: e.dma_start(out=R.ctmp[:], in_=rowv(mod, (ioff + 2) * D)), reads=(list(modB) if modB else []), writes=[R.ctmpB])
    P.op("vector", lambda e: e.scalar_tensor_tensor(out=R.crow[:], in0=R.crow[:], scalar=32.0 * gate_scale, in1=R.ctmp[:],
                                                    op0=ALU.mult, op1=ALU.mult),
         reads=[R.crowB, R.ctmpB], writes=[R.crowB])


def stage_a_sub(R, X_in, t, s, hbuf, part):
    P = R.P
    i = (t * 4 + s) % 2
    xa, xaB, xn, xnB = R.xa[i], R.xaB[i], R.xn[i], R.xnB[i]
    if part == 0:
        r0 = t * TT + s * 128
        P.dma("sync", lambda e: e.dma_start(out=xa[:], in_=X_in[r0:r0 + 128, :]), writes=[xaB])
        k = R.ss_i
        R.ss_i = (k + 1) % 4
        ss, ssB = R.ss[k], R.ssB[k]
        P.op("scalar", lambda e: e.activation(out=R.junk[:], in_=xa[:], func=AF.Square, accum_out=ss[:, 0:1]),
             reads=[xaB], writes=[R.junkB, ssB])
        P.op("scalar", lambda e: e.activation(out=ss[:, 1:2], in_=ss[:, 0:1], func=AF.Sqrt, bias=R.epsc[:, 0:1], scale=1.0),
             reads=[ssB, R.epscB], writes=[ssB])
        P.op("vector", lambda e: e.reciprocal(out=ss[:, 1:2], in_=ss[:, 1:2]), reads=[ssB], writes=[ssB])
        P.op("scalar", lambda e: e.activation(out=xn[:], in_=xa[:], func=AF.Copy, scale=ss[:, 1:2]),
             reads=[xaB, ssB], writes=[xnB])
    else:
        pt, ptB = R.pT[i], R.pTB[i]
        for kc in range(NKC):
            P.op("tensor", (lambda e, kc=kc: e.transpose(out=pt[:, kc, :], in_=xn[:, kc * 128:(kc + 1) * 128], identity=R.ident[:])),
                 reads=[xnB, R.identB], writes=[ptB], inc=(kc == NKC - 1))
        hT, hB = R.hT[hbuf], R.hTB[hbuf][s]
        for kc in range(NKC):
            eng = "vector" if kc % 2 == 0 else "gpsimd"
            if eng == "gpsimd":
                eng = "vector"
            P.op(eng, (lambda e, kc=kc: e.tensor_scalar(out=hT[:, kc, s * 128:(s + 1) * 128], in0=pt[:, kc, :],
                                                       scalar1=R.acol[:, kc:kc + 1], scalar2=R.bcol[:, kc:kc + 1],
                                                       op0=ALU.mult, op1=ALU.add)),
                 reads=[ptB, R.colB], writes=[hB])


def stage_b_group(R, wi, sB, hT, hTBl, j, nch, gidx, aj=None):
    P = R.P
    if aj is None:
        aj = j
    k = gidx % 2
    pg, pu, pgB, puB = R.pB[2 * k], R.pB[2 * k + 1], R.pBB[2 * k], R.pBB[2 * k + 1]
    sg, sgB = R.sg[k], R.sgB[k]

    def mm(pp, ppB, c0, kc):
        P.op("tensor", lambda e: e.matmul(pp[:], lhsT=wi[:, kc, c0:c0 + 128], rhs=hT[:, kc, :],
                                          start=(kc == 0), stop=(kc == NKC - 1)),
             reads=[sB] + hTBl, writes=[ppB], inc=(kc == NKC - 1))
    for (pp, ppB, c0) in ((pg, pgB, j * 128), (pu, puB, nch * 128 + j * 128)):
        for kc in range(NKC):
            mm(pp, ppB, c0, kc)
    P.op("scalar", lambda e: e.activation(out=sg[:], in_=pg[:], func=AF.Silu), reads=[pgB], writes=[sgB])
    P.op("vector", lambda e: e.tensor_tensor(out=R.aT[:, aj, :], in0=sg[:], in1=pu[:], op=ALU.mult),
         reads=[sgB, puB], writes=[R.aTB[aj]])


def stage_c_sub(R, wo, sB, nch, X_in, X_out, Yacc, yB, t, s, sl, last, lhs=None):
    P = R.P
    if lhs is None:
        lhs = [(R.aT, j, R.aTB[j]) for j in range(nch)]
    r0 = t * TT + s * 128
    yi = (t * 4 + s) % 2
    yb, ybB, xr, xrB = R.yb[yi], R.ybB[yi], R.xr[yi], R.xrB[yi]
    if sl > 0:
        P.dma("sync", lambda e: e.dma_start(out=yb[:], in_=Yacc[r0:r0 + 128, :]), reads=[yB], writes=[ybB])
    if last:
        P.dma("sync", lambda e: e.dma_start(out=xr[:], in_=X_in[r0:r0 + 128, :]), writes=[xrB])

    def mm(half, j):
        lt, li, lB = lhs[j]
        P.op("tensor", lambda e: e.matmul(R.pC[half][:], lhsT=lt[:, li, s * 128:(s + 1) * 128],
                                          rhs=wo[:, j, half * 512:(half + 1) * 512],
                                          start=(j == 0), stop=(j == nch - 1)),
             reads=[sB] + (lB if isinstance(lB, list) else [lB]), writes=[R.pCB], inc=(j == nch - 1))
    for half in range(2):
        for j in range(nch):
            mm(half, j)

    def evac(half):
        hs = slice(half * 512, (half + 1) * 512)
        if sl == 0:
            if half == 0:
                P.op("vector", lambda e: e.tensor_copy(out=yb[:, hs], in_=R.pC[half][:]), reads=[R.pCB], writes=[ybB])
            else:
                P.op("scalar", lambda e: e.copy(out=yb[:, hs], in_=R.pC[half][:]), reads=[R.pCB], writes=[ybB])
        else:
            P.op("vector", lambda e: e.tensor_tensor(out=yb[:, hs], in0=yb[:, hs], in1=R.pC[half][:], op=ALU.add),
                 reads=[R.pCB, ybB], writes=[ybB])
    evac(0)
    evac(1)
    if not last:
        P.dma("gpsimd", lambda e: e.dma_start(out=Yacc[r0:r0 + 128, :], in_=yb[:]), reads=[ybB], writes=[yB])
    else:
        k = R.ss_i
        R.ss_i = (k + 1) % 4
        ss, ssB = R.ss[k], R.ssB[k]
        P.op("scalar", lambda e: e.activation(out=R.junk[:], in_=yb[:], func=AF.Square, accum_out=ss[:, 0:1]),
             reads=[ybB], writes=[R.junkB, ssB])
        P.op("scalar", lambda e: e.activation(out=ss[:, 1:2], in_=ss[:, 0:1], func=AF.Sqrt, bias=R.epsc[:, 0:1], scale=1.0),
             reads=[ssB, R.epscB], writes=[ssB])
        P.op("vector", lambda e: e.reciprocal(out=ss[:, 1:2], in_=ss[:, 1:2]), reads=[ssB], writes=[ssB])
        P.op("scalar", lambda e: e.activation(out=yb[:], in_=yb[:], func=AF.Copy, scale=ss[:, 1:2]),
             reads=[ybB, ssB], writes=[ybB])
        P.op("gpsimd", lambda e: e.tensor_tensor(out=yb[:], in0=yb[:], in1=R.crow[:], op=ALU.mult),
             reads=[ybB, R.crowB], writes=[ybB])
        P.op("vector", lambda e: e.tensor_tensor(out=xr[:], in0=xr[:], in1=yb[:], op=ALU.add),
             reads=[ybB, xrB], writes=[xrB])
        P.dma("gpsimd", lambda e: e.dma_start(out=X_out[r0:r0 + 128, :], in_=xr[:]), reads=[xrB], writes=[yB])


def ffn_phase(R, X_in, X_out, Yacc, w_in, w_out, NT, first_slab_loaded=False, next_loader=None):
    P = R.P
    ntile = NT // TT
    nsl = len(SLABS)
    if not first_slab_loaded:
        load_slab(R, w_in, w_out, SLABS[0], 0)
    YB = [Buf("Y%d" % i) for i in range(ntile * 4)]
    gidx = 0
    for sl in range(nsl):
        buf = sl % 2
        j0, nch = SLABS[sl]
        last = sl == nsl - 1
        if sl + 1 < nsl:
            load_slab(R, w_in, w_out, SLABS[sl + 1], (sl + 1) % 2)
        elif next_loader is not None:
            next_loader((sl + 1) % 2)
        for s in range(4):
            stage_a_sub(R, X_in, 0, s, 0, 0)
            stage_a_sub(R, X_in, 0, s, 0, 1)
        for t in range(ntile):
            hb = t % 2
            for j in range(nch):
                stage_b_group(R, R.wi[buf], R.slabB[buf], R.hT[hb], R.hTB[hb], j, nch, gidx)
                gidx += 1
                if t + 1 < ntile and j < 8:
                    stage_a_sub(R, X_in, t + 1, j // 2, 1 - hb, j % 2)
            if t + 1 < ntile:
                for jj in range(nch, 8):
                    stage_a_sub(R, X_in, t + 1, jj // 2, 1 - hb, jj % 2)
            for s in range(4):
                stage_c_sub(R, R.wo[buf], R.slabB[buf], nch, X_in, X_out, Yacc, YB[t * 4 + s], t, s, sl, last)
    return YB


NH = 4
NCST = 384
STOP = 0


class StopEmit(Exception):
    pass


def stop_at(k):
    if STOP == k:
        raise StopEmit()


def make_consts():
    c = np.zeros((128, NCST), np.float32)
    c[:, 0:128] = np.eye(128)
    c[:, 128:256] = 1.0
    k = np.arange(64)
    c[0:64, 256:320] = (k[:, None] <= k[None, :])
    c[0:64, 320:384] = (k[:, None] > k[None, :])
    return c


def load_cols(R, w, c0, ncol, buf):
    P = R.P
    wi, sB = R.wi[buf], R.slabB[buf]

    def piece(kc, d0, w_):
        si = R.stg_i
        R.stg_i = (si + 1) % 3
        st, stB = R.stg[si], R.stgB[si]
        P.dma("sync", lambda e: e.dma_start(out=st[:, 0:w_], in_=w[kc * 128:(kc + 1) * 128, c0 + d0:c0 + d0 + w_]), writes=[stB])
        P.op("gpsimd", lambda e: e.tensor_copy(out=wi[:, kc, d0:d0 + w_], in_=st[:, 0:w_]), reads=[stB], writes=[sB])
    for kc in range(NKC):
        d0 = 0
        while d0 < ncol:
            w_ = min(1024, ncol - d0)
            piece(kc, d0, w_)
            d0 += w_


def proj_phase(R, X_in, w_mix, PT, NT, PTB):
    P = R.P
    load_cols(R, w_mix, 0, 2048, 0)
    load_cols(R, w_mix, 2048, 520, 1)
    ntile = NT // TT
    chunks = [(0, j * 128, 128, j * 128) for j in range(16)]
    chunks += [(1, 8 + j * 128, 128, 2056 + j * 128) for j in range(4)]
    chunks += [(1, 0, 8, 2048)]
    gi = 0
    for s in range(4):
        stage_a_sub(R, X_in, 0, s, 0, 0)
        stage_a_sub(R, X_in, 0, s, 0, 1)

    def out_chunk(t, hb, buf, lc, M, row, gi):
        pp, ppB = R.pB[gi % 4], R.pBB[gi % 4]
        sg, sgB = R.sg[gi % 2], R.sgB[gi % 2]
        wi, sB, hT = R.wi[buf], R.slabB[buf], R.hT[hb]

        def mm(kc):
            P.op("tensor", lambda e: e.matmul(pp[0:M, :], lhsT=wi[:, kc, lc:lc + M], rhs=hT[:, kc, :], start=(kc == 0), stop=(kc == NKC - 1)),
                 reads=[sB] + R.hTB[hb], writes=[ppB], inc=(kc == NKC - 1))
        for kc in range(NKC):
            mm(kc)
        if gi % 2 == 0:
            P.op("vector", lambda e: e.tensor_copy(out=sg[0:M, :], in_=pp[0:M, :]), reads=[ppB], writes=[sgB])
        else:
            P.op("scalar", lambda e: e.copy(out=sg[0:M, :], in_=pp[0:M, :]), reads=[ppB], writes=[sgB])
        P.dma("gpsimd", lambda e: e.dma_start(out=PT[row:row + M, t * TT:(t + 1) * TT], in_=sg[0:M, :]), reads=[sgB], writes=[PTB[t]])
    for t in range(ntile):
        hb = t % 2
        for ci, (buf, lc, M, row) in enumerate(chunks):
            out_chunk(t, hb, buf, lc, M, row, gi)
            gi += 1
            if t + 1 < ntile and ci < 8:
                stage_a_sub(R, X_in, t + 1, ci // 2, 1 - hb, ci % 2)


class MixRes:
    def __init__(self, nc, stack, P, tag=""):
        self.nc, self.P = nc, P
        self._sb = lambda name, shape, dt=F32: stack.enter_context(nc.sbuf_tensor("m_" + name + tag, shape, dt))
        self._ps = lambda name, shape, dt=F32: stack.enter_context(nc.psum_tensor("m_" + name + tag, shape, dt))
        self.bufs = {}

    def sb(self, name, shape, dt=F32):
        t = self._sb(name, shape, dt)
        self.bufs[name] = Buf(name)
        setattr(self, name, t)
        setattr(self, name + "B", self.bufs[name])
        return t

    def ps(self, name, shape, dt=F32):
        t = self._ps(name, shape, dt)
        self.bufs[name] = Buf(name)
        setattr(self, name, t)
        setattr(self, name + "B", self.bufs[name])
        return t


def gdn_phase(M, PT, YT, conv_w, a_log, dt_bias, gnw, cst_dram, NT, PTB, YTB):
    P = M.P
    nc = M.nc
    ntile = NT // TT
    sb, ps = M.sb, M.ps
    cst = sb("cst", [128, NCST])
    ident = cst[:, 0:128]
    ones = cst[:, 128:256]
    LE = cst[0:64, 256:320]
    GT = cst[0:64, 320:384]
    P.dma("sync", lambda e: e.dma_start(out=cst[:], in_=cst_dram), writes=[M.cstB])
    sb("LE4", [64, NH, 64]); sb("GT4", [64, NH, 64]); sb("I4", [64, NH, 64])
    for h in range(NH):
        P.op("vector", (lambda e, h=h: e.tensor_copy(out=M.LE4[:, h, :], in_=LE)), reads=[M.cstB], writes=[M.LE4B])
        P.op("vector", (lambda e, h=h: e.tensor_copy(out=M.GT4[:, h, :], in_=GT)), reads=[M.cstB], writes=[M.GT4B])
        P.op("vector", (lambda e, h=h: e.tensor_copy(out=M.I4[:, h, :], in_=cst[0:64, 0:64])), reads=[M.cstB], writes=[M.I4B])
    sb("epsk", [128, 1]); sb("epsq", [128, 1]); sb("epsn", [128, 1]); sb("one1", [128, 1])
    P.op("vector", lambda e: e.memset(M.epsk[:], 1e-6), writes=[M.epskB])
    P.op("vector", lambda e: e.memset(M.epsq[:], 128e-6), writes=[M.epsqB])
    P.op("vector", lambda e: e.memset(M.epsn[:], 1e-6), writes=[M.epsnB])
    P.op("vector", lambda e: e.memset(M.one1[:], 1.0), writes=[M.one1B])
    ps("psS", [128, 512]); ps("psD", [64, 2, NH, 64]); ps("psG", [64, 2, NH, 64]); ps("psI", [64, 2, NH, 64])
    ps("psU", [64, 2, NH, 64]); ps("psX", [128, 512]); ps("psY", [128, 512]); ps("psZ", [128, 512])
    sb("cwr", [4, 1536]); sb("cw", [128, 12, 4])
    P.dma("sync", lambda e: e.dma_start(out=M.cwr[:], in_=conv_w), writes=[M.cwrB])
    for ct in range(12):
        P.op("tensor", (lambda e, ct=ct: e.matmul(M.psS[:, ct * 4:(ct + 1) * 4], lhsT=M.cwr[0:4, ct * 128:(ct + 1) * 128], rhs=cst[0:4, 0:4], start=True, stop=True)),
             reads=[M.cwrB, M.cstB], writes=[M.psSB], inc=(ct == 11))
    P.op("vector", lambda e: e.tensor_copy(out=M.cw[:].rearrange("p a b -> p (a b)"), in_=M.psS[:, 0:48]), reads=[M.psSB], writes=[M.cwB])
    sb("gnr", [1, 128]); sb("gnc", [128, 1])
    P.dma("sync", lambda e: e.dma_start(out=M.gnr[:], in_=gnw.rearrange("(o f) -> o f", o=1)), writes=[M.gnrB])
    P.op("tensor", lambda e: e.matmul(M.psS[:, 0:1], lhsT=M.gnr[0:1, :], rhs=cst[0:1, 0:1], start=True, stop=True), reads=[M.gnrB, M.cstB], writes=[M.psSB])
    P.op("vector", lambda e: e.tensor_copy(out=M.gnc[:], in_=M.psS[:, 0:1]), reads=[M.psSB], writes=[M.gncB])
    sb("nA", [64, NH]); sb("dtb", [64, NH])
    sb("adr", [1, 8])
    P.dma("sync", lambda e: e.dma_start(out=M.adr[:, 0:4], in_=a_log.rearrange("(o f) -> o f", o=1)), writes=[M.adrB])
    P.dma("sync", lambda e: e.dma_start(out=M.adr[:, 4:8], in_=dt_bias.rearrange("(o f) -> o f", o=1)), writes=[M.adrB])
    P.op("tensor", lambda e: e.matmul(M.psS[0:64, 0:8], lhsT=cst[0:1, 128:192], rhs=M.adr[0:1, :], start=True, stop=True), reads=[M.adrB, M.cstB], writes=[M.psSB])
    P.op("vector", lambda e: e.tensor_copy(out=M.nA[:], in_=M.psS[0:64, 0:4]), reads=[M.psSB], writes=[M.nAB])
    P.op("vector", lambda e: e.tensor_copy(out=M.dtb[:], in_=M.psS[0:64, 4:8]), reads=[M.psSB], writes=[M.dtbB])
    P.op("scalar", lambda e: e.activation(out=M.nA[:], in_=M.nA[:], func=AF.Exp), reads=[M.nAB], writes=[M.nAB])
    P.op("vector", lambda e: e.tensor_scalar(out=M.nA[:], in0=M.nA[:], scalar1=-1.0, scalar2=None, op0=ALU.mult), reads=[M.nAB], writes=[M.nAB])
    stop_at(1)
    sb("qkv", [128, 12, TT]); sb("xin", [128, TT + 3]); sb("acc", [128, TT]); sb("sq", [128, TT]); sb("rn", [128, TT])
    sb("sz", [128, NH, TT]); sb("yg", [128, NH, TT]); sb("bar", [8, TT])
    sb("S", [128, NH, 128])
    P.op("vector", lambda e: e.memset(M.S[:], 0.0), writes=[M.SB])
    sb("ba", [64, 8]); sb("bt", [64, NH]); sb("nbt", [64, NH]); sb("g", [64, NH]); sb("gcs", [64, NH]); sb("egc", [64, NH])
    sb("egl", [128, NH]); sb("egd", [64, NH]); sb("begc", [64, NH])
    sb("G12", [64, 2, NH, 64]); sb("eD", [64, 2, NH, 64]); sb("dec", [64, 2, NH, 64])
    sb("AA", [64, 2, NH, 64]); sb("PU", [64, NH, 64]); sb("attT", [64, NH, 64]); sb("tL", [64, NH, 64])
    sb("vb", [64, NH, 128]); sb("kbg", [64, NH, 128]); sb("kst", [64, NH, 128]); sb("wv", [64, NH, 128]); sb("kcT", [128, NH, 64])
    sb("vn", [64, NH, 128]); sb("o1", [64, NH, 128]); sb("osq", [64, NH, 128]); sb("on", [64, NH, 128]); sb("ssq", [64, 2 * NH])

    def conv_tile(t, ct):
        t0 = t * TT
        r0 = ct * 128
        if t == 0:
            P.op("gpsimd", lambda e: e.memset(M.xin[:, 0:3], 0.0), writes=[M.xinB])
            P.dma("sync", lambda e: e.dma_start(out=M.xin[:, 3:TT + 3], in_=PT[r0:r0 + 128, 0:TT]), reads=[PTB[0]], writes=[M.xinB])
        else:
            P.dma("sync", lambda e: e.dma_start(out=M.xin[:], in_=PT[r0:r0 + 128, t0 - 3:t0 + TT]), reads=[PTB[t - 1], PTB[t]], writes=[M.xinB])
        P.op("vector", lambda e: e.tensor_scalar(out=M.acc[:], in0=M.xin[:, 0:TT], scalar1=M.cw[:, ct, 0:1], scalar2=None, op0=ALU.mult),
             reads=[M.xinB, M.cwB], writes=[M.accB])
        for j in range(1, 4):
            P.op("vector", (lambda e, j=j: e.scalar_tensor_tensor(out=M.acc[:], in0=M.xin[:, j:j + TT], scalar=M.cw[:, ct, j:j + 1], in1=M.acc[:],
                                                                  op0=ALU.mult, op1=ALU.add)), reads=[M.xinB, M.cwB, M.accB], writes=[M.accB])
        P.op("scalar", lambda e: e.activation(out=M.qkv[:, ct, :], in_=M.acc[:], func=AF.Silu), reads=[M.accB], writes=[M.qkvB])
        if ct < 8:
            P.op("scalar", lambda e: e.activation(out=M.sq[:], in_=M.qkv[:, ct, :], func=AF.Square), reads=[M.qkvB], writes=[M.sqB])
            P.op("tensor", lambda e: e.matmul(M.psX[:], lhsT=ones, rhs=M.sq[:], start=True, stop=True), reads=[M.sqB, M.cstB], writes=[M.psXB])
            if ct < 4:
                P.op("scalar", lambda e: e.activation(out=M.rn[:], in_=M.psX[:], func=AF.Sqrt, bias=M.epsq[:, 0:1], scale=128.0),
                     reads=[M.psXB, M.epsqB], writes=[M.rnB])
            else:
                P.op("scalar", lambda e: e.activation(out=M.rn[:], in_=M.psX[:], func=AF.Sqrt, bias=M.epsk[:, 0:1], scale=1.0),
                     reads=[M.psXB, M.epskB], writes=[M.rnB])
            P.op("vector", lambda e: e.reciprocal(out=M.rn[:], in_=M.rn[:]), reads=[M.rnB], writes=[M.rnB])
            P.op("gpsimd", lambda e: e.tensor_tensor(out=M.qkv[:, ct, :], in0=M.qkv[:, ct, :], in1=M.rn[:], op=ALU.mult),
                 reads=[M.qkvB, M.rnB], writes=[M.qkvB])

    def z_tile(t, h):
        t0 = t * TT
        P.dma("sync", lambda e: e.dma_start(out=M.sz[:, h, :], in_=PT[1536 + h * 128:1536 + (h + 1) * 128, t0:t0 + TT]), reads=[PTB[t]], writes=[M.szB])
        P.op("scalar", lambda e: e.activation(out=M.sz[:, h, :], in_=M.sz[:, h, :], func=AF.Silu), reads=[M.szB], writes=[M.szB])
        P.op("gpsimd", lambda e: e.tensor_scalar(out=M.sz[:, h, :], in0=M.sz[:, h, :], scalar1=M.gnc[:, 0:1], scalar2=None, op0=ALU.mult),
             reads=[M.szB, M.gncB], writes=[M.szB])

    def mm(out, lhsT, rhs, reads, writes, inc=True):
        P.op("tensor", lambda e: e.matmul(out, lhsT=lhsT, rhs=rhs, start=True, stop=True), reads=reads, writes=writes, inc=inc)

    def chunk(n):
        c0 = n * 64
        cs = slice(c0, c0 + 64)
        qT = lambda h: M.qkv[:, h, cs]
        kT = lambda h: M.qkv[:, 4 + h, cs]
        vT = lambda h: M.qkv[:, 8 + h, cs]
        mm(M.psS[0:64, 0:8], M.bar[0:8, cs], cst[0:8, 0:8], [M.barB, M.cstB], [M.psSB])
        P.op("vector", lambda e: e.tensor_copy(out=M.ba[:], in_=M.psS[0:64, 0:8]), reads=[M.psSB], writes=[M.baB])
        P.op("scalar", lambda e: e.activation(out=M.bt[:], in_=M.ba[:, 0:4], func=AF.Sigmoid), reads=[M.baB], writes=[M.btB])
        P.op("vector", lambda e: e.tensor_scalar(out=M.nbt[:], in0=M.bt[:], scalar1=-1.0, scalar2=None, op0=ALU.mult), reads=[M.btB], writes=[M.nbtB])
        P.op("vector", lambda e: e.tensor_tensor(out=M.g[:], in0=M.ba[:, 4:8], in1=M.dtb[:], op=ALU.add), reads=[M.baB, M.dtbB], writes=[M.gB])
        P.op("scalar", lambda e: e.activation(out=M.g[:], in_=M.g[:], func=AF.Exp), reads=[M.gB], writes=[M.gB])
        P.op("scalar", lambda e: e.activation(out=M.g[:], in_=M.g[:], func=AF.Ln, bias=M.one1[0:64, 0:1], scale=1.0), reads=[M.gB, M.one1B], writes=[M.gB])
        P.op("vector", lambda e: e.tensor_tensor(out=M.g[:], in0=M.g[:], in1=M.nA[:], op=ALU.mult), reads=[M.gB, M.nAB], writes=[M.gB])
        stop_at(4)
        mm(M.psS[0:64, 8:12], LE, M.g[:], [M.gB, M.cstB], [M.psSB], inc=False)
        mm(M.psS[:, 12:16], cst[0:64, 128:256], M.g[:], [M.gB, M.cstB], [M.psSB])
        P.op("vector", lambda e: e.tensor_copy(out=M.gcs[:], in_=M.psS[0:64, 8:12]), reads=[M.psSB], writes=[M.gcsB])
        P.op("scalar", lambda e: e.activation(out=M.egc[:], in_=M.psS[0:64, 8:12], func=AF.Exp), reads=[M.psSB], writes=[M.egcB])
        P.op("scalar", lambda e: e.activation(out=M.egl[:], in_=M.psS[:, 12:16], func=AF.Exp), reads=[M.psSB], writes=[M.eglB])
        P.op("vector", lambda e: e.tensor_tensor(out=M.egd[:], in0=M.psS[0:64, 12:16], in1=M.gcs[:], op=ALU.subtract), reads=[M.psSB, M.gcsB], writes=[M.egdB])
        P.op("scalar", lambda e: e.activation(out=M.egd[:], in_=M.egd[:], func=AF.Exp), reads=[M.egdB], writes=[M.egdB])
        P.op("vector", lambda e: e.tensor_tensor(out=M.begc[:], in0=M.bt[:], in1=M.egc[:], op=ALU.mult), reads=[M.btB, M.egcB], writes=[M.begcB])
        stop_at(5)
        for h in range(NH):
            P.op("gpsimd", (lambda e, h=h: e.tensor_scalar(out=M.G12[:, 0, h, :], in0=LE, scalar1=M.g[:, h:h + 1], scalar2=None, op0=ALU.mult)),
                 reads=[M.gB, M.cstB], writes=[M.G12B])
            P.op("gpsimd", (lambda e, h=h: e.tensor_scalar(out=M.G12[:, 1, h, :], in0=GT, scalar1=M.g[:, h:h + 1], scalar2=None, op0=ALU.mult)),
                 reads=[M.gB, M.cstB], writes=[M.G12B])
        for h in range(NH):
            mm(M.psD[:, 0, h, :], M.G12[:, 0, h, :], GT, [M.G12B, M.cstB], [M.psDB], inc=False)
            mm(M.psD[:, 1, h, :], M.G12[:, 1, h, :], LE, [M.G12B, M.cstB], [M.psDB], inc=(h == NH - 1))
        P.op("scalar", lambda e: e.activation(out=M.eD[:], in_=M.psD[:], func=AF.Exp), reads=[M.psDB], writes=[M.eDB])
        P.op("gpsimd", lambda e: e.tensor_tensor(out=M.dec[:, 0], in0=M.eD[:, 0], in1=M.GT4[:], op=ALU.mult), reads=[M.eDB, M.GT4B], writes=[M.decB])
        P.op("gpsimd", lambda e: e.tensor_tensor(out=M.dec[:, 1], in0=M.eD[:, 1], in1=M.LE4[:], op=ALU.mult), reads=[M.eDB, M.LE4B], writes=[M.decB])
        stop_at(6)
        for h in range(NH):
            mm(M.psG[:, 0, h, :], kT(h), kT(h), [M.qkvB], [M.psGB], inc=False)
            mm(M.psU[:, 1, h, :], kT(h), qT(h), [M.qkvB], [M.psUB], inc=(h == NH - 1))
        P.op("vector", lambda e: e.tensor_tensor(out=M.tL[:], in0=M.psG[:, 0], in1=M.dec[:, 0], op=ALU.mult), reads=[M.psGB, M.decB], writes=[M.tLB])
        P.op("vector", lambda e: e.tensor_tensor(out=M.attT[:], in0=M.psU[:, 1], in1=M.dec[:, 1], op=ALU.mult), reads=[M.psUB, M.decB], writes=[M.attTB])
        for h in range(NH):
            P.op("gpsimd", (lambda e, h=h: e.tensor_scalar(out=M.AA[:, 1, h, :], in0=M.tL[:, h, :], scalar1=M.nbt[:, h:h + 1], scalar2=None, op0=ALU.mult)),
                 reads=[M.tLB, M.nbtB], writes=[M.AAB])
        for h in range(NH):
            mm(M.psG[:, 1, h, :], M.AA[:, 1, h, :], cst[0:64, 0:64], [M.AAB, M.cstB], [M.psGB], inc=(h == NH - 1))
        P.op("vector", lambda e: e.tensor_copy(out=M.AA[:, 0], in_=M.psG[:, 1]), reads=[M.psGB], writes=[M.AAB])
        stop_at(7)
        P.op("vector", lambda e: e.tensor_tensor(out=M.PU[:], in0=M.AA[:, 0], in1=M.I4[:], op=ALU.add), reads=[M.AAB, M.I4B], writes=[M.PUB])
        for m in range(5):
            for h in range(NH):
                mm(M.psI[:, 0, h, :], M.AA[:, 1, h, :], M.AA[:, 0, h, :], [M.AAB], [M.psIB], inc=False)
                mm(M.psI[:, 1, h, :], M.AA[:, 0, h, :], M.AA[:, 1, h, :], [M.AAB], [M.psIB], inc=(h == NH - 1))
            P.op("vector", lambda e: e.tensor_copy(out=M.AA[:], in_=M.psI[:]), reads=[M.psIB], writes=[M.AAB])
            for h in range(NH):
                mm(M.psU[:, 0, h, :], M.AA[:, 1, h, :], M.PU[:, h, :], [M.AAB, M.PUB], [M.psUB], inc=(h == NH - 1))
            P.op("vector", lambda e: e.tensor_tensor(out=M.PU[:], in0=M.PU[:], in1=M.psU[:, 0], op=ALU.add), reads=[M.psUB, M.PUB], writes=[M.PUB])
        stop_at(8)
        X3 = M.psX[0:64, :].rearrange("p (h d) -> p h d", h=NH)
        Y3 = M.psY[0:64, :].rearrange("p (h d) -> p h d", h=NH)
        Z3 = M.psZ[0:64, :].rearrange("p (h d) -> p h d", h=NH)
        def tr(out, in_, wB, inc):
            P.op("tensor", lambda e: e.transpose(out=out, in_=in_, identity=ident), reads=[M.qkvB, M.cstB], writes=[wB], inc=inc)
        for h in range(NH):
            tr(X3[:, h, :], kT(h), M.psXB, False)
            tr(Y3[:, h, :], vT(h), M.psYB, h == NH - 1)
        stop_at(84)
        for h in range(NH):
            P.op("vector", (lambda e, h=h: e.tensor_scalar(out=M.vb[:, h, :], in0=Y3[:, h, :], scalar1=M.bt[:, h:h + 1], scalar2=None, op0=ALU.mult)),
                 reads=[M.psYB, M.btB], writes=[M.vbB])
            P.op("vector", (lambda e, h=h: e.tensor_scalar(out=M.kbg[:, h, :], in0=X3[:, h, :], scalar1=M.begc[:, h:h + 1], scalar2=None, op0=ALU.mult)),
                 reads=[M.psXB, M.begcB], writes=[M.kbgB])
            P.op("vector", (lambda e, h=h: e.tensor_scalar(out=M.kst[:, h, :], in0=X3[:, h, :], scalar1=M.egd[:, h:h + 1], scalar2=None, op0=ALU.mult)),
                 reads=[M.psXB, M.egdB], writes=[M.kstB])
        stop_at(85)
        YK = M.psY[:, 0:256].rearrange("p (h d) -> p h d", h=NH)
        for h in range(NH):
            mm(X3[:, h, :], M.PU[:, h, :], M.vb[:, h, :], [M.PUB, M.vbB], [M.psXB], inc=False)
            mm(YK[:, h, :], M.kbg[:, h, :], M.PU[:, h, :], [M.PUB, M.kbgB], [M.psYB], inc=(h == NH - 1))
        P.op("vector", lambda e: e.tensor_copy(out=M.wv[:], in_=X3), reads=[M.psXB], writes=[M.wvB])
        P.op("scalar", lambda e: e.copy(out=M.kcT[:], in_=YK), reads=[M.psYB], writes=[M.kcTB])
        stop_at(9)
        for h in range(NH):
            mm(X3[:, h, :], M.kcT[:, h, :], M.S[:, h, :], [M.kcTB, M.SB], [M.psXB], inc=(h == NH - 1))
        P.op("vector", lambda e: e.tensor_tensor(out=M.vn[:], in0=M.wv[:], in1=X3, op=ALU.subtract), reads=[M.wvB, M.psXB], writes=[M.vnB])
        for h in range(NH):
            mm(Y3[:, h, :], qT(h), M.S[:, h, :], [M.qkvB, M.SB], [M.psYB], inc=False)
            mm(Z3[:, h, :], M.attT[:, h, :], M.vn[:, h, :], [M.attTB, M.vnB], [M.psZB], inc=(h == NH - 1))
        for h in range(NH):
            P.op("vector", (lambda e, h=h: e.tensor_scalar(out=M.o1[:, h, :], in0=Y3[:, h, :], scalar1=M.egc[:, h:h + 1], scalar2=None, op0=ALU.mult)),
                 reads=[M.psYB, M.egcB], writes=[M.o1B])
        P.op("vector", lambda e: e.tensor_tensor(out=M.o1[:], in0=M.o1[:], in1=Z3, op=ALU.add), reads=[M.o1B, M.psZB], writes=[M.o1B])
        XS = M.psX[:, :].rearrange("p (h d) -> p h d", h=NH)
        for h in range(NH):
            mm(XS[:, h, :], M.kst[:, h, :], M.vn[:, h, :], [M.kstB, M.vnB], [M.psXB], inc=(h == NH - 1))
        for h in range(NH):
            P.op("gpsimd", (lambda e, h=h: e.tensor_scalar(out=M.S[:, h, :], in0=M.S[:, h, :], scalar1=M.egl[:, h:h + 1], scalar2=None, op0=ALU.mult)),
                 reads=[M.SB, M.eglB], writes=[M.SB])
        P.op("vector", lambda e: e.tensor_tensor(out=M.S[:], in0=M.S[:], in1=XS, op=ALU.add), reads=[M.SB, M.psXB], writes=[M.SB])
        stop_at(10)
        P.op("gpsimd", lambda e: e.tensor_tensor(out=M.osq[:], in0=M.o1[:], in1=M.o1[:], op=ALU.mult), reads=[M.o1B], writes=[M.osqB])
        P.op("vector", lambda e: e.reduce_sum(out=M.ssq[:, 0:NH], in_=M.osq[:], axis=mybir.AxisListType.X), reads=[M.osqB], writes=[M.ssqB])
        P.op("scalar", lambda e: e.activation(out=M.ssq[:, NH:2 * NH], in_=M.ssq[:, 0:NH], func=AF.Sqrt, bias=M.epsn[0:64, 0:1], scale=1.0 / 128),
             reads=[M.ssqB, M.epsnB], writes=[M.ssqB])
        P.op("vector", lambda e: e.reciprocal(out=M.ssq[:, NH:2 * NH], in_=M.ssq[:, NH:2 * NH]), reads=[M.ssqB], writes=[M.ssqB])
        for h in range(NH):
            P.op("gpsimd", (lambda e, h=h: e.tensor_scalar(out=M.on[:, h, :], in0=M.o1[:, h, :], scalar1=M.ssq[:, NH + h:NH + h + 1], scalar2=None, op0=ALU.mult)),
                 reads=[M.o1B, M.ssqB], writes=[M.onB])
        ZT = M.psZ[:, 0:256].rearrange("p (h d) -> p h d", h=NH)
        for h in range(NH):
            mm(ZT[:, h, :], M.on[:, h, :], cst[0:64, 0:64], [M.onB, M.cstB], [M.psZB], inc=(h == NH - 1))
        P.op("vector", lambda e: e.tensor_tensor(out=M.yg[:, :, cs], in0=ZT, in1=M.sz[:, :, cs], op=ALU.mult), reads=[M.psZB, M.szB], writes=[M.ygB])

    for t in range(ntile):
        t0 = t * TT
        for ct in range(12):
            conv_tile(t, ct)
        stop_at(2)
        for h in range(NH):
            z_tile(t, h)
        stop_at(3)
        P.dma("sync", (lambda e, t0=t0: e.dma_start(out=M.bar[:], in_=PT[2048:2056, t0:t0 + TT])), reads=[PTB[t]], writes=[M.barB])
        for n in range(8):
            chunk(n)
        for h in range(NH):
            P.dma("gpsimd", (lambda e, h=h, t0=t0: e.dma_start(out=YT[h * 128:(h + 1) * 128, t0:t0 + TT], in_=M.yg[:, h, :])), reads=[M.ygB], writes=[YTB[t]])


NCST2 = 129 + 128 + 16
SEG = 128
TWO_PI = 6.283185307179586


def make_consts2():
    c = np.zeros((128, NCST2), np.float32)
    c[:, 0:129] = np.arange(129)[None, :]
    g = np.arange(32)
    m = np.arange(128)
    c[0:32, 129:257] = (g[:, None] % 2 == (m[None, :] // 64))
    c[0:32, 257:273] = (g[:, None] // 2 == np.arange(16)[None, :])
    return c


def s5_phase(M, PT, YT, prm, cst_dram, cst2_dram, NT, PTB, YTB, tag=""):
    P = M.P
    ntile = NT // TT
    sb, ps = M.sb, M.ps
    c1 = sb("s_c1", [128, NCST]); c2 = sb("s_c2", [128, NCST2])
    c1B, c2B = M.s_c1B, M.s_c2B
    ident = c1[:, 0:128]
    P.dma("sync", lambda e: e.dma_start(out=c1[:], in_=cst_dram), writes=[c1B])
    P.dma("sync", lambda e: e.dma_start(out=c2[:], in_=cst2_dram), writes=[c2B])
    iota = c2[:, 0:129]
    psA = ps("s_psA", [128, 512]); psB = ps("s_psB", [128, 512]); psY = ps("s_psY", [128, 512]); psT = ps("s_psT", [128, 512])
    psAB, psBB, psYB, psTB = M.s_psAB, M.s_psBB, M.s_psYB, M.s_psTB
    NP_ = 16

    def V(eng, fn, reads, writes):
        P.op(eng, fn, reads=reads, writes=writes)

    rows = sb("s_rows", [32, 128]); rowsB = M.s_rowsB
    arc = sb("s_ar", [128, NP_]); aic = sb("s_ai", [128, NP_]); dtc = sb("s_dt", [128, NP_])

    def col_from_rows(dst, dstB, src_ap):
        P.dma("sync", lambda e: e.dma_start(out=rows[0:16, :], in_=src_ap), writes=[rowsB])
        P.op("tensor", lambda e: e.matmul(psT[:, 0:16], lhsT=rows[0:16, :], rhs=c1[0:16, 0:16], start=True, stop=True), reads=[rowsB, c1B], writes=[psTB])
        V("vector", lambda e: e.tensor_copy(out=dst[:], in_=psT[:, 0:16]), [psTB], [dstB])
    col_from_rows(arc, M.s_arB, prm["a_re"].rearrange("(a b) p -> a (b p)", b=2))
    col_from_rows(aic, M.s_aiB, prm["a_im"].rearrange("(a b) p -> a (b p)", b=2))
    ldr = sb("s_ldr", [1, 32]); ldc = sb("s_ldc", [32, 1]); Rm = sb("s_Rm", [32, 16])
    P.dma("sync", lambda e: e.dma_start(out=ldr[:], in_=prm["log_dt"].rearrange("(o f) -> o f", o=1)), writes=[M.s_ldrB])
    P.op("tensor", lambda e: e.matmul(psT[0:32, 16:17], lhsT=ldr[0:1, :], rhs=c1[0:1, 0:1], start=True, stop=True), reads=[M.s_ldrB, c1B], writes=[psTB])
    V("vector", lambda e: e.tensor_copy(out=ldc[:], in_=psT[0:32, 16:17]), [psTB], [M.s_ldcB])
    V("vector", lambda e: e.tensor_scalar(out=Rm[:], in0=c2[0:32, 257:273], scalar1=ldc[:, 0:1], scalar2=None, op0=ALU.mult), [c2B, M.s_ldcB], [M.s_RmB])
    P.op("tensor", lambda e: e.matmul(psT[:, 32:48], lhsT=c2[0:32, 129:257], rhs=Rm[:], start=True, stop=True), reads=[c2B, M.s_RmB], writes=[psTB])
    V("scalar", lambda e: e.activation(out=dtc[:], in_=psT[:, 32:48], func=AF.Exp), [psTB], [M.s_dtB])

    def sincos(x, xB, s_out, sB_, c_out, cB_, F, tmp, tmpB, tmpi, tmpiB):
        V("vector", lambda e: e.tensor_scalar(out=tmp, in0=x, scalar1=1.0 / TWO_PI, scalar2=None, op0=ALU.mult), [xB], [tmpB])
        V("vector", lambda e: e.tensor_copy(out=tmpi, in_=tmp), [tmpB], [tmpiB])
        V("vector", lambda e: e.tensor_copy(out=tmp, in_=tmpi), [tmpiB], [tmpB])
        V("vector", lambda e: e.scalar_tensor_tensor(out=x, in0=tmp, scalar=-TWO_PI, in1=x, op0=ALU.mult, op1=ALU.add), [tmpB, xB], [xB])
        V("scalar", lambda e: e.activation(out=tmp, in_=x, func=AF.Sin, scale=0.25), [xB], [tmpB])
        V("scalar", lambda e: e.activation(out=s_out, in_=x, func=AF.Sin, scale=0.5), [xB], [sB_])
        V("vector", lambda e: e.tensor_tensor(out=tmp, in0=tmp, in1=tmp, op=ALU.mult), [tmpB], [tmpB])
        V("vector", lambda e: e.tensor_scalar(out=tmp, in0=tmp, scalar1=-2.0, scalar2=1.0, op0=ALU.mult, op1=ALU.add), [tmpB], [tmpB])
        V("vector", lambda e: e.tensor_tensor(out=c_out, in0=s_out, in1=s_out, op=ALU.mult), [sB_], [cB_])
        V("vector", lambda e: e.scalar_tensor_tensor(out=s_out, in0=s_out, scalar=2.0, in1=tmp, op0=ALU.mult, op1=ALU.mult), [sB_, tmpB], [sB_])
        V("vector", lambda e: e.tensor_scalar(out=c_out, in0=c_out, scalar1=-2.0, scalar2=1.0, op0=ALU.mult, op1=ALU.add), [cB_], [cB_])

    mag = sb("s_mag", [128, NP_]); th = sb("s_th", [128, NP_]); sn = sb("s_sn", [128, NP_]); cs_ = sb("s_cs", [128, NP_])
    tp = sb("s_tp", [128, NP_]); tpi = sb("s_tpi", [128, NP_], I32); th2 = sb("s_th2", [128, NP_])
    V("vector", lambda e: e.tensor_scalar(out=arc[:], in0=arc[:], scalar1=-1e-4, scalar2=None, op0=ALU.min), [M.s_arB], [M.s_arB])
    V("vector", lambda e: e.tensor_tensor(out=mag[:], in0=dtc[:], in1=arc[:], op=ALU.mult), [M.s_dtB, M.s_arB], [M.s_magB])
    V("scalar", lambda e: e.activation(out=mag[:], in_=mag[:], func=AF.Exp), [M.s_magB], [M.s_magB])
    V("vector", lambda e: e.tensor_tensor(out=th[:], in0=dtc[:], in1=aic[:], op=ALU.mult), [M.s_dtB, M.s_aiB], [M.s_thB])
    V("vector", lambda e: e.tensor_copy(out=th2[:], in_=th[:]), [M.s_thB], [M.s_th2B])
    sincos(th2[:], M.s_th2B, sn[:], M.s_snB, cs_[:], M.s_csB, NP_, tp[:], M.s_tpB, tpi[:], M.s_tpiB)
    zr = sb("s_zr", [128, NP_]); zi = sb("s_zi", [128, NP_]); den = sb("s_den", [128, NP_]); fr = sb("s_fr", [128, NP_]); fi = sb("s_fi", [128, NP_])
    V("vector", lambda e: e.tensor_tensor(out=zr[:], in0=mag[:], in1=cs_[:], op=ALU.mult), [M.s_magB, M.s_csB], [M.s_zrB])
    V("vector", lambda e: e.tensor_scalar(out=zr[:], in0=zr[:], scalar1=-1.0, scalar2=None, op0=ALU.add), [M.s_zrB], [M.s_zrB])
    V("vector", lambda e: e.tensor_tensor(out=zi[:], in0=mag[:], in1=sn[:], op=ALU.mult), [M.s_magB, M.s_snB], [M.s_ziB])
    V("vector", lambda e: e.tensor_tensor(out=den[:], in0=arc[:], in1=arc[:], op=ALU.mult), [M.s_arB], [M.s_denB])
    V("vector", lambda e: e.tensor_tensor(out=tp[:], in0=aic[:], in1=aic[:], op=ALU.mult), [M.s_aiB], [M.s_tpB])
    V("vector", lambda e: e.tensor_tensor(out=den[:], in0=den[:], in1=tp[:], op=ALU.add), [M.s_denB, M.s_tpB], [M.s_denB])
    V("vector", lambda e: e.reciprocal(out=den[:], in_=den[:]), [M.s_denB], [M.s_denB])
    V("vector", lambda e: e.tensor_tensor(out=fr[:], in0=zr[:], in1=arc[:], op=ALU.mult), [M.s_zrB, M.s_arB], [M.s_frB])
    V("vector", lambda e: e.tensor_tensor(out=tp[:], in0=zi[:], in1=aic[:], op=ALU.mult), [M.s_ziB, M.s_aiB], [M.s_tpB])
    V("vector", lambda e: e.tensor_tensor(out=fr[:], in0=fr[:], in1=tp[:], op=ALU.add), [M.s_frB, M.s_tpB], [M.s_frB])
    V("vector", lambda e: e.tensor_tensor(out=fr[:], in0=fr[:], in1=den[:], op=ALU.mult), [M.s_frB, M.s_denB], [M.s_frB])
    V("vector", lambda e: e.tensor_tensor(out=fi[:], in0=zi[:], in1=arc[:], op=ALU.mult), [M.s_ziB, M.s_arB], [M.s_fiB])
    V("vector", lambda e: e.tensor_tensor(out=tp[:], in0=zr[:], in1=aic[:], op=ALU.mult), [M.s_zrB, M.s_aiB], [M.s_tpB])
    V("vector", lambda e: e.tensor_tensor(out=fi[:], in0=fi[:], in1=tp[:], op=ALU.subtract), [M.s_fiB, M.s_tpB], [M.s_fiB])
    V("vector", lambda e: e.tensor_tensor(out=fi[:], in0=fi[:], in1=den[:], op=ALU.mult), [M.s_fiB, M.s_denB], [M.s_fiB])

    CTb = sb("s_CT", [128, NP_, 129]); STb = sb("s_ST", [128, NP_, 129]); XT = sb("s_XT", [128, NP_, 129])
    TT1 = sb("s_TT1", [128, NP_, 129]); TTi = sb("s_TTi", [128, NP_, 129], I32); RM = sb("s_RM", [128, NP_, SEG])
    for gp in range(NP_):
        V("vector", (lambda e, gp=gp: e.tensor_scalar(out=XT[:, gp, :], in0=iota, scalar1=th[:, gp:gp + 1], scalar2=None, op0=ALU.mult)), [c2B, M.s_thB], [M.s_XTB])
        V("gpsimd", (lambda e, gp=gp: e.tensor_scalar(out=RM[:, gp, :], in0=c1[:, 128:256], scalar1=mag[:, gp:gp + 1], scalar2=None, op0=ALU.mult)), [c1B, M.s_magB], [M.s_RMB])
    fl = lambda t_: t_[:].rearrange("p a b -> p (a b)")
    sincos(fl(XT), M.s_XTB, fl(STb), M.s_STB, fl(CTb), M.s_CTB, NP_ * 129, fl(TT1), M.s_TT1B, fl(TTi), M.s_TTiB)

    BnR = sb("s_BnR", [128, 4, 128]); BnI = sb("s_BnI", [128, 4, 128]); bbR = sb("s_bbR", [128, 4, 128]); bbI = sb("s_bbI", [128, 4, 128])
    LBr = sb("s_LBr", [128, NP_, 128]); LBi = sb("s_LBi", [128, NP_, 128]); tmpm = sb("s_tmpm", [128, 128])
    V("vector", lambda e: e.memset(BnR[:], 0.0), [], [M.s_BnRB])
    V("vector", lambda e: e.memset(BnI[:], 0.0), [], [M.s_BnIB])
    V("gpsimd", lambda e: e.memset(bbR[:], 0.0), [], [M.s_bbRB])
    V("gpsimd", lambda e: e.memset(bbI[:], 0.0), [], [M.s_bbIB])
    for g in range(32):
        ct, gl, e_ = g // 8, g % 8, g % 2
        P.dma("sync", (lambda e, g=g, ct=ct, gl=gl, e_=e_: e.dma_start(out=BnR[e_ * 64:(e_ + 1) * 64, ct, gl * 16:(gl + 1) * 16], in_=prm["b_re"][g])), writes=[M.s_BnRB])
        P.dma("sync", (lambda e, g=g, ct=ct, gl=gl, e_=e_: e.dma_start(out=BnI[e_ * 64:(e_ + 1) * 64, ct, gl * 16:(gl + 1) * 16], in_=prm["b_im"][g])), writes=[M.s_BnIB])
    P.barrier()
    for g in range(32):
        ct, gl, e_, gp = g // 8, g % 8, g % 2, g // 2
        rs = slice(e_ * 64, (e_ + 1) * 64)
        csl = slice(gl * 16, (gl + 1) * 16)

        def bb(ct=ct, rs=rs, csl=csl, gp=gp):
            V("vector", lambda e: e.tensor_scalar(out=bbR[rs, ct, csl], in0=BnR[rs, ct, csl], scalar1=fr[rs, gp:gp + 1], scalar2=None, op0=ALU.mult), [M.s_BnRB, M.s_frB], [M.s_bbRB])
            V("vector", lambda e: e.tensor_scalar(out=tmpm[rs, 0:16], in0=BnI[rs, ct, csl], scalar1=fi[rs, gp:gp + 1], scalar2=None, op0=ALU.mult), [M.s_BnIB, M.s_fiB], [M.s_tmpmB])
            V("vector", lambda e: e.tensor_tensor(out=bbR[rs, ct, csl], in0=bbR[rs, ct, csl], in1=tmpm[rs, 0:16], op=ALU.subtract), [M.s_bbRB, M.s_tmpmB], [M.s_bbRB])
            V("vector", lambda e: e.tensor_scalar(out=bbI[rs, ct, csl], in0=BnI[rs, ct, csl], scalar1=fr[rs, gp:gp + 1], scalar2=None, op0=ALU.mult), [M.s_BnIB, M.s_frB], [M.s_bbIB])
            V("vector", lambda e: e.tensor_scalar(out=tmpm[rs, 16:32], in0=BnR[rs, ct, csl], scalar1=fi[rs, gp:gp + 1], scalar2=None, op0=ALU.mult), [M.s_BnRB, M.s_fiB], [M.s_tmpmB])
            V("vector", lambda e: e.tensor_tensor(out=bbI[rs, ct, csl], in0=bbI[rs, ct, csl], in1=tmpm[rs, 16:32], op=ALU.add), [M.s_bbIB, M.s_tmpmB], [M.s_bbIB])
        bb()
    for gp in range(NP_):
        ct, q4 = gp // 4, gp % 4
        csl = slice(q4 * 32, (q4 + 1) * 32)

        def mk(src, srcB, dst, dstB, ct=ct, csl=csl, gp=gp, neg=False):
            V("gpsimd", lambda e: e.memset(tmpm[:], 0.0), [], [M.s_tmpmB])
            V("gpsimd", lambda e: e.tensor_copy(out=tmpm[:, csl], in_=src[:, ct, csl]), [srcB], [M.s_tmpmB])
            P.op("tensor", lambda e: e.matmul(psT[:, 0:128], lhsT=tmpm[:], rhs=ident, start=True, stop=True), reads=[M.s_tmpmB, c1B], writes=[psTB])
            V("vector", lambda e: e.tensor_copy(out=dst[:, gp, :], in_=psT[:, 0:128]), [psTB], [dstB])
        mk(bbR, M.s_bbRB, LBr, M.s_LBrB)
        mk(bbI, M.s_bbIB, LBi, M.s_LBiB)

    CnR = sb("s_CnR", [128, 4, 128]); CnI = sb("s_CnI", [128, 4, 128]); CT2r = sb("s_CT2r", [128, 4, 128]); CT2i = sb("s_CT2i", [128, 4, 128])
    LCr = sb("s_LCr", [128, NP_, 128]); LCi = sb("s_LCi", [128, NP_, 128])
    V("vector", lambda e: e.memset(CnR[:], 0.0), [], [M.s_CnRB])
    V("vector", lambda e: e.memset(CnI[:], 0.0), [], [M.s_CnIB])
    V("gpsimd", lambda e: e.memset(LCr[:], 0.0), [], [M.s_LCrB])
    V("gpsimd", lambda e: e.memset(LCi[:], 0.0), [], [M.s_LCiB])
    for g in range(32):
        ct, gl, e_ = g // 8, g % 8, g % 2
        P.dma("sync", (lambda e, g=g, ct=ct, gl=gl, e_=e_: e.dma_start(out=CnR[gl * 16:(gl + 1) * 16, ct, e_ * 64:(e_ + 1) * 64], in_=prm["c_re"][g])), writes=[M.s_CnRB])
        P.dma("sync", (lambda e, g=g, ct=ct, gl=gl, e_=e_: e.dma_start(out=CnI[gl * 16:(gl + 1) * 16, ct, e_ * 64:(e_ + 1) * 64], in_=prm["c_im"][g])), writes=[M.s_CnIB])
    P.barrier()
    for ct in range(4):
        def trc(src, srcB, dst, dstB, ct=ct, neg=False):
            P.op("tensor", lambda e: e.matmul(psT[:, 0:128], lhsT=src[:, ct, :], rhs=ident, start=True, stop=True), reads=[srcB, c1B], writes=[psTB])
            if neg:
                V("vector", lambda e: e.tensor_scalar(out=dst[:, ct, :], in0=psT[:, 0:128], scalar1=-1.0, scalar2=None, op0=ALU.mult), [psTB], [dstB])
            else:
                V("vector", lambda e: e.tensor_copy(out=dst[:, ct, :], in_=psT[:, 0:128]), [psTB], [dstB])
        trc(CnR, M.s_CnRB, CT2r, M.s_CT2rB)
        trc(CnI, M.s_CnIB, CT2i, M.s_CT2iB, neg=True)
    for gp in range(NP_):
        ct, q4 = gp // 4, gp % 4
        csl = slice(q4 * 32, (q4 + 1) * 32)
        V("gpsimd", (lambda e, gp=gp, ct=ct, csl=csl: e.tensor_copy(out=LCr[:, gp, csl], in_=CT2r[:, ct, csl])), [M.s_CT2rB], [M.s_LCrB])
        V("gpsimd", (lambda e, gp=gp, ct=ct, csl=csl: e.tensor_copy(out=LCi[:, gp, csl], in_=CT2i[:, ct, csl])), [M.s_CT2iB], [M.s_LCiB])
    dcol = sb("s_dcol", [128, 4])
    P.dma("sync", lambda e: e.dma_start(out=rows[0:4, :], in_=prm["d"].rearrange("(a b) h -> a (b h)", b=8)), writes=[rowsB])
    P.op("tensor", lambda e: e.matmul(psT[:, 0:4], lhsT=rows[0:4, :], rhs=c1[0:4, 0:4], start=True, stop=True), reads=[rowsB, c1B], writes=[psTB])
    V("vector", lambda e: e.tensor_copy(out=dcol[:], in_=psT[:, 0:4]), [psTB], [M.s_dcolB])

    uT = sb("s_uT", [128, 4, TT]); bR = sb("s_bR", [128, TT]); bI = sb("s_bI", [128, TT]); t1 = sb("s_t1", [128, TT]); t2 = sb("s_t2", [128, TT])
    xR = sb("s_xR", [128, TT]); xI = sb("s_xI", [128, TT]); ys = sb("s_ys", [128, TT])
    cR = sb("s_cR", [128, NP_]); cI = sb("s_cI", [128, NP_]); cq = sb("s_cq", [128, 4])
    V("vector", lambda e: e.memset(cR[:], 0.0), [], [M.s_cRB])
    V("vector", lambda e: e.memset(cI[:], 0.0), [], [M.s_cIB])
    nseg = TT // SEG

    def pair_tile(t, gp):
        ct = gp // 4
        tabC = CTb[:, gp, 0:SEG].unsqueeze(1).broadcast_to([128, nseg, SEG])
        tabS = STb[:, gp, 0:SEG].unsqueeze(1).broadcast_to([128, nseg, SEG])
        v3 = lambda a: a[:].rearrange("p (s c) -> p s c", s=nseg)
        P.op("tensor", lambda e: e.matmul(psA[:], lhsT=LBr[:, gp, :], rhs=uT[:, ct, :], start=True, stop=True), reads=[M.s_LBrB, M.s_uTB], writes=[psAB])
        P.op("tensor", lambda e: e.matmul(psB[:], lhsT=LBi[:, gp, :], rhs=uT[:, ct, :], start=True, stop=True), reads=[M.s_LBiB, M.s_uTB], writes=[psBB])
        pA3 = psA[:].rearrange("p (s c) -> p s c", s=nseg)
        pB3 = psB[:].rearrange("p (s c) -> p s c", s=nseg)
        V("vector", lambda e: e.tensor_tensor(out=v3(bR), in0=pA3, in1=tabC, op=ALU.mult), [psAB, M.s_CTB], [M.s_bRB])
        V("vector", lambda e: e.tensor_tensor(out=v3(t1), in0=pB3, in1=tabS, op=ALU.mult), [psBB, M.s_STB], [M.s_t1B])
        V("gpsimd", lambda e: e.tensor_tensor(out=bR[:], in0=bR[:], in1=t1[:], op=ALU.add), [M.s_bRB, M.s_t1B], [M.s_bRB])
        V("vector", lambda e: e.tensor_tensor(out=v3(bI), in0=pB3, in1=tabC, op=ALU.mult), [psBB, M.s_CTB], [M.s_bIB])
        V("vector", lambda e: e.tensor_tensor(out=v3(t2), in0=pA3, in1=tabS, op=ALU.mult), [psAB, M.s_STB], [M.s_t2B])
        V("gpsimd", lambda e: e.tensor_tensor(out=bI[:], in0=bI[:], in1=t2[:], op=ALU.subtract), [M.s_bIB, M.s_t2B], [M.s_bIB])
        for s in range(nseg):
            sc = slice(s * SEG, (s + 1) * SEG)

            def seg(sc=sc):
                V("vector", lambda e: e.tensor_tensor_scan(out=xR[:, sc], data0=RM[:, gp, :], data1=bR[:, sc], initial=cR[:, gp:gp + 1], op0=ALU.mult, op1=ALU.add),
                  [M.s_RMB, M.s_bRB, M.s_cRB], [M.s_xRB])
                V("vector", lambda e: e.tensor_tensor_scan(out=xI[:, sc], data0=RM[:, gp, :], data1=bI[:, sc], initial=cI[:, gp:gp + 1], op0=ALU.mult, op1=ALU.add),
                  [M.s_RMB, M.s_bIB, M.s_cIB], [M.s_xIB])
                lr = xR[:, sc.stop - 1:sc.stop]
                li = xI[:, sc.stop - 1:sc.stop]
                c128 = CTb[:, gp, 128:129]
                s128 = STb[:, gp, 128:129]
                V("vector", lambda e: e.tensor_tensor(out=cq[:, 0:1], in0=li, in1=s128, op=ALU.mult), [M.s_xIB, M.s_STB], [M.s_cqB])
                V("vector", lambda e: e.tensor_tensor(out=cq[:, 1:2], in0=li, in1=c128, op=ALU.mult), [M.s_xIB, M.s_CTB], [M.s_cqB])
                V("vector", lambda e: e.scalar_tensor_tensor(out=cR[:, gp:gp + 1], in0=lr, scalar=c128, in1=cq[:, 0:1], op0=ALU.mult, op1=ALU.subtract),
                  [M.s_xRB, M.s_CTB, M.s_cqB], [M.s_cRB])
                V("vector", lambda e: e.scalar_tensor_tensor(out=cI[:, gp:gp + 1], in0=lr, scalar=s128, in1=cq[:, 1:2], op0=ALU.mult, op1=ALU.add),
                  [M.s_xRB, M.s_STB, M.s_cqB], [M.s_cIB])
            seg()
        V("gpsimd", lambda e: e.tensor_tensor(out=v3(t1), in0=v3(xI), in1=tabS, op=ALU.mult), [M.s_xIB, M.s_STB], [M.s_t1B])
        V("gpsimd", lambda e: e.tensor_tensor(out=v3(t2), in0=v3(xR), in1=tabS, op=ALU.mult), [M.s_xRB, M.s_STB], [M.s_t2B])
        V("vector", lambda e: e.tensor_tensor(out=v3(xR), in0=v3(xR), in1=tabC, op=ALU.mult), [M.s_xRB, M.s_CTB], [M.s_xRB])
        V("vector", lambda e: e.tensor_tensor(out=v3(xI), in0=v3(xI), in1=tabC, op=ALU.mult), [M.s_xIB, M.s_CTB], [M.s_xIB])
        V("gpsimd", lambda e: e.tensor_tensor(out=xR[:], in0=xR[:], in1=t1[:], op=ALU.subtract), [M.s_xRB, M.s_t1B], [M.s_xRB])
        V("gpsimd", lambda e: e.tensor_tensor(out=xI[:], in0=xI[:], in1=t2[:], op=ALU.add), [M.s_xIB, M.s_t2B], [M.s_xIB])
        q4 = gp % 4
        P.op("tensor", lambda e: e.matmul(psY[:], lhsT=LCr[:, gp, :], rhs=xR[:], start=(q4 == 0), stop=False), reads=[M.s_LCrB, M.s_xRB], writes=[psYB], inc=False)
        P.op("tensor", lambda e: e.matmul(psY[:], lhsT=LCi[:, gp, :], rhs=xI[:], start=False, stop=(q4 == 3)), reads=[M.s_LCiB, M.s_xIB], writes=[psYB])

    for t in range(ntile):
        t0 = t * TT
        for ct in range(4):
            P.dma("sync", (lambda e, ct=ct, t0=t0: e.dma_start(out=uT[:, ct, :], in_=PT[2056 + ct * 128:2056 + (ct + 1) * 128, t0:t0 + TT])), reads=[PTB[t]], writes=[M.s_uTB])
        for gp in range(NP_):
            pair_tile(t, gp)
            if gp % 4 == 3:
                ct = gp // 4

                def fin(ct=ct, t0=t0):
                    V("vector", lambda e: e.scalar_tensor_tensor(out=ys[:], in0=uT[:, ct, :], scalar=dcol[:, ct:ct + 1], in1=psY[:], op0=ALU.mult, op1=ALU.add),
                      [M.s_uTB, M.s_dcolB, psYB], [M.s_ysB])
                    V("scalar", lambda e: e.activation(out=ys[:], in_=ys[:], func=AF.Gelu), [M.s_ysB], [M.s_ysB])
                    P.dma("gpsimd", lambda e: e.dma_start(out=YT[512 + ct * 128:512 + (ct + 1) * 128, t0:t0 + TT], in_=ys[:]), reads=[M.s_ysB], writes=[YTB[t]])
                fin()


def mixpost_phase(R, YT, X_in, X_out, w_glu, w_out, NT):
    P = R.P
    ntile = NT // TT
    wi, wo, sB = R.wi[0], R.wo[0], R.slabB[0]

    def ld(src, dst, w_):
        si = R.stg_i
        R.stg_i = (si + 1) % 3
        st, stB = R.stg[si], R.stgB[si]
        P.dma("sync", lambda e: e.dma_start(out=st[:, 0:w_], in_=src), writes=[stB])
        P.op("gpsimd", lambda e: e.tensor_copy(out=dst, in_=st[:, 0:w_]), reads=[stB], writes=[sB])
    for kc in range(4):
        ld(w_glu[kc * 128:(kc + 1) * 128, 0:512], wi[:, kc, 0:512], 512)
    for kc in range(8):
        ld(w_out[kc * 128:(kc + 1) * 128, 0:1024], wo[:, kc, :], 1024)
    YB = [Buf("Yo%d" % i) for i in range(ntile * 4)]

    def load_y(t, hb):
        hT = R.hT[hb]
        for kc in range(8):
            def one(kc=kc):
                si = R.stg_i
                R.stg_i = (si + 1) % 3
                st, stB = R.stg[si], R.stgB[si]
                P.dma("sync", lambda e: e.dma_start(out=st[:, 0:TT], in_=YT[kc * 128:(kc + 1) * 128, t * TT:(t + 1) * TT]), writes=[stB])
                P.op("gpsimd" if kc % 2 else "vector", lambda e: e.tensor_copy(out=hT[:, kc, :], in_=st[:, 0:TT]), reads=[stB], writes=[R.hTB[hb][0]])
            one()
    load_y(0, 0)
    gi = 0
    for t in range(ntile):
        hb = t % 2
        hT = R.hT[hb]
        hB = R.hTB[hb][0]
        for j in range(4):
            def glu(j=j, gi=gi, hT=hT, hB=hB):
                pp, ppB = R.pB[gi % 4], R.pBB[gi % 4]
                sg, sgB = R.sg[gi % 2], R.sgB[gi % 2]

                def mm(kc):
                    P.op("tensor", lambda e: e.matmul(pp[:], lhsT=wi[:, kc, j * 128:(j + 1) * 128], rhs=hT[:, 4 + kc, :], start=(kc == 0), stop=(kc == 3)),
                         reads=[sB, hB], writes=[ppB], inc=(kc == 3))
                for kc in range(4):
                    mm(kc)
                P.op("scalar", lambda e: e.activation(out=sg[:], in_=pp[:], func=AF.Sigmoid), reads=[ppB], writes=[sgB])
                P.op("vector", lambda e: e.tensor_tensor(out=R.aT[:, j, :], in0=hT[:, 4 + j, :], in1=sg[:], op=ALU.mult), reads=[hB, sgB], writes=[R.aTB[j]])
            glu()
            gi += 1
        if t + 1 < ntile:
            load_y(t + 1, 1 - hb)
        lhs = [(hT, kc, hB) for kc in range(4)] + [(R.aT, kc, R.aTB[kc]) for kc in range(4)]
        for s in range(4):
            stage_c_sub(R, wo, sB, 8, X_in, X_out, None, YB[t * 4 + s], t, s, 0, True, lhs=lhs)
    return YB


DEPTH = 2
NT_CORE = 8192


def mod_phase(R, c_row, w_mod_l, b_mod_l, MOD, sbm):
    P = R.P
    cT, cTB, brow, browB, mrow, mrowB = sbm
    P.dma("sync", lambda e: e.dma_start(out=R.vrows[:, 0, :], in_=c_row.rearrange("(kc p) -> kc p", p=128)), writes=[R.vrowsB])
    pc = R.pC[0]
    P.op("tensor", lambda e: e.matmul(pc[:, 0:8], lhsT=R.vrows[:, 0, :], rhs=R.identf[0:8, 0:8], start=True, stop=True),
         reads=[R.vrowsB, R.identfB], writes=[R.pCB])
    P.op("scalar", lambda e: e.activation(out=cT[:], in_=pc[:, 0:8], func=AF.Silu), reads=[R.pCB], writes=[cTB])
    pm = R.pC[1]
    dummy = Buf("modw")

    def tile(n):
        def kstep(kc):
            si = R.stg_i
            R.stg_i = (si + 1) % 3
            st, stB = R.stg[si], R.stgB[si]
            P.dma("sync", lambda e: e.dma_start(out=st[:, 0:512], in_=w_mod_l[kc * 128:(kc + 1) * 128, n * 512:(n + 1) * 512]), writes=[stB])
            P.op("tensor", lambda e: e.matmul(pm[0:1, :], lhsT=cT[:, kc:kc + 1], rhs=st[:, 0:512], start=(kc == 0), stop=(kc == NKC - 1)),
                 reads=[stB, cTB], writes=[R.pCB])
        for kc in range(NKC):
            kstep(kc)
        P.dma("sync", lambda e: e.dma_start(out=brow[:], in_=b_mod_l[n * 512:(n + 1) * 512].rearrange("(o f) -> o f", o=1)), writes=[browB])
        P.op("vector", lambda e: e.tensor_tensor(out=mrow[:], in0=pm[0:1, :], in1=brow[:], op=ALU.add),
             reads=[R.pCB, browB], writes=[mrowB])
        P.dma("gpsimd", lambda e: e.dma_start(out=MOD[n * 512:(n + 1) * 512].rearrange("(o f) -> o f", o=1), in_=mrow[:]),
              reads=[mrowB], writes=[dummy])
    for n in range(9 * D // 512):
        tile(n)


WNAMES = ["w_mod", "b_mod", "ff1_norm_pre", "ff1_norm_post", "ff1_w_in", "ff1_w_out", "mix_norm_pre", "mix_norm_post", "mix_w_in",
          "conv_w", "a_log", "dt_bias", "gdn_norm_w", "s5_a_re", "s5_a_im", "s5_log_dt", "s5_b_re", "s5_b_im", "s5_c_re", "s5_c_im",
          "s5_d", "s5_w_glu", "mix_w_out", "ff2_norm_pre", "ff2_norm_post", "ff2_w_in", "ff2_w_out"]
WSHAPES = {"w_mod": [D, 9 * D], "b_mod": [9 * D], "ff1_norm_pre": [D], "ff1_norm_post": [D], "ff1_w_in": [D, 2 * DFF], "ff1_w_out": [DFF, D],
           "mix_norm_pre": [D], "mix_norm_post": [D], "mix_w_in": [D, 2568], "conv_w": [4, 1536], "a_log": [4], "dt_bias": [4], "gdn_norm_w": [128],
           "s5_a_re": [32, 64], "s5_a_im": [32, 64], "s5_log_dt": [32], "s5_b_re": [32, 64, 16], "s5_b_im": [32, 64, 16],
           "s5_c_re": [32, 16, 64], "s5_c_im": [32, 16, 64], "s5_d": [32, 16], "s5_w_glu": [512, 512], "mix_w_out": [D, D],
           "ff2_norm_pre": [D], "ff2_norm_post": [D], "ff2_w_in": [D, 2 * DFF], "ff2_w_out": [DFF, D]}


def build_nc(NT, depth=DEPTH):
    nc = bass.Bass("TRN2", target_bir_lowering=False)
    dr = lambda n, s, k="ExternalInput", dt=F32: nc.dram_tensor(n, s, dt, kind=k).ap()
    x = dr("x", [NT, D]); c_row = dr("c_row", [D])
    W = {n: dr(n, [depth] + WSHAPES[n]) for n in WNAMES}
    ident = dr("ident", [128, 128]); cst = dr("cst", [128, NCST]); cst2 = dr("cst2", [128, NCST2])
    y = dr("y", [NT, D], "ExternalOutput")
    scr = lambda n, s: nc.dram_tensor(n, s, F32).ap()
    Xs = [scr("xs0", [NT, D]), scr("xs1", [NT, D])]
    yacc = scr("yacc", [NT, D]); PT = scr("ptscr", [2568, NT]); YT = scr("ytscr", [1024, NT])
    MOD = scr("modscr", [depth, 9 * D])
    ntile = NT // TT
    dB = lambda: [Buf() for _ in range(ntile)]
    with ExitStack() as stack:
        P = Prog(nc, stack)
        phase = [0]

        def ffn_like(fn):
            phase[0] += 1
            with ExitStack() as ph:
                R = FFNRes(nc, ph, P, tag="_p%d" % phase[0])
                R.load_ident(ident)
                fn(R, ph)
            P.barrier()

        def mix_like(fn):
            phase[0] += 1
            with ExitStack() as ph:
                M = MixRes(nc, ph, P, tag="_p%d" % phase[0])
                fn(M)
            P.barrier()

        def do_mod(R, ph):
            sb = lambda name, shape, dt: ph.enter_context(nc.sbuf_tensor(name, shape, dt))
            sbm = (sb("cT", [128, NKC], F32), Buf("cT"), sb("brow", [1, 512], F32), Buf("brow"), sb("mrow", [1, 512], F32), Buf("mrow"))
            for l in range(depth):
                mod_phase(R, c_row, W["w_mod"][l], W["b_mod"][l], MOD[l], sbm)
        ffn_like(do_mod)
        cur = x
        for l in range(depth):
            last_layer = l == depth - 1
            nxt = Xs[0]

            def f1(R, ph, l=l, cur=cur, nxt=nxt):
                prep_vectors(R, W["ff1_norm_pre"][l], W["ff1_norm_post"][l], MOD[l], 0, 0.5)
                ffn_phase(R, cur, nxt, yacc, W["ff1_w_in"][l], W["ff1_w_out"][l], NT)
            ffn_like(f1)
            cur = nxt

            def m1(R, ph, l=l, cur=cur):
                prep_vectors(R, W["mix_norm_pre"][l], W["mix_norm_post"][l], MOD[l], 3, 1.0)
                proj_phase(R, cur, W["mix_w_in"][l], PT, NT, dB())
            ffn_like(m1)

            def g(M, l=l):
                gdn_phase(M, PT, YT, W["conv_w"][l], W["a_log"][l], W["dt_bias"][l], W["gdn_norm_w"][l], cst, NT, dB(), dB())
            mix_like(g)

            def s5(M, l=l):
                prm = {k: W["s5_" + k][l] for k in ("a_re", "a_im", "log_dt", "b_re", "b_im", "c_re", "c_im", "d")}
                s5_phase(M, PT, YT, prm, cst, cst2, NT, dB(), dB())
            mix_like(s5)
            nxt = Xs[1]

            def m3(R, ph, l=l, cur=cur, nxt=nxt):
                prep_vectors(R, W["mix_norm_pre"][l], W["mix_norm_post"][l], MOD[l], 3, 1.0)
                mixpost_phase(R, YT, cur, nxt, W["s5_w_glu"][l], W["mix_w_out"][l], NT)
            ffn_like(m3)
            cur = nxt
            nxt = y if last_layer else Xs[0]
            outB = []

            def f2(R, ph, l=l, cur=cur, nxt=nxt):
                prep_vectors(R, W["ff2_norm_pre"][l], W["ff2_norm_post"][l], MOD[l], 6, 0.5)
                outB.extend(ffn_phase(R, cur, nxt, yacc, W["ff2_w_in"][l], W["ff2_w_out"][l], NT))
            ffn_like(f2)
            cur = nxt
        P.final_wait("gpsimd", outB)
        P.emit()
    return nc


def kernel(**inputs):
    x = np.ascontiguousarray(inputs["x"], dtype=np.float32)
    c = np.ascontiguousarray(inputs["c"], dtype=np.float32)
    B, L, _ = x.shape
    common = {n: np.ascontiguousarray(inputs[n], dtype=np.float32) for n in WNAMES}
    common["ident"] = np.eye(128, dtype=np.float32)
    common["cst"] = make_consts()
    common["cst2"] = make_consts2()
    n_cores = B
    in_maps = []
    for core in range(n_cores):
        m = dict(common)
        m["x"] = np.ascontiguousarray(x[core])
        m["c_row"] = np.ascontiguousarray(c[core])
        in_maps.append(m)
    nc = build_nc(L)
    res = run_bass_kernel_spmd(nc, in_maps, core_ids=list(range(n_cores)))
    out = np.empty((B, L, D), dtype=np.float32)
    for core in range(n_cores):
        out[core] = res.results[core]["y"]
    return out
```

```python
import numpy as np
from contextlib import ExitStack
import concourse.bass as bass
import concourse.mybir as mybir
from concourse.bass_utils import run_bass_kernel_spmd

F32 = mybir.dt.float32
BF16 = mybir.dt.bfloat16
I32 = mybir.dt.int32
AF = mybir.ActivationFunctionType
ALU = mybir.AluOpType

ENGS = ("tensor", "vector", "scalar", "gpsimd", "sync")


class Buf:
    __slots__ = ("w", "r", "name")

    def __init__(self, name=""):
        self.w = []
        self.r = []
        self.name = name


class Prog:
    NDMA = 20

    def __init__(self, nc, stack):
        self.nc = nc
        self.q = {e: [] for e in ENGS}
        self.sems = {}
        self.cnt = {}
        for e in ("tensor", "vector", "scalar", "gpsimd"):
            self.sems[e] = stack.enter_context(nc.semaphore("pg_" + e))
            self.cnt[e] = 0
        self.dma_rr = {}
        for qn in ("sync", "gpsimd", "scalar"):
            self.dma_rr[qn] = 0
            for i in range(self.NDMA):
                k = ("dma", qn, i)
                self.sems[k] = stack.enter_context(nc.semaphore("pd_%s_%d" % (qn, i)))
                self.cnt[k] = 0
        self.seen = {e: {} for e in ENGS}
        self.pending_reads = {e: [] for e in ENGS}

    def _waits(self, eng, deps):
        out = []
        for tok in deps:
            if tok is None:
                continue
            key, val = tok
            if key == "tensor" and eng == "tensor":
                continue
            if self.seen[eng].get(key, 0) >= val:
                continue
            self.seen[eng][key] = val
            out.append((key, val))
        return out

    def op(self, eng, fn, reads=(), writes=(), inc=True):
        deps = []
        for b in reads:
            deps.extend(b.w)
        for b in writes:
            deps.extend(b.w)
            deps.extend(b.r)
        waits = self._waits(eng, deps)
        if inc:
            self.cnt[eng] += 1
            tok = (eng, self.cnt[eng])
        else:
            tok = (eng, self.cnt[eng] + 1)
        sem = self.sems[eng]
        self.q[eng].append((waits, fn, sem if inc else None, 1))
        for b in reads:
            b.r.append(tok)
        for b in writes:
            b.w = [tok]
            b.r = []
        return tok

    def dma(self, qn, fn, reads=(), writes=()):
        deps = []
        for b in reads:
            deps.extend(b.w)
        for b in writes:
            deps.extend(b.w)
            deps.extend(b.r)
        i = self.dma_rr[qn]
        self.dma_rr[qn] = (i + 1) % self.NDMA
        k = ("dma", qn, i)
        if self.cnt[k] > 0:
            deps.append((k, self.cnt[k]))
        waits = self._waits(qn, deps)
        self.cnt[k] += 16
        tok = (k, self.cnt[k])
        self.q[qn].append((waits, fn, self.sems[k], 16))
        for b in reads:
            b.r.append(tok)
        for b in writes:
            b.w = [tok]
            b.r = []
        return tok

    def barrier(self):
        toks = [(k, v) for k, v in self.cnt.items() if v > 0]
        for e in ENGS:
            waits = self._waits(e, toks)
            if waits:
                self.q[e].append((waits, None, None, 0))

    def final_wait(self, eng, bufs):
        deps = []
        for b in bufs:
            deps.extend(b.w)
        waits = self._waits(eng, deps)
        self.q[eng].append((waits, None, None, 0))

    def emit(self):
        nc = self.nc
        sems = self.sems
        with nc.Block() as block:
            def mk(name):
                def body(e):
                    for waits, fn, sem, inc in self.q[name]:
                        for key, val in waits:
                            e.wait_ge(sems[key], val)
                        if fn is not None:
                            ins = fn(e)
                            if sem is not None:
                                ins.then_inc(sem, inc)
                return body
            block.sync(mk("sync"))
            block.tensor(mk("tensor"))
            block.vector(mk("vector"))
            block.scalar(mk("scalar"))
            block.gpsimd(mk("gpsimd"))


def check_deadlock(P):
    pos = {e: 0 for e in ENGS}
    val = {}
    key_of = {id(s): k for k, s in P.sems.items()}
    progress = True
    while progress:
        progress = False
        for e in ENGS:
            q = P.q[e]
            while pos[e] < len(q):
                waits, fn, sem, inc = q[pos[e]]
                if all(val.get(k, 0) >= v for k, v in waits):
                    if sem is not None:
                        k = key_of[id(sem)]
                        val[k] = val.get(k, 0) + inc
                    pos[e] += 1
                    progress = True
                else:
                    break
    ok = all(pos[e] == len(P.q[e]) for e in ENGS)
    if not ok:
        for e in ENGS:
            if pos[e] < len(P.q[e]):
                waits = P.q[e][pos[e]][0]
                print("STUCK", e, pos[e], len(P.q[e]), [(k, v, val.get(k, 0)) for k, v in waits if val.get(k, 0) < v])
    return ok


D = 1024
DFF = 2816
NKC = 8
EPS = 1e-6
SLABS = [(0, 8), (8, 7), (15, 7)]
SLAB_MAX = 8
TT = 512


class FFNRes:
    def __init__(self, nc, stack, P, tag=""):
        self.nc, self.P = nc, P
        sb = lambda name, shape, dt: stack.enter_context(nc.sbuf_tensor(name + tag, shape, dt))
        ps = lambda name, shape, dt: stack.enter_context(nc.psum_tensor(name + tag, shape, dt))
        self.wi = [sb("wi%d" % i, [128, NKC, SLAB_MAX * 256], BF16) for i in range(2)]
        self.wo = [sb("wo%d" % i, [128, SLAB_MAX, D], BF16) for i in range(2)]
        self.slabB = [Buf("slab%d" % i) for i in range(2)]
        self.stg = [sb("stg%d" % i, [128, 1024], F32) for i in range(3)]
        self.stgB = [Buf("stg%d" % i) for i in range(3)]
        self.stg_i = 0
        self.hT = [sb("hT%d" % i, [128, NKC, TT], BF16) for i in range(2)]
        self.hTB = [[Buf("hT%d_%d" % (i, s)) for s in range(4)] for i in range(2)]
        self.aT = sb("aT", [128, SLAB_MAX, TT], BF16)
        self.aTB = [Buf("aT%d" % j) for j in range(SLAB_MAX)]
        self.xa = [sb("xa%d" % i, [128, D], F32) for i in range(2)]
        self.xaB = [Buf("xa%d" % i) for i in range(2)]
        self.xn = [sb("xn%d" % i, [128, D], BF16) for i in range(2)]
        self.xnB = [Buf("xn%d" % i) for i in range(2)]
        self.junk = sb("junk", [128, D], BF16)
        self.junkB = Buf("junk")
        self.ss = [sb("ss%d" % i, [128, 2], F32) for i in range(4)]
        self.ssB = [Buf("ss%d" % i) for i in range(4)]
        self.ss_i = 0
        self.sg = [sb("sg%d" % i, [128, TT], F32) for i in range(2)]
        self.sgB = [Buf("sg%d" % i) for i in range(2)]
        self.yb = [sb("yb%d" % i, [128, D], F32) for i in range(2)]
        self.ybB = [Buf("yb%d" % i) for i in range(2)]
        self.xr = [sb("xr%d" % i, [128, D], F32) for i in range(2)]
        self.xrB = [Buf("xr%d" % i) for i in range(2)]
        self.crow = sb("crow", [128, D], F32)
        self.crowB = Buf("crow")
        self.ctmp = sb("ctmp", [128, D], F32)
        self.ctmpB = Buf("ctmp")
        self.acol = sb("acol", [128, NKC], F32)
        self.bcol = sb("bcol", [128, NKC], F32)
        self.tcol = sb("tcol", [128, NKC], F32)
        self.colB = Buf("col")
        self.vrows = sb("vrows", [8, 3, 128], F32)
        self.vrowsB = Buf("vrows")
        self.identf = sb("identf", [128, 128], F32)
        self.identfB = Buf("identf")
        self.ident = sb("ident_sb", [128, 128], BF16)
        self.epsc = sb("epsc", [128, 1], F32)
        self.epscB = Buf("epsc")
        P.op("vector", lambda e: e.memset(self.epsc[:], D * EPS), writes=[self.epscB])
        self.identB = Buf("ident")
        self.pB = [ps("pB%d" % i, [128, TT], F32) for i in range(4)]
        self.pBB = [Buf("pB%d" % i) for i in range(4)]
        self.pC = [ps("pC%d" % i, [128, TT], F32) for i in range(2)]
        self.pCB = Buf("pC")
        self.pT = [ps("pT%d" % i, [128, NKC, 128], BF16) for i in range(2)]
        self.pTB = [Buf("pT%d" % i) for i in range(2)]

    def load_ident(self, ident_dram):
        P = self.P
        st = self.stg[0]
        P.dma("sync", lambda e: e.dma_start(out=st[:, 0:128], in_=ident_dram), writes=[self.stgB[0]])
        P.dma("sync", lambda e: e.dma_start(out=self.identf[:], in_=ident_dram), writes=[self.identfB])
        P.op("vector", lambda e: e.tensor_copy(out=self.ident[:], in_=st[:, 0:128]),
             reads=[self.stgB[0]], writes=[self.identB])


def load_slab(R, w_in, w_out, slab, buf, dff=DFF, gated=True):
    P = R.P
    j0, n = slab
    wi, wo, sB = R.wi[buf], R.wo[buf], R.slabB[buf]
    pieces = []
    ncol = n * 128
    for kc in range(NKC):
        for part in range(2 if gated else 1):
            c0 = part * dff + j0 * 128
            done = 0
            while done < ncol:
                w = min(1024, ncol - done)
                pieces.append(("in", kc, part * ncol + done, c0 + done, w))
                done += w
    if w_out is not None:
        for j in range(n):
            pieces.append(("out", j, 0, (j0 + j) * 128, 1024))
    for kind, a, dst0, src0, w in pieces:
        si = R.stg_i
        R.stg_i = (si + 1) % 3
        st, stB = R.stg[si], R.stgB[si]
        if kind == "in":
            src = w_in[a * 128:(a + 1) * 128, src0:src0 + w]
            dst = wi[:, a, dst0:dst0 + w]
        else:
            src = w_out[src0:src0 + 128, 0:1024]
            dst = wo[:, a, :]
        P.dma("sync", (lambda e, st=st, src=src, w=w: e.dma_start(out=st[:, 0:w], in_=src)), writes=[stB])
        P.op("gpsimd", (lambda e, st=st, dst=dst, w=w: e.tensor_copy(out=dst, in_=st[:, 0:w])),
             reads=[stB], writes=[sB])


def prep_vectors(R, w_pre, w_post, mod, ioff, gate_scale, modB=None):
    P = R.P
    colv = lambda v, off: v[off:off + D].rearrange("(kc p) -> p kc", p=128)
    rowv = lambda v, off: v[off:off + D].partition_broadcast(128)
    rows = lambda v, off: v[off:off + D].rearrange("(kc p) -> kc p", p=128)
    P.dma("sync", lambda e: e.dma_start(out=R.vrows[:, 0, :], in_=rows(w_pre, 0)), writes=[R.vrowsB])
    P.dma("sync", lambda e: e.dma_start(out=R.vrows[:, 1, :], in_=rows(mod, ioff * D)), reads=(list(modB) if modB else []), writes=[R.vrowsB])
    P.dma("sync", lambda e: e.dma_start(out=R.vrows[:, 2, :], in_=rows(mod, (ioff + 1) * D)), reads=(list(modB) if modB else []), writes=[R.vrowsB])
    pc = R.pC[0]

    def tr(i):
        P.op("tensor", lambda e: e.matmul(pc[:, i * 8:(i + 1) * 8], lhsT=R.vrows[:, i, :], rhs=R.identf[0:8, 0:8], start=True, stop=True),
             reads=[R.vrowsB, R.identfB], writes=[R.pCB], inc=(i == 2))
    for i in range(3):
        tr(i)
    P.op("vector", lambda e: e.tensor_copy(out=R.acol[:], in_=pc[:, 0:8]), reads=[R.pCB], writes=[R.colB])
    P.op("vector", lambda e: e.tensor_copy(out=R.bcol[:], in_=pc[:, 8:16]), reads=[R.pCB], writes=[R.colB])
    P.op("vector", lambda e: e.tensor_copy(out=R.tcol[:], in_=pc[:, 16:24]), reads=[R.pCB], writes=[R.colB])
    P.op("vector", lambda e: e.tensor_scalar(out=R.tcol[:], in0=R.tcol[:], scalar1=1.0, scalar2=32.0, op0=ALU.add, op1=ALU.mult),
         reads=[R.colB], writes=[R.colB])
    P.op("vector", lambda e: e.tensor_tensor(out=R.acol[:], in0=R.acol[:], in1=R.tcol[:], op=ALU.mult),
         reads=[R.colB], writes=[R.colB])
    P.dma("sync", lambda e: e.dma_start(out=R.crow[:], in_=rowv(w_post, 0)), writes=[R.crowB])
    P.dma("sync", lambda e: e.dma_start(out=R.ctmp[:], in_=rowv(mod, (ioff + 2) * D)), reads=(list(modB) if modB else []), writes=[R.ctmpB])
    P.op("vector", lambda e: e.scalar_tensor_tensor(out=R.crow[:], in0=R.crow[:], scalar=32.0 * gate_scale, in1=R.ctmp[:],
                                                    op0=ALU.mult, op1=ALU.mult),
         reads=[R.crowB, R.ctmpB], writes=[R.crowB])


def stage_a_sub(R, X_in, t, s, hbuf, part):
    P = R.P
    i = (t * 4 + s) % 2
    xa, xaB, xn, xnB = R.xa[i], R.xaB[i], R.xn[i], R.xnB[i]
    if part == 0:
        r0 = t * TT + s * 128
        P.dma("sync", lambda e: e.dma_start(out=xa[:], in_=X_in[r0:r0 + 128, :]), writes=[xaB])
        k = R.ss_i
        R.ss_i = (k + 1) % 4
        ss, ssB = R.ss[k], R.ssB[k]
        P.op("scalar", lambda e: e.activation(out=R.junk[:], in_=xa[:], func=AF.Square, accum_out=ss[:, 0:1]),
             reads=[xaB], writes=[R.junkB, ssB])
        P.op("scalar", lambda e: e.activation(out=ss[:, 1:2], in_=ss[:, 0:1], func=AF.Sqrt, bias=R.epsc[:, 0:1], scale=1.0),
             reads=[ssB, R.epscB], writes=[ssB])
        P.op("vector", lambda e: e.reciprocal(out=ss[:, 1:2], in_=ss[:, 1:2]), reads=[ssB], writes=[ssB])
        P.op("scalar", lambda e: e.activation(out=xn[:], in_=xa[:], func=AF.Copy, scale=ss[:, 1:2]),
             reads=[xaB, ssB], writes=[xnB])
    else:
        pt, ptB = R.pT[i], R.pTB[i]
        for kc in range(NKC):
            P.op("tensor", (lambda e, kc=kc: e.transpose(out=pt[:, kc, :], in_=xn[:, kc * 128:(kc + 1) * 128], identity=R.ident[:])),
                 reads=[xnB, R.identB], writes=[ptB], inc=(kc == NKC - 1))
        hT, hB = R.hT[hbuf], R.hTB[hbuf][s]
        for kc in range(NKC):
            eng = "vector" if kc % 2 == 0 else "gpsimd"
            if eng == "gpsimd":
                eng = "vector"
            P.op(eng, (lambda e, kc=kc: e.tensor_scalar(out=hT[:, kc, s * 128:(s + 1) * 128], in0=pt[:, kc, :],
                                                       scalar1=R.acol[:, kc:kc + 1], scalar2=R.bcol[:, kc:kc + 1],
                                                       op0=ALU.mult, op1=ALU.add)),
                 reads=[ptB, R.colB], writes=[hB])


def stage_b_group(R, wi, sB, hT, hTBl, j, nch, gidx, aj=None):
    P = R.P
    if aj is None:
        aj = j
    k = gidx % 2
    pg, pu, pgB, puB = R.pB[2 * k], R.pB[2 * k + 1], R.pBB[2 * k], R.pBB[2 * k + 1]
    sg, sgB = R.sg[k], R.sgB[k]

    def mm(pp, ppB, c0, kc):
        P.op("tensor", lambda e: e.matmul(pp[:], lhsT=wi[:, kc, c0:c0 + 128], rhs=hT[:, kc, :],
                                          start=(kc == 0), stop=(kc == NKC - 1)),
             reads=[sB] + hTBl, writes=[ppB], inc=(kc == NKC - 1))
    for (pp, ppB, c0) in ((pg, pgB, j * 128), (pu, puB, nch * 128 + j * 128)):
        for kc in range(NKC):
            mm(pp, ppB, c0, kc)
    P.op("scalar", lambda e: e.activation(out=sg[:], in_=pg[:], func=AF.Silu), reads=[pgB], writes=[sgB])
    P.op("vector", lambda e: e.tensor_tensor(out=R.aT[:, aj, :], in0=sg[:], in1=pu[:], op=ALU.mult),
         reads=[sgB, puB], writes=[R.aTB[aj]])


def stage_c_sub(R, wo, sB, nch, X_in, X_out, Yacc, yB, t, s, sl, last, lhs=None):
    P = R.P
    if lhs is None:
        lhs = [(R.aT, j, R.aTB[j]) for j in range(nch)]
    r0 = t * TT + s * 128
    yi = (t * 4 + s) % 2
    yb, ybB, xr, xrB = R.yb[yi], R.ybB[yi], R.xr[yi], R.xrB[yi]
    if sl > 0:
        P.dma("sync", lambda e: e.dma_start(out=yb[:], in_=Yacc[r0:r0 + 128, :]), reads=[yB], writes=[ybB])
    if last:
        P.dma("sync", lambda e: e.dma_start(out=xr[:], in_=X_in[r0:r0 + 128, :]), writes=[xrB])

    def mm(half, j):
        lt, li, lB = lhs[j]
        P.op("tensor", lambda e: e.matmul(R.pC[half][:], lhsT=lt[:, li, s * 128:(s + 1) * 128],
                                          rhs=wo[:, j, half * 512:(half + 1) * 512],
                                          start=(j == 0), stop=(j == nch - 1)),
             reads=[sB] + (lB if isinstance(lB, list) else [lB]), writes=[R.pCB], inc=(j == nch - 1))
    for half in range(2):
        for j in range(nch):
            mm(half, j)

    def evac(half):
        hs = slice(half * 512, (half + 1) * 512)
        if sl == 0:
            if half == 0:
                P.op("vector", lambda e: e.tensor_copy(out=yb[:, hs], in_=R.pC[half][:]), reads=[R.pCB], writes=[ybB])
            else:
                P.op("scalar", lambda e: e.copy(out=yb[:, hs], in_=R.pC[half][:]), reads=[R.pCB], writes=[ybB])
        else:
            P.op("vector", lambda e: e.tensor_tensor(out=yb[:, hs], in0=yb[:, hs], in1=R.pC[half][:], op=ALU.add),
                 reads=[R.pCB, ybB], writes=[ybB])
    evac(0)
    evac(1)
    if not last:
        P.dma("gpsimd", lambda e: e.dma_start(out=Yacc[r0:r0 + 128, :], in_=yb[:]), reads=[ybB], writes=[yB])
    else:
        k = R.ss_i
        R.ss_i = (k + 1) % 4
        ss, ssB = R.ss[k], R.ssB[k]
        P.op("scalar", lambda e: e.activation(out=R.junk[:], in_=yb[:], func=AF.Square, accum_out=ss[:, 0:1]),
             reads=[ybB], writes=[R.junkB, ssB])
        P.op("scalar", lambda e: e.activation(out=ss[:, 1:2], in_=ss[:, 0:1], func=AF.Sqrt, bias=R.epsc[:, 0:1], scale=1.0),
             reads=[ssB, R.epscB], writes=[ssB])
        P.op("vector", lambda e: e.reciprocal(out=ss[:, 1:2], in_=ss[:, 1:2]), reads=[ssB], writes=[ssB])
        P.op("scalar", lambda e: e.activation(out=yb[:], in_=yb[:], func=AF.Copy, scale=ss[:, 1:2]),
             reads=[ybB, ssB], writes=[ybB])
        P.op("gpsimd", lambda e: e.tensor_tensor(out=yb[:], in0=yb[:], in1=R.crow[:], op=ALU.mult),
             reads=[ybB, R.crowB], writes=[ybB])
        P.op("vector", lambda e: e.tensor_tensor(out=xr[:], in0=xr[:], in1=yb[:], op=ALU.add),
             reads=[ybB, xrB], writes=[xrB])
        P.dma("gpsimd", lambda e: e.dma_start(out=X_out[r0:r0 + 128, :], in_=xr[:]), reads=[xrB], writes=[yB])


def ffn_phase(R, X_in, X_out, Yacc, w_in, w_out, NT, first_slab_loaded=False, next_loader=None):
    P = R.P
    ntile = NT // TT
    nsl = len(SLABS)
    if not first_slab_loaded:
        load_slab(R, w_in, w_out, SLABS[0], 0)
    YB = [Buf("Y%d" % i) for i in range(ntile * 4)]
    gidx = 0
    for sl in range(nsl):
        buf = sl % 2
        j0, nch = SLABS[sl]
        last = sl == nsl - 1
        if sl + 1 < nsl:
            load_slab(R, w_in, w_out, SLABS[sl + 1], (sl + 1) % 2)
        elif next_loader is not None:
            next_loader((sl + 1) % 2)
        for s in range(4):
            stage_a_sub(R, X_in, 0, s, 0, 0)
            stage_a_sub(R, X_in, 0, s, 0, 1)
        for t in range(ntile):
            hb = t % 2
            for j in range(nch):
                stage_b_group(R, R.wi[buf], R.slabB[buf], R.hT[hb], R.hTB[hb], j, nch, gidx)
                gidx += 1
                if t + 1 < ntile and j < 8:
                    stage_a_sub(R, X_in, t + 1, j // 2, 1 - hb, j % 2)
            if t + 1 < ntile:
                for jj in range(nch, 8):
                    stage_a_sub(R, X_in, t + 1, jj // 2, 1 - hb, jj % 2)
            for s in range(4):
                stage_c_sub(R, R.wo[buf], R.slabB[buf], nch, X_in, X_out, Yacc, YB[t * 4 + s], t, s, sl, last)
    return YB


NH = 4
NCST = 384
STOP = 0


class StopEmit(Exception):
    pass


def stop_at(k):
    if STOP == k:
        raise StopEmit()


def make_consts():
    c = np.zeros((128, NCST), np.float32)
    c[:, 0:128] = np.eye(128)
    c[:, 128:256] = 1.0
    k = np.arange(64)
    c[0:64, 256:320] = (k[:, None] <= k[None, :])
    c[0:64, 320:384] = (k[:, None] > k[None, :])
    return c


def load_cols(R, w, c0, ncol, buf):
    P = R.P
    wi, sB = R.wi[buf], R.slabB[buf]

    def piece(kc, d0, w_):
        si = R.stg_i
        R.stg_i = (si + 1) % 3
        st, stB = R.stg[si], R.stgB[si]
        P.dma("sync", lambda e: e.dma_start(out=st[:, 0:w_], in_=w[kc * 128:(kc + 1) * 128, c0 + d0:c0 + d0 + w_]), writes=[stB])
        P.op("gpsimd", lambda e: e.tensor_copy(out=wi[:, kc, d0:d0 + w_], in_=st[:, 0:w_]), reads=[stB], writes=[sB])
    for kc in range(NKC):
        d0 = 0
        while d0 < ncol:
            w_ = min(1024, ncol - d0)
            piece(kc, d0, w_)
            d0 += w_


def proj_phase(R, X_in, w_mix, PT, NT, PTB):
    P = R.P
    load_cols(R, w_mix, 0, 2048, 0)
    load_cols(R, w_mix, 2048, 520, 1)
    ntile = NT // TT
    chunks = [(0, j * 128, 128, j * 128) for j in range(16)]
    chunks += [(1, 8 + j * 128, 128, 2056 + j * 128) for j in range(4)]
    chunks += [(1, 0, 8, 2048)]
    gi = 0
    for s in range(4):
        stage_a_sub(R, X_in, 0, s, 0, 0)
        stage_a_sub(R, X_in, 0, s, 0, 1)

    def out_chunk(t, hb, buf, lc, M, row, gi):
        pp, ppB = R.pB[gi % 4], R.pBB[gi % 4]
        sg, sgB = R.sg[gi % 2], R.sgB[gi % 2]
        wi, sB, hT = R.wi[buf], R.slabB[buf], R.hT[hb]

        def mm(kc):
            P.op("tensor", lambda e: e.matmul(pp[0:M, :], lhsT=wi[:, kc, lc:lc + M], rhs=hT[:, kc, :], start=(kc == 0), stop=(kc == NKC - 1)),
                 reads=[sB] + R.hTB[hb], writes=[ppB], inc=(kc == NKC - 1))
        for kc in range(NKC):
            mm(kc)
        if gi % 2 == 0:
            P.op("vector", lambda e: e.tensor_copy(out=sg[0:M, :], in_=pp[0:M, :]), reads=[ppB], writes=[sgB])
        else:
            P.op("scalar", lambda e: e.copy(out=sg[0:M, :], in_=pp[0:M, :]), reads=[ppB], writes=[sgB])
        P.dma("gpsimd", lambda e: e.dma_start(out=PT[row:row + M, t * TT:(t + 1) * TT], in_=sg[0:M, :]), reads=[sgB], writes=[PTB[t]])
    for t in range(ntile):
        hb = t % 2
        for ci, (buf, lc, M, row) in enumerate(chunks):
            out_chunk(t, hb, buf, lc, M, row, gi)
            gi += 1
            if t + 1 < ntile and ci < 8:
                stage_a_sub(R, X_in, t + 1, ci // 2, 1 - hb, ci % 2)


class MixRes:
    def __init__(self, nc, stack, P, tag=""):
        self.nc, self.P = nc, P
        self._sb = lambda name, shape, dt=F32: stack.enter_context(nc.sbuf_tensor("m_" + name + tag, shape, dt))
        self._ps = lambda name, shape, dt=F32: stack.enter_context(nc.psum_tensor("m_" + name + tag, shape, dt))
        self.bufs = {}

    def sb(self, name, shape, dt=F32):
        t = self._sb(name, shape, dt)
        self.bufs[name] = Buf(name)
        setattr(self, name, t)
        setattr(self, name + "B", self.bufs[name])
        return t

    def ps(self, name, shape, dt=F32):
        t = self._ps(name, shape, dt)
        self.bufs[name] = Buf(name)
        setattr(self, name, t)
        setattr(self, name + "B", self.bufs[name])
        return t


def gdn_phase(M, PT, YT, conv_w, a_log, dt_bias, gnw, cst_dram, NT, PTB, YTB):
    P = M.P
    ntile = NT // TT
    sb, ps = M.sb, M.ps
    G2 = 2
    cst = sb("cst", [128, NCST])
    ident = cst[:, 0:128]
    ones = cst[:, 128:256]
    LE = cst[0:64, 256:320]
    GT = cst[0:64, 320:384]
    P.dma("sync", lambda e: e.dma_start(out=cst[:], in_=cst_dram), writes=[M.cstB])
    sb("LE4", [64, G2, 64]); sb("GT4", [64, G2, 64]); sb("I4", [64, G2, 64])
    for h in range(G2):
        P.op("vector", (lambda e, h=h: e.tensor_copy(out=M.LE4[:, h, :], in_=LE)), reads=[M.cstB], writes=[M.LE4B])
        P.op("vector", (lambda e, h=h: e.tensor_copy(out=M.GT4[:, h, :], in_=GT)), reads=[M.cstB], writes=[M.GT4B])
        P.op("vector", (lambda e, h=h: e.tensor_copy(out=M.I4[:, h, :], in_=cst[0:64, 0:64])), reads=[M.cstB], writes=[M.I4B])
    sb("epsk", [128, 1]); sb("epsq", [128, 1]); sb("epsn", [128, 1]); sb("one1", [128, 1])
    P.op("vector", lambda e: e.memset(M.epsk[:], 1e-6), writes=[M.epskB])
    P.op("vector", lambda e: e.memset(M.epsq[:], 128e-6), writes=[M.epsqB])
    P.op("vector", lambda e: e.memset(M.epsn[:], 1e-6), writes=[M.epsnB])
    P.op("vector", lambda e: e.memset(M.one1[:], 1.0), writes=[M.one1B])

    class Grp:
        pass
    groups = []
    for gi in range(2):
        G = Grp()
        G.h0 = gi * G2
        sfx = "_g%d" % gi

        def gsb(name, shape, G=G, sfx=sfx):
            t = sb(name + sfx, shape)
            setattr(G, name, t)
            setattr(G, name + "B", getattr(M, name + sfx + "B"))

        def gps(name, shape, G=G, sfx=sfx):
            t = ps(name + sfx, shape)
            setattr(G, name, t)
            setattr(G, name + "B", getattr(M, name + sfx + "B"))
        banks = [ps("bk%d" % k + sfx, [128, 512]) for k in range(4)]
        r4 = lambda ap: ap.rearrange("p (a h c) -> p a h c", a=2, h=G2)
        for nm, ap in (("psD", r4(banks[0][0:64, 0:256])), ("psG", r4(banks[0][0:64, 256:512])),
                       ("psI", r4(banks[1][0:64, 0:256])), ("psU", r4(banks[1][0:64, 256:512])),
                       ("psX", banks[2][:, 0:256]), ("psY", banks[2][:, 256:512]),
                       ("psZ", banks[3][:, 0:256]), ("psS", banks[3][:, 256:384])):
            setattr(G, nm, ap)
        bankB = [Buf("bk%d" % k + sfx) for k in range(4)]
        for nm, k in (("psD", 0), ("psG", 0), ("psI", 1), ("psU", 1), ("psX", 2), ("psY", 2), ("psZ", 3), ("psS", 3)):
            setattr(G, nm + "B", bankB[k])
        gsb("S", [128, G2, 128]); gsb("yg", [128, G2, TT])
        gsb("ba", [64, 8]); gsb("bt", [64, G2]); gsb("nbt", [64, G2]); gsb("g", [64, G2]); gsb("gcs", [64, G2]); gsb("egc", [64, G2])
        gsb("egl", [128, G2]); gsb("egd", [64, G2]); gsb("begc", [64, G2])
        gsb("G12", [64, 2, G2, 64]); gsb("eD", [64, 2, G2, 64]); gsb("dec", [64, 2, G2, 64])
        gsb("AA", [64, 2, G2, 64]); gsb("PU", [64, G2, 64]); gsb("attT", [64, G2, 64]); gsb("tL", [64, G2, 64])
        gsb("vb", [64, G2, 128]); gsb("kbg", [64, G2, 128]); gsb("kst", [64, G2, 128]); gsb("wv", [64, G2, 128]); gsb("kcT", [128, G2, 64])
        gsb("vn", [64, G2, 128]); gsb("o1", [64, G2, 128]); gsb("osq", [64, G2, 128]); gsb("on", [64, G2, 128]); gsb("ssq", [64, 2 * G2])
        P.op("vector", (lambda e, G=G: e.memset(G.S[:], 0.0)), writes=[G.SB])
        groups.append(G)
    GA, GB = groups
    psS0, psS0B = GA.psS, GA.psSB
    sb("cwr", [4, 1536]); sb("cw", [128, 12, 4])
    P.dma("sync", lambda e: e.dma_start(out=M.cwr[:], in_=conv_w), writes=[M.cwrB])
    for ct in range(12):
        P.op("tensor", (lambda e, ct=ct: e.matmul(psS0[:, ct * 4:(ct + 1) * 4], lhsT=M.cwr[0:4, ct * 128:(ct + 1) * 128], rhs=cst[0:4, 0:4], start=True, stop=True)),
             reads=[M.cwrB, M.cstB], writes=[psS0B], inc=(ct == 11))
    P.op("vector", lambda e: e.tensor_copy(out=M.cw[:].rearrange("p a b -> p (a b)"), in_=psS0[:, 0:48]), reads=[psS0B], writes=[M.cwB])
    sb("gnr", [1, 128]); sb("gnc", [128, 1])
    P.dma("sync", lambda e: e.dma_start(out=M.gnr[:], in_=gnw.rearrange("(o f) -> o f", o=1)), writes=[M.gnrB])
    P.op("tensor", lambda e: e.matmul(psS0[:, 0:1], lhsT=M.gnr[0:1, :], rhs=cst[0:1, 0:1], start=True, stop=True), reads=[M.gnrB, M.cstB], writes=[psS0B])
    P.op("vector", lambda e: e.tensor_copy(out=M.gnc[:], in_=psS0[:, 0:1]), reads=[psS0B], writes=[M.gncB])
    sb("nA", [64, NH]); sb("dtb", [64, NH]); sb("adr", [1, 8])
    P.dma("sync", lambda e: e.dma_start(out=M.adr[:, 0:4], in_=a_log.rearrange("(o f) -> o f", o=1)), writes=[M.adrB])
    P.dma("sync", lambda e: e.dma_start(out=M.adr[:, 4:8], in_=dt_bias.rearrange("(o f) -> o f", o=1)), writes=[M.adrB])
    P.op("tensor", lambda e: e.matmul(psS0[0:64, 0:8], lhsT=cst[0:1, 128:192], rhs=M.adr[0:1, :], start=True, stop=True), reads=[M.adrB, M.cstB], writes=[psS0B])
    P.op("vector", lambda e: e.tensor_copy(out=M.nA[:], in_=psS0[0:64, 0:4]), reads=[psS0B], writes=[M.nAB])
    P.op("vector", lambda e: e.tensor_copy(out=M.dtb[:], in_=psS0[0:64, 4:8]), reads=[psS0B], writes=[M.dtbB])
    P.op("scalar", lambda e: e.activation(out=M.nA[:], in_=M.nA[:], func=AF.Exp), reads=[M.nAB], writes=[M.nAB])
    P.op("vector", lambda e: e.tensor_scalar(out=M.nA[:], in0=M.nA[:], scalar1=-1.0, scalar2=None, op0=ALU.mult), reads=[M.nAB], writes=[M.nAB])
    sb("qkv", [128, 12, TT]); sb("xin", [128, TT + 3]); sb("acc", [128, TT]); sb("sq", [128, TT]); sb("rn", [128, TT])
    sb("sz", [128, NH, TT]); sb("bar", [8, TT])

    def conv_tile(t, ct):
        t0 = t * TT
        r0 = ct * 128
        if t == 0:
            P.op("gpsimd", lambda e: e.memset(M.xin[:, 0:3], 0.0), writes=[M.xinB])
            P.dma("sync", lambda e: e.dma_start(out=M.xin[:, 3:TT + 3], in_=PT[r0:r0 + 128, 0:TT]), reads=[PTB[0]], writes=[M.xinB])
        else:
            P.dma("sync", lambda e: e.dma_start(out=M.xin[:], in_=PT[r0:r0 + 128, t0 - 3:t0 + TT]), reads=[PTB[t - 1], PTB[t]], writes=[M.xinB])
        P.op("vector", lambda e: e.tensor_scalar(out=M.acc[:], in0=M.xin[:, 0:TT], scalar1=M.cw[:, ct, 0:1], scalar2=None, op0=ALU.mult),
             reads=[M.xinB, M.cwB], writes=[M.accB])
        for j in range(1, 4):
            P.op("vector", (lambda e, j=j: e.scalar_tensor_tensor(out=M.acc[:], in0=M.xin[:, j:j + TT], scalar=M.cw[:, ct, j:j + 1], in1=M.acc[:],
                                                                  op0=ALU.mult, op1=ALU.add)), reads=[M.xinB, M.cwB, M.accB], writes=[M.accB])
        P.op("scalar", lambda e: e.activation(out=M.qkv[:, ct, :], in_=M.acc[:], func=AF.Silu), reads=[M.accB], writes=[M.qkvB])
        if ct < 8:
            P.op("scalar", lambda e: e.activation(out=M.sq[:], in_=M.qkv[:, ct, :], func=AF.Square), reads=[M.qkvB], writes=[M.sqB])
            for hf, Gx in enumerate((GA, GB)):
                def half(hf=hf, Gx=Gx):
                    cs_ = slice(hf * 256, (hf + 1) * 256)
                    P.op("tensor", lambda e: e.matmul(Gx.psX[:], lhsT=ones, rhs=M.sq[:, cs_], start=True, stop=True), reads=[M.sqB, M.cstB], writes=[Gx.psXB])
                    P.op("vector", lambda e: e.tensor_copy(out=M.rn[:, cs_], in_=Gx.psX[:]), reads=[Gx.psXB], writes=[M.rnB])
                    if ct < 4:
                        P.op("scalar", lambda e: e.activation(out=M.rn[:, cs_], in_=M.rn[:, cs_], func=AF.Sqrt, bias=M.epsq[:, 0:1], scale=128.0),
                             reads=[M.rnB, M.epsqB], writes=[M.rnB])
                    else:
                        P.op("scalar", lambda e: e.activation(out=M.rn[:, cs_], in_=M.rn[:, cs_], func=AF.Sqrt, bias=M.epsk[:, 0:1], scale=1.0),
                             reads=[M.rnB, M.epskB], writes=[M.rnB])
                half()
            P.op("vector", lambda e: e.reciprocal(out=M.rn[:], in_=M.rn[:]), reads=[M.rnB], writes=[M.rnB])
            P.op("gpsimd", lambda e: e.tensor_tensor(out=M.qkv[:, ct, :], in0=M.qkv[:, ct, :], in1=M.rn[:], op=ALU.mult),
                 reads=[M.qkvB, M.rnB], writes=[M.qkvB])

    def z_tile(t, h):
        t0 = t * TT
        P.dma("sync", lambda e: e.dma_start(out=M.sz[:, h, :], in_=PT[1536 + h * 128:1536 + (h + 1) * 128, t0:t0 + TT]), reads=[PTB[t]], writes=[M.szB])
        P.op("scalar", lambda e: e.activation(out=M.sz[:, h, :], in_=M.sz[:, h, :], func=AF.Silu), reads=[M.szB], writes=[M.szB])
        P.op("gpsimd", lambda e: e.tensor_scalar(out=M.sz[:, h, :], in0=M.sz[:, h, :], scalar1=M.gnc[:, 0:1], scalar2=None, op0=ALU.mult),
             reads=[M.szB, M.gncB], writes=[M.szB])

    def mm(out, lhsT, rhs, reads, writes, inc=True):
        P.op("tensor", lambda e: e.matmul(out, lhsT=lhsT, rhs=rhs, start=True, stop=True), reads=reads, writes=writes, inc=inc)

    def chunk(n, G):
        h0 = G.h0
        HR = range(G2)
        c0 = n * 64
        cs = slice(c0, c0 + 64)
        qT = lambda h: M.qkv[:, h0 + h, cs]
        kT = lambda h: M.qkv[:, 4 + h0 + h, cs]
        vT = lambda h: M.qkv[:, 8 + h0 + h, cs]
        mm(G.psS[0:64, 0:8], M.bar[0:8, cs], cst[0:8, 0:8], [M.barB, M.cstB], [G.psSB])
        P.op("vector", lambda e: e.tensor_copy(out=G.ba[:], in_=G.psS[0:64, 0:8]), reads=[G.psSB], writes=[G.baB])
        yield
        P.op("scalar", lambda e: e.activation(out=G.bt[:], in_=G.ba[:, h0:h0 + G2], func=AF.Sigmoid), reads=[G.baB], writes=[G.btB])
        P.op("vector", lambda e: e.tensor_tensor(out=G.g[:], in0=G.ba[:, 4 + h0:4 + h0 + G2], in1=M.dtb[:, h0:h0 + G2], op=ALU.add), reads=[G.baB, M.dtbB], writes=[G.gB])
        yield
        P.op("vector", lambda e: e.tensor_scalar(out=G.nbt[:], in0=G.bt[:], scalar1=-1.0, scalar2=None, op0=ALU.mult), reads=[G.btB], writes=[G.nbtB])
        P.op("scalar", lambda e: e.activation(out=G.g[:], in_=G.g[:], func=AF.Exp), reads=[G.gB], writes=[G.gB])
        yield
        P.op("scalar", lambda e: e.activation(out=G.g[:], in_=G.g[:], func=AF.Ln, bias=M.one1[0:64, 0:1], scale=1.0), reads=[G.gB, M.one1B], writes=[G.gB])
        yield
        P.op("vector", lambda e: e.tensor_tensor(out=G.g[:], in0=G.g[:], in1=M.nA[:, h0:h0 + G2], op=ALU.mult), reads=[G.gB, M.nAB], writes=[G.gB])
        yield
        mm(G.psS[0:64, 8:8 + G2], LE, G.g[:], [G.gB, M.cstB], [G.psSB], inc=False)
        mm(G.psS[:, 12:12 + G2], cst[0:64, 128:256], G.g[:], [G.gB, M.cstB], [G.psSB])
        for h in HR:
            P.op("gpsimd", (lambda e, h=h: e.tensor_scalar(out=G.G12[:, 0, h, :], in0=LE, scalar1=G.g[:, h:h + 1], scalar2=None, op0=ALU.mult)),
                 reads=[G.gB, M.cstB], writes=[G.G12B])
            P.op("gpsimd", (lambda e, h=h: e.tensor_scalar(out=G.G12[:, 1, h, :], in0=GT, scalar1=G.g[:, h:h + 1], scalar2=None, op0=ALU.mult)),
                 reads=[G.gB, M.cstB], writes=[G.G12B])
        yield
        P.op("vector", lambda e: e.tensor_copy(out=G.gcs[:], in_=G.psS[0:64, 8:8 + G2]), reads=[G.psSB], writes=[G.gcsB])
        P.op("vector", lambda e: e.tensor_copy(out=G.egl[:], in_=G.psS[:, 12:12 + G2]), reads=[G.psSB], writes=[G.eglB])
        P.op("scalar", lambda e: e.activation(out=G.egc[:], in_=G.gcs[:], func=AF.Exp), reads=[G.gcsB], writes=[G.egcB])
        P.op("scalar", lambda e: e.activation(out=G.egl[:], in_=G.egl[:], func=AF.Exp), reads=[G.eglB], writes=[G.eglB])
        for h in HR:
            mm(G.psD[:, 0, h, :], G.G12[:, 0, h, :], GT, [G.G12B, M.cstB], [G.psDB], inc=False)
            mm(G.psD[:, 1, h, :], G.G12[:, 1, h, :], LE, [G.G12B, M.cstB], [G.psDB], inc=(h == G2 - 1))
        for h in HR:
            mm(G.psG[:, 0, h, :], kT(h), kT(h), [M.qkvB], [G.psGB], inc=False)
            mm(G.psU[:, 1, h, :], kT(h), qT(h), [M.qkvB], [G.psUB], inc=(h == G2 - 1))
        yield
        P.op("vector", lambda e: e.tensor_tensor(out=G.egd[:], in0=G.psS[0:64, 12:12 + G2], in1=G.gcs[:], op=ALU.subtract), reads=[G.psSB, G.gcsB], writes=[G.egdB])
        P.op("vector", lambda e: e.tensor_tensor(out=G.begc[:], in0=G.bt[:], in1=G.egc[:], op=ALU.mult), reads=[G.btB, G.egcB], writes=[G.begcB])
        P.op("vector", lambda e: e.tensor_copy(out=G.eD[:], in_=G.psD[:]), reads=[G.psDB], writes=[G.eDB])
        P.op("scalar", lambda e: e.activation(out=G.eD[:], in_=G.eD[:], func=AF.Exp), reads=[G.eDB], writes=[G.eDB])
        yield
        P.op("scalar", lambda e: e.activation(out=G.egd[:], in_=G.egd[:], func=AF.Exp), reads=[G.egdB], writes=[G.egdB])
        P.op("gpsimd", lambda e: e.tensor_tensor(out=G.dec[:, 0], in0=G.eD[:, 0], in1=M.GT4[:], op=ALU.mult), reads=[G.eDB, M.GT4B], writes=[G.decB])
        P.op("gpsimd", lambda e: e.tensor_tensor(out=G.dec[:, 1], in0=G.eD[:, 1], in1=M.LE4[:], op=ALU.mult), reads=[G.eDB, M.LE4B], writes=[G.decB])
        yield
        P.op("vector", lambda e: e.tensor_tensor(out=G.tL[:], in0=G.psG[:, 0], in1=G.dec[:, 0], op=ALU.mult), reads=[G.psGB, G.decB], writes=[G.tLB])
        P.op("vector", lambda e: e.tensor_tensor(out=G.attT[:], in0=G.psU[:, 1], in1=G.dec[:, 1], op=ALU.mult), reads=[G.psUB, G.decB], writes=[G.attTB])
        yield
        for h in HR:
            P.op("gpsimd", (lambda e, h=h: e.tensor_scalar(out=G.AA[:, 1, h, :], in0=G.tL[:, h, :], scalar1=G.nbt[:, h:h + 1], scalar2=None, op0=ALU.mult)),
                 reads=[G.tLB, G.nbtB], writes=[G.AAB])
        yield
        for h in HR:
            mm(G.psG[:, 1, h, :], G.AA[:, 1, h, :], cst[0:64, 0:64], [G.AAB, M.cstB], [G.psGB], inc=(h == G2 - 1))
        yield
        P.op("vector", lambda e: e.tensor_copy(out=G.AA[:, 0], in_=G.psG[:, 1]), reads=[G.psGB], writes=[G.AAB])
        yield
        P.op("vector", lambda e: e.tensor_tensor(out=G.PU[:], in0=G.AA[:, 0], in1=M.I4[:], op=ALU.add), reads=[G.AAB, M.I4B], writes=[G.PUB])
        for m in range(5):
            for h in HR:
                mm(G.psI[:, 0, h, :], G.AA[:, 1, h, :], G.AA[:, 0, h, :], [G.AAB], [G.psIB], inc=False)
                mm(G.psI[:, 1, h, :], G.AA[:, 0, h, :], G.AA[:, 1, h, :], [G.AAB], [G.psIB], inc=(h == G2 - 1))
            yield
            P.op("vector", lambda e: e.tensor_copy(out=G.AA[:], in_=G.psI[:]), reads=[G.psIB], writes=[G.AAB])
            yield
            for h in HR:
                mm(G.psU[:, 0, h, :], G.AA[:, 1, h, :], G.PU[:, h, :], [G.AAB, G.PUB], [G.psUB], inc=(h == G2 - 1))
            yield
            P.op("vector", lambda e: e.tensor_tensor(out=G.PU[:], in0=G.PU[:], in1=G.psU[:, 0], op=ALU.add), reads=[G.psUB, G.PUB], writes=[G.PUB])
            yield
        X3 = G.psX[0:64, :].rearrange("p (h d) -> p h d", h=G2)
        Y3 = G.psY[0:64, :].rearrange("p (h d) -> p h d", h=G2)
        Z3 = G.psZ[0:64, :].rearrange("p (h d) -> p h d", h=G2)

        def tr(out, in_, wB, inc):
            P.op("tensor", lambda e: e.transpose(out=out, in_=in_, identity=ident), reads=[M.qkvB, M.cstB], writes=[wB], inc=inc)
        for h in HR:
            tr(X3[:, h, :], kT(h), G.psXB, False)
            tr(Y3[:, h, :], vT(h), G.psYB, h == G2 - 1)
        yield
        for h in HR:
            P.op("vector", (lambda e, h=h: e.tensor_scalar(out=G.vb[:, h, :], in0=Y3[:, h, :], scalar1=G.bt[:, h:h + 1], scalar2=None, op0=ALU.mult)),
                 reads=[G.psYB, G.btB], writes=[G.vbB])
            P.op("vector", (lambda e, h=h: e.tensor_scalar(out=G.kbg[:, h, :], in0=X3[:, h, :], scalar1=G.begc[:, h:h + 1], scalar2=None, op0=ALU.mult)),
                 reads=[G.psXB, G.begcB], writes=[G.kbgB])
            P.op("vector", (lambda e, h=h: e.tensor_scalar(out=G.kst[:, h, :], in0=X3[:, h, :], scalar1=G.egd[:, h:h + 1], scalar2=None, op0=ALU.mult)),
                 reads=[G.psXB, G.egdB], writes=[G.kstB])
        yield
        YK = G.psY[:, 0:G2 * 64].rearrange("p (h d) -> p h d", h=G2)
        for h in HR:
            mm(X3[:, h, :], G.PU[:, h, :], G.vb[:, h, :], [G.PUB, G.vbB], [G.psXB], inc=False)
            mm(YK[:, h, :], G.kbg[:, h, :], G.PU[:, h, :], [G.PUB, G.kbgB], [G.psYB], inc=(h == G2 - 1))
        yield
        P.op("vector", lambda e: e.tensor_copy(out=G.wv[:], in_=X3), reads=[G.psXB], writes=[G.wvB])
        P.op("vector", lambda e: e.tensor_copy(out=G.kcT[:], in_=YK), reads=[G.psYB], writes=[G.kcTB])
        yield
        for h in HR:
            mm(X3[:, h, :], G.kcT[:, h, :], G.S[:, h, :], [G.kcTB, G.SB], [G.psXB], inc=(h == G2 - 1))
        yield
        P.op("vector", lambda e: e.tensor_tensor(out=G.vn[:], in0=G.wv[:], in1=X3, op=ALU.subtract), reads=[G.wvB, G.psXB], writes=[G.vnB])
        yield
        for h in HR:
            mm(Y3[:, h, :], qT(h), G.S[:, h, :], [M.qkvB, G.SB], [G.psYB], inc=False)
            mm(Z3[:, h, :], G.attT[:, h, :], G.vn[:, h, :], [G.attTB, G.vnB], [G.psZB], inc=(h == G2 - 1))
        XS = G.psX[:, :].rearrange("p (h d) -> p h d", h=G2)
        for h in HR:
            mm(XS[:, h, :], G.kst[:, h, :], G.vn[:, h, :], [G.kstB, G.vnB], [G.psXB], inc=(h == G2 - 1))
        yield
        for h in HR:
            P.op("vector", (lambda e, h=h: e.tensor_scalar(out=G.o1[:, h, :], in0=Y3[:, h, :], scalar1=G.egc[:, h:h + 1], scalar2=None, op0=ALU.mult)),
                 reads=[G.psYB, G.egcB], writes=[G.o1B])
            P.op("gpsimd", (lambda e, h=h: e.tensor_scalar(out=G.S[:, h, :], in0=G.S[:, h, :], scalar1=G.egl[:, h:h + 1], scalar2=None, op0=ALU.mult)),
                 reads=[G.SB, G.eglB], writes=[G.SB])
        yield
        P.op("vector", lambda e: e.tensor_tensor(out=G.o1[:], in0=G.o1[:], in1=Z3, op=ALU.add), reads=[G.o1B, G.psZB], writes=[G.o1B])
        P.op("vector", lambda e: e.tensor_tensor(out=G.S[:], in0=G.S[:], in1=XS, op=ALU.add), reads=[G.SB, G.psXB], writes=[G.SB])
        yield
        P.op("gpsimd", lambda e: e.tensor_tensor(out=G.osq[:], in0=G.o1[:], in1=G.o1[:], op=ALU.mult), reads=[G.o1B], writes=[G.osqB])
        yield
        P.op("vector", lambda e: e.reduce_sum(out=G.ssq[:, 0:G2], in_=G.osq[:], axis=mybir.AxisListType.X), reads=[G.osqB], writes=[G.ssqB])
        yield
        P.op("scalar", lambda e: e.activation(out=G.ssq[:, G2:2 * G2], in_=G.ssq[:, 0:G2], func=AF.Sqrt, bias=M.epsn[0:64, 0:1], scale=1.0 / 128),
             reads=[G.ssqB, M.epsnB], writes=[G.ssqB])
        yield
        P.op("vector", lambda e: e.reciprocal(out=G.ssq[:, G2:2 * G2], in_=G.ssq[:, G2:2 * G2]), reads=[G.ssqB], writes=[G.ssqB])
        yield
        for h in HR:
            P.op("gpsimd", (lambda e, h=h: e.tensor_scalar(out=G.on[:, h, :], in0=G.o1[:, h, :], scalar1=G.ssq[:, G2 + h:G2 + h + 1], scalar2=None, op0=ALU.mult)),
                 reads=[G.o1B, G.ssqB], writes=[G.onB])
        yield
        ZT = G.psZ[:, 0:G2 * 64].rearrange("p (h d) -> p h d", h=G2)
        for h in HR:
            mm(ZT[:, h, :], G.on[:, h, :], cst[0:64, 0:64], [G.onB, M.cstB], [G.psZB], inc=(h == G2 - 1))
        yield
        P.op("vector", lambda e: e.tensor_tensor(out=G.yg[:, :, cs], in0=ZT, in1=M.sz[:, h0:h0 + G2, cs], op=ALU.mult), reads=[G.psZB, M.szB], writes=[G.ygB])

    for t in range(ntile):
        t0 = t * TT
        for ct in range(12):
            conv_tile(t, ct)
        for h in range(NH):
            z_tile(t, h)
        P.dma("sync", (lambda e, t0=t0: e.dma_start(out=M.bar[:], in_=PT[2048:2056, t0:t0 + TT])), reads=[PTB[t]], writes=[M.barB])
        for n in range(8):
            gens = [chunk(n, GA), chunk(n, GB)]
            alive = [True, True]
            while any(alive):
                for i, gen in enumerate(gens):
                    if alive[i]:
                        try:
                            next(gen)
                        except StopIteration:
                            alive[i] = False
        for G in (GA, GB):
            for h in range(G2):
                P.dma("gpsimd", (lambda e, G=G, h=h, t0=t0: e.dma_start(out=YT[(G.h0 + h) * 128:(G.h0 + h + 1) * 128, t0:t0 + TT], in_=G.yg[:, h, :])),
                      reads=[G.ygB], writes=[YTB[t]])


NCST2 = 129 + 128 + 16
SEG = 128
TWO_PI = 6.283185307179586


def make_consts2():
    c = np.zeros((128, NCST2), np.float32)
    c[:, 0:129] = np.arange(129)[None, :]
    g = np.arange(32)
    m = np.arange(128)
    c[0:32, 129:257] = (g[:, None] % 2 == (m[None, :] // 64))
    c[0:32, 257:273] = (g[:, None] // 2 == np.arange(16)[None, :])
    return c


def s5_phase(M, PT, YT, prm, cst_dram, cst2_dram, NT, PTB, YTB, tag=""):
    P = M.P
    ntile = NT // TT
    sb, ps = M.sb, M.ps
    c1 = sb("s_c1", [128, NCST]); c2 = sb("s_c2", [128, NCST2])
    c1B, c2B = M.s_c1B, M.s_c2B
    ident = c1[:, 0:128]
    P.dma("sync", lambda e: e.dma_start(out=c1[:], in_=cst_dram), writes=[c1B])
    P.dma("sync", lambda e: e.dma_start(out=c2[:], in_=cst2_dram), writes=[c2B])
    iota = c2[:, 0:129]
    psA = ps("s_psA", [128, 512]); psB = ps("s_psB", [128, 512]); psY = ps("s_psY", [128, 512]); psT = ps("s_psT", [128, 512])
    psAB, psBB, psYB, psTB = M.s_psAB, M.s_psBB, M.s_psYB, M.s_psTB
    NP_ = 16

    def V(eng, fn, reads, writes):
        P.op(eng, fn, reads=reads, writes=writes)

    rows = sb("s_rows", [32, 128]); rowsB = M.s_rowsB
    arc = sb("s_ar", [128, NP_]); aic = sb("s_ai", [128, NP_]); dtc = sb("s_dt", [128, NP_])

    def col_from_rows(dst, dstB, src_ap):
        P.dma("sync", lambda e: e.dma_start(out=rows[0:16, :], in_=src_ap), writes=[rowsB])
        P.op("tensor", lambda e: e.matmul(psT[:, 0:16], lhsT=rows[0:16, :], rhs=c1[0:16, 0:16], start=True, stop=True), reads=[rowsB, c1B], writes=[psTB])
        V("vector", lambda e: e.tensor_copy(out=dst[:], in_=psT[:, 0:16]), [psTB], [dstB])
    col_from_rows(arc, M.s_arB, prm["a_re"].rearrange("(a b) p -> a (b p)", b=2))
    col_from_rows(aic, M.s_aiB, prm["a_im"].rearrange("(a b) p -> a (b p)", b=2))
    ldr = sb("s_ldr", [1, 32]); ldc = sb("s_ldc", [32, 1]); Rm = sb("s_Rm", [32, 16])
    P.dma("sync", lambda e: e.dma_start(out=ldr[:], in_=prm["log_dt"].rearrange("(o f) -> o f", o=1)), writes=[M.s_ldrB])
    P.op("tensor", lambda e: e.matmul(psT[0:32, 16:17], lhsT=ldr[0:1, :], rhs=c1[0:1, 0:1], start=True, stop=True), reads=[M.s_ldrB, c1B], writes=[psTB])
    V("vector", lambda e: e.tensor_copy(out=ldc[:], in_=psT[0:32, 16:17]), [psTB], [M.s_ldcB])
    V("vector", lambda e: e.tensor_scalar(out=Rm[:], in0=c2[0:32, 257:273], scalar1=ldc[:, 0:1], scalar2=None, op0=ALU.mult), [c2B, M.s_ldcB], [M.s_RmB])
    P.op("tensor", lambda e: e.matmul(psT[:, 32:48], lhsT=c2[0:32, 129:257], rhs=Rm[:], start=True, stop=True), reads=[c2B, M.s_RmB], writes=[psTB])
    V("scalar", lambda e: e.activation(out=dtc[:], in_=psT[:, 32:48], func=AF.Exp), [psTB], [M.s_dtB])

    def sincos(x, xB, s_out, sB_, c_out, cB_, F, tmp, tmpB, tmpi, tmpiB):
        V("vector", lambda e: e.tensor_scalar(out=tmp, in0=x, scalar1=1.0 / TWO_PI, scalar2=None, op0=ALU.mult), [xB], [tmpB])
        V("vector", lambda e: e.tensor_copy(out=tmpi, in_=tmp), [tmpB], [tmpiB])
        V("vector", lambda e: e.tensor_copy(out=tmp, in_=tmpi), [tmpiB], [tmpB])
        V("vector", lambda e: e.scalar_tensor_tensor(out=x, in0=tmp, scalar=-TWO_PI, in1=x, op0=ALU.mult, op1=ALU.add), [tmpB, xB], [xB])
        V("scalar", lambda e: e.activation(out=tmp, in_=x, func=AF.Sin, scale=0.25), [xB], [tmpB])
        V("scalar", lambda e: e.activation(out=s_out, in_=x, func=AF.Sin, scale=0.5), [xB], [sB_])
        V("vector", lambda e: e.tensor_tensor(out=tmp, in0=tmp, in1=tmp, op=ALU.mult), [tmpB], [tmpB])
        V("vector", lambda e: e.tensor_scalar(out=tmp, in0=tmp, scalar1=-2.0, scalar2=1.0, op0=ALU.mult, op1=ALU.add), [tmpB], [tmpB])
        V("vector", lambda e: e.tensor_tensor(out=c_out, in0=s_out, in1=s_out, op=ALU.mult), [sB_], [cB_])
        V("vector", lambda e: e.scalar_tensor_tensor(out=s_out, in0=s_out, scalar=2.0, in1=tmp, op0=ALU.mult, op1=ALU.mult), [sB_, tmpB], [sB_])
        V("vector", lambda e: e.tensor_scalar(out=c_out, in0=c_out, scalar1=-2.0, scalar2=1.0, op0=ALU.mult, op1=ALU.add), [cB_], [cB_])

    mag = sb("s_mag", [128, NP_]); th = sb("s_th", [128, NP_]); sn = sb("s_sn", [128, NP_]); cs_ = sb("s_cs", [128, NP_])
    tp = sb("s_tp", [128, NP_]); tpi = sb("s_tpi", [128, NP_], I32); th2 = sb("s_th2", [128, NP_])
    V("vector", lambda e: e.tensor_scalar(out=arc[:], in0=arc[:], scalar1=-1e-4, scalar2=None, op0=ALU.min), [M.s_arB], [M.s_arB])
    V("vector", lambda e: e.tensor_tensor(out=mag[:], in0=dtc[:], in1=arc[:], op=ALU.mult), [M.s_dtB, M.s_arB], [M.s_magB])
    V("scalar", lambda e: e.activation(out=mag[:], in_=mag[:], func=AF.Exp), [M.s_magB], [M.s_magB])
    V("vector", lambda e: e.tensor_tensor(out=th[:], in0=dtc[:], in1=aic[:], op=ALU.mult), [M.s_dtB, M.s_aiB], [M.s_thB])
    V("vector", lambda e: e.tensor_copy(out=th2[:], in_=th[:]), [M.s_thB], [M.s_th2B])
    sincos(th2[:], M.s_th2B, sn[:], M.s_snB, cs_[:], M.s_csB, NP_, tp[:], M.s_tpB, tpi[:], M.s_tpiB)
    zr = sb("s_zr", [128, NP_]); zi = sb("s_zi", [128, NP_]); den = sb("s_den", [128, NP_]); fr = sb("s_fr", [128, NP_]); fi = sb("s_fi", [128, NP_])
    V("vector", lambda e: e.tensor_tensor(out=zr[:], in0=mag[:], in1=cs_[:], op=ALU.mult), [M.s_magB, M.s_csB], [M.s_zrB])
    V("vector", lambda e: e.tensor_scalar(out=zr[:], in0=zr[:], scalar1=-1.0, scalar2=None, op0=ALU.add), [M.s_zrB], [M.s_zrB])
    V("vector", lambda e: e.tensor_tensor(out=zi[:], in0=mag[:], in1=sn[:], op=ALU.mult), [M.s_magB, M.s_snB], [M.s_ziB])
    V("vector", lambda e: e.tensor_tensor(out=den[:], in0=arc[:], in1=arc[:], op=ALU.mult), [M.s_arB], [M.s_denB])
    V("vector", lambda e: e.tensor_tensor(out=tp[:], in0=aic[:], in1=aic[:], op=ALU.mult), [M.s_aiB], [M.s_tpB])
    V("vector", lambda e: e.tensor_tensor(out=den[:], in0=den[:], in1=tp[:], op=ALU.add), [M.s_denB, M.s_tpB], [M.s_denB])
    V("vector", lambda e: e.reciprocal(out=den[:], in_=den[:]), [M.s_denB], [M.s_denB])
    V("vector", lambda e: e.tensor_tensor(out=fr[:], in0=zr[:], in1=arc[:], op=ALU.mult), [M.s_zrB, M.s_arB], [M.s_frB])
    V("vector", lambda e: e.tensor_tensor(out=tp[:], in0=zi[:], in1=aic[:], op=ALU.mult), [M.s_ziB, M.s_aiB], [M.s_tpB])
    V("vector", lambda e: e.tensor_tensor(out=fr[:], in0=fr[:], in1=tp[:], op=ALU.add), [M.s_frB, M.s_tpB], [M.s_frB])
    V("vector", lambda e: e.tensor_tensor(out=fr[:], in0=fr[:], in1=den[:], op=ALU.mult), [M.s_frB, M.s_denB], [M.s_frB])
    V("vector", lambda e: e.tensor_tensor(out=fi[:], in0=zi[:], in1=arc[:], op=ALU.mult), [M.s_ziB, M.s_arB], [M.s_fiB])
    V("vector", lambda e: e.tensor_tensor(out=tp[:], in0=zr[:], in1=aic[:], op=ALU.mult), [M.s_zrB, M.s_aiB], [M.s_tpB])
    V("vector", lambda e: e.tensor_tensor(out=fi[:], in0=fi[:], in1=tp[:], op=ALU.subtract), [M.s_fiB, M.s_tpB], [M.s_fiB])
    V("vector", lambda e: e.tensor_tensor(out=fi[:], in0=fi[:], in1=den[:], op=ALU.mult), [M.s_fiB, M.s_denB], [M.s_fiB])

    CTb = sb("s_CT", [128, NP_, 129]); STb = sb("s_ST", [128, NP_, 129]); XT = sb("s_XT", [128, NP_, 129])
    TT1 = sb("s_TT1", [128, NP_, 129]); TTi = sb("s_TTi", [128, NP_, 129], I32); RM = sb("s_RM", [128, NP_, SEG])
    for gp in range(NP_):
        V("vector", (lambda e, gp=gp: e.tensor_scalar(out=XT[:, gp, :], in0=iota, scalar1=th[:, gp:gp + 1], scalar2=None, op0=ALU.mult)), [c2B, M.s_thB], [M.s_XTB])
        V("gpsimd", (lambda e, gp=gp: e.tensor_scalar(out=RM[:, gp, :], in0=c1[:, 128:256], scalar1=mag[:, gp:gp + 1], scalar2=None, op0=ALU.mult)), [c1B, M.s_magB], [M.s_RMB])
    fl = lambda t_: t_[:].rearrange("p a b -> p (a b)")
    sincos(fl(XT), M.s_XTB, fl(STb), M.s_STB, fl(CTb), M.s_CTB, NP_ * 129, fl(TT1), M.s_TT1B, fl(TTi), M.s_TTiB)

    BnR = sb("s_BnR", [128, 4, 128]); BnI = sb("s_BnI", [128, 4, 128]); bbR = sb("s_bbR", [128, 4, 128]); bbI = sb("s_bbI", [128, 4, 128])
    LBr = sb("s_LBr", [128, NP_, 128]); LBi = sb("s_LBi", [128, NP_, 128]); tmpm = sb("s_tmpm", [128, 128])
    V("vector", lambda e: e.memset(BnR[:], 0.0), [], [M.s_BnRB])
    V("vector", lambda e: e.memset(BnI[:], 0.0), [], [M.s_BnIB])
    V("gpsimd", lambda e: e.memset(bbR[:], 0.0), [], [M.s_bbRB])
    V("gpsimd", lambda e: e.memset(bbI[:], 0.0), [], [M.s_bbIB])
    for g in range(32):
        ct, gl, e_ = g // 8, g % 8, g % 2
        P.dma("sync", (lambda e, g=g, ct=ct, gl=gl, e_=e_: e.dma_start(out=BnR[e_ * 64:(e_ + 1) * 64, ct, gl * 16:(gl + 1) * 16], in_=prm["b_re"][g])), writes=[M.s_BnRB])
        P.dma("sync", (lambda e, g=g, ct=ct, gl=gl, e_=e_: e.dma_start(out=BnI[e_ * 64:(e_ + 1) * 64, ct, gl * 16:(gl + 1) * 16], in_=prm["b_im"][g])), writes=[M.s_BnIB])
    P.barrier()
    for g in range(32):
        ct, gl, e_, gp = g // 8, g % 8, g % 2, g // 2
        rs = slice(e_ * 64, (e_ + 1) * 64)
        csl = slice(gl * 16, (gl + 1) * 16)

        def bb(ct=ct, rs=rs, csl=csl, gp=gp):
            V("vector", lambda e: e.tensor_scalar(out=bbR[rs, ct, csl], in0=BnR[rs, ct, csl], scalar1=fr[rs, gp:gp + 1], scalar2=None, op0=ALU.mult), [M.s_BnRB, M.s_frB], [M.s_bbRB])
            V("vector", lambda e: e.tensor_scalar(out=tmpm[rs, 0:16], in0=BnI[rs, ct, csl], scalar1=fi[rs, gp:gp + 1], scalar2=None, op0=ALU.mult), [M.s_BnIB, M.s_fiB], [M.s_tmpmB])
            V("vector", lambda e: e.tensor_tensor(out=bbR[rs, ct, csl], in0=bbR[rs, ct, csl], in1=tmpm[rs, 0:16], op=ALU.subtract), [M.s_bbRB, M.s_tmpmB], [M.s_bbRB])
            V("vector", lambda e: e.tensor_scalar(out=bbI[rs, ct, csl], in0=BnI[rs, ct, csl], scalar1=fr[rs, gp:gp + 1], scalar2=None, op0=ALU.mult), [M.s_BnIB, M.s_frB], [M.s_bbIB])
            V("vector", lambda e: e.tensor_scalar(out=tmpm[rs, 16:32], in0=BnR[rs, ct, csl], scalar1=fi[rs, gp:gp + 1], scalar2=None, op0=ALU.mult), [M.s_BnRB, M.s_fiB], [M.s_tmpmB])
            V("vector", lambda e: e.tensor_tensor(out=bbI[rs, ct, csl], in0=bbI[rs, ct, csl], in1=tmpm[rs, 16:32], op=ALU.add), [M.s_bbIB, M.s_tmpmB], [M.s_bbIB])
        bb()
    for gp in range(NP_):
        ct, q4 = gp // 4, gp % 4
        csl = slice(q4 * 32, (q4 + 1) * 32)

        def mk(src, srcB, dst, dstB, ct=ct, csl=csl, gp=gp, neg=False):
            V("gpsimd", lambda e: e.memset(tmpm[:], 0.0), [], [M.s_tmpmB])
            V("gpsimd", lambda e: e.tensor_copy(out=tmpm[:, csl], in_=src[:, ct, csl]), [srcB], [M.s_tmpmB])
            P.op("tensor", lambda e: e.matmul(psT[:, 0:128], lhsT=tmpm[:], rhs=ident, start=True, stop=True), reads=[M.s_tmpmB, c1B], writes=[psTB])
            V("vector", lambda e: e.tensor_copy(out=dst[:, gp, :], in_=psT[:, 0:128]), [psTB], [dstB])
        mk(bbR, M.s_bbRB, LBr, M.s_LBrB)
        mk(bbI, M.s_bbIB, LBi, M.s_LBiB)

    CnR = sb("s_CnR", [128, 4, 128]); CnI = sb("s_CnI", [128, 4, 128]); CT2r = sb("s_CT2r", [128, 4, 128]); CT2i = sb("s_CT2i", [128, 4, 128])
    LCr = sb("s_LCr", [128, NP_, 128]); LCi = sb("s_LCi", [128, NP_, 128])
    V("vector", lambda e: e.memset(CnR[:], 0.0), [], [M.s_CnRB])
    V("vector", lambda e: e.memset(CnI[:], 0.0), [], [M.s_CnIB])
    V("gpsimd", lambda e: e.memset(LCr[:], 0.0), [], [M.s_LCrB])
    V("gpsimd", lambda e: e.memset(LCi[:], 0.0), [], [M.s_LCiB])
    for g in range(32):
        ct, gl, e_ = g // 8, g % 8, g % 2
        P.dma("sync", (lambda e, g=g, ct=ct, gl=gl, e_=e_: e.dma_start(out=CnR[gl * 16:(gl + 1) * 16, ct, e_ * 64:(e_ + 1) * 64], in_=prm["c_re"][g])), writes=[M.s_CnRB])
        P.dma("sync", (lambda e, g=g, ct=ct, gl=gl, e_=e_: e.dma_start(out=CnI[gl * 16:(gl + 1) * 16, ct, e_ * 64:(e_ + 1) * 64], in_=prm["c_im"][g])), writes=[M.s_CnIB])
    P.barrier()
    for ct in range(4):
        def trc(src, srcB, dst, dstB, ct=ct, neg=False):
            P.op("tensor", lambda e: e.matmul(psT[:, 0:128], lhsT=src[:, ct, :], rhs=ident, start=True, stop=True), reads=[srcB, c1B], writes=[psTB])
            if neg:
                V("vector", lambda e: e.tensor_scalar(out=dst[:, ct, :], in0=psT[:, 0:128], scalar1=-1.0, scalar2=None, op0=ALU.mult), [psTB], [dstB])
            else:
                V("vector", lambda e: e.tensor_copy(out=dst[:, ct, :], in_=psT[:, 0:128]), [psTB], [dstB])
        trc(CnR, M.s_CnRB, CT2r, M.s_CT2rB)
        trc(CnI, M.s_CnIB, CT2i, M.s_CT2iB, neg=True)
    for gp in range(NP_):
        ct, q4 = gp // 4, gp % 4
        csl = slice(q4 * 32, (q4 + 1) * 32)
        V("gpsimd", (lambda e, gp=gp, ct=ct, csl=csl: e.tensor_copy(out=LCr[:, gp, csl], in_=CT2r[:, ct, csl])), [M.s_CT2rB], [M.s_LCrB])
        V("gpsimd", (lambda e, gp=gp, ct=ct, csl=csl: e.tensor_copy(out=LCi[:, gp, csl], in_=CT2i[:, ct, csl])), [M.s_CT2iB], [M.s_LCiB])
    dcol = sb("s_dcol", [128, 4])
    P.dma("sync", lambda e: e.dma_start(out=rows[0:4, :], in_=prm["d"].rearrange("(a b) h -> a (b h)", b=8)), writes=[rowsB])
    P.op("tensor", lambda e: e.matmul(psT[:, 0:4], lhsT=rows[0:4, :], rhs=c1[0:4, 0:4], start=True, stop=True), reads=[rowsB, c1B], writes=[psTB])
    V("vector", lambda e: e.tensor_copy(out=dcol[:], in_=psT[:, 0:4]), [psTB], [M.s_dcolB])

    uT = sb("s_uT", [128, 4, TT]); bR = sb("s_bR", [128, TT]); bI = sb("s_bI", [128, TT]); t1 = sb("s_t1", [128, TT]); t2 = sb("s_t2", [128, TT])
    xR = sb("s_xR", [128, TT]); xI = sb("s_xI", [128, TT]); ys = sb("s_ys", [128, TT])
    cR = sb("s_cR", [128, NP_]); cI = sb("s_cI", [128, NP_]); cq = sb("s_cq", [128, 4])
    V("vector", lambda e: e.memset(cR[:], 0.0), [], [M.s_cRB])
    V("vector", lambda e: e.memset(cI[:], 0.0), [], [M.s_cIB])
    nseg = TT // SEG

    def pair_tile(t, gp):
        ct = gp // 4
        tabC = CTb[:, gp, 0:SEG].unsqueeze(1).broadcast_to([128, nseg, SEG])
        tabS = STb[:, gp, 0:SEG].unsqueeze(1).broadcast_to([128, nseg, SEG])
        v3 = lambda a: a[:].rearrange("p (s c) -> p s c", s=nseg)
        P.op("tensor", lambda e: e.matmul(psA[:], lhsT=LBr[:, gp, :], rhs=uT[:, ct, :], start=True, stop=True), reads=[M.s_LBrB, M.s_uTB], writes=[psAB])
        P.op("tensor", lambda e: e.matmul(psB[:], lhsT=LBi[:, gp, :], rhs=uT[:, ct, :], start=True, stop=True), reads=[M.s_LBiB, M.s_uTB], writes=[psBB])
        pA3 = psA[:].rearrange("p (s c) -> p s c", s=nseg)
        pB3 = psB[:].rearrange("p (s c) -> p s c", s=nseg)
        V("vector", lambda e: e.tensor_tensor(out=v3(bR), in0=pA3, in1=tabC, op=ALU.mult), [psAB, M.s_CTB], [M.s_bRB])
        V("vector", lambda e: e.tensor_tensor(out=v3(t1), in0=pB3, in1=tabS, op=ALU.mult), [psBB, M.s_STB], [M.s_t1B])
        V("gpsimd", lambda e: e.tensor_tensor(out=bR[:], in0=bR[:], in1=t1[:], op=ALU.add), [M.s_bRB, M.s_t1B], [M.s_bRB])
        V("vector", lambda e: e.tensor_tensor(out=v3(bI), in0=pB3, in1=tabC, op=ALU.mult), [psBB, M.s_CTB], [M.s_bIB])
        V("vector", lambda e: e.tensor_tensor(out=v3(t2), in0=pA3, in1=tabS, op=ALU.mult), [psAB, M.s_STB], [M.s_t2B])
        V("gpsimd", lambda e: e.tensor_tensor(out=bI[:], in0=bI[:], in1=t2[:], op=ALU.subtract), [M.s_bIB, M.s_t2B], [M.s_bIB])
        for s in range(nseg):
            sc = slice(s * SEG, (s + 1) * SEG)

            def seg(sc=sc):
                V("vector", lambda e: e.tensor_tensor_scan(out=xR[:, sc], data0=RM[:, gp, :], data1=bR[:, sc], initial=cR[:, gp:gp + 1], op0=ALU.mult, op1=ALU.add),
                  [M.s_RMB, M.s_bRB, M.s_cRB], [M.s_xRB])
                V("vector", lambda e: e.tensor_tensor_scan(out=xI[:, sc], data0=RM[:, gp, :], data1=bI[:, sc], initial=cI[:, gp:gp + 1], op0=ALU.mult, op1=ALU.add),
                  [M.s_RMB, M.s_bIB, M.s_cIB], [M.s_xIB])
                lr = xR[:, sc.stop - 1:sc.stop]
                li = xI[:, sc.stop - 1:sc.stop]
                c128 = CTb[:, gp, 128:129]
                s128 = STb[:, gp, 128:129]
                V("vector", lambda e: e.tensor_tensor(out=cq[:, 0:1], in0=li, in1=s128, op=ALU.mult), [M.s_xIB, M.s_STB], [M.s_cqB])
                V("vector", lambda e: e.tensor_tensor(out=cq[:, 1:2], in0=li, in1=c128, op=ALU.mult), [M.s_xIB, M.s_CTB], [M.s_cqB])
                V("vector", lambda e: e.scalar_tensor_tensor(out=cR[:, gp:gp + 1], in0=lr, scalar=c128, in1=cq[:, 0:1], op0=ALU.mult, op1=ALU.subtract),
                  [M.s_xRB, M.s_CTB, M.s_cqB], [M.s_cRB])
                V("vector", lambda e: e.scalar_tensor_tensor(out=cI[:, gp:gp + 1], in0=lr, scalar=s128, in1=cq[:, 1:2], op0=ALU.mult, op1=ALU.add),
                  [M.s_xRB, M.s_STB, M.s_cqB], [M.s_cIB])
            seg()
        V("gpsimd", lambda e: e.tensor_tensor(out=v3(t1), in0=v3(xI), in1=tabS, op=ALU.mult), [M.s_xIB, M.s_STB], [M.s_t1B])
        V("gpsimd", lambda e: e.tensor_tensor(out=v3(t2), in0=v3(xR), in1=tabS, op=ALU.mult), [M.s_xRB, M.s_STB], [M.s_t2B])
        V("vector", lambda e: e.tensor_tensor(out=v3(xR), in0=v3(xR), in1=tabC, op=ALU.mult), [M.s_xRB, M.s_CTB], [M.s_xRB])
        V("vector", lambda e: e.tensor_tensor(out=v3(xI), in0=v3(xI), in1=tabC, op=ALU.mult), [M.s_xIB, M.s_CTB], [M.s_xIB])
        V("gpsimd", lambda e: e.tensor_tensor(out=xR[:], in0=xR[:], in1=t1[:], op=ALU.subtract), [M.s_xRB, M.s_t1B], [M.s_xRB])
        V("gpsimd", lambda e: e.tensor_tensor(out=xI[:], in0=xI[:], in1=t2[:], op=ALU.add), [M.s_xIB, M.s_t2B], [M.s_xIB])
        q4 = gp % 4
        P.op("tensor", lambda e: e.matmul(psY[:], lhsT=LCr[:, gp, :], rhs=xR[:], start=(q4 == 0), stop=False), reads=[M.s_LCrB, M.s_xRB], writes=[psYB], inc=False)
        P.op("tensor", lambda e: e.matmul(psY[:], lhsT=LCi[:, gp, :], rhs=xI[:], start=False, stop=(q4 == 3)), reads=[M.s_LCiB, M.s_xIB], writes=[psYB])

    for t in range(ntile):
        t0 = t * TT
        for ct in range(4):
            P.dma("sync", (lambda e, ct=ct, t0=t0: e.dma_start(out=uT[:, ct, :], in_=PT[2056 + ct * 128:2056 + (ct + 1) * 128, t0:t0 + TT])), reads=[PTB[t]], writes=[M.s_uTB])
        for gp in range(NP_):
            pair_tile(t, gp)
            if gp % 4 == 3:
                ct = gp // 4

                def fin(ct=ct, t0=t0):
                    V("vector", lambda e: e.scalar_tensor_tensor(out=ys[:], in0=uT[:, ct, :], scalar=dcol[:, ct:ct + 1], in1=psY[:], op0=ALU.mult, op1=ALU.add),
                      [M.s_uTB, M.s_dcolB, psYB], [M.s_ysB])
                    V("scalar", lambda e: e.activation(out=ys[:], in_=ys[:], func=AF.Gelu), [M.s_ysB], [M.s_ysB])
                    P.dma("gpsimd", lambda e: e.dma_start(out=YT[512 + ct * 128:512 + (ct + 1) * 128, t0:t0 + TT], in_=ys[:]), reads=[M.s_ysB], writes=[YTB[t]])
                fin()


def mixpost_phase(R, YT, X_in, X_out, w_glu, w_out, NT):
    P = R.P
    ntile = NT // TT
    wi, wo, sB = R.wi[0], R.wo[0], R.slabB[0]

    def ld(src, dst, w_):
        si = R.stg_i
        R.stg_i = (si + 1) % 3
        st, stB = R.stg[si], R.stgB[si]
        P.dma("sync", lambda e: e.dma_start(out=st[:, 0:w_], in_=src), writes=[stB])
        P.op("gpsimd", lambda e: e.tensor_copy(out=dst, in_=st[:, 0:w_]), reads=[stB], writes=[sB])
    for kc in range(4):
        ld(w_glu[kc * 128:(kc + 1) * 128, 0:512], wi[:, kc, 0:512], 512)
    for kc in range(8):
        ld(w_out[kc * 128:(kc + 1) * 128, 0:1024], wo[:, kc, :], 1024)
    YB = [Buf("Yo%d" % i) for i in range(ntile * 4)]

    def load_y(t, hb):
        hT = R.hT[hb]
        for kc in range(8):
            def one(kc=kc):
                si = R.stg_i
                R.stg_i = (si + 1) % 3
                st, stB = R.stg[si], R.stgB[si]
                P.dma("sync", lambda e: e.dma_start(out=st[:, 0:TT], in_=YT[kc * 128:(kc + 1) * 128, t * TT:(t + 1) * TT]), writes=[stB])
                P.op("gpsimd" if kc % 2 else "vector", lambda e: e.tensor_copy(out=hT[:, kc, :], in_=st[:, 0:TT]), reads=[stB], writes=[R.hTB[hb][0]])
            one()
    load_y(0, 0)
    gi = 0
    for t in range(ntile):
        hb = t % 2
        hT = R.hT[hb]
        hB = R.hTB[hb][0]
        for j in range(4):
            def glu(j=j, gi=gi, hT=hT, hB=hB):
                pp, ppB = R.pB[gi % 4], R.pBB[gi % 4]
                sg, sgB = R.sg[gi % 2], R.sgB[gi % 2]

                def mm(kc):
                    P.op("tensor", lambda e: e.matmul(pp[:], lhsT=wi[:, kc, j * 128:(j + 1) * 128], rhs=hT[:, 4 + kc, :], start=(kc == 0), stop=(kc == 3)),
                         reads=[sB, hB], writes=[ppB], inc=(kc == 3))
                for kc in range(4):
                    mm(kc)
                P.op("scalar", lambda e: e.activation(out=sg[:], in_=pp[:], func=AF.Sigmoid), reads=[ppB], writes=[sgB])
                P.op("vector", lambda e: e.tensor_tensor(out=R.aT[:, j, :], in0=hT[:, 4 + j, :], in1=sg[:], op=ALU.mult), reads=[hB, sgB], writes=[R.aTB[j]])
            glu()
            gi += 1
        if t + 1 < ntile:
            load_y(t + 1, 1 - hb)
        lhs = [(hT, kc, hB) for kc in range(4)] + [(R.aT, kc, R.aTB[kc]) for kc in range(4)]
        for s in range(4):
            stage_c_sub(R, wo, sB, 8, X_in, X_out, None, YB[t * 4 + s], t, s, 0, True, lhs=lhs)
    return YB


DEPTH = 2
NT_CORE = 8192


def mod_phase(R, c_row, w_mod_l, b_mod_l, MOD, sbm):
    P = R.P
    cT, cTB, brow, browB, mrow, mrowB = sbm
    P.dma("sync", lambda e: e.dma_start(out=R.vrows[:, 0, :], in_=c_row.rearrange("(kc p) -> kc p", p=128)), writes=[R.vrowsB])
    pc = R.pC[0]
    P.op("tensor", lambda e: e.matmul(pc[:, 0:8], lhsT=R.vrows[:, 0, :], rhs=R.identf[0:8, 0:8], start=True, stop=True),
         reads=[R.vrowsB, R.identfB], writes=[R.pCB])
    P.op("scalar", lambda e: e.activation(out=cT[:], in_=pc[:, 0:8], func=AF.Silu), reads=[R.pCB], writes=[cTB])
    pm = R.pC[1]
    dummy = Buf("modw")

    def tile(n):
        def kstep(kc):
            si = R.stg_i
            R.stg_i = (si + 1) % 3
            st, stB = R.stg[si], R.stgB[si]
            P.dma("sync", lambda e: e.dma_start(out=st[:, 0:512], in_=w_mod_l[kc * 128:(kc + 1) * 128, n * 512:(n + 1) * 512]), writes=[stB])
            P.op("tensor", lambda e: e.matmul(pm[0:1, :], lhsT=cT[:, kc:kc + 1], rhs=st[:, 0:512], start=(kc == 0), stop=(kc == NKC - 1)),
                 reads=[stB, cTB], writes=[R.pCB])
        for kc in range(NKC):
            kstep(kc)
        P.dma("sync", lambda e: e.dma_start(out=brow[:], in_=b_mod_l[n * 512:(n + 1) * 512].rearrange("(o f) -> o f", o=1)), writes=[browB])
        P.op("vector", lambda e: e.tensor_tensor(out=mrow[:], in0=pm[0:1, :], in1=brow[:], op=ALU.add),
             reads=[R.pCB, browB], writes=[mrowB])
        P.dma("gpsimd", lambda e: e.dma_start(out=MOD[n * 512:(n + 1) * 512].rearrange("(o f) -> o f", o=1), in_=mrow[:]),
              reads=[mrowB], writes=[dummy])
    for n in range(9 * D // 512):
        tile(n)


WNAMES = ["w_mod", "b_mod", "ff1_norm_pre", "ff1_norm_post", "ff1_w_in", "ff1_w_out", "mix_norm_pre", "mix_norm_post", "mix_w_in",
          "conv_w", "a_log", "dt_bias", "gdn_norm_w", "s5_a_re", "s5_a_im", "s5_log_dt", "s5_b_re", "s5_b_im", "s5_c_re", "s5_c_im",
          "s5_d", "s5_w_glu", "mix_w_out", "ff2_norm_pre", "ff2_norm_post", "ff2_w_in", "ff2_w_out"]
WSHAPES = {"w_mod": [D, 9 * D], "b_mod": [9 * D], "ff1_norm_pre": [D], "ff1_norm_post": [D], "ff1_w_in": [D, 2 * DFF], "ff1_w_out": [DFF, D],
           "mix_norm_pre": [D], "mix_norm_post": [D], "mix_w_in": [D, 2568], "conv_w": [4, 1536], "a_log": [4], "dt_bias": [4], "gdn_norm_w": [128],
           "s5_a_re": [32, 64], "s5_a_im": [32, 64], "s5_log_dt": [32], "s5_b_re": [32, 64, 16], "s5_b_im": [32, 64, 16],
           "s5_c_re": [32, 16, 64], "s5_c_im": [32, 16, 64], "s5_d": [32, 16], "s5_w_glu": [512, 512], "mix_w_out": [D, D],
           "ff2_norm_pre": [D], "ff2_norm_post": [D], "ff2_w_in": [D, 2 * DFF], "ff2_w_out": [DFF, D]}


def build_nc(NT, depth=DEPTH):
    nc = bass.Bass("TRN2", target_bir_lowering=False)
    dr = lambda n, s, k="ExternalInput", dt=F32: nc.dram_tensor(n, s, dt, kind=k).ap()
    x = dr("x", [NT, D]); c_row = dr("c_row", [D])
    W = {n: dr(n, [depth] + WSHAPES[n]) for n in WNAMES}
    ident = dr("ident", [128, 128]); cst = dr("cst", [128, NCST]); cst2 = dr("cst2", [128, NCST2])
    y = dr("y", [NT, D], "ExternalOutput")
    scr = lambda n, s: nc.dram_tensor(n, s, F32).ap()
    Xs = [scr("xs0", [NT, D]), scr("xs1", [NT, D])]
    yacc = scr("yacc", [NT, D]); PT = scr("ptscr", [2568, NT]); YT = scr("ytscr", [1024, NT])
    MOD = scr("modscr", [depth, 9 * D])
    ntile = NT // TT
    dB = lambda: [Buf() for _ in range(ntile)]
    with ExitStack() as stack:
        P = Prog(nc, stack)
        phase = [0]

        def ffn_like(fn):
            phase[0] += 1
            with ExitStack() as ph:
                R = FFNRes(nc, ph, P, tag="_p%d" % phase[0])
                R.load_ident(ident)
                fn(R, ph)
            P.barrier()

        def mix_like(fn):
            phase[0] += 1
            with ExitStack() as ph:
                M = MixRes(nc, ph, P, tag="_p%d" % phase[0])
                fn(M)
            P.barrier()

        def do_mod(R, ph):
            sb = lambda name, shape, dt: ph.enter_context(nc.sbuf_tensor(name, shape, dt))
            sbm = (sb("cT", [128, NKC], F32), Buf("cT"), sb("brow", [1, 512], F32), Buf("brow"), sb("mrow", [1, 512], F32), Buf("mrow"))
            for l in range(depth):
                mod_phase(R, c_row, W["w_mod"][l], W["b_mod"][l], MOD[l], sbm)
        ffn_like(do_mod)
        cur = x
        for l in range(depth):
            last_layer = l == depth - 1
            nxt = Xs[0]

            def f1(R, ph, l=l, cur=cur, nxt=nxt):
                prep_vectors(R, W["ff1_norm_pre"][l], W["ff1_norm_post"][l], MOD[l], 0, 0.5)
                ffn_phase(R, cur, nxt, yacc, W["ff1_w_in"][l], W["ff1_w_out"][l], NT)
            ffn_like(f1)
            cur = nxt

            def m1(R, ph, l=l, cur=cur):
                prep_vectors(R, W["mix_norm_pre"][l], W["mix_norm_post"][l], MOD[l], 3, 1.0)
                proj_phase(R, cur, W["mix_w_in"][l], PT, NT, dB())
            ffn_like(m1)

            def g(M, l=l):
                gdn_phase(M, PT, YT, W["conv_w"][l], W["a_log"][l], W["dt_bias"][l], W["gdn_norm_w"][l], cst, NT, dB(), dB())
            mix_like(g)

            def s5(M, l=l):
                prm = {k: W["s5_" + k][l] for k in ("a_re", "a_im", "log_dt", "b_re", "b_im", "c_re", "c_im", "d")}
                s5_phase(M, PT, YT, prm, cst, cst2, NT, dB(), dB())
            mix_like(s5)
            nxt = Xs[1]

            def m3(R, ph, l=l, cur=cur, nxt=nxt):
                prep_vectors(R, W["mix_norm_pre"][l], W["mix_norm_post"][l], MOD[l], 3, 1.0)
                mixpost_phase(R, YT, cur, nxt, W["s5_w_glu"][l], W["mix_w_out"][l], NT)
            ffn_like(m3)
            cur = nxt
            nxt = y if last_layer else Xs[0]
            outB = []

            def f2(R, ph, l=l, cur=cur, nxt=nxt):
                prep_vectors(R, W["ff2_norm_pre"][l], W["ff2_norm_post"][l], MOD[l], 6, 0.5)
                outB.extend(ffn_phase(R, cur, nxt, yacc, W["ff2_w_in"][l], W["ff2_w_out"][l], NT))
            ffn_like(f2)
            cur = nxt
        P.final_wait("gpsimd", outB)
        P.emit()
    return nc


def kernel(**inputs):
    x = np.ascontiguousarray(inputs["x"], dtype=np.float32)
    c = np.ascontiguousarray(inputs["c"], dtype=np.float32)
    B, L, _ = x.shape
    common = {n: np.ascontiguousarray(inputs[n], dtype=np.float32) for n in WNAMES}
    common["ident"] = np.eye(128, dtype=np.float32)
    common["cst"] = make_consts()
    common["cst2"] = make_consts2()
    n_cores = B
    in_maps = []
    for core in range(n_cores):
        m = dict(common)
        m["x"] = np.ascontiguousarray(x[core])
        m["c_row"] = np.ascontiguousarray(c[core])
        in_maps.append(m)
    nc = build_nc(L)
    res = run_bass_kernel_spmd(nc, in_maps, core_ids=list(range(n_cores)))
    out = np.empty((B, L, D), dtype=np.float32)
    for core in range(n_cores):
        out[core] = res.results[core]["y"]
    return out
```

```python
import numpy as np
from contextlib import ExitStack
import concourse.bass as bass
import concourse.mybir as mybir
from concourse.bass_utils import run_bass_kernel_spmd

F32 = mybir.dt.float32
BF16 = mybir.dt.bfloat16
I32 = mybir.dt.int32
AF = mybir.ActivationFunctionType
ALU = mybir.AluOpType

ENGS = ("tensor", "vector", "scalar", "gpsimd", "sync")


class Buf:
    __slots__ = ("w", "r", "name")

    def __init__(self, name=""):
        self.w = []
        self.r = []
        self.name = name


class Prog:
    NDMA = 20

    def __init__(self, nc, stack):
        self.nc = nc
        self.q = {e: [] for e in ENGS}
        self.sems = {}
        self.cnt = {}
        for e in ("tensor", "vector", "scalar", "gpsimd"):
            self.sems[e] = stack.enter_context(nc.semaphore("pg_" + e))
            self.cnt[e] = 0
        self.dma_rr = {}
        for qn in ("sync", "gpsimd", "scalar"):
            self.dma_rr[qn] = 0
            for i in range(self.NDMA):
                k = ("dma", qn, i)
                self.sems[k] = stack.enter_context(nc.semaphore("pd_%s_%d" % (qn, i)))
                self.cnt[k] = 0
        self.seen = {e: {} for e in ENGS}
        self.pending_reads = {e: [] for e in ENGS}

    def _waits(self, eng, deps):
        out = []
        for tok in deps:
            if tok is None:
                continue
            key, val = tok
            if key == "tensor" and eng == "tensor":
                continue
            if self.seen[eng].get(key, 0) >= val:
                continue
            self.seen[eng][key] = val
            out.append((key, val))
        return out

    def op(self, eng, fn, reads=(), writes=(), inc=True):
        deps = []
        for b in reads:
            deps.extend(b.w)
        for b in writes:
            deps.extend(b.w)
            deps.extend(b.r)
        waits = self._waits(eng, deps)
        if inc:
            self.cnt[eng] += 1
            tok = (eng, self.cnt[eng])
        else:
            tok = (eng, self.cnt[eng] + 1)
        sem = self.sems[eng]
        self.q[eng].append((waits, fn, sem if inc else None, 1))
        for b in reads:
            b.r.append(tok)
        for b in writes:
            b.w = [tok]
            b.r = []
        return tok

    def dma(self, qn, fn, reads=(), writes=()):
        deps = []
        for b in reads:
            deps.extend(b.w)
        for b in writes:
            deps.extend(b.w)
            deps.extend(b.r)
        i = self.dma_rr[qn]
        self.dma_rr[qn] = (i + 1) % self.NDMA
        k = ("dma", qn, i)
        if self.cnt[k] > 0:
            deps.append((k, self.cnt[k]))
        waits = self._waits(qn, deps)
        self.cnt[k] += 16
        tok = (k, self.cnt[k])
        self.q[qn].append((waits, fn, self.sems[k], 16))
        for b in reads:
            b.r.append(tok)
        for b in writes:
            b.w = [tok]
            b.r = []
        return tok

    def barrier(self):
        toks = [(k, v) for k, v in self.cnt.items() if v > 0]
        for e in ENGS:
            waits = self._waits(e, toks)
            if waits:
                self.q[e].append((waits, None, None, 0))

    def final_wait(self, eng, bufs):
        deps = []
        for b in bufs:
            deps.extend(b.w)
        waits = self._waits(eng, deps)
        self.q[eng].append((waits, None, None, 0))

    def emit(self):
        nc = self.nc
        sems = self.sems
        with nc.Block() as block:
            def mk(name):
                def body(e):
                    for waits, fn, sem, inc in self.q[name]:
                        for key, val in waits:
                            e.wait_ge(sems[key], val)
                        if fn is not None:
                            ins = fn(e)
                            if sem is not None:
                                ins.then_inc(sem, inc)
                return body
            block.sync(mk("sync"))
            block.tensor(mk("tensor"))
            block.vector(mk("vector"))
            block.scalar(mk("scalar"))
            block.gpsimd(mk("gpsimd"))


def check_deadlock(P):
    pos = {e: 0 for e in ENGS}
    val = {}
    key_of = {id(s): k for k, s in P.sems.items()}
    progress = True
    while progress:
        progress = False
        for e in ENGS:
            q = P.q[e]
            while pos[e] < len(q):
                waits, fn, sem, inc = q[pos[e]]
                if all(val.get(k, 0) >= v for k, v in waits):
                    if sem is not None:
                        k = key_of[id(sem)]
                        val[k] = val.get(k, 0) + inc
                    pos[e] += 1
                    progress = True
                else:
                    break
    ok = all(pos[e] == len(P.q[e]) for e in ENGS)
    if not ok:
        for e in ENGS:
            if pos[e] < len(P.q[e]):
                waits = P.q[e][pos[e]][0]
                print("STUCK", e, pos[e], len(P.q[e]), [(k, v, val.get(k, 0)) for k, v in waits if val.get(k, 0) < v])
    return ok


D = 1024
DFF = 2816
NKC = 8
EPS = 1e-6
SLABS = [(0, 8), (8, 7), (15, 7)]
SLAB_MAX = 8
TT = 512


class FFNRes:
    def __init__(self, nc, stack, P, tag=""):
        self.nc, self.P = nc, P
        sb = lambda name, shape, dt: stack.enter_context(nc.sbuf_tensor(name + tag, shape, dt))
        ps = lambda name, shape, dt: stack.enter_context(nc.psum_tensor(name + tag, shape, dt))
        self.wi = [sb("wi%d" % i, [128, NKC, SLAB_MAX * 256], BF16) for i in range(2)]
        self.wo = [sb("wo%d" % i, [128, SLAB_MAX, D], BF16) for i in range(2)]
        self.slabB = [Buf("slab%d" % i) for i in range(2)]
        self.stg = [sb("stg%d" % i, [128, 1024], F32) for i in range(3)]
        self.stgB = [Buf("stg%d" % i) for i in range(3)]
        self.stg_i = 0
        self.hT = [sb("hT%d" % i, [128, NKC, TT], BF16) for i in range(2)]
        self.hTB = [[Buf("hT%d_%d" % (i, s)) for s in range(4)] for i in range(2)]
        self.aT = sb("aT", [128, SLAB_MAX, TT], BF16)
        self.aTB = [Buf("aT%d" % j) for j in range(SLAB_MAX)]
        self.xa = [sb("xa%d" % i, [128, D], F32) for i in range(2)]
        self.xaB = [Buf("xa%d" % i) for i in range(2)]
        self.xn = [sb("xn%d" % i, [128, D], BF16) for i in range(2)]
        self.xnB = [Buf("xn%d" % i) for i in range(2)]
        self.junk = sb("junk", [128, D], BF16)
        self.junkB = Buf("junk")
        self.ss = [sb("ss%d" % i, [128, 2], F32) for i in range(4)]
        self.ssB = [Buf("ss%d" % i) for i in range(4)]
        self.ss_i = 0
        self.sg = [sb("sg%d" % i, [128, TT], F32) for i in range(2)]
        self.sgB = [Buf("sg%d" % i) for i in range(2)]
        self.yb = [sb("yb%d" % i, [128, D], F32) for i in range(2)]
        self.ybB = [Buf("yb%d" % i) for i in range(2)]
        self.xr = [sb("xr%d" % i, [128, D], F32) for i in range(2)]
        self.xrB = [Buf("xr%d" % i) for i in range(2)]
        self.crow = sb("crow", [128, D], F32)
        self.crowB = Buf("crow")
        self.ctmp = sb("ctmp", [128, D], F32)
        self.ctmpB = Buf("ctmp")
        self.acol = sb("acol", [128, NKC], F32)
        self.bcol = sb("bcol", [128, NKC], F32)
        self.tcol = sb("tcol", [128, NKC], F32)
        self.colB = Buf("col")
        self.vrows = sb("vrows", [8, 3, 128], F32)
        self.vrowsB = Buf("vrows")
        self.identf = sb("identf", [128, 128], F32)
        self.identfB = Buf("identf")
        self.ident = sb("ident_sb", [128, 128], BF16)
        self.epsc = sb("epsc", [128, 1], F32)
        self.epscB = Buf("epsc")
        P.op("vector", lambda e: e.memset(self.epsc[:], D * EPS), writes=[self.epscB])
        self.identB = Buf("ident")
        self.pB = [ps("pB%d" % i, [128, TT], F32) for i in range(4)]
        self.pBB = [Buf("pB%d" % i) for i in range(4)]
        self.pC = [ps("pC%d" % i, [128, TT], F32) for i in range(2)]
        self.pCB = Buf("pC")
        self.pT = [ps("pT%d" % i, [128, NKC, 128], BF16) for i in range(2)]
        self.pTB = [Buf("pT%d" % i) for i in range(2)]

    def load_ident(self, ident_dram):
        P = self.P
        st = self.stg[0]
        P.dma("sync", lambda e: e.dma_start(out=st[:, 0:128], in_=ident_dram), writes=[self.stgB[0]])
        P.dma("sync", lambda e: e.dma_start(out=self.identf[:], in_=ident_dram), writes=[self.identfB])
        P.op("vector", lambda e: e.tensor_copy(out=self.ident[:], in_=st[:, 0:128]),
             reads=[self.stgB[0]], writes=[self.identB])


def load_slab(R, w_in, w_out, slab, buf, dff=DFF, gated=True):
    P = R.P
    j0, n = slab
    wi, wo, sB = R.wi[buf], R.wo[buf], R.slabB[buf]
    pieces = []
    ncol = n * 128
    for kc in range(NKC):
        for part in range(2 if gated else 1):
            c0 = part * dff + j0 * 128
            done = 0
            while done < ncol:
                w = min(1024, ncol - done)
                pieces.append(("in", kc, part * ncol + done, c0 + done, w))
                done += w
    if w_out is not None:
        for j in range(n):
            pieces.append(("out", j, 0, (j0 + j) * 128, 1024))
    for kind, a, dst0, src0, w in pieces:
        si = R.stg_i
        R.stg_i = (si + 1) % 3
        st, stB = R.stg[si], R.stgB[si]
        if kind == "in":
            src = w_in[a * 128:(a + 1) * 128, src0:src0 + w]
            dst = wi[:, a, dst0:dst0 + w]
        else:
            src = w_out[src0:src0 + 128, 0:1024]
            dst = wo[:, a, :]
        P.dma("sync", (lambda e, st=st, src=src, w=w: e.dma_start(out=st[:, 0:w], in_=src)), writes=[stB])
        P.op("gpsimd", (lambda e, st=st, dst=dst, w=w: e.tensor_copy(out=dst, in_=st[:, 0:w])),
             reads=[stB], writes=[sB])


def prep_vectors(R, w_pre, w_post, mod, ioff, gate_scale, modB=None):
    P = R.P
    colv = lambda v, off: v[off:off + D].rearrange("(kc p) -> p kc", p=128)
    rowv = lambda v, off: v[off:off + D].partition_broadcast(128)
    rows = lambda v, off: v[off:off + D].rearrange("(kc p) -> kc p", p=128)
    P.dma("sync", lambda e: e.dma_start(out=R.vrows[:, 0, :], in_=rows(w_pre, 0)), writes=[R.vrowsB])
    P.dma("sync", lambda e: e.dma_start(out=R.vrows[:, 1, :], in_=rows(mod, ioff * D)), reads=(list(modB) if modB else []), writes=[R.vrowsB])
    P.dma("sync", lambda e: e.dma_start(out=R.vrows[:, 2, :], in_=rows(mod, (ioff + 1) * D)), reads=(list(modB) if modB else []), writes=[R.vrowsB])
    pc = R.pC[0]

    def tr(i):
        P.op("tensor", lambda e: e.matmul(pc[:, i * 8:(i + 1) * 8], lhsT=R.vrows[:, i, :], rhs=R.identf[0:8, 0:8], start=True, stop=True),
             reads=[R.vrowsB, R.identfB], writes=[R.pCB], inc=(i == 2))
    for i in range(3):
        tr(i)
    P.op("vector", lambda e: e.tensor_copy(out=R.acol[:], in_=pc[:, 0:8]), reads=[R.pCB], writes=[R.colB])
    P.op("vector", lambda e: e.tensor_copy(out=R.bcol[:], in_=pc[:, 8:16]), reads=[R.pCB], writes=[R.colB])
    P.op("vector", lambda e: e.tensor_copy(out=R.tcol[:], in_=pc[:, 16:24]), reads=[R.pCB], writes=[R.colB])
    P.op("vector", lambda e: e.tensor_scalar(out=R.tcol[:], in0=R.tcol[:], scalar1=1.0, scalar2=32.0, op0=ALU.add, op1=ALU.mult),
         reads=[R.colB], writes=[R.colB])
    P.op("vector", lambda e: e.tensor_tensor(out=R.acol[:], in0=R.acol[:], in1=R.tcol[:], op=ALU.mult),
         reads=[R.colB], writes=[R.colB])
    P.dma("sync", lambda e: e.dma_start(out=R.crow[:], in_=rowv(w_post, 0)), writes=[R.crowB])
    P.dma("sync", lambda e: e.dma_start(out=R.ctmp[:], in_=rowv(mod, (ioff + 2) * D)), reads=(list(modB) if modB else []), writes=[R.ctmpB])
    P.op("vector", lambda e: e.scalar_tensor_tensor(out=R.crow[:], in0=R.crow[:], scalar=32.0 * gate_scale, in1=R.ctmp[:],
                                                    op0=ALU.mult, op1=ALU.mult),
         reads=[R.crowB, R.ctmpB], writes=[R.crowB])


def stage_a_sub(R, X_in, t, s, hbuf, part):
    P = R.P
    i = (t * 4 + s) % 2
    xa, xaB, xn, xnB = R.xa[i], R.xaB[i], R.xn[i], R.xnB[i]
    if part == 0:
        r0 = t * TT + s * 128
        P.dma("sync", lambda e: e.dma_start(out=xa[:], in_=X_in[r0:r0 + 128, :]), writes=[xaB])
        k = R.ss_i
        R.ss_i = (k + 1) % 4
        ss, ssB = R.ss[k], R.ssB[k]
        P.op("scalar", lambda e: e.activation(out=R.junk[:], in_=xa[:], func=AF.Square, accum_out=ss[:, 0:1]),
             reads=[xaB], writes=[R.junkB, ssB])
        P.op("scalar", lambda e: e.activation(out=ss[:, 1:2], in_=ss[:, 0:1], func=AF.Sqrt, bias=R.epsc[:, 0:1], scale=1.0),
             reads=[ssB, R.epscB], writes=[ssB])
        P.op("vector", lambda e: e.reciprocal(out=ss[:, 1:2], in_=ss[:, 1:2]), reads=[ssB], writes=[ssB])
        P.op("scalar", lambda e: e.activation(out=xn[:], in_=xa[:], func=AF.Copy, scale=ss[:, 1:2]),
             reads=[xaB, ssB], writes=[xnB])
    else:
        pt, ptB = R.pT[i], R.pTB[i]
        for kc in range(NKC):
            P.op("tensor", (lambda e, kc=kc: e.transpose(out=pt[:, kc, :], in_=xn[:, kc * 128:(kc + 1) * 128], identity=R.ident[:])),
                 reads=[xnB, R.identB], writes=[ptB], inc=(kc == NKC - 1))
        hT, hB = R.hT[hbuf], R.hTB[hbuf][s]
        for kc in range(NKC):
            eng = "vector" if kc % 2 == 0 else "gpsimd"
            if eng == "gpsimd":
                eng = "vector"
            P.op(eng, (lambda e, kc=kc: e.tensor_scalar(out=hT[:, kc, s * 128:(s + 1) * 128], in0=pt[:, kc, :],
                                                       scalar1=R.acol[:, kc:kc + 1], scalar2=R.bcol[:, kc:kc + 1],
                                                       op0=ALU.mult, op1=ALU.add)),
                 reads=[ptB, R.colB], writes=[hB])


def stage_b_group(R, wi, sB, hT, hTBl, j, nch, gidx, aj=None):
    P = R.P
    if aj is None:
        aj = j
    k = gidx % 2
    pg, pu, pgB, puB = R.pB[2 * k], R.pB[2 * k + 1], R.pBB[2 * k], R.pBB[2 * k + 1]
    sg, sgB = R.sg[k], R.sgB[k]

    def mm(pp, ppB, c0, kc):
        P.op("tensor", lambda e: e.matmul(pp[:], lhsT=wi[:, kc, c0:c0 + 128], rhs=hT[:, kc, :],
                                          start=(kc == 0), stop=(kc == NKC - 1)),
             reads=[sB] + hTBl, writes=[ppB], inc=(kc == NKC - 1))
    for (pp, ppB, c0) in ((pg, pgB, j * 128), (pu, puB, nch * 128 + j * 128)):
        for kc in range(NKC):
            mm(pp, ppB, c0, kc)
    P.op("scalar", lambda e: e.activation(out=sg[:], in_=pg[:], func=AF.Silu), reads=[pgB], writes=[sgB])
    P.op("vector", lambda e: e.tensor_tensor(out=R.aT[:, aj, :], in0=sg[:], in1=pu[:], op=ALU.mult),
         reads=[sgB, puB], writes=[R.aTB[aj]])


def stage_c_sub(R, wo, sB, nch, X_in, X_out, Yacc, yB, t, s, sl, last, lhs=None):
    P = R.P
    if lhs is None:
        lhs = [(R.aT, j, R.aTB[j]) for j in range(nch)]
    r0 = t * TT + s * 128
    yi = (t * 4 + s) % 2
    yb, ybB, xr, xrB = R.yb[yi], R.ybB[yi], R.xr[yi], R.xrB[yi]
    if sl > 0:
        P.dma("sync", lambda e: e.dma_start(out=yb[:], in_=Yacc[r0:r0 + 128, :]), reads=[yB], writes=[ybB])
    if last:
        P.dma("sync", lambda e: e.dma_start(out=xr[:], in_=X_in[r0:r0 + 128, :]), writes=[xrB])

    def mm(half, j):
        lt, li, lB = lhs[j]
        P.op("tensor", lambda e: e.matmul(R.pC[half][:], lhsT=lt[:, li, s * 128:(s + 1) * 128],
                                          rhs=wo[:, j, half * 512:(half + 1) * 512],
                                          start=(j == 0), stop=(j == nch - 1)),
             reads=[sB] + (lB if isinstance(lB, list) else [lB]), writes=[R.pCB], inc=(j == nch - 1))
    for half in range(2):
        for j in range(nch):
            mm(half, j)

    def evac(half):
        hs = slice(half * 512, (half + 1) * 512)
        if sl == 0:
            if half == 0:
                P.op("vector", lambda e: e.tensor_copy(out=yb[:, hs], in_=R.pC[half][:]), reads=[R.pCB], writes=[ybB])
            else:
                P.op("scalar", lambda e: e.copy(out=yb[:, hs], in_=R.pC[half][:]), reads=[R.pCB], writes=[ybB])
        else:
            P.op("vector", lambda e: e.tensor_tensor(out=yb[:, hs], in0=yb[:, hs], in1=R.pC[half][:], op=ALU.add),
                 reads=[R.pCB, ybB], writes=[ybB])
    evac(0)
    evac(1)
    if not last:
        P.dma("gpsimd", lambda e: e.dma_start(out=Yacc[r0:r0 + 128, :], in_=yb[:]), reads=[ybB], writes=[yB])
    else:
        k = R.ss_i
        R.ss_i = (k + 1) % 4
        ss, ssB = R.ss[k], R.ssB[k]
        P.op("scalar", lambda e: e.activation(out=R.junk[:], in_=yb[:], func=AF.Square, accum_out=ss[:, 0:1]),
             reads=[ybB], writes=[R.junkB, ssB])
        P.op("scalar", lambda e: e.activation(out=ss[:, 1:2], in_=ss[:, 0:1], func=AF.Sqrt, bias=R.epsc[:, 0:1], scale=1.0),
             reads=[ssB, R.epscB], writes=[ssB])
        P.op("vector", lambda e: e.reciprocal(out=ss[:, 1:2], in_=ss[:, 1:2]), reads=[ssB], writes=[ssB])
        P.op("scalar", lambda e: e.activation(out=yb[:], in_=yb[:], func=AF.Copy, scale=ss[:, 1:2]),
             reads=[ybB, ssB], writes=[ybB])
        P.op("gpsimd", lambda e: e.tensor_tensor(out=yb[:], in0=yb[:], in1=R.crow[:], op=ALU.mult),
             reads=[ybB, R.crowB], writes=[ybB])
        P.op("vector", lambda e: e.tensor_tensor(out=xr[:], in0=xr[:], in1=yb[:], op=ALU.add),
             reads=[ybB, xrB], writes=[xrB])
        P.dma("gpsimd", lambda e: e.dma_start(out=X_out[r0:r0 + 128, :], in_=xr[:]), reads=[xrB], writes=[yB])


def ffn_phase(R, X_in, X_out, Yacc, w_in, w_out, NT, first_slab_loaded=False, next_loader=None):
    P = R.P
    ntile = NT // TT
    nsl = len(SLABS)
    if not first_slab_loaded:
        load_slab(R, w_in, w_out, SLABS[0], 0)
    YB = [Buf("Y%d" % i) for i in range(ntile * 4)]
    gidx = 0
    for sl in range(nsl):
        buf = sl % 2
        j0, nch = SLABS[sl]
        last = sl == nsl - 1
        if sl + 1 < nsl:
            load_slab(R, w_in, w_out, SLABS[sl + 1], (sl + 1) % 2)
        elif next_loader is not None:
            next_loader((sl + 1) % 2)
        for s in range(4):
            stage_a_sub(R, X_in, 0, s, 0, 0)
            stage_a_sub(R, X_in, 0, s, 0, 1)
        for t in range(ntile):
            hb = t % 2
            for j in range(nch):
                stage_b_group(R, R.wi[buf], R.slabB[buf], R.hT[hb], R.hTB[hb], j, nch, gidx)
                gidx += 1
                if t + 1 < ntile and j < 8:
                    stage_a_sub(R, X_in, t + 1, j // 2, 1 - hb, j % 2)
            if t + 1 < ntile:
                for jj in range(nch, 8):
                    stage_a_sub(R, X_in, t + 1, jj // 2, 1 - hb, jj % 2)
            for s in range(4):
                stage_c_sub(R, R.wo[buf], R.slabB[buf], nch, X_in, X_out, Yacc, YB[t * 4 + s], t, s, sl, last)
    return YB


NH = 4
NCST = 384
STOP = 0


class StopEmit(Exception):
    pass


def stop_at(k):
    if STOP == k:
        raise StopEmit()


def make_consts():
    c = np.zeros((128, NCST), np.float32)
    c[:, 0:128] = np.eye(128)
    c[:, 128:256] = 1.0
    k = np.arange(64)
    c[0:64, 256:320] = (k[:, None] <= k[None, :])
    c[0:64, 320:384] = (k[:, None] > k[None, :])
    return c


def load_cols(R, w, c0, ncol, buf):
    P = R.P
    wi, sB = R.wi[buf], R.slabB[buf]

    def piece(kc, d0, w_):
        si = R.stg_i
        R.stg_i = (si + 1) % 3
        st, stB = R.stg[si], R.stgB[si]
        P.dma("sync", lambda e: e.dma_start(out=st[:, 0:w_], in_=w[kc * 128:(kc + 1) * 128, c0 + d0:c0 + d0 + w_]), writes=[stB])
        P.op("gpsimd", lambda e: e.tensor_copy(out=wi[:, kc, d0:d0 + w_], in_=st[:, 0:w_]), reads=[stB], writes=[sB])
    for kc in range(NKC):
        d0 = 0
        while d0 < ncol:
            w_ = min(1024, ncol - d0)
            piece(kc, d0, w_)
            d0 += w_


def proj_phase(R, X_in, w_mix, PT, NT, PTB):
    P = R.P
    load_cols(R, w_mix, 0, 2048, 0)
    load_cols(R, w_mix, 2048, 520, 1)
    ntile = NT // TT
    chunks = [(0, j * 128, 128, j * 128) for j in range(16)]
    chunks += [(1, 8 + j * 128, 128, 2056 + j * 128) for j in range(4)]
    chunks += [(1, 0, 8, 2048)]
    gi = 0
    for s in range(4):
        stage_a_sub(R, X_in, 0, s, 0, 0)
        stage_a_sub(R, X_in, 0, s, 0, 1)

    def out_chunk(t, hb, buf, lc, M, row, gi):
        pp, ppB = R.pB[gi % 4], R.pBB[gi % 4]
        sg, sgB = R.sg[gi % 2], R.sgB[gi % 2]
        wi, sB, hT = R.wi[buf], R.slabB[buf], R.hT[hb]

        def mm(kc):
            P.op("tensor", lambda e: e.matmul(pp[0:M, :], lhsT=wi[:, kc, lc:lc + M], rhs=hT[:, kc, :], start=(kc == 0), stop=(kc == NKC - 1)),
                 reads=[sB] + R.hTB[hb], writes=[ppB], inc=(kc == NKC - 1))
        for kc in range(NKC):
            mm(kc)
        if gi % 2 == 0:
            P.op("vector", lambda e: e.tensor_copy(out=sg[0:M, :], in_=pp[0:M, :]), reads=[ppB], writes=[sgB])
        else:
            P.op("scalar", lambda e: e.copy(out=sg[0:M, :], in_=pp[0:M, :]), reads=[ppB], writes=[sgB])
        P.dma("gpsimd", lambda e: e.dma_start(out=PT[row:row + M, t * TT:(t + 1) * TT], in_=sg[0:M, :]), reads=[sgB], writes=[PTB[t]])
    for t in range(ntile):
        hb = t % 2
        for ci, (buf, lc, M, row) in enumerate(chunks):
            out_chunk(t, hb, buf, lc, M, row, gi)
            gi += 1
            if t + 1 < ntile and ci < 8:
                stage_a_sub(R, X_in, t + 1, ci // 2, 1 - hb, ci % 2)


class MixRes:
    def __init__(self, nc, stack, P, tag=""):
        self.nc, self.P = nc, P
        self._sb = lambda name, shape, dt=F32: stack.enter_context(nc.sbuf_tensor("m_" + name + tag, shape, dt))
        self._ps = lambda name, shape, dt=F32: stack.enter_context(nc.psum_tensor("m_" + name + tag, shape, dt))
        self.bufs = {}

    def sb(self, name, shape, dt=F32):
        t = self._sb(name, shape, dt)
        self.bufs[name] = Buf(name)
        setattr(self, name, t)
        setattr(self, name + "B", self.bufs[name])
        return t

    def ps(self, name, shape, dt=F32):
        t = self._ps(name, shape, dt)
        self.bufs[name] = Buf(name)
        setattr(self, name, t)
        setattr(self, name + "B", self.bufs[name])
        return t


def gdn_phase(M, PT, YT, conv_w, a_log, dt_bias, gnw, cst_dram, NT, PTB, YTB):
    P = M.P
    ntile = NT // TT
    sb, ps = M.sb, M.ps
    G2 = 2
    cst = sb("cst", [128, NCST])
    ident = cst[:, 0:128]
    ones = cst[:, 128:256]
    LE = cst[0:64, 256:320]
    GT = cst[0:64, 320:384]
    P.dma("sync", lambda e: e.dma_start(out=cst[:], in_=cst_dram), writes=[M.cstB])
    sb("LE4", [64, G2, 64]); sb("GT4", [64, G2, 64]); sb("I4", [64, G2, 64])
    for h in range(G2):
        P.op("vector", (lambda e, h=h: e.tensor_copy(out=M.LE4[:, h, :], in_=LE)), reads=[M.cstB], writes=[M.LE4B])
        P.op("vector", (lambda e, h=h: e.tensor_copy(out=M.GT4[:, h, :], in_=GT)), reads=[M.cstB], writes=[M.GT4B])
        P.op("vector", (lambda e, h=h: e.tensor_copy(out=M.I4[:, h, :], in_=cst[0:64, 0:64])), reads=[M.cstB], writes=[M.I4B])
    sb("epsk", [128, 1]); sb("epsq", [128, 1]); sb("epsn", [128, 1]); sb("one1", [128, 1])
    P.op("vector", lambda e: e.memset(M.epsk[:], 1e-6), writes=[M.epskB])
    P.op("vector", lambda e: e.memset(M.epsq[:], 128e-6), writes=[M.epsqB])
    P.op("vector", lambda e: e.memset(M.epsn[:], 1e-6), writes=[M.epsnB])
    P.op("vector", lambda e: e.memset(M.one1[:], 1.0), writes=[M.one1B])

    class Grp:
        pass
    groups = []
    for gi in range(2):
        G = Grp()
        G.h0 = gi * G2
        sfx = "_g%d" % gi

        def gsb(name, shape, G=G, sfx=sfx):
            t = sb(name + sfx, shape)
            setattr(G, name, t)
            setattr(G, name + "B", getattr(M, name + sfx + "B"))

        def gps(name, shape, G=G, sfx=sfx):
            t = ps(name + sfx, shape)
            setattr(G, name, t)
            setattr(G, name + "B", getattr(M, name + sfx + "B"))
        banks = [ps("bk%d" % k + sfx, [128, 512]) for k in range(4)]
        r4 = lambda ap: ap.rearrange("p (a h c) -> p a h c", a=2, h=G2)
        for nm, ap in (("psD", r4(banks[0][0:64, 0:256])), ("psG", r4(banks[0][0:64, 256:512])),
                       ("psI", r4(banks[1][0:64, 0:256])), ("psU", r4(banks[1][0:64, 256:512])),
                       ("psX", banks[2][:, 0:256]), ("psY", banks[2][:, 256:512]),
                       ("psZ", banks[3][:, 0:256]), ("psS", banks[3][:, 256:384])):
            setattr(G, nm, ap)
        bankB = [Buf("bk%d" % k + sfx) for k in range(4)]
        for nm, k in (("psD", 0), ("psG", 0), ("psI", 1), ("psU", 1), ("psX", 2), ("psY", 2), ("psZ", 3), ("psS", 3)):
            setattr(G, nm + "B", bankB[k])
        gsb("S", [128, G2, 128]); gsb("yg", [128, G2, TT])
        gsb("ba", [64, 8]); gsb("bt", [64, G2]); gsb("nbt", [64, G2]); gsb("g", [64, G2]); gsb("gcs", [64, G2]); gsb("egc", [64, G2])
        gsb("egl", [128, G2]); gsb("egd", [64, G2]); gsb("begc", [64, G2])
        gsb("G12", [64, 2, G2, 64]); gsb("eD", [64, 2, G2, 64]); gsb("dec", [64, 2, G2, 64])
        gsb("AA", [64, 2, G2, 64]); gsb("PU", [64, G2, 64]); gsb("attT", [64, G2, 64]); gsb("tL", [64, G2, 64])
        gsb("vb", [64, G2, 128]); gsb("kbg", [64, G2, 128]); gsb("kst", [64, G2, 128]); gsb("wv", [64, G2, 128]); gsb("kcT", [128, G2, 64])
        gsb("vn", [64, G2, 128]); gsb("o1", [64, G2, 128]); gsb("osq", [64, G2, 128]); gsb("on", [64, G2, 128]); gsb("ssq", [64, 2 * G2])
        P.op("vector", (lambda e, G=G: e.memset(G.S[:], 0.0)), writes=[G.SB])
        groups.append(G)
    GA, GB = groups
    psS0, psS0B = GA.psS, GA.psSB
    sb("cwr", [4, 1536]); sb("cw", [128, 12, 4])
    P.dma("sync", lambda e: e.dma_start(out=M.cwr[:], in_=conv_w), writes=[M.cwrB])
    for ct in range(12):
        P.op("tensor", (lambda e, ct=ct: e.matmul(psS0[:, ct * 4:(ct + 1) * 4], lhsT=M.cwr[0:4, ct * 128:(ct + 1) * 128], rhs=cst[0:4, 0:4], start=True, stop=True)),
             reads=[M.cwrB, M.cstB], writes=[psS0B], inc=(ct == 11))
    P.op("vector", lambda e: e.tensor_copy(out=M.cw[:].rearrange("p a b -> p (a b)"), in_=psS0[:, 0:48]), reads=[psS0B], writes=[M.cwB])
    sb("gnr", [1, 128]); sb("gnc", [128, 1])
    P.dma("sync", lambda e: e.dma_start(out=M.gnr[:], in_=gnw.rearrange("(o f) -> o f", o=1)), writes=[M.gnrB])
    P.op("tensor", lambda e: e.matmul(psS0[:, 0:1], lhsT=M.gnr[0:1, :], rhs=cst[0:1, 0:1], start=True, stop=True), reads=[M.gnrB, M.cstB], writes=[psS0B])
    P.op("vector", lambda e: e.tensor_copy(out=M.gnc[:], in_=psS0[:, 0:1]), reads=[psS0B], writes=[M.gncB])
    sb("nA", [64, NH]); sb("dtb", [64, NH]); sb("adr", [1, 8])
    P.dma("sync", lambda e: e.dma_start(out=M.adr[:, 0:4], in_=a_log.rearrange("(o f) -> o f", o=1)), writes=[M.adrB])
    P.dma("sync", lambda e: e.dma_start(out=M.adr[:, 4:8], in_=dt_bias.rearrange("(o f) -> o f", o=1)), writes=[M.adrB])
    P.op("tensor", lambda e: e.matmul(psS0[0:64, 0:8], lhsT=cst[0:1, 128:192], rhs=M.adr[0:1, :], start=True, stop=True), reads=[M.adrB, M.cstB], writes=[psS0B])
    P.op("vector", lambda e: e.tensor_copy(out=M.nA[:], in_=psS0[0:64, 0:4]), reads=[psS0B], writes=[M.nAB])
    P.op("vector", lambda e: e.tensor_copy(out=M.dtb[:], in_=psS0[0:64, 4:8]), reads=[psS0B], writes=[M.dtbB])
    P.op("scalar", lambda e: e.activation(out=M.nA[:], in_=M.nA[:], func=AF.Exp), reads=[M.nAB], writes=[M.nAB])
    P.op("vector", lambda e: e.tensor_scalar(out=M.nA[:], in0=M.nA[:], scalar1=-1.0, scalar2=None, op0=ALU.mult), reads=[M.nAB], writes=[M.nAB])
    sb("qkv", [128, 12, TT]); sb("xin", [128, TT + 3]); sb("acc", [128, TT]); sb("sq", [128, TT]); sb("rn", [128, TT])
    sb("sz", [128, NH, TT]); sb("bar", [8, TT])

    def conv_tile(t, ct):
        t0 = t * TT
        r0 = ct * 128
        if t == 0:
            P.op("gpsimd", lambda e: e.memset(M.xin[:, 0:3], 0.0), writes=[M.xinB])
            P.dma("sync", lambda e: e.dma_start(out=M.xin[:, 3:TT + 3], in_=PT[r0:r0 + 128, 0:TT]), reads=[PTB[0]], writes=[M.xinB])
        else:
            P.dma("sync", lambda e: e.dma_start(out=M.xin[:], in_=PT[r0:r0 + 128, t0 - 3:t0 + TT]), reads=[PTB[t - 1], PTB[t]], writes=[M.xinB])
        P.op("vector", lambda e: e.tensor_scalar(out=M.acc[:], in0=M.xin[:, 0:TT], scalar1=M.cw[:, ct, 0:1], scalar2=None, op0=ALU.mult),
             reads=[M.xinB, M.cwB], writes=[M.accB])
        for j in range(1, 4):
            P.op("vector", (lambda e, j=j: e.scalar_tensor_tensor(out=M.acc[:], in0=M.xin[:, j:j + TT], scalar=M.cw[:, ct, j:j + 1], in1=M.acc[:],
                                                                  op0=ALU.mult, op1=ALU.add)), reads=[M.xinB, M.cwB, M.accB], writes=[M.accB])
        P.op("scalar", lambda e: e.activation(out=M.qkv[:, ct, :], in_=M.acc[:], func=AF.Silu), reads=[M.accB], writes=[M.qkvB])
        if ct < 8:
            P.op("scalar", lambda e: e.activation(out=M.sq[:], in_=M.qkv[:, ct, :], func=AF.Square), reads=[M.qkvB], writes=[M.sqB])
            for hf, Gx in enumerate((GA, GB)):
                def half(hf=hf, Gx=Gx):
                    cs_ = slice(hf * 256, (hf + 1) * 256)
                    P.op("tensor", lambda e: e.matmul(Gx.psX[:], lhsT=ones, rhs=M.sq[:, cs_], start=True, stop=True), reads=[M.sqB, M.cstB], writes=[Gx.psXB])
                    P.op("vector", lambda e: e.tensor_copy(out=M.rn[:, cs_], in_=Gx.psX[:]), reads=[Gx.psXB], writes=[M.rnB])
                    if ct < 4:
                        P.op("scalar", lambda e: e.activation(out=M.rn[:, cs_], in_=M.rn[:, cs_], func=AF.Sqrt, bias=M.epsq[:, 0:1], scale=128.0),
                             reads=[M.rnB, M.epsqB], writes=[M.rnB])
                    else:
                        P.op("scalar", lambda e: e.activation(out=M.rn[:, cs_], in_=M.rn[:, cs_], func=AF.Sqrt, bias=M.epsk[:, 0:1], scale=1.0),
                             reads=[M.rnB, M.epskB], writes=[M.rnB])
                half()
            P.op("vector", lambda e: e.reciprocal(out=M.rn[:], in_=M.rn[:]), reads=[M.rnB], writes=[M.rnB])
            P.op("gpsimd", lambda e: e.tensor_tensor(out=M.qkv[:, ct, :], in0=M.qkv[:, ct, :], in1=M.rn[:], op=ALU.mult),
                 reads=[M.qkvB, M.rnB], writes=[M.qkvB])

    def z_tile(t, h):
        t0 = t * TT
        P.dma("sync", lambda e: e.dma_start(out=M.sz[:, h, :], in_=PT[1536 + h * 128:1536 + (h + 1) * 128, t0:t0 + TT]), reads=[PTB[t]], writes=[M.szB])
        P.op("scalar", lambda e: e.activation(out=M.sz[:, h, :], in_=M.sz[:, h, :], func=AF.Silu), reads=[M.szB], writes=[M.szB])
        P.op("gpsimd", lambda e: e.tensor_scalar(out=M.sz[:, h, :], in0=M.sz[:, h, :], scalar1=M.gnc[:, 0:1], scalar2=None, op0=ALU.mult),
             reads=[M.szB, M.gncB], writes=[M.szB])

    def mm(out, lhsT, rhs, reads, writes, inc=True):
        P.op("tensor", lambda e: e.matmul(out, lhsT=lhsT, rhs=rhs, start=True, stop=True), reads=reads, writes=writes, inc=inc)

    def chunk(n, G):
        h0 = G.h0
        HR = range(G2)
        c0 = n * 64
        cs = slice(c0, c0 + 64)
        qT = lambda h: M.qkv[:, h0 + h, cs]
        kT = lambda h: M.qkv[:, 4 + h0 + h, cs]
        vT = lambda h: M.qkv[:, 8 + h0 + h, cs]
        mm(G.psS[0:64, 0:8], M.bar[0:8, cs], cst[0:8, 0:8], [M.barB, M.cstB], [G.psSB])
        P.op("vector", lambda e: e.tensor_copy(out=G.ba[:], in_=G.psS[0:64, 0:8]), reads=[G.psSB], writes=[G.baB])
        yield
        P.op("scalar", lambda e: e.activation(out=G.bt[:], in_=G.ba[:, h0:h0 + G2], func=AF.Sigmoid), reads=[G.baB], writes=[G.btB])
        P.op("vector", lambda e: e.tensor_tensor(out=G.g[:], in0=G.ba[:, 4 + h0:4 + h0 + G2], in1=M.dtb[:, h0:h0 + G2], op=ALU.add), reads=[G.baB, M.dtbB], writes=[G.gB])
        yield
        P.op("vector", lambda e: e.tensor_scalar(out=G.nbt[:], in0=G.bt[:], scalar1=-1.0, scalar2=None, op0=ALU.mult), reads=[G.btB], writes=[G.nbtB])
        P.op("scalar", lambda e: e.activation(out=G.g[:], in_=G.g[:], func=AF.Exp), reads=[G.gB], writes=[G.gB])
        yield
        P.op("scalar", lambda e: e.activation(out=G.g[:], in_=G.g[:], func=AF.Ln, bias=M.one1[0:64, 0:1], scale=1.0), reads=[G.gB, M.one1B], writes=[G.gB])
        yield
        P.op("vector", lambda e: e.tensor_tensor(out=G.g[:], in0=G.g[:], in1=M.nA[:, h0:h0 + G2], op=ALU.mult), reads=[G.gB, M.nAB], writes=[G.gB])
        yield
        mm(G.psS[0:64, 8:8 + G2], LE, G.g[:], [G.gB, M.cstB], [G.psSB], inc=False)
        mm(G.psS[:, 12:12 + G2], cst[0:64, 128:256], G.g[:], [G.gB, M.cstB], [G.psSB])
        for h in HR:
            P.op("gpsimd", (lambda e, h=h: e.tensor_scalar(out=G.G12[:, 0, h, :], in0=LE, scalar1=G.g[:, h:h + 1], scalar2=None, op0=ALU.mult)),
                 reads=[G.gB, M.cstB], writes=[G.G12B])
            P.op("gpsimd", (lambda e, h=h: e.tensor_scalar(out=G.G12[:, 1, h, :], in0=GT, scalar1=G.g[:, h:h + 1], scalar2=None, op0=ALU.mult)),
                 reads=[G.gB, M.cstB], writes=[G.G12B])
        yield
        P.op("vector", lambda e: e.tensor_copy(out=G.gcs[:], in_=G.psS[0:64, 8:8 + G2]), reads=[G.psSB], writes=[G.gcsB])
        P.op("vector", lambda e: e.tensor_copy(out=G.egl[:], in_=G.psS[:, 12:12 + G2]), reads=[G.psSB], writes=[G.eglB])
        P.op("scalar", lambda e: e.activation(out=G.egc[:], in_=G.gcs[:], func=AF.Exp), reads=[G.gcsB], writes=[G.egcB])
        P.op("scalar", lambda e: e.activation(out=G.egl[:], in_=G.egl[:], func=AF.Exp), reads=[G.eglB], writes=[G.eglB])
        for h in HR:
            mm(G.psD[:, 0, h, :], G.G12[:, 0, h, :], GT, [G.G12B, M.cstB], [G.psDB], inc=False)
            mm(G.psD[:, 1, h, :], G.G12[:, 1, h, :], LE, [G.G12B, M.cstB], [G.psDB], inc=(h == G2 - 1))
        for h in HR:
            mm(G.psG[:, 0, h, :], kT(h), kT(h), [M.qkvB], [G.psGB], inc=False)
            mm(G.psU[:, 1, h, :], kT(h), qT(h), [M.qkvB], [G.psUB], inc=(h == G2 - 1))
        yield
        P.op("vector", lambda e: e.tensor_tensor(out=G.egd[:], in0=G.psS[0:64, 12:12 + G2], in1=G.gcs[:], op=ALU.subtract), reads=[G.psSB, G.gcsB], writes=[G.egdB])
        P.op("vector", lambda e: e.tensor_tensor(out=G.begc[:], in0=G.bt[:], in1=G.egc[:], op=ALU.mult), reads=[G.btB, G.egcB], writes=[G.begcB])
        P.op("vector", lambda e: e.tensor_copy(out=G.eD[:], in_=G.psD[:]), reads=[G.psDB], writes=[G.eDB])
        P.op("scalar", lambda e: e.activation(out=G.eD[:], in_=G.eD[:], func=AF.Exp), reads=[G.eDB], writes=[G.eDB])
        yield
        P.op("scalar", lambda e: e.activation(out=G.egd[:], in_=G.egd[:], func=AF.Exp), reads=[G.egdB], writes=[G.egdB])
        P.op("gpsimd", lambda e: e.tensor_tensor(out=G.dec[:, 0], in0=G.eD[:, 0], in1=M.GT4[:], op=ALU.mult), reads=[G.eDB, M.GT4B], writes=[G.decB])
        P.op("gpsimd", lambda e: e.tensor_tensor(out=G.dec[:, 1], in0=G.eD[:, 1], in1=M.LE4[:], op=ALU.mult), reads=[G.eDB, M.LE4B], writes=[G.decB])
        yield
        P.op("vector", lambda e: e.tensor_tensor(out=G.tL[:], in0=G.psG[:, 0], in1=G.dec[:, 0], op=ALU.mult), reads=[G.psGB, G.decB], writes=[G.tLB])
        P.op("vector", lambda e: e.tensor_tensor(out=G.attT[:], in0=G.psU[:, 1], in1=G.dec[:, 1], op=ALU.mult), reads=[G.psUB, G.decB], writes=[G.attTB])
        yield
        for h in HR:
            P.op("gpsimd", (lambda e, h=h: e.tensor_scalar(out=G.AA[:, 1, h, :], in0=G.tL[:, h, :], scalar1=G.nbt[:, h:h + 1], scalar2=None, op0=ALU.mult)),
                 reads=[G.tLB, G.nbtB], writes=[G.AAB])
        yield
        for h in HR:
            mm(G.psG[:, 1, h, :], G.AA[:, 1, h, :], cst[0:64, 0:64], [G.AAB, M.cstB], [G.psGB], inc=(h == G2 - 1))
        yield
        P.op("vector", lambda e: e.tensor_copy(out=G.AA[:, 0], in_=G.psG[:, 1]), reads=[G.psGB], writes=[G.AAB])
        yield
        P.op("vector", lambda e: e.tensor_tensor(out=G.PU[:], in0=G.AA[:, 0], in1=M.I4[:], op=ALU.add), reads=[G.AAB, M.I4B], writes=[G.PUB])
        for m in range(5):
            for h in HR:
                mm(G.psI[:, 0, h, :], G.AA[:, 1, h, :], G.AA[:, 0, h, :], [G.AAB], [G.psIB], inc=False)
                mm(G.psI[:, 1, h, :], G.AA[:, 0, h, :], G.AA[:, 1, h, :], [G.AAB], [G.psIB], inc=(h == G2 - 1))
            yield
            P.op("vector", lambda e: e.tensor_copy(out=G.AA[:], in_=G.psI[:]), reads=[G.psIB], writes=[G.AAB])
            yield
            for h in HR:
                mm(G.psU[:, 0, h, :], G.AA[:, 1, h, :], G.PU[:, h, :], [G.AAB, G.PUB], [G.psUB], inc=(h == G2 - 1))
            yield
            P.op("vector", lambda e: e.tensor_tensor(out=G.PU[:], in0=G.PU[:], in1=G.psU[:, 0], op=ALU.add), reads=[G.psUB, G.PUB], writes=[G.PUB])
            yield
        X3 = G.psX[0:64, :].rearrange("p (h d) -> p h d", h=G2)
        Y3 = G.psY[0:64, :].rearrange("p (h d) -> p h d", h=G2)
        Z3 = G.psZ[0:64, :].rearrange("p (h d) -> p h d", h=G2)

        def tr(out, in_, wB, inc):
            P.op("tensor", lambda e: e.transpose(out=out, in_=in_, identity=ident), reads=[M.qkvB, M.cstB], writes=[wB], inc=inc)
        for h in HR:
            tr(X3[:, h, :], kT(h), G.psXB, False)
            tr(Y3[:, h, :], vT(h), G.psYB, h == G2 - 1)
        yield
        for h in HR:
            P.op("vector", (lambda e, h=h: e.tensor_scalar(out=G.vb[:, h, :], in0=Y3[:, h, :], scalar1=G.bt[:, h:h + 1], scalar2=None, op0=ALU.mult)),
                 reads=[G.psYB, G.btB], writes=[G.vbB])
            P.op("vector", (lambda e, h=h: e.tensor_scalar(out=G.kbg[:, h, :], in0=X3[:, h, :], scalar1=G.begc[:, h:h + 1], scalar2=None, op0=ALU.mult)),
                 reads=[G.psXB, G.begcB], writes=[G.kbgB])
            P.op("vector", (lambda e, h=h: e.tensor_scalar(out=G.kst[:, h, :], in0=X3[:, h, :], scalar1=G.egd[:, h:h + 1], scalar2=None, op0=ALU.mult)),
                 reads=[G.psXB, G.egdB], writes=[G.kstB])
        yield
        YK = G.psY[:, 0:G2 * 64].rearrange("p (h d) -> p h d", h=G2)
        for h in HR:
            mm(X3[:, h, :], G.PU[:, h, :], G.vb[:, h, :], [G.PUB, G.vbB], [G.psXB], inc=False)
            mm(YK[:, h, :], G.kbg[:, h, :], G.PU[:, h, :], [G.PUB, G.kbgB], [G.psYB], inc=(h == G2 - 1))
        yield
        P.op("vector", lambda e: e.tensor_copy(out=G.wv[:], in_=X3), reads=[G.psXB], writes=[G.wvB])
        P.op("vector", lambda e: e.tensor_copy(out=G.kcT[:], in_=YK), reads=[G.psYB], writes=[G.kcTB])
        yield
        for h in HR:
            mm(X3[:, h, :], G.kcT[:, h, :], G.S[:, h, :], [G.kcTB, G.SB], [G.psXB], inc=(h == G2 - 1))
        yield
        P.op("vector", lambda e: e.tensor_tensor(out=G.vn[:], in0=G.wv[:], in1=X3, op=ALU.subtract), reads=[G.wvB, G.psXB], writes=[G.vnB])
        yield
        for h in HR:
            mm(Y3[:, h, :], qT(h), G.S[:, h, :], [M.qkvB, G.SB], [G.psYB], inc=False)
            mm(Z3[:, h, :], G.attT[:, h, :], G.vn[:, h, :], [G.attTB, G.vnB], [G.psZB], inc=(h == G2 - 1))
        XS = G.psX[:, :].rearrange("p (h d) -> p h d", h=G2)
        for h in HR:
            mm(XS[:, h, :], G.kst[:, h, :], G.vn[:, h, :], [G.kstB, G.vnB], [G.psXB], inc=(h == G2 - 1))
        yield
        for h in HR:
            P.op("vector", (lambda e, h=h: e.tensor_scalar(out=G.o1[:, h, :], in0=Y3[:, h, :], scalar1=G.egc[:, h:h + 1], scalar2=None, op0=ALU.mult)),
                 reads=[G.psYB, G.egcB], writes=[G.o1B])
            P.op("gpsimd", (lambda e, h=h: e.tensor_scalar(out=G.S[:, h, :], in0=G.S[:, h, :], scalar1=G.egl[:, h:h + 1], scalar2=None, op0=ALU.mult)),
                 reads=[G.SB, G.eglB], writes=[G.SB])
        yield
        P.op("vector", lambda e: e.tensor_tensor(out=G.o1[:], in0=G.o1[:], in1=Z3, op=ALU.add), reads=[G.o1B, G.psZB], writes=[G.o1B])
        P.op("vector", lambda e: e.tensor_tensor(out=G.S[:], in0=G.S[:], in1=XS, op=ALU.add), reads=[G.SB, G.psXB], writes=[G.SB])
        yield
        P.op("gpsimd", lambda e: e.tensor_tensor(out=G.osq[:], in0=G.o1[:], in1=G.o1[:], op=ALU.mult), reads=[G.o1B], writes=[G.osqB])
        yield
        P.op("vector", lambda e: e.reduce_sum(out=G.ssq[:, 0:G2], in_=G.osq[:], axis=mybir.AxisListType.X), reads=[G.osqB], writes=[G.ssqB])
        yield
        P.op("scalar", lambda e: e.activation(out=G.ssq[:, G2:2 * G2], in_=G.ssq[:, 0:G2], func=AF.Sqrt, bias=M.epsn[0:64, 0:1], scale=1.0 / 128),
             reads=[G.ssqB, M.epsnB], writes=[G.ssqB])
        yield
        P.op("vector", lambda e: e.reciprocal(out=G.ssq[:, G2:2 * G2], in_=G.ssq[:, G2:2 * G2]), reads=[G.ssqB], writes=[G.ssqB])
        yield
        for h in HR:
            P.op("gpsimd", (lambda e, h=h: e.tensor_scalar(out=G.on[:, h, :], in0=G.o1[:, h, :], scalar1=G.ssq[:, G2 + h:G2 + h + 1], scalar2=None, op0=ALU.mult)),
                 reads=[G.o1B, G.ssqB], writes=[G.onB])
        yield
        ZT = G.psZ[:, 0:G2 * 64].rearrange("p (h d) -> p h d", h=G2)
        for h in HR:
            mm(ZT[:, h, :], G.on[:, h, :], cst[0:64, 0:64], [G.onB, M.cstB], [G.psZB], inc=(h == G2 - 1))
        yield
        P.op("vector", lambda e: e.tensor_tensor(out=G.yg[:, :, cs], in0=ZT, in1=M.sz[:, h0:h0 + G2, cs], op=ALU.mult), reads=[G.psZB, M.szB], writes=[G.ygB])

    for t in range(ntile):
        t0 = t * TT
        for ct in range(12):
            conv_tile(t, ct)
        for h in range(NH):
            z_tile(t, h)
        P.dma("sync", (lambda e, t0=t0: e.dma_start(out=M.bar[:], in_=PT[2048:2056, t0:t0 + TT])), reads=[PTB[t]], writes=[M.barB])
        for n in range(8):
            gens = [chunk(n, GA), chunk(n, GB)]
            alive = [True, True]
            while any(alive):
                for i, gen in enumerate(gens):
                    if alive[i]:
                        try:
                            next(gen)
                        except StopIteration:
                            alive[i] = False
        for G in (GA, GB):
            for h in range(G2):
                P.dma("gpsimd", (lambda e, G=G, h=h, t0=t0: e.dma_start(out=YT[(G.h0 + h) * 128:(G.h0 + h + 1) * 128, t0:t0 + TT], in_=G.yg[:, h, :])),
                      reads=[G.ygB], writes=[YTB[t]])


NCST2 = 129 + 128 + 16
SEG = 128
TWO_PI = 6.283185307179586


def make_consts2():
    c = np.zeros((128, NCST2), np.float32)
    c[:, 0:129] = np.arange(129)[None, :]
    g = np.arange(32)
    m = np.arange(128)
    c[0:32, 129:257] = (g[:, None] % 2 == (m[None, :] // 64))
    c[0:32, 257:273] = (g[:, None] // 2 == np.arange(16)[None, :])
    return c


def s5_phase(M, PT, YT, prm, cst_dram, cst2_dram, NT, PTB, YTB, tag=""):
    P = M.P
    ntile = NT // TT
    sb, ps = M.sb, M.ps
    c1 = sb("s_c1", [128, NCST]); c2 = sb("s_c2", [128, NCST2])
    c1B, c2B = M.s_c1B, M.s_c2B
    ident = c1[:, 0:128]
    P.dma("sync", lambda e: e.dma_start(out=c1[:], in_=cst_dram), writes=[c1B])
    P.dma("sync", lambda e: e.dma_start(out=c2[:], in_=cst2_dram), writes=[c2B])
    iota = c2[:, 0:129]
    psA = ps("s_psA", [128, 512]); psB = ps("s_psB", [128, 512]); psY = ps("s_psY", [128, 512]); psT = ps("s_psT", [128, 512])
    psAB, psBB, psYB, psTB = M.s_psAB, M.s_psBB, M.s_psYB, M.s_psTB
    NP_ = 16

    def V(eng, fn, reads, writes):
        P.op(eng, fn, reads=reads, writes=writes)

    rows = sb("s_rows", [32, 128]); rowsB = M.s_rowsB
    arc = sb("s_ar", [128, NP_]); aic = sb("s_ai", [128, NP_]); dtc = sb("s_dt", [128, NP_])

    def col_from_rows(dst, dstB, src_ap):
        P.dma("sync", lambda e: e.dma_start(out=rows[0:16, :], in_=src_ap), writes=[rowsB])
        P.op("tensor", lambda e: e.matmul(psT[:, 0:16], lhsT=rows[0:16, :], rhs=c1[0:16, 0:16], start=True, stop=True), reads=[rowsB, c1B], writes=[psTB])
        V("vector", lambda e: e.tensor_copy(out=dst[:], in_=psT[:, 0:16]), [psTB], [dstB])
    col_from_rows(arc, M.s_arB, prm["a_re"].rearrange("(a b) p -> a (b p)", b=2))
    col_from_rows(aic, M.s_aiB, prm["a_im"].rearrange("(a b) p -> a (b p)", b=2))
    ldr = sb("s_ldr", [1, 32]); ldc = sb("s_ldc", [32, 1]); Rm = sb("s_Rm", [32, 16])
    P.dma("sync", lambda e: e.dma_start(out=ldr[:], in_=prm["log_dt"].rearrange("(o f) -> o f", o=1)), writes=[M.s_ldrB])
    P.op("tensor", lambda e: e.matmul(psT[0:32, 16:17], lhsT=ldr[0:1, :], rhs=c1[0:1, 0:1], start=True, stop=True), reads=[M.s_ldrB, c1B], writes=[psTB])
    V("vector", lambda e: e.tensor_copy(out=ldc[:], in_=psT[0:32, 16:17]), [psTB], [M.s_ldcB])
    V("vector", lambda e: e.tensor_scalar(out=Rm[:], in0=c2[0:32, 257:273], scalar1=ldc[:, 0:1], scalar2=None, op0=ALU.mult), [c2B, M.s_ldcB], [M.s_RmB])
    P.op("tensor", lambda e: e.matmul(psT[:, 32:48], lhsT=c2[0:32, 129:257], rhs=Rm[:], start=True, stop=True), reads=[c2B, M.s_RmB], writes=[psTB])
    V("scalar", lambda e: e.activation(out=dtc[:], in_=psT[:, 32:48], func=AF.Exp), [psTB], [M.s_dtB])

    def sincos(x, xB, s_out, sB_, c_out, cB_, F, tmp, tmpB, tmpi, tmpiB):
        V("vector", lambda e: e.tensor_scalar(out=tmp, in0=x, scalar1=1.0 / TWO_PI, scalar2=None, op0=ALU.mult), [xB], [tmpB])
        V("vector", lambda e: e.tensor_copy(out=tmpi, in_=tmp), [tmpB], [tmpiB])
        V("vector", lambda e: e.tensor_copy(out=tmp, in_=tmpi), [tmpiB], [tmpB])
        V("vector", lambda e: e.scalar_tensor_tensor(out=x, in0=tmp, scalar=-TWO_PI, in1=x, op0=ALU.mult, op1=ALU.add), [tmpB, xB], [xB])
        V("scalar", lambda e: e.activation(out=tmp, in_=x, func=AF.Sin, scale=0.25), [xB], [tmpB])
        V("scalar", lambda e: e.activation(out=s_out, in_=x, func=AF.Sin, scale=0.5), [xB], [sB_])
        V("vector", lambda e: e.tensor_tensor(out=tmp, in0=tmp, in1=tmp, op=ALU.mult), [tmpB], [tmpB])
        V("vector", lambda e: e.tensor_scalar(out=tmp, in0=tmp, scalar1=-2.0, scalar2=1.0, op0=ALU.mult, op1=ALU.add), [tmpB], [tmpB])
        V("vector", lambda e: e.tensor_tensor(out=c_out, in0=s_out, in1=s_out, op=ALU.mult), [sB_], [cB_])
        V("vector", lambda e: e.scalar_tensor_tensor(out=s_out, in0=s_out, scalar=2.0, in1=tmp, op0=ALU.mult, op1=ALU.mult), [sB_, tmpB], [sB_])
        V("vector", lambda e: e.tensor_scalar(out=c_out, in0=c_out, scalar1=-2.0, scalar2=1.0, op0=ALU.mult, op1=ALU.add), [cB_], [cB_])

    mag = sb("s_mag", [128, NP_]); th = sb("s_th", [128, NP_]); sn = sb("s_sn", [128, NP_]); cs_ = sb("s_cs", [128, NP_])
    tp = sb("s_tp", [128, NP_]); tpi = sb("s_tpi", [128, NP_], I32); th2 = sb("s_th2", [128, NP_])
    V("vector", lambda e: e.tensor_scalar(out=arc[:], in0=arc[:], scalar1=-1e-4, scalar2=None, op0=ALU.min), [M.s_arB], [M.s_arB])
    V("vector", lambda e: e.tensor_tensor(out=mag[:], in0=dtc[:], in1=arc[:], op=ALU.mult), [M.s_dtB, M.s_arB], [M.s_magB])
    V("scalar", lambda e: e.activation(out=mag[:], in_=mag[:], func=AF.Exp), [M.s_magB], [M.s_magB])
    V("vector", lambda e: e.tensor_tensor(out=th[:], in0=dtc[:], in1=aic[:], op=ALU.mult), [M.s_dtB, M.s_aiB], [M.s_thB])
    V("vector", lambda e: e.tensor_copy(out=th2[:], in_=th[:]), [M.s_thB], [M.s_th2B])
    sincos(th2[:], M.s_th2B, sn[:], M.s_snB, cs_[:], M.s_csB, NP_, tp[:], M.s_tpB, tpi[:], M.s_tpiB)
    zr = sb("s_zr", [128, NP_]); zi = sb("s_zi", [128, NP_]); den = sb("s_den", [128, NP_]); fr = sb("s_fr", [128, NP_]); fi = sb("s_fi", [128, NP_])
    V("vector", lambda e: e.tensor_tensor(out=zr[:], in0=mag[:], in1=cs_[:], op=ALU.mult), [M.s_magB, M.s_csB], [M.s_zrB])
    V("vector", lambda e: e.tensor_scalar(out=zr[:], in0=zr[:], scalar1=-1.0, scalar2=None, op0=ALU.add), [M.s_zrB], [M.s_zrB])
    V("vector", lambda e: e.tensor_tensor(out=zi[:], in0=mag[:], in1=sn[:], op=ALU.mult), [M.s_magB, M.s_snB], [M.s_ziB])
    V("vector", lambda e: e.tensor_tensor(out=den[:], in0=arc[:], in1=arc[:], op=ALU.mult), [M.s_arB], [M.s_denB])
    V("vector", lambda e: e.tensor_tensor(out=tp[:], in0=aic[:], in1=aic[:], op=ALU.mult), [M.s_aiB], [M.s_tpB])
    V("vector", lambda e: e.tensor_tensor(out=den[:], in0=den[:], in1=tp[:], op=ALU.add), [M.s_denB, M.s_tpB], [M.s_denB])
    V("vector", lambda e: e.reciprocal(out=den[:], in_=den[:]), [M.s_denB], [M.s_denB])
    V("vector", lambda e: e.tensor_tensor(out=fr[:], in0=zr[:], in1=arc[:], op=ALU.mult), [M.s_zrB, M.s_arB], [M.s_frB])
    V("vector", lambda e: e.tensor_tensor(out=tp[:], in0=zi[:], in1=aic[:], op=ALU.mult), [M.s_ziB, M.s_aiB], [M.s_tpB])
    V("vector", lambda e: e.tensor_tensor(out=fr[:], in0=fr[:], in1=tp[:], op=ALU.add), [M.s_frB, M.s_tpB], [M.s_frB])
    V("vector", lambda e: e.tensor_tensor(out=fr[:], in0=fr[:], in1=den[:], op=ALU.mult), [M.s_frB, M.s_denB], [M.s_frB])
    V("vector", lambda e: e.tensor_tensor(out=fi[:], in0=zi[:], in1=arc[:], op=ALU.mult), [M.s_ziB, M.s_arB], [M.s_fiB])
    V("vector", lambda e: e.tensor_tensor(out=tp[:], in0=zr[:], in1=aic[:], op=ALU.mult), [M.s_zrB, M.s_aiB], [M.s_tpB])
    V("vector", lambda e: e.tensor_tensor(out=fi[:], in0=fi[:], in1=tp[:], op=ALU.subtract), [M.s_fiB, M.s_tpB], [M.s_fiB])
    V("vector", lambda e: e.tensor_tensor(out=fi[:], in0=fi[:], in1=den[:], op=ALU.mult), [M.s_fiB, M.s_denB], [M.s_fiB])

    CTb = sb("s_CT", [128, NP_, 129]); STb = sb("s_ST", [128, NP_, 129]); XT = sb("s_XT", [128, NP_, 129])
    TT1 = sb("s_TT1", [128, NP_, 129]); TTi = sb("s_TTi", [128, NP_, 129], I32); RM = sb("s_RM", [128, NP_, SEG])
    for gp in range(NP_):
        V("vector", (lambda e, gp=gp: e.tensor_scalar(out=XT[:, gp, :], in0=iota, scalar1=th[:, gp:gp + 1], scalar2=None, op0=ALU.mult)), [c2B, M.s_thB], [M.s_XTB])
        V("gpsimd", (lambda e, gp=gp: e.tensor_scalar(out=RM[:, gp, :], in0=c1[:, 128:256], scalar1=mag[:, gp:gp + 1], scalar2=None, op0=ALU.mult)), [c1B, M.s_magB], [M.s_RMB])
    fl = lambda t_: t_[:].rearrange("p a b -> p (a b)")
    sincos(fl(XT), M.s_XTB, fl(STb), M.s_STB, fl(CTb), M.s_CTB, NP_ * 129, fl(TT1), M.s_TT1B, fl(TTi), M.s_TTiB)

    BnR = sb("s_BnR", [128, 4, 128]); BnI = sb("s_BnI", [128, 4, 128]); bbR = sb("s_bbR", [128, 4, 128]); bbI = sb("s_bbI", [128, 4, 128])
    LBr = sb("s_LBr", [128, NP_, 128]); LBi = sb("s_LBi", [128, NP_, 128]); tmpm = sb("s_tmpm", [128, 128])
    V("vector", lambda e: e.memset(BnR[:], 0.0), [], [M.s_BnRB])
    V("vector", lambda e: e.memset(BnI[:], 0.0), [], [M.s_BnIB])
    V("gpsimd", lambda e: e.memset(bbR[:], 0.0), [], [M.s_bbRB])
    V("gpsimd", lambda e: e.memset(bbI[:], 0.0), [], [M.s_bbIB])
    for g in range(32):
        ct, gl, e_ = g // 8, g % 8, g % 2
        P.dma("sync", (lambda e, g=g, ct=ct, gl=gl, e_=e_: e.dma_start(out=BnR[e_ * 64:(e_ + 1) * 64, ct, gl * 16:(gl + 1) * 16], in_=prm["b_re"][g])), writes=[M.s_BnRB])
        P.dma("sync", (lambda e, g=g, ct=ct, gl=gl, e_=e_: e.dma_start(out=BnI[e_ * 64:(e_ + 1) * 64, ct, gl * 16:(gl + 1) * 16], in_=prm["b_im"][g])), writes=[M.s_BnIB])
    P.barrier()
    for g in range(32):
        ct, gl, e_, gp = g // 8, g % 8, g % 2, g // 2
        rs = slice(e_ * 64, (e_ + 1) * 64)
        csl = slice(gl * 16, (gl + 1) * 16)

        def bb(ct=ct, rs=rs, csl=csl, gp=gp):
            V("vector", lambda e: e.tensor_scalar(out=bbR[rs, ct, csl], in0=BnR[rs, ct, csl], scalar1=fr[rs, gp:gp + 1], scalar2=None, op0=ALU.mult), [M.s_BnRB, M.s_frB], [M.s_bbRB])
            V("vector", lambda e: e.tensor_scalar(out=tmpm[rs, 0:16], in0=BnI[rs, ct, csl], scalar1=fi[rs, gp:gp + 1], scalar2=None, op0=ALU.mult), [M.s_BnIB, M.s_fiB], [M.s_tmpmB])
            V("vector", lambda e: e.tensor_tensor(out=bbR[rs, ct, csl], in0=bbR[rs, ct, csl], in1=tmpm[rs, 0:16], op=ALU.subtract), [M.s_bbRB, M.s_tmpmB], [M.s_bbRB])
            V("vector", lambda e: e.tensor_scalar(out=bbI[rs, ct, csl], in0=BnI[rs, ct, csl], scalar1=fr[rs, gp:gp + 1], scalar2=None, op0=ALU.mult), [M.s_BnIB, M.s_frB], [M.s_bbIB])
            V("vector", lambda e: e.tensor_scalar(out=tmpm[rs, 16:32], in0=BnR[rs, ct, csl], scalar1=fi[rs, gp:gp + 1], scalar2=None, op0=ALU.mult), [M.s_BnRB, M.s_fiB], [M.s_tmpmB])
            V("vector", lambda e: e.tensor_tensor(out=bbI[rs, ct, csl], in0=bbI[rs, ct, csl], in1=tmpm[rs, 16:32], op=ALU.add), [M.s_bbIB, M.s_tmpmB], [M.s_bbIB])
        bb()
    for gp in range(NP_):
        ct, q4 = gp // 4, gp % 4
        csl = slice(q4 * 32, (q4 + 1) * 32)

        def mk(src, srcB, dst, dstB, ct=ct, csl=csl, gp=gp, neg=False):
            V("gpsimd", lambda e: e.memset(tmpm[:], 0.0), [], [M.s_tmpmB])
            V("gpsimd", lambda e: e.tensor_copy(out=tmpm[:, csl], in_=src[:, ct, csl]), [srcB], [M.s_tmpmB])
            P.op("tensor", lambda e: e.matmul(psT[:, 0:128], lhsT=tmpm[:], rhs=ident, start=True, stop=True), reads=[M.s_tmpmB, c1B], writes=[psTB])
            V("vector", lambda e: e.tensor_copy(out=dst[:, gp, :], in_=psT[:, 0:128]), [psTB], [dstB])
        mk(bbR, M.s_bbRB, LBr, M.s_LBrB)
        mk(bbI, M.s_bbIB, LBi, M.s_LBiB)

    CnR = sb("s_CnR", [128, 4, 128]); CnI = sb("s_CnI", [128, 4, 128]); CT2r = sb("s_CT2r", [128, 4, 128]); CT2i = sb("s_CT2i", [128, 4, 128])
    LCr = sb("s_LCr", [128, NP_, 128]); LCi = sb("s_LCi", [128, NP_, 128])
    V("vector", lambda e: e.memset(CnR[:], 0.0), [], [M.s_CnRB])
    V("vector", lambda e: e.memset(CnI[:], 0.0), [], [M.s_CnIB])
    V("gpsimd", lambda e: e.memset(LCr[:], 0.0), [], [M.s_LCrB])
    V("gpsimd", lambda e: e.memset(LCi[:], 0.0), [], [M.s_LCiB])
    for g in range(32):
        ct, gl, e_ = g // 8, g % 8, g % 2
        P.dma("sync", (lambda e, g=g, ct=ct, gl=gl, e_=e_: e.dma_start(out=CnR[gl * 16:(gl + 1) * 16, ct, e_ * 64:(e_ + 1) * 64], in_=prm["c_re"][g])), writes=[M.s_CnRB])
        P.dma("sync", (lambda e, g=g, ct=ct, gl=gl, e_=e_: e.dma_start(out=CnI[gl * 16:(gl + 1) * 16, ct, e_ * 64:(e_ + 1) * 64], in_=prm["c_im"][g])), writes=[M.s_CnIB])
    P.barrier()
    for ct in range(4):
        def trc(src, srcB, dst, dstB, ct=ct, neg=False):
            P.op("tensor", lambda e: e.matmul(psT[:, 0:128], lhsT=src[:, ct, :], rhs=ident, start=True, stop=True), reads=[srcB, c1B], writes=[psTB])
            if neg:
                V("vector", lambda e: e.tensor_scalar(out=dst[:, ct, :], in0=psT[:, 0:128], scalar1=-1.0, scalar2=None, op0=ALU.mult), [psTB], [dstB])
            else:
                V("vector", lambda e: e.tensor_copy(out=dst[:, ct, :], in_=psT[:, 0:128]), [psTB], [dstB])
        trc(CnR, M.s_CnRB, CT2r, M.s_CT2rB)
        trc(CnI, M.s_CnIB, CT2i, M.s_CT2iB, neg=True)
    for gp in range(NP_):
        ct, q4 = gp // 4, gp % 4
        csl = slice(q4 * 32, (q4 + 1) * 32)
        V("gpsimd", (lambda e, gp=gp, ct=ct, csl=csl: e.tensor_copy(out=LCr[:, gp, csl], in_=CT2r[:, ct, csl])), [M.s_CT2rB], [M.s_LCrB])
        V("gpsimd", (lambda e, gp=gp, ct=ct, csl=csl: e.tensor_copy(out=LCi[:, gp, csl], in_=CT2i[:, ct, csl])), [M.s_CT2iB], [M.s_LCiB])
    dcol = sb("s_dcol", [128, 4])
    P.dma("sync", lambda e: e.dma_start(out=rows[0:4, :], in_=prm["d"].rearrange("(a b) h -> a (b h)", b=8)), writes=[rowsB])
    P.op("tensor", lambda e: e.matmul(psT[:, 0:4], lhsT=rows[0:4, :], rhs=c1[0:4, 0:4], start=True, stop=True), reads=[rowsB, c1B], writes=[psTB])
    V("vector", lambda e: e.tensor_copy(out=dcol[:], in_=psT[:, 0:4]), [psTB], [M.s_dcolB])

    uT = sb("s_uT", [128, 4, TT]); ys = sb("s_ys", [128, TT])
    cR = sb("s_cR", [128, NP_]); cI = sb("s_cI", [128, NP_])
    cRB = [Buf("cR%d" % i) for i in range(NP_)]; cIB = [Buf("cI%d" % i) for i in range(NP_)]
    P.op("vector", lambda e: e.memset(cR[:], 0.0), writes=cRB)
    P.op("vector", lambda e: e.memset(cI[:], 0.0), writes=cIB)
    nseg = TT // SEG

    class Lane:
        pass
    lanes = []
    for li in range(2):
        Ln_ = Lane()
        for nm in ("bR", "bI", "t1", "t2", "xR", "xI"):
            setattr(Ln_, nm, sb("s_%s_l%d" % (nm, li), [128, TT]))
            setattr(Ln_, nm + "B", getattr(M, "s_%s_l%dB" % (nm, li)))
        Ln_.cq = sb("s_cq_l%d" % li, [128, 4]); Ln_.cqB = getattr(M, "s_cq_l%dB" % li)
        if li == 0:
            Ln_.psA, Ln_.psAB, Ln_.psB, Ln_.psBB = psA, psAB, psB, psBB
        else:
            Ln_.psA = ps("s_psA2", [128, 512]); Ln_.psAB = M.s_psA2B
            Ln_.psB = ps("s_psB2", [128, 512]); Ln_.psBB = M.s_psB2B
        lanes.append(Ln_)

    def pair_gen(t, gp, L):
        ct = gp // 4
        tabC = CTb[:, gp, 0:SEG].unsqueeze(1).broadcast_to([128, nseg, SEG])
        tabS = STb[:, gp, 0:SEG].unsqueeze(1).broadcast_to([128, nseg, SEG])
        v3 = lambda a: a[:].rearrange("p (s c) -> p s c", s=nseg)
        bR, bI, t1, t2, xR, xI, cq = L.bR, L.bI, L.t1, L.t2, L.xR, L.xI, L.cq
        P.op("tensor", lambda e: e.matmul(L.psA[:], lhsT=LBr[:, gp, :], rhs=uT[:, ct, :], start=True, stop=True), reads=[M.s_LBrB, M.s_uTB], writes=[L.psAB])
        P.op("tensor", lambda e: e.matmul(L.psB[:], lhsT=LBi[:, gp, :], rhs=uT[:, ct, :], start=True, stop=True), reads=[M.s_LBiB, M.s_uTB], writes=[L.psBB])
        pA3 = L.psA[:].rearrange("p (s c) -> p s c", s=nseg)
        pB3 = L.psB[:].rearrange("p (s c) -> p s c", s=nseg)
        yield
        V("vector", lambda e: e.tensor_tensor(out=v3(bR), in0=pA3, in1=tabC, op=ALU.mult), [L.psAB, M.s_CTB], [L.bRB])
        V("vector", lambda e: e.tensor_tensor(out=v3(t1), in0=pB3, in1=tabS, op=ALU.mult), [L.psBB, M.s_STB], [L.t1B])
        yield
        V("gpsimd", lambda e: e.tensor_tensor(out=bR[:], in0=bR[:], in1=t1[:], op=ALU.add), [L.bRB, L.t1B], [L.bRB])
        V("vector", lambda e: e.tensor_tensor(out=v3(bI), in0=pB3, in1=tabC, op=ALU.mult), [L.psBB, M.s_CTB], [L.bIB])
        V("vector", lambda e: e.tensor_tensor(out=v3(t2), in0=pA3, in1=tabS, op=ALU.mult), [L.psAB, M.s_STB], [L.t2B])
        yield
        V("gpsimd", lambda e: e.tensor_tensor(out=bI[:], in0=bI[:], in1=t2[:], op=ALU.subtract), [L.bIB, L.t2B], [L.bIB])
        yield
        c128 = CTb[:, gp, 128:129]
        s128 = STb[:, gp, 128:129]
        for s_ in range(nseg):
            sc = slice(s_ * SEG, (s_ + 1) * SEG)
            lr = xR[:, sc.stop - 1:sc.stop]
            li_ = xI[:, sc.stop - 1:sc.stop]

            def scans(sc=sc):
                V("vector", lambda e: e.tensor_tensor_scan(out=xR[:, sc], data0=RM[:, gp, :], data1=bR[:, sc], initial=cR[:, gp:gp + 1], op0=ALU.mult, op1=ALU.add),
                  [M.s_RMB, L.bRB, cRB[gp]], [L.xRB])
                V("vector", lambda e: e.tensor_tensor_scan(out=xI[:, sc], data0=RM[:, gp, :], data1=bI[:, sc], initial=cI[:, gp:gp + 1], op0=ALU.mult, op1=ALU.add),
                  [M.s_RMB, L.bIB, cIB[gp]], [L.xIB])
            scans()
            yield

            def carry1(lr=lr, li_=li_):
                V("vector", lambda e: e.tensor_tensor(out=cq[:, 0:1], in0=li_, in1=s128, op=ALU.mult), [L.xIB, M.s_STB], [L.cqB])
                V("vector", lambda e: e.tensor_tensor(out=cq[:, 1:2], in0=li_, in1=c128, op=ALU.mult), [L.xIB, M.s_CTB], [L.cqB])
            carry1()
            yield

            def carry2(lr=lr):
                V("vector", lambda e: e.scalar_tensor_tensor(out=cR[:, gp:gp + 1], in0=lr, scalar=c128, in1=cq[:, 0:1], op0=ALU.mult, op1=ALU.subtract),
                  [L.xRB, M.s_CTB, L.cqB], [cRB[gp]])
                V("vector", lambda e: e.scalar_tensor_tensor(out=cI[:, gp:gp + 1], in0=lr, scalar=s128, in1=cq[:, 1:2], op0=ALU.mult, op1=ALU.add),
                  [L.xRB, M.s_STB, L.cqB], [cIB[gp]])
            carry2()
            yield
        V("gpsimd", lambda e: e.tensor_tensor(out=v3(t1), in0=v3(xI), in1=tabS, op=ALU.mult), [L.xIB, M.s_STB], [L.t1B])
        V("gpsimd", lambda e: e.tensor_tensor(out=v3(t2), in0=v3(xR), in1=tabS, op=ALU.mult), [L.xRB, M.s_STB], [L.t2B])
        yield
        V("vector", lambda e: e.tensor_tensor(out=v3(xR), in0=v3(xR), in1=tabC, op=ALU.mult), [L.xRB, M.s_CTB], [L.xRB])
        V("vector", lambda e: e.tensor_tensor(out=v3(xI), in0=v3(xI), in1=tabC, op=ALU.mult), [L.xIB, M.s_CTB], [L.xIB])
        yield
        V("gpsimd", lambda e: e.tensor_tensor(out=xR[:], in0=xR[:], in1=t1[:], op=ALU.subtract), [L.xRB, L.t1B], [L.xRB])
        V("gpsimd", lambda e: e.tensor_tensor(out=xI[:], in0=xI[:], in1=t2[:], op=ALU.add), [L.xIB, L.t2B], [L.xIB])
        yield
        q4 = gp % 4
        P.op("tensor", lambda e: e.matmul(psY[:], lhsT=LCr[:, gp, :], rhs=xR[:], start=(q4 == 0), stop=False), reads=[M.s_LCrB, L.xRB], writes=[psYB], inc=False)
        P.op("tensor", lambda e: e.matmul(psY[:], lhsT=LCi[:, gp, :], rhs=xI[:], start=False, stop=(q4 == 3)), reads=[M.s_LCiB, L.xIB], writes=[psYB])

    for t in range(ntile):
        t0 = t * TT
        for ct in range(4):
            P.dma("sync", (lambda e, ct=ct, t0=t0: e.dma_start(out=uT[:, ct, :], in_=PT[2056 + ct * 128:2056 + (ct + 1) * 128, t0:t0 + TT])), reads=[PTB[t]], writes=[M.s_uTB])
        for gp0 in range(0, NP_, 2):
            gens = [pair_gen(t, gp0, lanes[0]), pair_gen(t, gp0 + 1, lanes[1])]
            alive = [True, True]
            while any(alive):
                for i, gen in enumerate(gens):
                    if alive[i]:
                        try:
                            next(gen)
                        except StopIteration:
                            alive[i] = False
            if gp0 % 4 == 2:
                ct = gp0 // 4

                def fin(ct=ct, t0=t0):
                    V("vector", lambda e: e.scalar_tensor_tensor(out=ys[:], in0=uT[:, ct, :], scalar=dcol[:, ct:ct + 1], in1=psY[:], op0=ALU.mult, op1=ALU.add),
                      [M.s_uTB, M.s_dcolB, psYB], [M.s_ysB])
                    V("scalar", lambda e: e.activation(out=ys[:], in_=ys[:], func=AF.Gelu), [M.s_ysB], [M.s_ysB])
                    P.dma("gpsimd", lambda e: e.dma_start(out=YT[512 + ct * 128:512 + (ct + 1) * 128, t0:t0 + TT], in_=ys[:]), reads=[M.s_ysB], writes=[YTB[t]])
                fin()


def mixpost_phase(R, YT, X_in, X_out, w_glu, w_out, NT):
    P = R.P
    ntile = NT // TT
    wi, wo, sB = R.wi[0], R.wo[0], R.slabB[0]

    def ld(src, dst, w_):
        si = R.stg_i
        R.stg_i = (si + 1) % 3
        st, stB = R.stg[si], R.stgB[si]
        P.dma("sync", lambda e: e.dma_start(out=st[:, 0:w_], in_=src), writes=[stB])
        P.op("gpsimd", lambda e: e.tensor_copy(out=dst, in_=st[:, 0:w_]), reads=[stB], writes=[sB])
    for kc in range(4):
        ld(w_glu[kc * 128:(kc + 1) * 128, 0:512], wi[:, kc, 0:512], 512)
    for kc in range(8):
        ld(w_out[kc * 128:(kc + 1) * 128, 0:1024], wo[:, kc, :], 1024)
    YB = [Buf("Yo%d" % i) for i in range(ntile * 4)]

    def load_y(t, hb):
        hT = R.hT[hb]
        for kc in range(8):
            def one(kc=kc):
                si = R.stg_i
                R.stg_i = (si + 1) % 3
                st, stB = R.stg[si], R.stgB[si]
                P.dma("sync", lambda e: e.dma_start(out=st[:, 0:TT], in_=YT[kc * 128:(kc + 1) * 128, t * TT:(t + 1) * TT]), writes=[stB])
                P.op("gpsimd" if kc % 2 else "vector", lambda e: e.tensor_copy(out=hT[:, kc, :], in_=st[:, 0:TT]), reads=[stB], writes=[R.hTB[hb][0]])
            one()
    load_y(0, 0)
    gi = 0
    for t in range(ntile):
        hb = t % 2
        hT = R.hT[hb]
        hB = R.hTB[hb][0]
        for j in range(4):
            def glu(j=j, gi=gi, hT=hT, hB=hB):
                pp, ppB = R.pB[gi % 4], R.pBB[gi % 4]
                sg, sgB = R.sg[gi % 2], R.sgB[gi % 2]

                def mm(kc):
                    P.op("tensor", lambda e: e.matmul(pp[:], lhsT=wi[:, kc, j * 128:(j + 1) * 128], rhs=hT[:, 4 + kc, :], start=(kc == 0), stop=(kc == 3)),
                         reads=[sB, hB], writes=[ppB], inc=(kc == 3))
                for kc in range(4):
                    mm(kc)
                P.op("scalar", lambda e: e.activation(out=sg[:], in_=pp[:], func=AF.Sigmoid), reads=[ppB], writes=[sgB])
                P.op("vector", lambda e: e.tensor_tensor(out=R.aT[:, j, :], in0=hT[:, 4 + j, :], in1=sg[:], op=ALU.mult), reads=[hB, sgB], writes=[R.aTB[j]])
            glu()
            gi += 1
        if t + 1 < ntile:
            load_y(t + 1, 1 - hb)
        lhs = [(hT, kc, hB) for kc in range(4)] + [(R.aT, kc, R.aTB[kc]) for kc in range(4)]
        for s in range(4):
            stage_c_sub(R, wo, sB, 8, X_in, X_out, None, YB[t * 4 + s], t, s, 0, True, lhs=lhs)
    return YB


DEPTH = 2
NT_CORE = 8192


def mod_phase(R, c_row, w_mod_l, b_mod_l, MOD, sbm):
    P = R.P
    cT, cTB, brow, browB, mrow, mrowB = sbm
    P.dma("sync", lambda e: e.dma_start(out=R.vrows[:, 0, :], in_=c_row.rearrange("(kc p) -> kc p", p=128)), writes=[R.vrowsB])
    pc = R.pC[0]
    P.op("tensor", lambda e: e.matmul(pc[:, 0:8], lhsT=R.vrows[:, 0, :], rhs=R.identf[0:8, 0:8], start=True, stop=True),
         reads=[R.vrowsB, R.identfB], writes=[R.pCB])
    P.op("scalar", lambda e: e.activation(out=cT[:], in_=pc[:, 0:8], func=AF.Silu), reads=[R.pCB], writes=[cTB])
    pm = R.pC[1]
    dummy = Buf("modw")

    def tile(n):
        def kstep(kc):
            si = R.stg_i
            R.stg_i = (si + 1) % 3
            st, stB = R.stg[si], R.stgB[si]
            P.dma("sync", lambda e: e.dma_start(out=st[:, 0:512], in_=w_mod_l[kc * 128:(kc + 1) * 128, n * 512:(n + 1) * 512]), writes=[stB])
            P.op("tensor", lambda e: e.matmul(pm[0:1, :], lhsT=cT[:, kc:kc + 1], rhs=st[:, 0:512], start=(kc == 0), stop=(kc == NKC - 1)),
                 reads=[stB, cTB], writes=[R.pCB])
        for kc in range(NKC):
            kstep(kc)
        P.dma("sync", lambda e: e.dma_start(out=brow[:], in_=b_mod_l[n * 512:(n + 1) * 512].rearrange("(o f) -> o f", o=1)), writes=[browB])
        P.op("vector", lambda e: e.tensor_tensor(out=mrow[:], in0=pm[0:1, :], in1=brow[:], op=ALU.add),
             reads=[R.pCB, browB], writes=[mrowB])
        P.dma("gpsimd", lambda e: e.dma_start(out=MOD[n * 512:(n + 1) * 512].rearrange("(o f) -> o f", o=1), in_=mrow[:]),
              reads=[mrowB], writes=[dummy])
    for n in range(9 * D // 512):
        tile(n)


WNAMES = ["w_mod", "b_mod", "ff1_norm_pre", "ff1_norm_post", "ff1_w_in", "ff1_w_out", "mix_norm_pre", "mix_norm_post", "mix_w_in",
          "conv_w", "a_log", "dt_bias", "gdn_norm_w", "s5_a_re", "s5_a_im", "s5_log_dt", "s5_b_re", "s5_b_im", "s5_c_re", "s5_c_im",
          "s5_d", "s5_w_glu", "mix_w_out", "ff2_norm_pre", "ff2_norm_post", "ff2_w_in", "ff2_w_out"]
WSHAPES = {"w_mod": [D, 9 * D], "b_mod": [9 * D], "ff1_norm_pre": [D], "ff1_norm_post": [D], "ff1_w_in": [D, 2 * DFF], "ff1_w_out": [DFF, D],
           "mix_norm_pre": [D], "mix_norm_post": [D], "mix_w_in": [D, 2568], "conv_w": [4, 1536], "a_log": [4], "dt_bias": [4], "gdn_norm_w": [128],
           "s5_a_re": [32, 64], "s5_a_im": [32, 64], "s5_log_dt": [32], "s5_b_re": [32, 64, 16], "s5_b_im": [32, 64, 16],
           "s5_c_re": [32, 16, 64], "s5_c_im": [32, 16, 64], "s5_d": [32, 16], "s5_w_glu": [512, 512], "mix_w_out": [D, D],
           "ff2_norm_pre": [D], "ff2_norm_post": [D], "ff2_w_in": [D, 2 * DFF], "ff2_w_out": [DFF, D]}


def build_nc(NT, depth=DEPTH):
    nc = bass.Bass("TRN2", target_bir_lowering=False)
    dr = lambda n, s, k="ExternalInput", dt=F32: nc.dram_tensor(n, s, dt, kind=k).ap()
    x = dr("x", [NT, D]); c_row = dr("c_row", [D])
    W = {n: dr(n, [depth] + WSHAPES[n]) for n in WNAMES}
    ident = dr("ident", [128, 128]); cst = dr("cst", [128, NCST]); cst2 = dr("cst2", [128, NCST2])
    y = dr("y", [NT, D], "ExternalOutput")
    scr = lambda n, s: nc.dram_tensor(n, s, F32).ap()
    Xs = [scr("xs0", [NT, D]), scr("xs1", [NT, D])]
    yacc = scr("yacc", [NT, D]); PT = scr("ptscr", [2568, NT]); YT = scr("ytscr", [1024, NT])
    MOD = scr("modscr", [depth, 9 * D])
    ntile = NT // TT
    dB = lambda: [Buf() for _ in range(ntile)]
    with ExitStack() as stack:
        P = Prog(nc, stack)
        phase = [0]

        def ffn_like(fn):
            phase[0] += 1
            with ExitStack() as ph:
                R = FFNRes(nc, ph, P, tag="_p%d" % phase[0])
                R.load_ident(ident)
                fn(R, ph)
            P.barrier()

        def mix_like(fn):
            phase[0] += 1
            with ExitStack() as ph:
                M = MixRes(nc, ph, P, tag="_p%d" % phase[0])
                fn(M)
            P.barrier()

        def do_mod(R, ph):
            sb = lambda name, shape, dt: ph.enter_context(nc.sbuf_tensor(name, shape, dt))
            sbm = (sb("cT", [128, NKC], F32), Buf("cT"), sb("brow", [1, 512], F32), Buf("brow"), sb("mrow", [1, 512], F32), Buf("mrow"))
            for l in range(depth):
                mod_phase(R, c_row, W["w_mod"][l], W["b_mod"][l], MOD[l], sbm)
        ffn_like(do_mod)
        cur = x
        for l in range(depth):
            last_layer = l == depth - 1
            nxt = Xs[0]

            def f1(R, ph, l=l, cur=cur, nxt=nxt):
                prep_vectors(R, W["ff1_norm_pre"][l], W["ff1_norm_post"][l], MOD[l], 0, 0.5)
                ffn_phase(R, cur, nxt, yacc, W["ff1_w_in"][l], W["ff1_w_out"][l], NT)
            ffn_like(f1)
            cur = nxt

            def m1(R, ph, l=l, cur=cur):
                prep_vectors(R, W["mix_norm_pre"][l], W["mix_norm_post"][l], MOD[l], 3, 1.0)
                proj_phase(R, cur, W["mix_w_in"][l], PT, NT, dB())
            ffn_like(m1)

            def g(M, l=l):
                gdn_phase(M, PT, YT, W["conv_w"][l], W["a_log"][l], W["dt_bias"][l], W["gdn_norm_w"][l], cst, NT, dB(), dB())
            mix_like(g)

            def s5(M, l=l):
                prm = {k: W["s5_" + k][l] for k in ("a_re", "a_im", "log_dt", "b_re", "b_im", "c_re", "c_im", "d")}
                s5_phase(M, PT, YT, prm, cst, cst2, NT, dB(), dB())
            mix_like(s5)
            nxt = Xs[1]

            def m3(R, ph, l=l, cur=cur, nxt=nxt):
                prep_vectors(R, W["mix_norm_pre"][l], W["mix_norm_post"][l], MOD[l], 3, 1.0)
                mixpost_phase(R, YT, cur, nxt, W["s5_w_glu"][l], W["mix_w_out"][l], NT)
            ffn_like(m3)
            cur = nxt
            nxt = y if last_layer else Xs[0]
            outB = []

            def f2(R, ph, l=l, cur=cur, nxt=nxt):
                prep_vectors(R, W["ff2_norm_pre"][l], W["ff2_norm_post"][l], MOD[l], 6, 0.5)
                outB.extend(ffn_phase(R, cur, nxt, yacc, W["ff2_w_in"][l], W["ff2_w_out"][l], NT))
            ffn_like(f2)
            cur = nxt
        P.final_wait("gpsimd", outB)
        P.emit()
    return nc


def kernel(**inputs):
    x = np.ascontiguousarray(inputs["x"], dtype=np.float32)
    c = np.ascontiguousarray(inputs["c"], dtype=np.float32)
    B, L, _ = x.shape
    common = {n: np.ascontiguousarray(inputs[n], dtype=np.float32) for n in WNAMES}
    common["ident"] = np.eye(128, dtype=np.float32)
    common["cst"] = make_consts()
    common["cst2"] = make_consts2()
    n_cores = B
    in_maps = []
    for core in range(n_cores):
        m = dict(common)
        m["x"] = np.ascontiguousarray(x[core])
        m["c_row"] = np.ascontiguousarray(c[core])
        in_maps.append(m)
    nc = build_nc(L)
    res = run_bass_kernel_spmd(nc, in_maps, core_ids=list(range(n_cores)))
    out = np.empty((B, L, D), dtype=np.float32)
    for core in range(n_cores):
        out[core] = res.results[core]["y"]
    return out
```

```python
import numpy as np
from contextlib import ExitStack
import concourse.bass as bass
import concourse.mybir as mybir
from concourse.bass_utils import run_bass_kernel_spmd

F32 = mybir.dt.float32
BF16 = mybir.dt.bfloat16
I32 = mybir.dt.int32
AF = mybir.ActivationFunctionType
ALU = mybir.AluOpType

ENGS = ("tensor", "vector", "scalar", "gpsimd", "sync")


class Buf:
    __slots__ = ("w", "r", "name")

    def __init__(self, name=""):
        self.w = []
        self.r = []
        self.name = name


class Prog:
    NDMA = 20

    def __init__(self, nc, stack):
        self.nc = nc
        self.q = {e: [] for e in ENGS}
        self.sems = {}
        self.cnt = {}
        for e in ("tensor", "vector", "scalar", "gpsimd"):
            self.sems[e] = stack.enter_context(nc.semaphore("pg_" + e))
            self.cnt[e] = 0
        self.dma_rr = {}
        for qn in ("sync", "gpsimd", "scalar"):
            self.dma_rr[qn] = 0
            for i in range(self.NDMA):
                k = ("dma", qn, i)
                self.sems[k] = stack.enter_context(nc.semaphore("pd_%s_%d" % (qn, i)))
                self.cnt[k] = 0
        self.seen = {e: {} for e in ENGS}
        self.pending_reads = {e: [] for e in ENGS}

    def _waits(self, eng, deps):
        out = []
        for tok in deps:
            if tok is None:
                continue
            key, val = tok
            if key == "tensor" and eng == "tensor":
                continue
            if self.seen[eng].get(key, 0) >= val:
                continue
            self.seen[eng][key] = val
            out.append((key, val))
        return out

    def op(self, eng, fn, reads=(), writes=(), inc=True):
        deps = []
        for b in reads:
            deps.extend(b.w)
        for b in writes:
            deps.extend(b.w)
            deps.extend(b.r)
        waits = self._waits(eng, deps)
        if inc:
            self.cnt[eng] += 1
            tok = (eng, self.cnt[eng])
        else:
            tok = (eng, self.cnt[eng] + 1)
        sem = self.sems[eng]
        self.q[eng].append((waits, fn, sem if inc else None, 1))
        for b in reads:
            b.r.append(tok)
        for b in writes:
            b.w = [tok]
            b.r = []
        return tok

    def dma(self, qn, fn, reads=(), writes=()):
        deps = []
        for b in reads:
            deps.extend(b.w)
        for b in writes:
            deps.extend(b.w)
            deps.extend(b.r)
        i = self.dma_rr[qn]
        self.dma_rr[qn] = (i + 1) % self.NDMA
        k = ("dma", qn, i)
        if self.cnt[k] > 0:
            deps.append((k, self.cnt[k]))
        waits = self._waits(qn, deps)
        self.cnt[k] += 16
        tok = (k, self.cnt[k])
        self.q[qn].append((waits, fn, self.sems[k], 16))
        for b in reads:
            b.r.append(tok)
        for b in writes:
            b.w = [tok]
            b.r = []
        return tok

    def barrier(self):
        toks = [(k, v) for k, v in self.cnt.items() if v > 0]
        for e in ENGS:
            waits = self._waits(e, toks)
            if waits:
                self.q[e].append((waits, None, None, 0))

    def final_wait(self, eng, bufs):
        deps = []
        for b in bufs:
            deps.extend(b.w)
        waits = self._waits(eng, deps)
        self.q[eng].append((waits, None, None, 0))

    def emit(self):
        nc = self.nc
        sems = self.sems
        with nc.Block() as block:
            def mk(name):
                def body(e):
                    for waits, fn, sem, inc in self.q[name]:
                        for key, val in waits:
                            e.wait_ge(sems[key], val)
                        if fn is not None:
                            ins = fn(e)
                            if sem is not None:
                                ins.then_inc(sem, inc)
                return body
            block.sync(mk("sync"))
            block.tensor(mk("tensor"))
            block.vector(mk("vector"))
            block.scalar(mk("scalar"))
            block.gpsimd(mk("gpsimd"))


def check_deadlock(P):
    pos = {e: 0 for e in ENGS}
    val = {}
    key_of = {id(s): k for k, s in P.sems.items()}
    progress = True
    while progress:
        progress = False
        for e in ENGS:
            q = P.q[e]
            while pos[e] < len(q):
                waits, fn, sem, inc = q[pos[e]]
                if all(val.get(k, 0) >= v for k, v in waits):
                    if sem is not None:
                        k = key_of[id(sem)]
                        val[k] = val.get(k, 0) + inc
                    pos[e] += 1
                    progress = True
                else:
                    break
    ok = all(pos[e] == len(P.q[e]) for e in ENGS)
    if not ok:
        for e in ENGS:
            if pos[e] < len(P.q[e]):
                waits = P.q[e][pos[e]][0]
                print("STUCK", e, pos[e], len(P.q[e]), [(k, v, val.get(k, 0)) for k, v in waits if val.get(k, 0) < v])
    return ok


D = 1024
DFF = 2816
NKC = 8
EPS = 1e-6
SLABS = [(0, 8), (8, 7), (15, 7)]
SLAB_MAX = 8
TT = 512


class FFNRes:
    def __init__(self, nc, stack, P, tag=""):
        self.nc, self.P = nc, P
        sb = lambda name, shape, dt: stack.enter_context(nc.sbuf_tensor(name + tag, shape, dt))
        ps = lambda name, shape, dt: stack.enter_context(nc.psum_tensor(name + tag, shape, dt))
        self.wi = [sb("wi%d" % i, [128, NKC, SLAB_MAX * 256], BF16) for i in range(2)]
        self.wo = [sb("wo%d" % i, [128, SLAB_MAX, D], BF16) for i in range(2)]
        self.slabB = [Buf("slab%d" % i) for i in range(2)]
        self.stg = [sb("stg%d" % i, [128, 1024], F32) for i in range(3)]
        self.stgB = [Buf("stg%d" % i) for i in range(3)]
        self.stg_i = 0
        self.hT = [sb("hT%d" % i, [128, NKC, TT], BF16) for i in range(2)]
        self.hTB = [[Buf("hT%d_%d" % (i, s)) for s in range(4)] for i in range(2)]
        self.aT = sb("aT", [128, SLAB_MAX, TT], BF16)
        self.aTB = [Buf("aT%d" % j) for j in range(SLAB_MAX)]
        self.xa = [sb("xa%d" % i, [128, D], F32) for i in range(2)]
        self.xaB = [Buf("xa%d" % i) for i in range(2)]
        self.xn = [sb("xn%d" % i, [128, D], BF16) for i in range(2)]
        self.xnB = [Buf("xn%d" % i) for i in range(2)]
        self.junk = sb("junk", [128, D], BF16)
        self.junkB = Buf("junk")
        self.ss = [sb("ss%d" % i, [128, 2], F32) for i in range(4)]
        self.ssB = [Buf("ss%d" % i) for i in range(4)]
        self.ss_i = 0
        self.sg = [sb("sg%d" % i, [128, TT], F32) for i in range(2)]
        self.sgB = [Buf("sg%d" % i) for i in range(2)]
        self.yb = [sb("yb%d" % i, [128, D], F32) for i in range(2)]
        self.ybB = [Buf("yb%d" % i) for i in range(2)]
        self.xr = [sb("xr%d" % i, [128, D], F32) for i in range(2)]
        self.xrB = [Buf("xr%d" % i) for i in range(2)]
        self.crow = sb("crow", [128, D], F32)
        self.crowB = Buf("crow")
        self.ctmp = sb("ctmp", [128, D], F32)
        self.ctmpB = Buf("ctmp")
        self.acol = sb("acol", [128, NKC], F32)
        self.bcol = sb("bcol", [128, NKC], F32)
        self.tcol = sb("tcol", [128, NKC], F32)
        self.colB = Buf("col")
        self.vrows = sb("vrows", [8, 3, 128], F32)
        self.vrowsB = Buf("vrows")
        self.identf = sb("identf", [128, 128], F32)
        self.identfB = Buf("identf")
        self.ident = sb("ident_sb", [128, 128], BF16)
        self.epsc = sb("epsc", [128, 1], F32)
        self.epscB = Buf("epsc")
        P.op("vector", lambda e: e.memset(self.epsc[:], D * EPS), writes=[self.epscB])
        self.identB = Buf("ident")
        self.pB = [ps("pB%d" % i, [128, TT], F32) for i in range(4)]
        self.pBB = [Buf("pB%d" % i) for i in range(4)]
        self.pC = [ps("pC%d" % i, [128, TT], F32) for i in range(2)]
        self.pCB = Buf("pC")
        self.pT = [ps("pT%d" % i, [128, NKC, 128], BF16) for i in range(2)]
        self.pTB = [Buf("pT%d" % i) for i in range(2)]

    def load_ident(self, ident_dram):
        P = self.P
        st = self.stg[0]
        P.dma("sync", lambda e: e.dma_start(out=st[:, 0:128], in_=ident_dram), writes=[self.stgB[0]])
        P.dma("sync", lambda e: e.dma_start(out=self.identf[:], in_=ident_dram), writes=[self.identfB])
        P.op("vector", lambda e: e.tensor_copy(out=self.ident[:], in_=st[:, 0:128]),
             reads=[self.stgB[0]], writes=[self.identB])


def load_slab_gen(R, w_in, w_out, slab, buf, dff=DFF, gated=True):
    P = R.P
    j0, n = slab
    wi, wo, sB = R.wi[buf], R.wo[buf], R.slabB[buf]
    pieces = []
    ncol = n * 128
    for kc in range(NKC):
        for part in range(2 if gated else 1):
            c0 = part * dff + j0 * 128
            done = 0
            while done < ncol:
                w = min(1024, ncol - done)
                pieces.append(("in", kc, part * ncol + done, c0 + done, w))
                done += w
    if w_out is not None:
        for j in range(n):
            pieces.append(("out", j, 0, (j0 + j) * 128, 1024))
    for kind, a, dst0, src0, w in pieces:
        si = R.stg_i
        R.stg_i = (si + 1) % 3
        st, stB = R.stg[si], R.stgB[si]
        if kind == "in":
            src = w_in[a * 128:(a + 1) * 128, src0:src0 + w]
            dst = wi[:, a, dst0:dst0 + w]
        else:
            src = w_out[src0:src0 + 128, 0:1024]
            dst = wo[:, a, :]
        P.dma("sync", (lambda e, st=st, src=src, w=w: e.dma_start(out=st[:, 0:w], in_=src)), writes=[stB])
        P.op("gpsimd", (lambda e, st=st, dst=dst, w=w: e.tensor_copy(out=dst, in_=st[:, 0:w])),
             reads=[stB], writes=[sB])
        yield


def load_slab(R, w_in, w_out, slab, buf, dff=DFF, gated=True):
    for _ in load_slab_gen(R, w_in, w_out, slab, buf, dff, gated):
        pass


def prep_vectors(R, w_pre, w_post, mod, ioff, gate_scale, modB=None):
    P = R.P
    colv = lambda v, off: v[off:off + D].rearrange("(kc p) -> p kc", p=128)
    rowv = lambda v, off: v[off:off + D].partition_broadcast(128)
    rows = lambda v, off: v[off:off + D].rearrange("(kc p) -> kc p", p=128)
    P.dma("sync", lambda e: e.dma_start(out=R.vrows[:, 0, :], in_=rows(w_pre, 0)), writes=[R.vrowsB])
    P.dma("sync", lambda e: e.dma_start(out=R.vrows[:, 1, :], in_=rows(mod, ioff * D)), reads=(list(modB) if modB else []), writes=[R.vrowsB])
    P.dma("sync", lambda e: e.dma_start(out=R.vrows[:, 2, :], in_=rows(mod, (ioff + 1) * D)), reads=(list(modB) if modB else []), writes=[R.vrowsB])
    pc = R.pC[0]

    def tr(i):
        P.op("tensor", lambda e: e.matmul(pc[:, i * 8:(i + 1) * 8], lhsT=R.vrows[:, i, :], rhs=R.identf[0:8, 0:8], start=True, stop=True),
             reads=[R.vrowsB, R.identfB], writes=[R.pCB], inc=(i == 2))
    for i in range(3):
        tr(i)
    P.op("vector", lambda e: e.tensor_copy(out=R.acol[:], in_=pc[:, 0:8]), reads=[R.pCB], writes=[R.colB])
    P.op("vector", lambda e: e.tensor_copy(out=R.bcol[:], in_=pc[:, 8:16]), reads=[R.pCB], writes=[R.colB])
    P.op("vector", lambda e: e.tensor_copy(out=R.tcol[:], in_=pc[:, 16:24]), reads=[R.pCB], writes=[R.colB])
    P.op("vector", lambda e: e.tensor_scalar(out=R.tcol[:], in0=R.tcol[:], scalar1=1.0, scalar2=32.0, op0=ALU.add, op1=ALU.mult),
         reads=[R.colB], writes=[R.colB])
    P.op("vector", lambda e: e.tensor_tensor(out=R.acol[:], in0=R.acol[:], in1=R.tcol[:], op=ALU.mult),
         reads=[R.colB], writes=[R.colB])
    P.dma("sync", lambda e: e.dma_start(out=R.crow[:], in_=rowv(w_post, 0)), writes=[R.crowB])
    P.dma("sync", lambda e: e.dma_start(out=R.ctmp[:], in_=rowv(mod, (ioff + 2) * D)), reads=(list(modB) if modB else []), writes=[R.ctmpB])
    P.op("vector", lambda e: e.scalar_tensor_tensor(out=R.crow[:], in0=R.crow[:], scalar=32.0 * gate_scale, in1=R.ctmp[:],
                                                    op0=ALU.mult, op1=ALU.mult),
         reads=[R.crowB, R.ctmpB], writes=[R.crowB])


def stage_a_sub(R, X_in, t, s, hbuf, part):
    P = R.P
    i = (t * 4 + s) % 2
    xa, xaB, xn, xnB = R.xa[i], R.xaB[i], R.xn[i], R.xnB[i]
    if part == 0:
        r0 = t * TT + s * 128
        P.dma("sync", lambda e: e.dma_start(out=xa[:], in_=X_in[r0:r0 + 128, :]), writes=[xaB])
        k = R.ss_i
        R.ss_i = (k + 1) % 4
        ss, ssB = R.ss[k], R.ssB[k]
        P.op("scalar", lambda e: e.activation(out=R.junk[:], in_=xa[:], func=AF.Square, accum_out=ss[:, 0:1]),
             reads=[xaB], writes=[R.junkB, ssB])
        P.op("scalar", lambda e: e.activation(out=ss[:, 1:2], in_=ss[:, 0:1], func=AF.Sqrt, bias=R.epsc[:, 0:1], scale=1.0),
             reads=[ssB, R.epscB], writes=[ssB])
        P.op("vector", lambda e: e.reciprocal(out=ss[:, 1:2], in_=ss[:, 1:2]), reads=[ssB], writes=[ssB])
        P.op("scalar", lambda e: e.activation(out=xn[:], in_=xa[:], func=AF.Copy, scale=ss[:, 1:2]),
             reads=[xaB, ssB], writes=[xnB])
    else:
        pt, ptB = R.pT[i], R.pTB[i]
        for kc in range(NKC):
            P.op("tensor", (lambda e, kc=kc: e.transpose(out=pt[:, kc, :], in_=xn[:, kc * 128:(kc + 1) * 128], identity=R.ident[:])),
                 reads=[xnB, R.identB], writes=[ptB], inc=(kc == NKC - 1))
        hT, hB = R.hT[hbuf], R.hTB[hbuf][s]
        for kc in range(NKC):
            eng = "vector" if kc % 2 == 0 else "gpsimd"
            if eng == "gpsimd":
                eng = "vector"
            P.op(eng, (lambda e, kc=kc: e.tensor_scalar(out=hT[:, kc, s * 128:(s + 1) * 128], in0=pt[:, kc, :],
                                                       scalar1=R.acol[:, kc:kc + 1], scalar2=R.bcol[:, kc:kc + 1],
                                                       op0=ALU.mult, op1=ALU.add)),
                 reads=[ptB, R.colB], writes=[hB])


def stage_b_group(R, wi, sB, hT, hTBl, j, nch, gidx, aj=None):
    P = R.P
    if aj is None:
        aj = j
    k = gidx % 2
    pg, pu, pgB, puB = R.pB[2 * k], R.pB[2 * k + 1], R.pBB[2 * k], R.pBB[2 * k + 1]
    sg, sgB = R.sg[k], R.sgB[k]

    def mm(pp, ppB, c0, kc):
        P.op("tensor", lambda e: e.matmul(pp[:], lhsT=wi[:, kc, c0:c0 + 128], rhs=hT[:, kc, :],
                                          start=(kc == 0), stop=(kc == NKC - 1)),
             reads=[sB] + hTBl, writes=[ppB], inc=(kc == NKC - 1))
    for (pp, ppB, c0) in ((pg, pgB, j * 128), (pu, puB, nch * 128 + j * 128)):
        for kc in range(NKC):
            mm(pp, ppB, c0, kc)
    P.op("scalar", lambda e: e.activation(out=sg[:], in_=pg[:], func=AF.Silu), reads=[pgB], writes=[sgB])
    P.op("vector", lambda e: e.tensor_tensor(out=R.aT[:, aj, :], in0=sg[:], in1=pu[:], op=ALU.mult),
         reads=[sgB, puB], writes=[R.aTB[aj]])


def stage_c_sub(R, wo, sB, nch, X_in, X_out, Yacc, yB, t, s, sl, last, lhs=None):
    P = R.P
    if lhs is None:
        lhs = [(R.aT, j, R.aTB[j]) for j in range(nch)]
    r0 = t * TT + s * 128
    yi = (t * 4 + s) % 2
    yb, ybB, xr, xrB = R.yb[yi], R.ybB[yi], R.xr[yi], R.xrB[yi]
    if sl > 0:
        P.dma("sync", lambda e: e.dma_start(out=yb[:], in_=Yacc[r0:r0 + 128, :]), reads=[yB], writes=[ybB])
    if last:
        P.dma("sync", lambda e: e.dma_start(out=xr[:], in_=X_in[r0:r0 + 128, :]), writes=[xrB])

    def mm(half, j):
        lt, li, lB = lhs[j]
        P.op("tensor", lambda e: e.matmul(R.pC[half][:], lhsT=lt[:, li, s * 128:(s + 1) * 128],
                                          rhs=wo[:, j, half * 512:(half + 1) * 512],
                                          start=(j == 0), stop=(j == nch - 1)),
             reads=[sB] + (lB if isinstance(lB, list) else [lB]), writes=[R.pCB], inc=(j == nch - 1))
    for half in range(2):
        for j in range(nch):
            mm(half, j)

    def evac(half):
        hs = slice(half * 512, (half + 1) * 512)
        if sl == 0:
            if half == 0:
                P.op("vector", lambda e: e.tensor_copy(out=yb[:, hs], in_=R.pC[half][:]), reads=[R.pCB], writes=[ybB])
            else:
                P.op("scalar", lambda e: e.copy(out=yb[:, hs], in_=R.pC[half][:]), reads=[R.pCB], writes=[ybB])
        else:
            P.op("vector", lambda e: e.tensor_tensor(out=yb[:, hs], in0=yb[:, hs], in1=R.pC[half][:], op=ALU.add),
                 reads=[R.pCB, ybB], writes=[ybB])
    evac(0)
    evac(1)
    if not last:
        P.dma("gpsimd", lambda e: e.dma_start(out=Yacc[r0:r0 + 128, :], in_=yb[:]), reads=[ybB], writes=[yB])
    else:
        k = R.ss_i
        R.ss_i = (k + 1) % 4
        ss, ssB = R.ss[k], R.ssB[k]
        P.op("scalar", lambda e: e.activation(out=R.junk[:], in_=yb[:], func=AF.Square, accum_out=ss[:, 0:1]),
             reads=[ybB], writes=[R.junkB, ssB])
        P.op("scalar", lambda e: e.activation(out=ss[:, 1:2], in_=ss[:, 0:1], func=AF.Sqrt, bias=R.epsc[:, 0:1], scale=1.0),
             reads=[ssB, R.epscB], writes=[ssB])
        P.op("vector", lambda e: e.reciprocal(out=ss[:, 1:2], in_=ss[:, 1:2]), reads=[ssB], writes=[ssB])
        P.op("scalar", lambda e: e.activation(out=yb[:], in_=yb[:], func=AF.Copy, scale=ss[:, 1:2]),
             reads=[ybB, ssB], writes=[ybB])
        P.op("gpsimd", lambda e: e.tensor_tensor(out=yb[:], in0=yb[:], in1=R.crow[:], op=ALU.mult),
             reads=[ybB, R.crowB], writes=[ybB])
        P.op("vector", lambda e: e.tensor_tensor(out=xr[:], in0=xr[:], in1=yb[:], op=ALU.add),
             reads=[ybB, xrB], writes=[xrB])
        P.dma("gpsimd", lambda e: e.dma_start(out=X_out[r0:r0 + 128, :], in_=xr[:]), reads=[xrB], writes=[yB])


def ffn_phase(R, X_in, X_out, Yacc, w_in, w_out, NT, first_slab_loaded=False, next_loader=None):
    P = R.P
    ntile = NT // TT
    nsl = len(SLABS)
    if not first_slab_loaded:
        load_slab(R, w_in, w_out, SLABS[0], 0)
    YB = [Buf("Y%d" % i) for i in range(ntile * 4)]
    gidx = 0
    for sl in range(nsl):
        buf = sl % 2
        j0, nch = SLABS[sl]
        last = sl == nsl - 1
        ldr = None
        if sl + 1 < nsl:
            ldr = load_slab_gen(R, w_in, w_out, SLABS[sl + 1], (sl + 1) % 2)
        elif next_loader is not None:
            next_loader((sl + 1) % 2)
        for s in range(4):
            stage_a_sub(R, X_in, 0, s, 0, 0)
            stage_a_sub(R, X_in, 0, s, 0, 1)
        for t in range(ntile):
            hb = t % 2
            for j in range(nch):
                stage_b_group(R, R.wi[buf], R.slabB[buf], R.hT[hb], R.hTB[hb], j, nch, gidx)
                gidx += 1
                if ldr is not None:
                    next(ldr, None)
                if t + 1 < ntile and j < 8:
                    stage_a_sub(R, X_in, t + 1, j // 2, 1 - hb, j % 2)
            if t + 1 < ntile:
                for jj in range(nch, 8):
                    stage_a_sub(R, X_in, t + 1, jj // 2, 1 - hb, jj % 2)
            for s in range(4):
                stage_c_sub(R, R.wo[buf], R.slabB[buf], nch, X_in, X_out, Yacc, YB[t * 4 + s], t, s, sl, last)
        if ldr is not None:
            for _ in ldr:
                pass
    return YB


NH = 4
NCST = 384
STOP = 0


class StopEmit(Exception):
    pass


def stop_at(k):
    if STOP == k:
        raise StopEmit()


def make_consts():
    c = np.zeros((128, NCST), np.float32)
    c[:, 0:128] = np.eye(128)
    c[:, 128:256] = 1.0
    k = np.arange(64)
    c[0:64, 256:320] = (k[:, None] <= k[None, :])
    c[0:64, 320:384] = (k[:, None] > k[None, :])
    return c


def load_cols(R, w, c0, ncol, buf):
    P = R.P
    wi, sB = R.wi[buf], R.slabB[buf]

    def piece(kc, d0, w_):
        si = R.stg_i
        R.stg_i = (si + 1) % 3
        st, stB = R.stg[si], R.stgB[si]
        P.dma("sync", lambda e: e.dma_start(out=st[:, 0:w_], in_=w[kc * 128:(kc + 1) * 128, c0 + d0:c0 + d0 + w_]), writes=[stB])
        P.op("gpsimd", lambda e: e.tensor_copy(out=wi[:, kc, d0:d0 + w_], in_=st[:, 0:w_]), reads=[stB], writes=[sB])
    for kc in range(NKC):
        d0 = 0
        while d0 < ncol:
            w_ = min(1024, ncol - d0)
            piece(kc, d0, w_)
            d0 += w_


def proj_phase(R, X_in, w_mix, PT, NT, PTB):
    P = R.P
    load_cols(R, w_mix, 0, 2048, 0)
    load_cols(R, w_mix, 2048, 520, 1)
    ntile = NT // TT
    chunks = [(0, j * 128, 128, j * 128) for j in range(16)]
    chunks += [(1, 8 + j * 128, 128, 2056 + j * 128) for j in range(4)]
    chunks += [(1, 0, 8, 2048)]
    gi = 0
    for s in range(4):
        stage_a_sub(R, X_in, 0, s, 0, 0)
        stage_a_sub(R, X_in, 0, s, 0, 1)

    def out_chunk(t, hb, buf, lc, M, row, gi):
        pp, ppB = R.pB[gi % 4], R.pBB[gi % 4]
        sg, sgB = R.sg[gi % 2], R.sgB[gi % 2]
        wi, sB, hT = R.wi[buf], R.slabB[buf], R.hT[hb]

        def mm(kc):
            P.op("tensor", lambda e: e.matmul(pp[0:M, :], lhsT=wi[:, kc, lc:lc + M], rhs=hT[:, kc, :], start=(kc == 0), stop=(kc == NKC - 1)),
                 reads=[sB] + R.hTB[hb], writes=[ppB], inc=(kc == NKC - 1))
        for kc in range(NKC):
            mm(kc)
        if gi % 2 == 0:
            P.op("vector", lambda e: e.tensor_copy(out=sg[0:M, :], in_=pp[0:M, :]), reads=[ppB], writes=[sgB])
        else:
            P.op("scalar", lambda e: e.copy(out=sg[0:M, :], in_=pp[0:M, :]), reads=[ppB], writes=[sgB])
        P.dma("gpsimd", lambda e: e.dma_start(out=PT[row:row + M, t * TT:(t + 1) * TT], in_=sg[0:M, :]), reads=[sgB], writes=[PTB[t]])
    for t in range(ntile):
        hb = t % 2
        for ci, (buf, lc, M, row) in enumerate(chunks):
            out_chunk(t, hb, buf, lc, M, row, gi)
            gi += 1
            if t + 1 < ntile and ci < 8:
                stage_a_sub(R, X_in, t + 1, ci // 2, 1 - hb, ci % 2)


class MixRes:
    def __init__(self, nc, stack, P, tag=""):
        self.nc, self.P = nc, P
        self._sb = lambda name, shape, dt=F32: stack.enter_context(nc.sbuf_tensor("m_" + name + tag, shape, dt))
        self._ps = lambda name, shape, dt=F32: stack.enter_context(nc.psum_tensor("m_" + name + tag, shape, dt))
        self.bufs = {}

    def sb(self, name, shape, dt=F32):
        t = self._sb(name, shape, dt)
        self.bufs[name] = Buf(name)
        setattr(self, name, t)
        setattr(self, name + "B", self.bufs[name])
        return t

    def ps(self, name, shape, dt=F32):
        t = self._ps(name, shape, dt)
        self.bufs[name] = Buf(name)
        setattr(self, name, t)
        setattr(self, name + "B", self.bufs[name])
        return t


def gdn_phase(M, PT, YT, conv_w, a_log, dt_bias, gnw, cst_dram, NT, PTB, YTB):
    P = M.P
    ntile = NT // TT
    sb, ps = M.sb, M.ps
    G2 = 2
    cst = sb("cst", [128, NCST])
    ident = cst[:, 0:128]
    ones = cst[:, 128:256]
    LE = cst[0:64, 256:320]
    GT = cst[0:64, 320:384]
    P.dma("sync", lambda e: e.dma_start(out=cst[:], in_=cst_dram), writes=[M.cstB])
    sb("LE4", [64, G2, 64]); sb("GT4", [64, G2, 64]); sb("I4", [64, G2, 64])
    for h in range(G2):
        P.op("vector", (lambda e, h=h: e.tensor_copy(out=M.LE4[:, h, :], in_=LE)), reads=[M.cstB], writes=[M.LE4B])
        P.op("vector", (lambda e, h=h: e.tensor_copy(out=M.GT4[:, h, :], in_=GT)), reads=[M.cstB], writes=[M.GT4B])
        P.op("vector", (lambda e, h=h: e.tensor_copy(out=M.I4[:, h, :], in_=cst[0:64, 0:64])), reads=[M.cstB], writes=[M.I4B])
    sb("epsk", [128, 1]); sb("epsq", [128, 1]); sb("epsn", [128, 1]); sb("one1", [128, 1])
    P.op("vector", lambda e: e.memset(M.epsk[:], 1e-6), writes=[M.epskB])
    P.op("vector", lambda e: e.memset(M.epsq[:], 128e-6), writes=[M.epsqB])
    P.op("vector", lambda e: e.memset(M.epsn[:], 1e-6), writes=[M.epsnB])
    P.op("vector", lambda e: e.memset(M.one1[:], 1.0), writes=[M.one1B])

    class Grp:
        pass
    groups = []
    for gi in range(2):
        G = Grp()
        G.h0 = gi * G2
        sfx = "_g%d" % gi

        def gsb(name, shape, G=G, sfx=sfx):
            t = sb(name + sfx, shape)
            setattr(G, name, t)
            setattr(G, name + "B", getattr(M, name + sfx + "B"))

        def gps(name, shape, G=G, sfx=sfx):
            t = ps(name + sfx, shape)
            setattr(G, name, t)
            setattr(G, name + "B", getattr(M, name + sfx + "B"))
        banks = [ps("bk%d" % k + sfx, [128, 512]) for k in range(4)]
        r4 = lambda ap: ap.rearrange("p (a h c) -> p a h c", a=2, h=G2)
        for nm, ap in (("psD", r4(banks[0][0:64, 0:256])), ("psG", r4(banks[0][0:64, 256:512])),
                       ("psI", r4(banks[1][0:64, 0:256])), ("psU", r4(banks[1][0:64, 256:512])),
                       ("psX", banks[2][:, 0:256]), ("psY", banks[2][:, 256:512]),
                       ("psZ", banks[3][:, 0:256]), ("psS", banks[3][:, 256:384])):
            setattr(G, nm, ap)
        bankB = [Buf("bk%d" % k + sfx) for k in range(4)]
        for nm, k in (("psD", 0), ("psG", 0), ("psI", 1), ("psU", 1), ("psX", 2), ("psY", 2), ("psZ", 3), ("psS", 3)):
            setattr(G, nm + "B", bankB[k])
        gsb("S", [128, G2, 128]); gsb("yg", [128, G2, TT])
        gsb("ba", [64, 8]); gsb("bt", [64, G2]); gsb("nbt", [64, G2]); gsb("g", [64, G2]); gsb("gcs", [64, G2]); gsb("egc", [64, G2])
        gsb("egl", [128, G2]); gsb("egd", [64, G2]); gsb("begc", [64, G2])
        gsb("G12", [64, 2, G2, 64]); gsb("eD", [64, 2, G2, 64]); gsb("dec", [64, 2, G2, 64])
        gsb("AA", [64, 2, G2, 64]); gsb("PU", [64, G2, 64]); gsb("attT", [64, G2, 64]); gsb("tL", [64, G2, 64])
        gsb("vb", [64, G2, 128]); gsb("kbg", [64, G2, 128]); gsb("kst", [64, G2, 128]); gsb("wv", [64, G2, 128]); gsb("kcT", [128, G2, 64])
        gsb("vn", [64, G2, 128]); gsb("o1", [64, G2, 128]); gsb("osq", [64, G2, 128]); gsb("on", [64, G2, 128]); gsb("ssq", [64, 2 * G2])
        P.op("vector", (lambda e, G=G: e.memset(G.S[:], 0.0)), writes=[G.SB])
        groups.append(G)
    GA, GB = groups
    psS0, psS0B = GA.psS, GA.psSB
    sb("cwr", [4, 1536]); sb("cw", [128, 12, 4])
    P.dma("sync", lambda e: e.dma_start(out=M.cwr[:], in_=conv_w), writes=[M.cwrB])
    for ct in range(12):
        P.op("tensor", (lambda e, ct=ct: e.matmul(psS0[:, ct * 4:(ct + 1) * 4], lhsT=M.cwr[0:4, ct * 128:(ct + 1) * 128], rhs=cst[0:4, 0:4], start=True, stop=True)),
             reads=[M.cwrB, M.cstB], writes=[psS0B], inc=(ct == 11))
    P.op("vector", lambda e: e.tensor_copy(out=M.cw[:].rearrange("p a b -> p (a b)"), in_=psS0[:, 0:48]), reads=[psS0B], writes=[M.cwB])
    sb("gnr", [1, 128]); sb("gnc", [128, 1])
    P.dma("sync", lambda e: e.dma_start(out=M.gnr[:], in_=gnw.rearrange("(o f) -> o f", o=1)), writes=[M.gnrB])
    P.op("tensor", lambda e: e.matmul(psS0[:, 0:1], lhsT=M.gnr[0:1, :], rhs=cst[0:1, 0:1], start=True, stop=True), reads=[M.gnrB, M.cstB], writes=[psS0B])
    P.op("vector", lambda e: e.tensor_copy(out=M.gnc[:], in_=psS0[:, 0:1]), reads=[psS0B], writes=[M.gncB])
    sb("nA", [64, NH]); sb("dtb", [64, NH]); sb("adr", [1, 8])
    P.dma("sync", lambda e: e.dma_start(out=M.adr[:, 0:4], in_=a_log.rearrange("(o f) -> o f", o=1)), writes=[M.adrB])
    P.dma("sync", lambda e: e.dma_start(out=M.adr[:, 4:8], in_=dt_bias.rearrange("(o f) -> o f", o=1)), writes=[M.adrB])
    P.op("tensor", lambda e: e.matmul(psS0[0:64, 0:8], lhsT=cst[0:1, 128:192], rhs=M.adr[0:1, :], start=True, stop=True), reads=[M.adrB, M.cstB], writes=[psS0B])
    P.op("vector", lambda e: e.tensor_copy(out=M.nA[:], in_=psS0[0:64, 0:4]), reads=[psS0B], writes=[M.nAB])
    P.op("vector", lambda e: e.tensor_copy(out=M.dtb[:], in_=psS0[0:64, 4:8]), reads=[psS0B], writes=[M.dtbB])
    P.op("scalar", lambda e: e.activation(out=M.nA[:], in_=M.nA[:], func=AF.Exp), reads=[M.nAB], writes=[M.nAB])
    P.op("vector", lambda e: e.tensor_scalar(out=M.nA[:], in0=M.nA[:], scalar1=-1.0, scalar2=None, op0=ALU.mult), reads=[M.nAB], writes=[M.nAB])
    sb("qkv", [128, 12, TT]); sb("xin", [128, TT + 3]); sb("acc", [128, TT]); sb("sq", [128, TT]); sb("rn", [128, TT])
    sb("sz", [128, NH, TT]); sb("bar", [8, TT])

    def conv_tile(t, ct):
        t0 = t * TT
        r0 = ct * 128
        if t == 0:
            P.op("gpsimd", lambda e: e.memset(M.xin[:, 0:3], 0.0), writes=[M.xinB])
            P.dma("sync", lambda e: e.dma_start(out=M.xin[:, 3:TT + 3], in_=PT[r0:r0 + 128, 0:TT]), reads=[PTB[0]], writes=[M.xinB])
        else:
            P.dma("sync", lambda e: e.dma_start(out=M.xin[:], in_=PT[r0:r0 + 128, t0 - 3:t0 + TT]), reads=[PTB[t - 1], PTB[t]], writes=[M.xinB])
        P.op("vector", lambda e: e.tensor_scalar(out=M.acc[:], in0=M.xin[:, 0:TT], scalar1=M.cw[:, ct, 0:1], scalar2=None, op0=ALU.mult),
             reads=[M.xinB, M.cwB], writes=[M.accB])
        for j in range(1, 4):
            P.op("vector", (lambda e, j=j: e.scalar_tensor_tensor(out=M.acc[:], in0=M.xin[:, j:j + TT], scalar=M.cw[:, ct, j:j + 1], in1=M.acc[:],
                                                                  op0=ALU.mult, op1=ALU.add)), reads=[M.xinB, M.cwB, M.accB], writes=[M.accB])
        P.op("scalar", lambda e: e.activation(out=M.qkv[:, ct, :], in_=M.acc[:], func=AF.Silu), reads=[M.accB], writes=[M.qkvB])
        if ct < 8:
            P.op("scalar", lambda e: e.activation(out=M.sq[:], in_=M.qkv[:, ct, :], func=AF.Square), reads=[M.qkvB], writes=[M.sqB])
            for hf, Gx in enumerate((GA, GB)):
                def half(hf=hf, Gx=Gx):
                    cs_ = slice(hf * 256, (hf + 1) * 256)
                    P.op("tensor", lambda e: e.matmul(Gx.psX[:], lhsT=ones, rhs=M.sq[:, cs_], start=True, stop=True), reads=[M.sqB, M.cstB], writes=[Gx.psXB])
                    P.op("vector", lambda e: e.tensor_copy(out=M.rn[:, cs_], in_=Gx.psX[:]), reads=[Gx.psXB], writes=[M.rnB])
                    if ct < 4:
                        P.op("scalar", lambda e: e.activation(out=M.rn[:, cs_], in_=M.rn[:, cs_], func=AF.Sqrt, bias=M.epsq[:, 0:1], scale=128.0),
                             reads=[M.rnB, M.epsqB], writes=[M.rnB])
                    else:
                        P.op("scalar", lambda e: e.activation(out=M.rn[:, cs_], in_=M.rn[:, cs_], func=AF.Sqrt, bias=M.epsk[:, 0:1], scale=1.0),
                             reads=[M.rnB, M.epskB], writes=[M.rnB])
                half()
            P.op("vector", lambda e: e.reciprocal(out=M.rn[:], in_=M.rn[:]), reads=[M.rnB], writes=[M.rnB])
            P.op("gpsimd", lambda e: e.tensor_tensor(out=M.qkv[:, ct, :], in0=M.qkv[:, ct, :], in1=M.rn[:], op=ALU.mult),
                 reads=[M.qkvB, M.rnB], writes=[M.qkvB])

    def z_tile(t, h):
        t0 = t * TT
        P.dma("sync", lambda e: e.dma_start(out=M.sz[:, h, :], in_=PT[1536 + h * 128:1536 + (h + 1) * 128, t0:t0 + TT]), reads=[PTB[t]], writes=[M.szB])
        P.op("scalar", lambda e: e.activation(out=M.sz[:, h, :], in_=M.sz[:, h, :], func=AF.Silu), reads=[M.szB], writes=[M.szB])
        P.op("gpsimd", lambda e: e.tensor_scalar(out=M.sz[:, h, :], in0=M.sz[:, h, :], scalar1=M.gnc[:, 0:1], scalar2=None, op0=ALU.mult),
             reads=[M.szB, M.gncB], writes=[M.szB])

    def mm(out, lhsT, rhs, reads, writes, inc=True):
        P.op("tensor", lambda e: e.matmul(out, lhsT=lhsT, rhs=rhs, start=True, stop=True), reads=reads, writes=writes, inc=inc)

    def chunk(n, G):
        h0 = G.h0
        HR = range(G2)
        c0 = n * 64
        cs = slice(c0, c0 + 64)
        qT = lambda h: M.qkv[:, h0 + h, cs]
        kT = lambda h: M.qkv[:, 4 + h0 + h, cs]
        vT = lambda h: M.qkv[:, 8 + h0 + h, cs]
        mm(G.psS[0:64, 0:8], M.bar[0:8, cs], cst[0:8, 0:8], [M.barB, M.cstB], [G.psSB])
        P.op("vector", lambda e: e.tensor_copy(out=G.ba[:], in_=G.psS[0:64, 0:8]), reads=[G.psSB], writes=[G.baB])
        yield
        P.op("scalar", lambda e: e.activation(out=G.bt[:], in_=G.ba[:, h0:h0 + G2], func=AF.Sigmoid), reads=[G.baB], writes=[G.btB])
        P.op("vector", lambda e: e.tensor_tensor(out=G.g[:], in0=G.ba[:, 4 + h0:4 + h0 + G2], in1=M.dtb[:, h0:h0 + G2], op=ALU.add), reads=[G.baB, M.dtbB], writes=[G.gB])
        yield
        P.op("vector", lambda e: e.tensor_scalar(out=G.nbt[:], in0=G.bt[:], scalar1=-1.0, scalar2=None, op0=ALU.mult), reads=[G.btB], writes=[G.nbtB])
        P.op("scalar", lambda e: e.activation(out=G.g[:], in_=G.g[:], func=AF.Exp), reads=[G.gB], writes=[G.gB])
        yield
        P.op("scalar", lambda e: e.activation(out=G.g[:], in_=G.g[:], func=AF.Ln, bias=M.one1[0:64, 0:1], scale=1.0), reads=[G.gB, M.one1B], writes=[G.gB])
        yield
        P.op("vector", lambda e: e.tensor_tensor(out=G.g[:], in0=G.g[:], in1=M.nA[:, h0:h0 + G2], op=ALU.mult), reads=[G.gB, M.nAB], writes=[G.gB])
        yield
        mm(G.psS[0:64, 8:8 + G2], LE, G.g[:], [G.gB, M.cstB], [G.psSB], inc=False)
        mm(G.psS[:, 12:12 + G2], cst[0:64, 128:256], G.g[:], [G.gB, M.cstB], [G.psSB])
        for h in HR:
            P.op("gpsimd", (lambda e, h=h: e.tensor_scalar(out=G.G12[:, 0, h, :], in0=LE, scalar1=G.g[:, h:h + 1], scalar2=None, op0=ALU.mult)),
                 reads=[G.gB, M.cstB], writes=[G.G12B])
            P.op("gpsimd", (lambda e, h=h: e.tensor_scalar(out=G.G12[:, 1, h, :], in0=GT, scalar1=G.g[:, h:h + 1], scalar2=None, op0=ALU.mult)),
                 reads=[G.gB, M.cstB], writes=[G.G12B])
        yield
        P.op("vector", lambda e: e.tensor_copy(out=G.gcs[:], in_=G.psS[0:64, 8:8 + G2]), reads=[G.psSB], writes=[G.gcsB])
        P.op("vector", lambda e: e.tensor_copy(out=G.egl[:], in_=G.psS[:, 12:12 + G2]), reads=[G.psSB], writes=[G.eglB])
        P.op("scalar", lambda e: e.activation(out=G.egc[:], in_=G.gcs[:], func=AF.Exp), reads=[G.gcsB], writes=[G.egcB])
        P.op("scalar", lambda e: e.activation(out=G.egl[:], in_=G.egl[:], func=AF.Exp), reads=[G.eglB], writes=[G.eglB])
        for h in HR:
            mm(G.psD[:, 0, h, :], G.G12[:, 0, h, :], GT, [G.G12B, M.cstB], [G.psDB], inc=False)
            mm(G.psD[:, 1, h, :], G.G12[:, 1, h, :], LE, [G.G12B, M.cstB], [G.psDB], inc=(h == G2 - 1))
        for h in HR:
            mm(G.psG[:, 0, h, :], kT(h), kT(h), [M.qkvB], [G.psGB], inc=False)
            mm(G.psU[:, 1, h, :], kT(h), qT(h), [M.qkvB], [G.psUB], inc=(h == G2 - 1))
        yield
        P.op("vector", lambda e: e.tensor_tensor(out=G.egd[:], in0=G.psS[0:64, 12:12 + G2], in1=G.gcs[:], op=ALU.subtract), reads=[G.psSB, G.gcsB], writes=[G.egdB])
        P.op("vector", lambda e: e.tensor_tensor(out=G.begc[:], in0=G.bt[:], in1=G.egc[:], op=ALU.mult), reads=[G.btB, G.egcB], writes=[G.begcB])
        P.op("vector", lambda e: e.tensor_copy(out=G.eD[:], in_=G.psD[:]), reads=[G.psDB], writes=[G.eDB])
        P.op("scalar", lambda e: e.activation(out=G.eD[:], in_=G.eD[:], func=AF.Exp), reads=[G.eDB], writes=[G.eDB])
        yield
        P.op("scalar", lambda e: e.activation(out=G.egd[:], in_=G.egd[:], func=AF.Exp), reads=[G.egdB], writes=[G.egdB])
        P.op("gpsimd", lambda e: e.tensor_tensor(out=G.dec[:, 0], in0=G.eD[:, 0], in1=M.GT4[:], op=ALU.mult), reads=[G.eDB, M.GT4B], writes=[G.decB])
        P.op("gpsimd", lambda e: e.tensor_tensor(out=G.dec[:, 1], in0=G.eD[:, 1], in1=M.LE4[:], op=ALU.mult), reads=[G.eDB, M.LE4B], writes=[G.decB])
        yield
        P.op("vector", lambda e: e.tensor_tensor(out=G.tL[:], in0=G.psG[:, 0], in1=G.dec[:, 0], op=ALU.mult), reads=[G.psGB, G.decB], writes=[G.tLB])
        P.op("vector", lambda e: e.tensor_tensor(out=G.attT[:], in0=G.psU[:, 1], in1=G.dec[:, 1], op=ALU.mult), reads=[G.psUB, G.decB], writes=[G.attTB])
        yield
        for h in HR:
            P.op("gpsimd", (lambda e, h=h: e.tensor_scalar(out=G.AA[:, 1, h, :], in0=G.tL[:, h, :], scalar1=G.nbt[:, h:h + 1], scalar2=None, op0=ALU.mult)),
                 reads=[G.tLB, G.nbtB], writes=[G.AAB])
        yield
        for h in HR:
            mm(G.psG[:, 1, h, :], G.AA[:, 1, h, :], cst[0:64, 0:64], [G.AAB, M.cstB], [G.psGB], inc=(h == G2 - 1))
        yield
        P.op("vector", lambda e: e.tensor_copy(out=G.AA[:, 0], in_=G.psG[:, 1]), reads=[G.psGB], writes=[G.AAB])
        yield
        P.op("vector", lambda e: e.tensor_tensor(out=G.PU[:], in0=G.AA[:, 0], in1=M.I4[:], op=ALU.add), reads=[G.AAB, M.I4B], writes=[G.PUB])
        for m in range(5):
            for h in HR:
                mm(G.psI[:, 0, h, :], G.AA[:, 1, h, :], G.AA[:, 0, h, :], [G.AAB], [G.psIB], inc=False)
                mm(G.psI[:, 1, h, :], G.AA[:, 0, h, :], G.AA[:, 1, h, :], [G.AAB], [G.psIB], inc=(h == G2 - 1))
            yield
            P.op("vector", lambda e: e.tensor_copy(out=G.AA[:], in_=G.psI[:]), reads=[G.psIB], writes=[G.AAB])
            yield
            for h in HR:
                mm(G.psU[:, 0, h, :], G.AA[:, 1, h, :], G.PU[:, h, :], [G.AAB, G.PUB], [G.psUB], inc=(h == G2 - 1))
            yield
            P.op("vector", lambda e: e.tensor_tensor(out=G.PU[:], in0=G.PU[:], in1=G.psU[:, 0], op=ALU.add), reads=[G.psUB, G.PUB], writes=[G.PUB])
            yield
        X3 = G.psX[0:64, :].rearrange("p (h d) -> p h d", h=G2)
        Y3 = G.psY[0:64, :].rearrange("p (h d) -> p h d", h=G2)
        Z3 = G.psZ[0:64, :].rearrange("p (h d) -> p h d", h=G2)

        def tr(out, in_, wB, inc):
            P.op("tensor", lambda e: e.transpose(out=out, in_=in_, identity=ident), reads=[M.qkvB, M.cstB], writes=[wB], inc=inc)
        for h in HR:
            tr(X3[:, h, :], kT(h), G.psXB, False)
            tr(Y3[:, h, :], vT(h), G.psYB, h == G2 - 1)
        yield
        for h in HR:
            P.op("vector", (lambda e, h=h: e.tensor_scalar(out=G.vb[:, h, :], in0=Y3[:, h, :], scalar1=G.bt[:, h:h + 1], scalar2=None, op0=ALU.mult)),
                 reads=[G.psYB, G.btB], writes=[G.vbB])
            P.op("vector", (lambda e, h=h: e.tensor_scalar(out=G.kbg[:, h, :], in0=X3[:, h, :], scalar1=G.begc[:, h:h + 1], scalar2=None, op0=ALU.mult)),
                 reads=[G.psXB, G.begcB], writes=[G.kbgB])
            P.op("vector", (lambda e, h=h: e.tensor_scalar(out=G.kst[:, h, :], in0=X3[:, h, :], scalar1=G.egd[:, h:h + 1], scalar2=None, op0=ALU.mult)),
                 reads=[G.psXB, G.egdB], writes=[G.kstB])
        yield
        YK = G.psY[:, 0:G2 * 64].rearrange("p (h d) -> p h d", h=G2)
        for h in HR:
            mm(X3[:, h, :], G.PU[:, h, :], G.vb[:, h, :], [G.PUB, G.vbB], [G.psXB], inc=False)
            mm(YK[:, h, :], G.kbg[:, h, :], G.PU[:, h, :], [G.PUB, G.kbgB], [G.psYB], inc=(h == G2 - 1))
        yield
        P.op("vector", lambda e: e.tensor_copy(out=G.wv[:], in_=X3), reads=[G.psXB], writes=[G.wvB])
        P.op("vector", lambda e: e.tensor_copy(out=G.kcT[:], in_=YK), reads=[G.psYB], writes=[G.kcTB])
        yield
        for h in HR:
            mm(X3[:, h, :], G.kcT[:, h, :], G.S[:, h, :], [G.kcTB, G.SB], [G.psXB], inc=(h == G2 - 1))
        yield
        P.op("vector", lambda e: e.tensor_tensor(out=G.vn[:], in0=G.wv[:], in1=X3, op=ALU.subtract), reads=[G.wvB, G.psXB], writes=[G.vnB])
        yield
        for h in HR:
            mm(Y3[:, h, :], qT(h), G.S[:, h, :], [M.qkvB, G.SB], [G.psYB], inc=False)
            mm(Z3[:, h, :], G.attT[:, h, :], G.vn[:, h, :], [G.attTB, G.vnB], [G.psZB], inc=(h == G2 - 1))
        XS = G.psX[:, :].rearrange("p (h d) -> p h d", h=G2)
        for h in HR:
            mm(XS[:, h, :], G.kst[:, h, :], G.vn[:, h, :], [G.kstB, G.vnB], [G.psXB], inc=(h == G2 - 1))
        yield
        for h in HR:
            P.op("vector", (lambda e, h=h: e.tensor_scalar(out=G.o1[:, h, :], in0=Y3[:, h, :], scalar1=G.egc[:, h:h + 1], scalar2=None, op0=ALU.mult)),
                 reads=[G.psYB, G.egcB], writes=[G.o1B])
            P.op("gpsimd", (lambda e, h=h: e.tensor_scalar(out=G.S[:, h, :], in0=G.S[:, h, :], scalar1=G.egl[:, h:h + 1], scalar2=None, op0=ALU.mult)),
                 reads=[G.SB, G.eglB], writes=[G.SB])
        yield
        P.op("vector", lambda e: e.tensor_tensor(out=G.o1[:], in0=G.o1[:], in1=Z3, op=ALU.add), reads=[G.o1B, G.psZB], writes=[G.o1B])
        P.op("vector", lambda e: e.tensor_tensor(out=G.S[:], in0=G.S[:], in1=XS, op=ALU.add), reads=[G.SB, G.psXB], writes=[G.SB])
        yield
        P.op("gpsimd", lambda e: e.tensor_tensor(out=G.osq[:], in0=G.o1[:], in1=G.o1[:], op=ALU.mult), reads=[G.o1B], writes=[G.osqB])
        yield
        P.op("vector", lambda e: e.reduce_sum(out=G.ssq[:, 0:G2], in_=G.osq[:], axis=mybir.AxisListType.X), reads=[G.osqB], writes=[G.ssqB])
        yield
        P.op("scalar", lambda e: e.activation(out=G.ssq[:, G2:2 * G2], in_=G.ssq[:, 0:G2], func=AF.Sqrt, bias=M.epsn[0:64, 0:1], scale=1.0 / 128),
             reads=[G.ssqB, M.epsnB], writes=[G.ssqB])
        yield
        P.op("vector", lambda e: e.reciprocal(out=G.ssq[:, G2:2 * G2], in_=G.ssq[:, G2:2 * G2]), reads=[G.ssqB], writes=[G.ssqB])
        yield
        for h in HR:
            P.op("gpsimd", (lambda e, h=h: e.tensor_scalar(out=G.on[:, h, :], in0=G.o1[:, h, :], scalar1=G.ssq[:, G2 + h:G2 + h + 1], scalar2=None, op0=ALU.mult)),
                 reads=[G.o1B, G.ssqB], writes=[G.onB])
        yield
        ZT = G.psZ[:, 0:G2 * 64].rearrange("p (h d) -> p h d", h=G2)
        for h in HR:
            mm(ZT[:, h, :], G.on[:, h, :], cst[0:64, 0:64], [G.onB, M.cstB], [G.psZB], inc=(h == G2 - 1))
        yield
        P.op("vector", lambda e: e.tensor_tensor(out=G.yg[:, :, cs], in0=ZT, in1=M.sz[:, h0:h0 + G2, cs], op=ALU.mult), reads=[G.psZB, M.szB], writes=[G.ygB])

    for t in range(ntile):
        t0 = t * TT
        for ct in range(12):
            conv_tile(t, ct)
        for h in range(NH):
            z_tile(t, h)
        P.dma("sync", (lambda e, t0=t0: e.dma_start(out=M.bar[:], in_=PT[2048:2056, t0:t0 + TT])), reads=[PTB[t]], writes=[M.barB])
        for n in range(8):
            gens = [chunk(n, GA), chunk(n, GB)]
            alive = [True, True]
            while any(alive):
                for i, gen in enumerate(gens):
                    if alive[i]:
                        try:
                            next(gen)
                        except StopIteration:
                            alive[i] = False
        for G in (GA, GB):
            for h in range(G2):
                P.dma("gpsimd", (lambda e, G=G, h=h, t0=t0: e.dma_start(out=YT[(G.h0 + h) * 128:(G.h0 + h + 1) * 128, t0:t0 + TT], in_=G.yg[:, h, :])),
                      reads=[G.ygB], writes=[YTB[t]])


NCST2 = 129 + 128 + 16
SEG = 128
TWO_PI = 6.283185307179586


def make_consts2():
    c = np.zeros((128, NCST2), np.float32)
    c[:, 0:129] = np.arange(129)[None, :]
    g = np.arange(32)
    m = np.arange(128)
    c[0:32, 129:257] = (g[:, None] % 2 == (m[None, :] // 64))
    c[0:32, 257:273] = (g[:, None] // 2 == np.arange(16)[None, :])
    return c


def s5_phase(M, PT, YT, prm, cst_dram, cst2_dram, NT, PTB, YTB, tag=""):
    P = M.P
    ntile = NT // TT
    sb, ps = M.sb, M.ps
    c1 = sb("s_c1", [128, NCST]); c2 = sb("s_c2", [128, NCST2])
    c1B, c2B = M.s_c1B, M.s_c2B
    ident = c1[:, 0:128]
    P.dma("sync", lambda e: e.dma_start(out=c1[:], in_=cst_dram), writes=[c1B])
    P.dma("sync", lambda e: e.dma_start(out=c2[:], in_=cst2_dram), writes=[c2B])
    iota = c2[:, 0:129]
    psA = ps("s_psA", [128, 512]); psB = ps("s_psB", [128, 512]); psY = ps("s_psY", [128, 512]); psT = ps("s_psT", [128, 512])
    psAB, psBB, psYB, psTB = M.s_psAB, M.s_psBB, M.s_psYB, M.s_psTB
    NP_ = 16

    def V(eng, fn, reads, writes):
        P.op(eng, fn, reads=reads, writes=writes)

    rows = sb("s_rows", [32, 128]); rowsB = M.s_rowsB
    arc = sb("s_ar", [128, NP_]); aic = sb("s_ai", [128, NP_]); dtc = sb("s_dt", [128, NP_])

    def col_from_rows(dst, dstB, src_ap):
        P.dma("sync", lambda e: e.dma_start(out=rows[0:16, :], in_=src_ap), writes=[rowsB])
        P.op("tensor", lambda e: e.matmul(psT[:, 0:16], lhsT=rows[0:16, :], rhs=c1[0:16, 0:16], start=True, stop=True), reads=[rowsB, c1B], writes=[psTB])
        V("vector", lambda e: e.tensor_copy(out=dst[:], in_=psT[:, 0:16]), [psTB], [dstB])
    col_from_rows(arc, M.s_arB, prm["a_re"].rearrange("(a b) p -> a (b p)", b=2))
    col_from_rows(aic, M.s_aiB, prm["a_im"].rearrange("(a b) p -> a (b p)", b=2))
    ldr = sb("s_ldr", [1, 32]); ldc = sb("s_ldc", [32, 1]); Rm = sb("s_Rm", [32, 16])
    P.dma("sync", lambda e: e.dma_start(out=ldr[:], in_=prm["log_dt"].rearrange("(o f) -> o f", o=1)), writes=[M.s_ldrB])
    P.op("tensor", lambda e: e.matmul(psT[0:32, 16:17], lhsT=ldr[0:1, :], rhs=c1[0:1, 0:1], start=True, stop=True), reads=[M.s_ldrB, c1B], writes=[psTB])
    V("vector", lambda e: e.tensor_copy(out=ldc[:], in_=psT[0:32, 16:17]), [psTB], [M.s_ldcB])
    V("vector", lambda e: e.tensor_scalar(out=Rm[:], in0=c2[0:32, 257:273], scalar1=ldc[:, 0:1], scalar2=None, op0=ALU.mult), [c2B, M.s_ldcB], [M.s_RmB])
    P.op("tensor", lambda e: e.matmul(psT[:, 32:48], lhsT=c2[0:32, 129:257], rhs=Rm[:], start=True, stop=True), reads=[c2B, M.s_RmB], writes=[psTB])
    V("scalar", lambda e: e.activation(out=dtc[:], in_=psT[:, 32:48], func=AF.Exp), [psTB], [M.s_dtB])

    def sincos(x, xB, s_out, sB_, c_out, cB_, F, tmp, tmpB, tmpi, tmpiB):
        V("vector", lambda e: e.tensor_scalar(out=tmp, in0=x, scalar1=1.0 / TWO_PI, scalar2=None, op0=ALU.mult), [xB], [tmpB])
        V("vector", lambda e: e.tensor_copy(out=tmpi, in_=tmp), [tmpB], [tmpiB])
        V("vector", lambda e: e.tensor_copy(out=tmp, in_=tmpi), [tmpiB], [tmpB])
        V("vector", lambda e: e.scalar_tensor_tensor(out=x, in0=tmp, scalar=-TWO_PI, in1=x, op0=ALU.mult, op1=ALU.add), [tmpB, xB], [xB])
        V("scalar", lambda e: e.activation(out=tmp, in_=x, func=AF.Sin, scale=0.25), [xB], [tmpB])
        V("scalar", lambda e: e.activation(out=s_out, in_=x, func=AF.Sin, scale=0.5), [xB], [sB_])
        V("vector", lambda e: e.tensor_tensor(out=tmp, in0=tmp, in1=tmp, op=ALU.mult), [tmpB], [tmpB])
        V("vector", lambda e: e.tensor_scalar(out=tmp, in0=tmp, scalar1=-2.0, scalar2=1.0, op0=ALU.mult, op1=ALU.add), [tmpB], [tmpB])
        V("vector", lambda e: e.tensor_tensor(out=c_out, in0=s_out, in1=s_out, op=ALU.mult), [sB_], [cB_])
        V("vector", lambda e: e.scalar_tensor_tensor(out=s_out, in0=s_out, scalar=2.0, in1=tmp, op0=ALU.mult, op1=ALU.mult), [sB_, tmpB], [sB_])
        V("vector", lambda e: e.tensor_scalar(out=c_out, in0=c_out, scalar1=-2.0, scalar2=1.0, op0=ALU.mult, op1=ALU.add), [cB_], [cB_])

    mag = sb("s_mag", [128, NP_]); th = sb("s_th", [128, NP_]); sn = sb("s_sn", [128, NP_]); cs_ = sb("s_cs", [128, NP_])
    tp = sb("s_tp", [128, NP_]); tpi = sb("s_tpi", [128, NP_], I32); th2 = sb("s_th2", [128, NP_])
    V("vector", lambda e: e.tensor_scalar(out=arc[:], in0=arc[:], scalar1=-1e-4, scalar2=None, op0=ALU.min), [M.s_arB], [M.s_arB])
    V("vector", lambda e: e.tensor_tensor(out=mag[:], in0=dtc[:], in1=arc[:], op=ALU.mult), [M.s_dtB, M.s_arB], [M.s_magB])
    V("scalar", lambda e: e.activation(out=mag[:], in_=mag[:], func=AF.Exp), [M.s_magB], [M.s_magB])
    V("vector", lambda e: e.tensor_tensor(out=th[:], in0=dtc[:], in1=aic[:], op=ALU.mult), [M.s_dtB, M.s_aiB], [M.s_thB])
    V("vector", lambda e: e.tensor_copy(out=th2[:], in_=th[:]), [M.s_thB], [M.s_th2B])
    sincos(th2[:], M.s_th2B, sn[:], M.s_snB, cs_[:], M.s_csB, NP_, tp[:], M.s_tpB, tpi[:], M.s_tpiB)
    zr = sb("s_zr", [128, NP_]); zi = sb("s_zi", [128, NP_]); den = sb("s_den", [128, NP_]); fr = sb("s_fr", [128, NP_]); fi = sb("s_fi", [128, NP_])
    V("vector", lambda e: e.tensor_tensor(out=zr[:], in0=mag[:], in1=cs_[:], op=ALU.mult), [M.s_magB, M.s_csB], [M.s_zrB])
    V("vector", lambda e: e.tensor_scalar(out=zr[:], in0=zr[:], scalar1=-1.0, scalar2=None, op0=ALU.add), [M.s_zrB], [M.s_zrB])
    V("vector", lambda e: e.tensor_tensor(out=zi[:], in0=mag[:], in1=sn[:], op=ALU.mult), [M.s_magB, M.s_snB], [M.s_ziB])
    V("vector", lambda e: e.tensor_tensor(out=den[:], in0=arc[:], in1=arc[:], op=ALU.mult), [M.s_arB], [M.s_denB])
    V("vector", lambda e: e.tensor_tensor(out=tp[:], in0=aic[:], in1=aic[:], op=ALU.mult), [M.s_aiB], [M.s_tpB])
    V("vector", lambda e: e.tensor_tensor(out=den[:], in0=den[:], in1=tp[:], op=ALU.add), [M.s_denB, M.s_tpB], [M.s_denB])
    V("vector", lambda e: e.reciprocal(out=den[:], in_=den[:]), [M.s_denB], [M.s_denB])
    V("vector", lambda e: e.tensor_tensor(out=fr[:], in0=zr[:], in1=arc[:], op=ALU.mult), [M.s_zrB, M.s_arB], [M.s_frB])
    V("vector", lambda e: e.tensor_tensor(out=tp[:], in0=zi[:], in1=aic[:], op=ALU.mult), [M.s_ziB, M.s_aiB], [M.s_tpB])
    V("vector", lambda e: e.tensor_tensor(out=fr[:], in0=fr[:], in1=tp[:], op=ALU.add), [M.s_frB, M.s_tpB], [M.s_frB])
    V("vector", lambda e: e.tensor_tensor(out=fr[:], in0=fr[:], in1=den[:], op=ALU.mult), [M.s_frB, M.s_denB], [M.s_frB])
    V("vector", lambda e: e.tensor_tensor(out=fi[:], in0=zi[:], in1=arc[:], op=ALU.mult), [M.s_ziB, M.s_arB], [M.s_fiB])
    V("vector", lambda e: e.tensor_tensor(out=tp[:], in0=zr[:], in1=aic[:], op=ALU.mult), [M.s_zrB, M.s_aiB], [M.s_tpB])
    V("vector", lambda e: e.tensor_tensor(out=fi[:], in0=fi[:], in1=tp[:], op=ALU.subtract), [M.s_fiB, M.s_tpB], [M.s_fiB])
    V("vector", lambda e: e.tensor_tensor(out=fi[:], in0=fi[:], in1=den[:], op=ALU.mult), [M.s_fiB, M.s_denB], [M.s_fiB])

    CTb = sb("s_CT", [128, NP_, 129]); STb = sb("s_ST", [128, NP_, 129]); XT = sb("s_XT", [128, NP_, 129])
    TT1 = sb("s_TT1", [128, NP_, 129]); TTi = sb("s_TTi", [128, NP_, 129], I32); RM = sb("s_RM", [128, NP_, SEG])
    for gp in range(NP_):
        V("vector", (lambda e, gp=gp: e.tensor_scalar(out=XT[:, gp, :], in0=iota, scalar1=th[:, gp:gp + 1], scalar2=None, op0=ALU.mult)), [c2B, M.s_thB], [M.s_XTB])
        V("gpsimd", (lambda e, gp=gp: e.tensor_scalar(out=RM[:, gp, :], in0=c1[:, 128:256], scalar1=mag[:, gp:gp + 1], scalar2=None, op0=ALU.mult)), [c1B, M.s_magB], [M.s_RMB])
    fl = lambda t_: t_[:].rearrange("p a b -> p (a b)")
    sincos(fl(XT), M.s_XTB, fl(STb), M.s_STB, fl(CTb), M.s_CTB, NP_ * 129, fl(TT1), M.s_TT1B, fl(TTi), M.s_TTiB)

    BnR = sb("s_BnR", [128, 4, 128]); BnI = sb("s_BnI", [128, 4, 128]); bbR = sb("s_bbR", [128, 4, 128]); bbI = sb("s_bbI", [128, 4, 128])
    LBr = sb("s_LBr", [128, NP_, 128]); LBi = sb("s_LBi", [128, NP_, 128]); tmpm = sb("s_tmpm", [128, 128])
    V("vector", lambda e: e.memset(BnR[:], 0.0), [], [M.s_BnRB])
    V("vector", lambda e: e.memset(BnI[:], 0.0), [], [M.s_BnIB])
    V("gpsimd", lambda e: e.memset(bbR[:], 0.0), [], [M.s_bbRB])
    V("gpsimd", lambda e: e.memset(bbI[:], 0.0), [], [M.s_bbIB])
    for g in range(32):
        ct, gl, e_ = g // 8, g % 8, g % 2
        P.dma("sync", (lambda e, g=g, ct=ct, gl=gl, e_=e_: e.dma_start(out=BnR[e_ * 64:(e_ + 1) * 64, ct, gl * 16:(gl + 1) * 16], in_=prm["b_re"][g])), writes=[M.s_BnRB])
        P.dma("sync", (lambda e, g=g, ct=ct, gl=gl, e_=e_: e.dma_start(out=BnI[e_ * 64:(e_ + 1) * 64, ct, gl * 16:(gl + 1) * 16], in_=prm["b_im"][g])), writes=[M.s_BnIB])
    P.barrier()
    for g in range(32):
        ct, gl, e_, gp = g // 8, g % 8, g % 2, g // 2
        rs = slice(e_ * 64, (e_ + 1) * 64)
        csl = slice(gl * 16, (gl + 1) * 16)

        def bb(ct=ct, rs=rs, csl=csl, gp=gp):
            V("vector", lambda e: e.tensor_scalar(out=bbR[rs, ct, csl], in0=BnR[rs, ct, csl], scalar1=fr[rs, gp:gp + 1], scalar2=None, op0=ALU.mult), [M.s_BnRB, M.s_frB], [M.s_bbRB])
            V("vector", lambda e: e.tensor_scalar(out=tmpm[rs, 0:16], in0=BnI[rs, ct, csl], scalar1=fi[rs, gp:gp + 1], scalar2=None, op0=ALU.mult), [M.s_BnIB, M.s_fiB], [M.s_tmpmB])
            V("vector", lambda e: e.tensor_tensor(out=bbR[rs, ct, csl], in0=bbR[rs, ct, csl], in1=tmpm[rs, 0:16], op=ALU.subtract), [M.s_bbRB, M.s_tmpmB], [M.s_bbRB])
            V("vector", lambda e: e.tensor_scalar(out=bbI[rs, ct, csl], in0=BnI[rs, ct, csl], scalar1=fr[rs, gp:gp + 1], scalar2=None, op0=ALU.mult), [M.s_BnIB, M.s_frB], [M.s_bbIB])
            V("vector", lambda e: e.tensor_scalar(out=tmpm[rs, 16:32], in0=BnR[rs, ct, csl], scalar1=fi[rs, gp:gp + 1], scalar2=None, op0=ALU.mult), [M.s_BnRB, M.s_fiB], [M.s_tmpmB])
            V("vector", lambda e: e.tensor_tensor(out=bbI[rs, ct, csl], in0=bbI[rs, ct, csl], in1=tmpm[rs, 16:32], op=ALU.add), [M.s_bbIB, M.s_tmpmB], [M.s_bbIB])
        bb()
    for gp in range(NP_):
        ct, q4 = gp // 4, gp % 4
        csl = slice(q4 * 32, (q4 + 1) * 32)

        def mk(src, srcB, dst, dstB, ct=ct, csl=csl, gp=gp, neg=False):
            V("gpsimd", lambda e: e.memset(tmpm[:], 0.0), [], [M.s_tmpmB])
            V("gpsimd", lambda e: e.tensor_copy(out=tmpm[:, csl], in_=src[:, ct, csl]), [srcB], [M.s_tmpmB])
            P.op("tensor", lambda e: e.matmul(psT[:, 0:128], lhsT=tmpm[:], rhs=ident, start=True, stop=True), reads=[M.s_tmpmB, c1B], writes=[psTB])
            V("vector", lambda e: e.tensor_copy(out=dst[:, gp, :], in_=psT[:, 0:128]), [psTB], [dstB])
        mk(bbR, M.s_bbRB, LBr, M.s_LBrB)
        mk(bbI, M.s_bbIB, LBi, M.s_LBiB)

    CnR = sb("s_CnR", [128, 4, 128]); CnI = sb("s_CnI", [128, 4, 128]); CT2r = sb("s_CT2r", [128, 4, 128]); CT2i = sb("s_CT2i", [128, 4, 128])
    LCr = sb("s_LCr", [128, NP_, 128]); LCi = sb("s_LCi", [128, NP_, 128])
    V("vector", lambda e: e.memset(CnR[:], 0.0), [], [M.s_CnRB])
    V("vector", lambda e: e.memset(CnI[:], 0.0), [], [M.s_CnIB])
    V("gpsimd", lambda e: e.memset(LCr[:], 0.0), [], [M.s_LCrB])
    V("gpsimd", lambda e: e.memset(LCi[:], 0.0), [], [M.s_LCiB])
    for g in range(32):
        ct, gl, e_ = g // 8, g % 8, g % 2
        P.dma("sync", (lambda e, g=g, ct=ct, gl=gl, e_=e_: e.dma_start(out=CnR[gl * 16:(gl + 1) * 16, ct, e_ * 64:(e_ + 1) * 64], in_=prm["c_re"][g])), writes=[M.s_CnRB])
        P.dma("sync", (lambda e, g=g, ct=ct, gl=gl, e_=e_: e.dma_start(out=CnI[gl * 16:(gl + 1) * 16, ct, e_ * 64:(e_ + 1) * 64], in_=prm["c_im"][g])), writes=[M.s_CnIB])
    P.barrier()
    for ct in range(4):
        def trc(src, srcB, dst, dstB, ct=ct, neg=False):
            P.op("tensor", lambda e: e.matmul(psT[:, 0:128], lhsT=src[:, ct, :], rhs=ident, start=True, stop=True), reads=[srcB, c1B], writes=[psTB])
            if neg:
                V("vector", lambda e: e.tensor_scalar(out=dst[:, ct, :], in0=psT[:, 0:128], scalar1=-1.0, scalar2=None, op0=ALU.mult), [psTB], [dstB])
            else:
                V("vector", lambda e: e.tensor_copy(out=dst[:, ct, :], in_=psT[:, 0:128]), [psTB], [dstB])
        trc(CnR, M.s_CnRB, CT2r, M.s_CT2rB)
        trc(CnI, M.s_CnIB, CT2i, M.s_CT2iB, neg=True)
    for gp in range(NP_):
        ct, q4 = gp // 4, gp % 4
        csl = slice(q4 * 32, (q4 + 1) * 32)
        V("gpsimd", (lambda e, gp=gp, ct=ct, csl=csl: e.tensor_copy(out=LCr[:, gp, csl], in_=CT2r[:, ct, csl])), [M.s_CT2rB], [M.s_LCrB])
        V("gpsimd", (lambda e, gp=gp, ct=ct, csl=csl: e.tensor_copy(out=LCi[:, gp, csl], in_=CT2i[:, ct, csl])), [M.s_CT2iB], [M.s_LCiB])
    dcol = sb("s_dcol", [128, 4])
    P.dma("sync", lambda e: e.dma_start(out=rows[0:4, :], in_=prm["d"].rearrange("(a b) h -> a (b h)", b=8)), writes=[rowsB])
    P.op("tensor", lambda e: e.matmul(psT[:, 0:4], lhsT=rows[0:4, :], rhs=c1[0:4, 0:4], start=True, stop=True), reads=[rowsB, c1B], writes=[psTB])
    V("vector", lambda e: e.tensor_copy(out=dcol[:], in_=psT[:, 0:4]), [psTB], [M.s_dcolB])

    uT = sb("s_uT", [128, 4, TT]); ys = sb("s_ys", [128, TT])
    cR = sb("s_cR", [128, NP_]); cI = sb("s_cI", [128, NP_])
    cRB = [Buf("cR%d" % i) for i in range(NP_)]; cIB = [Buf("cI%d" % i) for i in range(NP_)]
    P.op("vector", lambda e: e.memset(cR[:], 0.0), writes=cRB)
    P.op("vector", lambda e: e.memset(cI[:], 0.0), writes=cIB)
    nseg = TT // SEG

    class Lane:
        pass
    lanes = []
    for li in range(2):
        Ln_ = Lane()
        for nm in ("bR", "bI", "t1", "t2", "xR", "xI"):
            setattr(Ln_, nm, sb("s_%s_l%d" % (nm, li), [128, TT]))
            setattr(Ln_, nm + "B", getattr(M, "s_%s_l%dB" % (nm, li)))
        Ln_.cq = sb("s_cq_l%d" % li, [128, 4]); Ln_.cqB = getattr(M, "s_cq_l%dB" % li)
        if li == 0:
            Ln_.psA, Ln_.psAB, Ln_.psB, Ln_.psBB = psA, psAB, psB, psBB
        else:
            Ln_.psA = ps("s_psA2", [128, 512]); Ln_.psAB = M.s_psA2B
            Ln_.psB = ps("s_psB2", [128, 512]); Ln_.psBB = M.s_psB2B
        lanes.append(Ln_)

    def pair_gen(t, gp, L):
        ct = gp // 4
        tabC = CTb[:, gp, 0:SEG].unsqueeze(1).broadcast_to([128, nseg, SEG])
        tabS = STb[:, gp, 0:SEG].unsqueeze(1).broadcast_to([128, nseg, SEG])
        v3 = lambda a: a[:].rearrange("p (s c) -> p s c", s=nseg)
        bR, bI, t1, t2, xR, xI, cq = L.bR, L.bI, L.t1, L.t2, L.xR, L.xI, L.cq
        P.op("tensor", lambda e: e.matmul(L.psA[:], lhsT=LBr[:, gp, :], rhs=uT[:, ct, :], start=True, stop=True), reads=[M.s_LBrB, M.s_uTB], writes=[L.psAB])
        P.op("tensor", lambda e: e.matmul(L.psB[:], lhsT=LBi[:, gp, :], rhs=uT[:, ct, :], start=True, stop=True), reads=[M.s_LBiB, M.s_uTB], writes=[L.psBB])
        pA3 = L.psA[:].rearrange("p (s c) -> p s c", s=nseg)
        pB3 = L.psB[:].rearrange("p (s c) -> p s c", s=nseg)
        yield
        V("vector", lambda e: e.tensor_tensor(out=v3(bR), in0=pA3, in1=tabC, op=ALU.mult), [L.psAB, M.s_CTB], [L.bRB])
        V("vector", lambda e: e.tensor_tensor(out=v3(t1), in0=pB3, in1=tabS, op=ALU.mult), [L.psBB, M.s_STB], [L.t1B])
        yield
        V("gpsimd", lambda e: e.tensor_tensor(out=bR[:], in0=bR[:], in1=t1[:], op=ALU.add), [L.bRB, L.t1B], [L.bRB])
        V("vector", lambda e: e.tensor_tensor(out=v3(bI), in0=pB3, in1=tabC, op=ALU.mult), [L.psBB, M.s_CTB], [L.bIB])
        V("vector", lambda e: e.tensor_tensor(out=v3(t2), in0=pA3, in1=tabS, op=ALU.mult), [L.psAB, M.s_STB], [L.t2B])
        yield
        V("gpsimd", lambda e: e.tensor_tensor(out=bI[:], in0=bI[:], in1=t2[:], op=ALU.subtract), [L.bIB, L.t2B], [L.bIB])
        yield
        c128 = CTb[:, gp, 128:129]
        s128 = STb[:, gp, 128:129]
        for s_ in range(nseg):
            sc = slice(s_ * SEG, (s_ + 1) * SEG)
            lr = xR[:, sc.stop - 1:sc.stop]
            li_ = xI[:, sc.stop - 1:sc.stop]

            def scans(sc=sc):
                V("vector", lambda e: e.tensor_tensor_scan(out=xR[:, sc], data0=RM[:, gp, :], data1=bR[:, sc], initial=cR[:, gp:gp + 1], op0=ALU.mult, op1=ALU.add),
                  [M.s_RMB, L.bRB, cRB[gp]], [L.xRB])
                V("vector", lambda e: e.tensor_tensor_scan(out=xI[:, sc], data0=RM[:, gp, :], data1=bI[:, sc], initial=cI[:, gp:gp + 1], op0=ALU.mult, op1=ALU.add),
                  [M.s_RMB, L.bIB, cIB[gp]], [L.xIB])
            scans()
            yield

            def carry1(lr=lr, li_=li_):
                V("vector", lambda e: e.tensor_tensor(out=cq[:, 0:1], in0=li_, in1=s128, op=ALU.mult), [L.xIB, M.s_STB], [L.cqB])
                V("vector", lambda e: e.tensor_tensor(out=cq[:, 1:2], in0=li_, in1=c128, op=ALU.mult), [L.xIB, M.s_CTB], [L.cqB])
            carry1()
            yield

            def carry2(lr=lr):
                V("vector", lambda e: e.scalar_tensor_tensor(out=cR[:, gp:gp + 1], in0=lr, scalar=c128, in1=cq[:, 0:1], op0=ALU.mult, op1=ALU.subtract),
                  [L.xRB, M.s_CTB, L.cqB], [cRB[gp]])
                V("vector", lambda e: e.scalar_tensor_tensor(out=cI[:, gp:gp + 1], in0=lr, scalar=s128, in1=cq[:, 1:2], op0=ALU.mult, op1=ALU.add),
                  [L.xRB, M.s_STB, L.cqB], [cIB[gp]])
            carry2()
            yield
        V("gpsimd", lambda e: e.tensor_tensor(out=v3(t1), in0=v3(xI), in1=tabS, op=ALU.mult), [L.xIB, M.s_STB], [L.t1B])
        V("gpsimd", lambda e: e.tensor_tensor(out=v3(t2), in0=v3(xR), in1=tabS, op=ALU.mult), [L.xRB, M.s_STB], [L.t2B])
        yield
        V("vector", lambda e: e.tensor_tensor(out=v3(xR), in0=v3(xR), in1=tabC, op=ALU.mult), [L.xRB, M.s_CTB], [L.xRB])
        V("vector", lambda e: e.tensor_tensor(out=v3(xI), in0=v3(xI), in1=tabC, op=ALU.mult), [L.xIB, M.s_CTB], [L.xIB])
        yield
        V("gpsimd", lambda e: e.tensor_tensor(out=xR[:], in0=xR[:], in1=t1[:], op=ALU.subtract), [L.xRB, L.t1B], [L.xRB])
        V("gpsimd", lambda e: e.tensor_tensor(out=xI[:], in0=xI[:], in1=t2[:], op=ALU.add), [L.xIB, L.t2B], [L.xIB])
        yield
        q4 = gp % 4
        P.op("tensor", lambda e: e.matmul(psY[:], lhsT=LCr[:, gp, :], rhs=xR[:], start=(q4 == 0), stop=False), reads=[M.s_LCrB, L.xRB], writes=[psYB], inc=False)
        P.op("tensor", lambda e: e.matmul(psY[:], lhsT=LCi[:, gp, :], rhs=xI[:], start=False, stop=(q4 == 3)), reads=[M.s_LCiB, L.xIB], writes=[psYB])

    for t in range(ntile):
        t0 = t * TT
        for ct in range(4):
            P.dma("sync", (lambda e, ct=ct, t0=t0: e.dma_start(out=uT[:, ct, :], in_=PT[2056 + ct * 128:2056 + (ct + 1) * 128, t0:t0 + TT])), reads=[PTB[t]], writes=[M.s_uTB])
        for gp0 in range(0, NP_, 2):
            gens = [pair_gen(t, gp0, lanes[0]), pair_gen(t, gp0 + 1, lanes[1])]
            alive = [True, True]
            while any(alive):
                for i, gen in enumerate(gens):
                    if alive[i]:
                        try:
                            next(gen)
                        except StopIteration:
                            alive[i] = False
            if gp0 % 4 == 2:
                ct = gp0 // 4

                def fin(ct=ct, t0=t0):
                    V("vector", lambda e: e.scalar_tensor_tensor(out=ys[:], in0=uT[:, ct, :], scalar=dcol[:, ct:ct + 1], in1=psY[:], op0=ALU.mult, op1=ALU.add),
                      [M.s_uTB, M.s_dcolB, psYB], [M.s_ysB])
                    V("scalar", lambda e: e.activation(out=ys[:], in_=ys[:], func=AF.Gelu), [M.s_ysB], [M.s_ysB])
                    P.dma("gpsimd", lambda e: e.dma_start(out=YT[512 + ct * 128:512 + (ct + 1) * 128, t0:t0 + TT], in_=ys[:]), reads=[M.s_ysB], writes=[YTB[t]])
                fin()


def mixpost_phase(R, YT, X_in, X_out, w_glu, w_out, NT):
    P = R.P
    ntile = NT // TT
    wi, wo, sB = R.wi[0], R.wo[0], R.slabB[0]

    def ld(src, dst, w_):
        si = R.stg_i
        R.stg_i = (si + 1) % 3
        st, stB = R.stg[si], R.stgB[si]
        P.dma("sync", lambda e: e.dma_start(out=st[:, 0:w_], in_=src), writes=[stB])
        P.op("gpsimd", lambda e: e.tensor_copy(out=dst, in_=st[:, 0:w_]), reads=[stB], writes=[sB])
    for kc in range(4):
        ld(w_glu[kc * 128:(kc + 1) * 128, 0:512], wi[:, kc, 0:512], 512)
    for kc in range(8):
        ld(w_out[kc * 128:(kc + 1) * 128, 0:1024], wo[:, kc, :], 1024)
    YB = [Buf("Yo%d" % i) for i in range(ntile * 4)]

    def load_y(t, hb):
        hT = R.hT[hb]
        for kc in range(8):
            def one(kc=kc):
                si = R.stg_i
                R.stg_i = (si + 1) % 3
                st, stB = R.stg[si], R.stgB[si]
                P.dma("sync", lambda e: e.dma_start(out=st[:, 0:TT], in_=YT[kc * 128:(kc + 1) * 128, t * TT:(t + 1) * TT]), writes=[stB])
                P.op("gpsimd" if kc % 2 else "vector", lambda e: e.tensor_copy(out=hT[:, kc, :], in_=st[:, 0:TT]), reads=[stB], writes=[R.hTB[hb][0]])
            one()
    load_y(0, 0)
    gi = 0
    for t in range(ntile):
        hb = t % 2
        hT = R.hT[hb]
        hB = R.hTB[hb][0]
        for j in range(4):
            def glu(j=j, gi=gi, hT=hT, hB=hB):
                pp, ppB = R.pB[gi % 4], R.pBB[gi % 4]
                sg, sgB = R.sg[gi % 2], R.sgB[gi % 2]

                def mm(kc):
                    P.op("tensor", lambda e: e.matmul(pp[:], lhsT=wi[:, kc, j * 128:(j + 1) * 128], rhs=hT[:, 4 + kc, :], start=(kc == 0), stop=(kc == 3)),
                         reads=[sB, hB], writes=[ppB], inc=(kc == 3))
                for kc in range(4):
                    mm(kc)
                P.op("scalar", lambda e: e.activation(out=sg[:], in_=pp[:], func=AF.Sigmoid), reads=[ppB], writes=[sgB])
                P.op("vector", lambda e: e.tensor_tensor(out=R.aT[:, j, :], in0=hT[:, 4 + j, :], in1=sg[:], op=ALU.mult), reads=[hB, sgB], writes=[R.aTB[j]])
            glu()
            gi += 1
        if t + 1 < ntile:
            load_y(t + 1, 1 - hb)
        lhs = [(hT, kc, hB) for kc in range(4)] + [(R.aT, kc, R.aTB[kc]) for kc in range(4)]
        for s in range(4):
            stage_c_sub(R, wo, sB, 8, X_in, X_out, None, YB[t * 4 + s], t, s, 0, True, lhs=lhs)
    return YB


DEPTH = 2
NT_CORE = 8192


def mod_phase(R, c_row, w_mod_l, b_mod_l, MOD, sbm):
    P = R.P
    cT, cTB, brow, browB, mrow, mrowB = sbm
    P.dma("sync", lambda e: e.dma_start(out=R.vrows[:, 0, :], in_=c_row.rearrange("(kc p) -> kc p", p=128)), writes=[R.vrowsB])
    pc = R.pC[0]
    P.op("tensor", lambda e: e.matmul(pc[:, 0:8], lhsT=R.vrows[:, 0, :], rhs=R.identf[0:8, 0:8], start=True, stop=True),
         reads=[R.vrowsB, R.identfB], writes=[R.pCB])
    P.op("scalar", lambda e: e.activation(out=cT[:], in_=pc[:, 0:8], func=AF.Silu), reads=[R.pCB], writes=[cTB])
    pm = R.pC[1]
    dummy = Buf("modw")

    def tile(n):
        def kstep(kc):
            si = R.stg_i
            R.stg_i = (si + 1) % 3
            st, stB = R.stg[si], R.stgB[si]
            P.dma("sync", lambda e: e.dma_start(out=st[:, 0:512], in_=w_mod_l[kc * 128:(kc + 1) * 128, n * 512:(n + 1) * 512]), writes=[stB])
            P.op("tensor", lambda e: e.matmul(pm[0:1, :], lhsT=cT[:, kc:kc + 1], rhs=st[:, 0:512], start=(kc == 0), stop=(kc == NKC - 1)),
                 reads=[stB, cTB], writes=[R.pCB])
        for kc in range(NKC):
            kstep(kc)
        P.dma("sync", lambda e: e.dma_start(out=brow[:], in_=b_mod_l[n * 512:(n + 1) * 512].rearrange("(o f) -> o f", o=1)), writes=[browB])
        P.op("vector", lambda e: e.tensor_tensor(out=mrow[:], in0=pm[0:1, :], in1=brow[:], op=ALU.add),
             reads=[R.pCB, browB], writes=[mrowB])
        P.dma("gpsimd", lambda e: e.dma_start(out=MOD[n * 512:(n + 1) * 512].rearrange("(o f) -> o f", o=1), in_=mrow[:]),
              reads=[mrowB], writes=[dummy])
    for n in range(9 * D // 512):
        tile(n)


WNAMES = ["w_mod", "b_mod", "ff1_norm_pre", "ff1_norm_post", "ff1_w_in", "ff1_w_out", "mix_norm_pre", "mix_norm_post", "mix_w_in",
          "conv_w", "a_log", "dt_bias", "gdn_norm_w", "s5_a_re", "s5_a_im", "s5_log_dt", "s5_b_re", "s5_b_im", "s5_c_re", "s5_c_im",
          "s5_d", "s5_w_glu", "mix_w_out", "ff2_norm_pre", "ff2_norm_post", "ff2_w_in", "ff2_w_out"]
WSHAPES = {"w_mod": [D, 9 * D], "b_mod": [9 * D], "ff1_norm_pre": [D], "ff1_norm_post": [D], "ff1_w_in": [D, 2 * DFF], "ff1_w_out": [DFF, D],
           "mix_norm_pre": [D], "mix_norm_post": [D], "mix_w_in": [D, 2568], "conv_w": [4, 1536], "a_log": [4], "dt_bias": [4], "gdn_norm_w": [128],
           "s5_a_re": [32, 64], "s5_a_im": [32, 64], "s5_log_dt": [32], "s5_b_re": [32, 64, 16], "s5_b_im": [32, 64, 16],
           "s5_c_re": [32, 16, 64], "s5_c_im": [32, 16, 64], "s5_d": [32, 16], "s5_w_glu": [512, 512], "mix_w_out": [D, D],
           "ff2_norm_pre": [D], "ff2_norm_post": [D], "ff2_w_in": [D, 2 * DFF], "ff2_w_out": [DFF, D]}


def build_nc(NT, depth=DEPTH):
    nc = bass.Bass("TRN2", target_bir_lowering=False)
    dr = lambda n, s, k="ExternalInput", dt=F32: nc.dram_tensor(n, s, dt, kind=k).ap()
    x = dr("x", [NT, D]); c_row = dr("c_row", [D])
    W = {n: dr(n, [depth] + WSHAPES[n]) for n in WNAMES}
    ident = dr("ident", [128, 128]); cst = dr("cst", [128, NCST]); cst2 = dr("cst2", [128, NCST2])
    y = dr("y", [NT, D], "ExternalOutput")
    scr = lambda n, s: nc.dram_tensor(n, s, F32).ap()
    Xs = [scr("xs0", [NT, D]), scr("xs1", [NT, D])]
    yacc = scr("yacc", [NT, D]); PT = scr("ptscr", [2568, NT]); YT = scr("ytscr", [1024, NT])
    MOD = scr("modscr", [depth, 9 * D])
    ntile = NT // TT
    dB = lambda: [Buf() for _ in range(ntile)]
    with ExitStack() as stack:
        P = Prog(nc, stack)
        phase = [0]

        def ffn_like(fn):
            phase[0] += 1
            with ExitStack() as ph:
                R = FFNRes(nc, ph, P, tag="_p%d" % phase[0])
                R.load_ident(ident)
                fn(R, ph)
            P.barrier()

        def mix_like(fn):
            phase[0] += 1
            with ExitStack() as ph:
                M = MixRes(nc, ph, P, tag="_p%d" % phase[0])
                fn(M)
            P.barrier()

        def do_mod(R, ph):
            sb = lambda name, shape, dt: ph.enter_context(nc.sbuf_tensor(name, shape, dt))
            sbm = (sb("cT", [128, NKC], F32), Buf("cT"), sb("brow", [1, 512], F32), Buf("brow"), sb("mrow", [1, 512], F32), Buf("mrow"))
            for l in range(depth):
                mod_phase(R, c_row, W["w_mod"][l], W["b_mod"][l], MOD[l], sbm)
        ffn_like(do_mod)
        cur = x
        for l in range(depth):
            last_layer = l == depth - 1
            nxt = Xs[0]

            def f1(R, ph, l=l, cur=cur, nxt=nxt):
                prep_vectors(R, W["ff1_norm_pre"][l], W["ff1_norm_post"][l], MOD[l], 0, 0.5)
                ffn_phase(R, cur, nxt, yacc, W["ff1_w_in"][l], W["ff1_w_out"][l], NT)
            ffn_like(f1)
            cur = nxt

            def m1(R, ph, l=l, cur=cur):
                prep_vectors(R, W["mix_norm_pre"][l], W["mix_norm_post"][l], MOD[l], 3, 1.0)
                proj_phase(R, cur, W["mix_w_in"][l], PT, NT, dB())
            ffn_like(m1)

            def g(M, l=l):
                gdn_phase(M, PT, YT, W["conv_w"][l], W["a_log"][l], W["dt_bias"][l], W["gdn_norm_w"][l], cst, NT, dB(), dB())
            mix_like(g)

            def s5(M, l=l):
                prm = {k: W["s5_" + k][l] for k in ("a_re", "a_im", "log_dt", "b_re", "b_im", "c_re", "c_im", "d")}
                s5_phase(M, PT, YT, prm, cst, cst2, NT, dB(), dB())
            mix_like(s5)
            nxt = Xs[1]

            def m3(R, ph, l=l, cur=cur, nxt=nxt):
                prep_vectors(R, W["mix_norm_pre"][l], W["mix_norm_post"][l], MOD[l], 3, 1.0)
                mixpost_phase(R, YT, cur, nxt, W["s5_w_glu"][l], W["mix_w_out"][l], NT)
            ffn_like(m3)
            cur = nxt
            nxt = y if last_layer else Xs[0]
            outB = []

            def f2(R, ph, l=l, cur=cur, nxt=nxt):
                prep_vectors(R, W["ff2_norm_pre"][l], W["ff2_norm_post"][l], MOD[l], 6, 0.5)
                outB.extend(ffn_phase(R, cur, nxt, yacc, W["ff2_w_in"][l], W["ff2_w_out"][l], NT))
            ffn_like(f2)
            cur = nxt
        P.final_wait("gpsimd", outB)
        P.emit()
    return nc


def kernel(**inputs):
    x = np.ascontiguousarray(inputs["x"], dtype=np.float32)
    c = np.ascontiguousarray(inputs["c"], dtype=np.float32)
    B, L, _ = x.shape
    common = {n: np.ascontiguousarray(inputs[n], dtype=np.float32) for n in WNAMES}
    common["ident"] = np.eye(128, dtype=np.float32)
    common["cst"] = make_consts()
    common["cst2"] = make_consts2()
    n_cores = B
    in_maps = []
    for core in range(n_cores):
        m = dict(common)
        m["x"] = np.ascontiguousarray(x[core])
        m["c_row"] = np.ascontiguousarray(c[core])
        in_maps.append(m)
    nc = build_nc(L)
    res = run_bass_kernel_spmd(nc, in_maps, core_ids=list(range(n_cores)))
    out = np.empty((B, L, D), dtype=np.float32)
    for core in range(n_cores):
        out[core] = res.results[core]["y"]
    return out
```

```python
import numpy as np
from contextlib import ExitStack
import concourse.bass as bass
import concourse.mybir as mybir
from concourse.bass_utils import run_bass_kernel_spmd

F32 = mybir.dt.float32
BF16 = mybir.dt.bfloat16
I32 = mybir.dt.int32
AF = mybir.ActivationFunctionType
ALU = mybir.AluOpType

ENGS = ("tensor", "vector", "scalar", "gpsimd", "sync")


class Buf:
    __slots__ = ("w", "r", "name")

    def __init__(self, name=""):
        self.w = []
        self.r = []
        self.name = name


class Prog:
    NDMA = 20

    def __init__(self, nc, stack):
        self.nc = nc
        self.q = {e: [] for e in ENGS}
        self.sems = {}
        self.cnt = {}
        for e in ("tensor", "vector", "scalar", "gpsimd"):
            self.sems[e] = stack.enter_context(nc.semaphore("pg_" + e))
            self.cnt[e] = 0
        self.dma_rr = {}
        for qn in ("sync", "gpsimd", "scalar"):
            self.dma_rr[qn] = 0
            for i in range(self.NDMA):
                k = ("dma", qn, i)
                self.sems[k] = stack.enter_context(nc.semaphore("pd_%s_%d" % (qn, i)))
                self.cnt[k] = 0
        self.seen = {e: {} for e in ENGS}
        self.pending_reads = {e: [] for e in ENGS}

    def _waits(self, eng, deps):
        out = []
        for tok in deps:
            if tok is None:
                continue
            key, val = tok
            if key == "tensor" and eng == "tensor":
                continue
            if self.seen[eng].get(key, 0) >= val:
                continue
            self.seen[eng][key] = val
            out.append((key, val))
        return out

    def op(self, eng, fn, reads=(), writes=(), inc=True):
        deps = []
        for b in reads:
            deps.extend(b.w)
        for b in writes:
            deps.extend(b.w)
            deps.extend(b.r)
        waits = self._waits(eng, deps)
        if inc:
            self.cnt[eng] += 1
            tok = (eng, self.cnt[eng])
        else:
            tok = (eng, self.cnt[eng] + 1)
        sem = self.sems[eng]
        self.q[eng].append((waits, fn, sem if inc else None, 1))
        for b in reads:
            b.r.append(tok)
        for b in writes:
            b.w = [tok]
            b.r = []
        return tok

    def dma(self, qn, fn, reads=(), writes=()):
        deps = []
        for b in reads:
            deps.extend(b.w)
        for b in writes:
            deps.extend(b.w)
            deps.extend(b.r)
        i = self.dma_rr[qn]
        self.dma_rr[qn] = (i + 1) % self.NDMA
        k = ("dma", qn, i)
        if self.cnt[k] > 0:
            deps.append((k, self.cnt[k]))
        waits = self._waits(qn, deps)
        self.cnt[k] += 16
        tok = (k, self.cnt[k])
        self.q[qn].append((waits, fn, self.sems[k], 16))
        for b in reads:
            b.r.append(tok)
        for b in writes:
            b.w = [tok]
            b.r = []
        return tok

    def barrier(self):
        toks = [(k, v) for k, v in self.cnt.items() if v > 0]
        for e in ENGS:
            waits = self._waits(e, toks)
            if waits:
                self.q[e].append((waits, None, None, 0))

    def final_wait(self, eng, bufs):
        deps = []
        for b in bufs:
            deps.extend(b.w)
        waits = self._waits(eng, deps)
        self.q[eng].append((waits, None, None, 0))

    def emit(self):
        nc = self.nc
        sems = self.sems
        with nc.Block() as block:
            def mk(name):
                def body(e):
                    for waits, fn, sem, inc in self.q[name]:
                        for key, val in waits:
                            e.wait_ge(sems[key], val)
                        if fn is not None:
                            ins = fn(e)
                            if sem is not None:
                                ins.then_inc(sem, inc)
                return body
            block.sync(mk("sync"))
            block.tensor(mk("tensor"))
            block.vector(mk("vector"))
            block.scalar(mk("scalar"))
            block.gpsimd(mk("gpsimd"))


def check_deadlock(P):
    pos = {e: 0 for e in ENGS}
    val = {}
    key_of = {id(s): k for k, s in P.sems.items()}
    progress = True
    while progress:
        progress = False
        for e in ENGS:
            q = P.q[e]
            while pos[e] < len(q):
                waits, fn, sem, inc = q[pos[e]]
                if all(val.get(k, 0) >= v for k, v in waits):
                    if sem is not None:
                        k = key_of[id(sem)]
                        val[k] = val.get(k, 0) + inc
                    pos[e] += 1
                    progress = True
                else:
                    break
    ok = all(pos[e] == len(P.q[e]) for e in ENGS)
    if not ok:
        for e in ENGS:
            if pos[e] < len(P.q[e]):
                waits = P.q[e][pos[e]][0]
                print("STUCK", e, pos[e], len(P.q[e]), [(k, v, val.get(k, 0)) for k, v in waits if val.get(k, 0) < v])
    return ok


D = 1024
DFF = 2816
NKC = 8
EPS = 1e-6
SLABS = [(0, 8), (8, 7), (15, 7)]
SLAB_MAX = 8
TT = 512


class FFNRes:
    def __init__(self, nc, stack, P, tag=""):
        self.nc, self.P = nc, P
        sb = lambda name, shape, dt: stack.enter_context(nc.sbuf_tensor(name + tag, shape, dt))
        ps = lambda name, shape, dt: stack.enter_context(nc.psum_tensor(name + tag, shape, dt))
        self.wi = [sb("wi%d" % i, [128, NKC, SLAB_MAX * 256], BF16) for i in range(2)]
        self.wo = [sb("wo%d" % i, [128, SLAB_MAX, D], BF16) for i in range(2)]
        self.slabB = [Buf("slab%d" % i) for i in range(2)]
        self.stg = [sb("stg%d" % i, [128, 1024], F32) for i in range(3)]
        self.stgB = [Buf("stg%d" % i) for i in range(3)]
        self.stg_i = 0
        self.hT = [sb("hT%d" % i, [128, NKC, TT], BF16) for i in range(2)]
        self.hTB = [[Buf("hT%d_%d" % (i, s)) for s in range(4)] for i in range(2)]
        self.aT = sb("aT", [128, SLAB_MAX, TT], BF16)
        self.aTB = [Buf("aT%d" % j) for j in range(SLAB_MAX)]
        self.xa = [sb("xa%d" % i, [128, D], F32) for i in range(2)]
        self.xaB = [Buf("xa%d" % i) for i in range(2)]
        self.xn = [sb("xn%d" % i, [128, D], BF16) for i in range(2)]
        self.xnB = [Buf("xn%d" % i) for i in range(2)]
        self.junk = sb("junk", [128, D], BF16)
        self.junkB = Buf("junk")
        self.ss = [sb("ss%d" % i, [128, 2], F32) for i in range(4)]
        self.ssB = [Buf("ss%d" % i) for i in range(4)]
        self.ss_i = 0
        self.sg = [sb("sg%d" % i, [128, TT], F32) for i in range(2)]
        self.sgB = [Buf("sg%d" % i) for i in range(2)]
        self.yb = [sb("yb%d" % i, [128, D], F32) for i in range(2)]
        self.ybB = [Buf("yb%d" % i) for i in range(2)]
        self.xr = [sb("xr%d" % i, [128, D], F32) for i in range(2)]
        self.xrB = [Buf("xr%d" % i) for i in range(2)]
        self.crow = sb("crow", [128, D], F32)
        self.crowB = Buf("crow")
        self.ctmp = sb("ctmp", [128, D], F32)
        self.ctmpB = Buf("ctmp")
        self.acol = sb("acol", [128, NKC], F32)
        self.bcol = sb("bcol", [128, NKC], F32)
        self.tcol = sb("tcol", [128, NKC], F32)
        self.colB = Buf("col")
        self.vrows = sb("vrows", [8, 3, 128], F32)
        self.vrowsB = Buf("vrows")
        self.identf = sb("identf", [128, 128], F32)
        self.identfB = Buf("identf")
        self.ident = sb("ident_sb", [128, 128], BF16)
        self.epsc = sb("epsc", [128, 1], F32)
        self.epscB = Buf("epsc")
        P.op("vector", lambda e: e.memset(self.epsc[:], D * EPS), writes=[self.epscB])
        self.identB = Buf("ident")
        self.pB = [ps("pB%d" % i, [128, TT], F32) for i in range(4)]
        self.pBB = [Buf("pB%d" % i) for i in range(4)]
        self.pC = [ps("pC%d" % i, [128, TT], F32) for i in range(2)]
        self.pCB = Buf("pC")
        self.pT = [ps("pT%d" % i, [128, NKC, 128], BF16) for i in range(2)]
        self.pTB = [Buf("pT%d" % i) for i in range(2)]

    def load_ident(self, ident_dram):
        P = self.P
        st = self.stg[0]
        P.dma("sync", lambda e: e.dma_start(out=st[:, 0:128], in_=ident_dram), writes=[self.stgB[0]])
        P.dma("sync", lambda e: e.dma_start(out=self.identf[:], in_=ident_dram), writes=[self.identfB])
        P.op("vector", lambda e: e.tensor_copy(out=self.ident[:], in_=st[:, 0:128]),
             reads=[self.stgB[0]], writes=[self.identB])


def load_slab_gen(R, w_in, w_out, slab, buf, dff=DFF, gated=True):
    P = R.P
    j0, n = slab
    wi, wo, sB = R.wi[buf], R.wo[buf], R.slabB[buf]
    pieces = []
    ncol = n * 128
    for kc in range(NKC):
        for part in range(2 if gated else 1):
            c0 = part * dff + j0 * 128
            done = 0
            while done < ncol:
                w = min(1024, ncol - done)
                pieces.append(("in", kc, part * ncol + done, c0 + done, w))
                done += w
    if w_out is not None:
        for j in range(n):
            pieces.append(("out", j, 0, (j0 + j) * 128, 1024))
    for kind, a, dst0, src0, w in pieces:
        si = R.stg_i
        R.stg_i = (si + 1) % 3
        st, stB = R.stg[si], R.stgB[si]
        if kind == "in":
            src = w_in[a * 128:(a + 1) * 128, src0:src0 + w]
            dst = wi[:, a, dst0:dst0 + w]
        else:
            src = w_out[src0:src0 + 128, 0:1024]
            dst = wo[:, a, :]
        P.dma("sync", (lambda e, st=st, src=src, w=w: e.dma_start(out=st[:, 0:w], in_=src)), writes=[stB])
        P.op("gpsimd", (lambda e, st=st, dst=dst, w=w: e.tensor_copy(out=dst, in_=st[:, 0:w])),
             reads=[stB], writes=[sB])
        yield


def load_slab(R, w_in, w_out, slab, buf, dff=DFF, gated=True):
    for _ in load_slab_gen(R, w_in, w_out, slab, buf, dff, gated):
        pass


def prep_vectors(R, w_pre, w_post, mod, ioff, gate_scale, modB=None):
    P = R.P
    colv = lambda v, off: v[off:off + D].rearrange("(kc p) -> p kc", p=128)
    rowv = lambda v, off: v[off:off + D].partition_broadcast(128)
    rows = lambda v, off: v[off:off + D].rearrange("(kc p) -> kc p", p=128)
    P.dma("sync", lambda e: e.dma_start(out=R.vrows[:, 0, :], in_=rows(w_pre, 0)), writes=[R.vrowsB])
    P.dma("sync", lambda e: e.dma_start(out=R.vrows[:, 1, :], in_=rows(mod, ioff * D)), reads=(list(modB) if modB else []), writes=[R.vrowsB])
    P.dma("sync", lambda e: e.dma_start(out=R.vrows[:, 2, :], in_=rows(mod, (ioff + 1) * D)), reads=(list(modB) if modB else []), writes=[R.vrowsB])
    pc = R.pC[0]

    def tr(i):
        P.op("tensor", lambda e: e.matmul(pc[:, i * 8:(i + 1) * 8], lhsT=R.vrows[:, i, :], rhs=R.identf[0:8, 0:8], start=True, stop=True),
             reads=[R.vrowsB, R.identfB], writes=[R.pCB], inc=(i == 2))
    for i in range(3):
        tr(i)
    P.op("vector", lambda e: e.tensor_copy(out=R.acol[:], in_=pc[:, 0:8]), reads=[R.pCB], writes=[R.colB])
    P.op("vector", lambda e: e.tensor_copy(out=R.bcol[:], in_=pc[:, 8:16]), reads=[R.pCB], writes=[R.colB])
    P.op("vector", lambda e: e.tensor_copy(out=R.tcol[:], in_=pc[:, 16:24]), reads=[R.pCB], writes=[R.colB])
    P.op("vector", lambda e: e.tensor_scalar(out=R.tcol[:], in0=R.tcol[:], scalar1=1.0, scalar2=32.0, op0=ALU.add, op1=ALU.mult),
         reads=[R.colB], writes=[R.colB])
    P.op("vector", lambda e: e.tensor_tensor(out=R.acol[:], in0=R.acol[:], in1=R.tcol[:], op=ALU.mult),
         reads=[R.colB], writes=[R.colB])
    P.dma("sync", lambda e: e.dma_start(out=R.crow[:], in_=rowv(w_post, 0)), writes=[R.crowB])
    P.dma("sync", lambda e: e.dma_start(out=R.ctmp[:], in_=rowv(mod, (ioff + 2) * D)), reads=(list(modB) if modB else []), writes=[R.ctmpB])
    P.op("vector", lambda e: e.scalar_tensor_tensor(out=R.crow[:], in0=R.crow[:], scalar=32.0 * gate_scale, in1=R.ctmp[:],
                                                    op0=ALU.mult, op1=ALU.mult),
         reads=[R.crowB, R.ctmpB], writes=[R.crowB])


def stage_a_sub(R, X_in, t, s, hbuf, part):
    P = R.P
    i = (t * 4 + s) % 2
    xa, xaB, xn, xnB = R.xa[i], R.xaB[i], R.xn[i], R.xnB[i]
    if part == 0:
        r0 = t * TT + s * 128
        P.dma("sync", lambda e: e.dma_start(out=xa[:], in_=X_in[r0:r0 + 128, :]), writes=[xaB])
        k = R.ss_i
        R.ss_i = (k + 1) % 4
        ss, ssB = R.ss[k], R.ssB[k]
        P.op("scalar", lambda e: e.activation(out=R.junk[:], in_=xa[:], func=AF.Square, accum_out=ss[:, 0:1]),
             reads=[xaB], writes=[R.junkB, ssB])
        P.op("scalar", lambda e: e.activation(out=ss[:, 1:2], in_=ss[:, 0:1], func=AF.Sqrt, bias=R.epsc[:, 0:1], scale=1.0),
             reads=[ssB, R.epscB], writes=[ssB])
        P.op("vector", lambda e: e.reciprocal(out=ss[:, 1:2], in_=ss[:, 1:2]), reads=[ssB], writes=[ssB])
        P.op("scalar", lambda e: e.activation(out=xn[:], in_=xa[:], func=AF.Copy, scale=ss[:, 1:2]),
             reads=[xaB, ssB], writes=[xnB])
    else:
        pt, ptB = R.pT[i], R.pTB[i]
        for kc in range(NKC):
            P.op("tensor", (lambda e, kc=kc: e.transpose(out=pt[:, kc, :], in_=xn[:, kc * 128:(kc + 1) * 128], identity=R.ident[:])),
                 reads=[xnB, R.identB], writes=[ptB], inc=(kc == NKC - 1))
        hT, hB = R.hT[hbuf], R.hTB[hbuf][s]
        for kc in range(NKC):
            eng = "vector" if kc % 2 == 0 else "gpsimd"
            if eng == "gpsimd":
                eng = "vector"
            P.op(eng, (lambda e, kc=kc: e.tensor_scalar(out=hT[:, kc, s * 128:(s + 1) * 128], in0=pt[:, kc, :],
                                                       scalar1=R.acol[:, kc:kc + 1], scalar2=R.bcol[:, kc:kc + 1],
                                                       op0=ALU.mult, op1=ALU.add)),
                 reads=[ptB, R.colB], writes=[hB])


def stage_b_group(R, wi, sB, hT, hTBl, j, nch, gidx, aj=None):
    P = R.P
    if aj is None:
        aj = j
    k = gidx % 2
    pg, pu, pgB, puB = R.pB[2 * k], R.pB[2 * k + 1], R.pBB[2 * k], R.pBB[2 * k + 1]
    sg, sgB = R.sg[k], R.sgB[k]

    def mm(pp, ppB, c0, kc):
        P.op("tensor", lambda e: e.matmul(pp[:], lhsT=wi[:, kc, c0:c0 + 128], rhs=hT[:, kc, :],
                                          start=(kc == 0), stop=(kc == NKC - 1)),
             reads=[sB] + hTBl, writes=[ppB], inc=(kc == NKC - 1))
    for (pp, ppB, c0) in ((pg, pgB, j * 128), (pu, puB, nch * 128 + j * 128)):
        for kc in range(NKC):
            mm(pp, ppB, c0, kc)
    P.op("scalar", lambda e: e.activation(out=sg[:], in_=pg[:], func=AF.Silu), reads=[pgB], writes=[sgB])
    P.op("vector", lambda e: e.tensor_tensor(out=R.aT[:, aj, :], in0=sg[:], in1=pu[:], op=ALU.mult),
         reads=[sgB, puB], writes=[R.aTB[aj]])


def stage_c_sub(R, wo, sB, nch, X_in, X_out, Yacc, yB, t, s, sl, last, lhs=None):
    P = R.P
    if lhs is None:
        lhs = [(R.aT, j, R.aTB[j]) for j in range(nch)]
    r0 = t * TT + s * 128
    yi = (t * 4 + s) % 2
    yb, ybB, xr, xrB = R.yb[yi], R.ybB[yi], R.xr[yi], R.xrB[yi]
    if sl > 0:
        P.dma("sync", lambda e: e.dma_start(out=yb[:], in_=Yacc[r0:r0 + 128, :]), reads=[yB], writes=[ybB])
    if last:
        P.dma("sync", lambda e: e.dma_start(out=xr[:], in_=X_in[r0:r0 + 128, :]), writes=[xrB])

    def mm(half, j):
        lt, li, lB = lhs[j]
        P.op("tensor", lambda e: e.matmul(R.pC[half][:], lhsT=lt[:, li, s * 128:(s + 1) * 128],
                                          rhs=wo[:, j, half * 512:(half + 1) * 512],
                                          start=(j == 0), stop=(j == nch - 1)),
             reads=[sB] + (lB if isinstance(lB, list) else [lB]), writes=[R.pCB], inc=(j == nch - 1))
    for half in range(2):
        for j in range(nch):
            mm(half, j)

    def evac(half):
        hs = slice(half * 512, (half + 1) * 512)
        if sl == 0:
            if half == 0:
                P.op("vector", lambda e: e.tensor_copy(out=yb[:, hs], in_=R.pC[half][:]), reads=[R.pCB], writes=[ybB])
            else:
                P.op("scalar", lambda e: e.copy(out=yb[:, hs], in_=R.pC[half][:]), reads=[R.pCB], writes=[ybB])
        else:
            P.op("vector", lambda e: e.tensor_tensor(out=yb[:, hs], in0=yb[:, hs], in1=R.pC[half][:], op=ALU.add),
                 reads=[R.pCB, ybB], writes=[ybB])
    evac(0)
    evac(1)
    if not last:
        P.dma("gpsimd", lambda e: e.dma_start(out=Yacc[r0:r0 + 128, :], in_=yb[:]), reads=[ybB], writes=[yB])
    else:
        k = R.ss_i
        R.ss_i = (k + 1) % 4
        ss, ssB = R.ss[k], R.ssB[k]
        P.op("scalar", lambda e: e.activation(out=R.junk[:], in_=yb[:], func=AF.Square, accum_out=ss[:, 0:1]),
             reads=[ybB], writes=[R.junkB, ssB])
        P.op("scalar", lambda e: e.activation(out=ss[:, 1:2], in_=ss[:, 0:1], func=AF.Sqrt, bias=R.epsc[:, 0:1], scale=1.0),
             reads=[ssB, R.epscB], writes=[ssB])
        P.op("vector", lambda e: e.reciprocal(out=ss[:, 1:2], in_=ss[:, 1:2]), reads=[ssB], writes=[ssB])
        P.op("scalar", lambda e: e.activation(out=yb[:], in_=yb[:], func=AF.Copy, scale=ss[:, 1:2]),
             reads=[ybB, ssB], writes=[ybB])
        P.op("gpsimd", lambda e: e.tensor_tensor(out=yb[:], in0=yb[:], in1=R.crow[:], op=ALU.mult),
             reads=[ybB, R.crowB], writes=[ybB])
        P.op("vector", lambda e: e.tensor_tensor(out=xr[:], in0=xr[:], in1=yb[:], op=ALU.add),
             reads=[ybB, xrB], writes=[xrB])
        P.dma("gpsimd", lambda e: e.dma_start(out=X_out[r0:r0 + 128, :], in_=xr[:]), reads=[xrB], writes=[yB])


def ffn_phase(R, X_in, X_out, Yacc, w_in, w_out, NT, first_slab_loaded=False, next_loader=None):
    P = R.P
    ntile = NT // TT
    nsl = len(SLABS)
    if not first_slab_loaded:
        load_slab(R, w_in, w_out, SLABS[0], 0)
    YB = [Buf("Y%d" % i) for i in range(ntile * 4)]
    gidx = 0
    for sl in range(nsl):
        buf = sl % 2
        j0, nch = SLABS[sl]
        last = sl == nsl - 1
        ldr = None
        if sl + 1 < nsl:
            ldr = load_slab_gen(R, w_in, w_out, SLABS[sl + 1], (sl + 1) % 2)
        elif next_loader is not None:
            next_loader((sl + 1) % 2)
        for s in range(4):
            stage_a_sub(R, X_in, 0, s, 0, 0)
            stage_a_sub(R, X_in, 0, s, 0, 1)
        for t in range(ntile):
            hb = t % 2
            for j in range(nch):
                stage_b_group(R, R.wi[buf], R.slabB[buf], R.hT[hb], R.hTB[hb], j, nch, gidx)
                gidx += 1
                if ldr is not None:
                    next(ldr, None)
                if t + 1 < ntile and j < 8:
                    stage_a_sub(R, X_in, t + 1, j // 2, 1 - hb, j % 2)
            if t + 1 < ntile:
                for jj in range(nch, 8):
                    stage_a_sub(R, X_in, t + 1, jj // 2, 1 - hb, jj % 2)
            for s in range(4):
                stage_c_sub(R, R.wo[buf], R.slabB[buf], nch, X_in, X_out, Yacc, YB[t * 4 + s], t, s, sl, last)
        if ldr is not None:
            for _ in ldr:
                pass
    return YB


NH = 4
NCST = 384
STOP = 0


class StopEmit(Exception):
    pass


def stop_at(k):
    if STOP == k:
        raise StopEmit()


def make_consts():
    c = np.zeros((128, NCST), np.float32)
    c[:, 0:128] = np.eye(128)
    c[:, 128:256] = 1.0
    k = np.arange(64)
    c[0:64, 256:320] = (k[:, None] <= k[None, :])
    c[0:64, 320:384] = (k[:, None] > k[None, :])
    return c


def load_cols(R, w, c0, ncol, buf):
    P = R.P
    wi, sB = R.wi[buf], R.slabB[buf]

    def piece(kc, d0, w_):
        si = R.stg_i
        R.stg_i = (si + 1) % 3
        st, stB = R.stg[si], R.stgB[si]
        P.dma("sync", lambda e: e.dma_start(out=st[:, 0:w_], in_=w[kc * 128:(kc + 1) * 128, c0 + d0:c0 + d0 + w_]), writes=[stB])
        P.op("gpsimd", lambda e: e.tensor_copy(out=wi[:, kc, d0:d0 + w_], in_=st[:, 0:w_]), reads=[stB], writes=[sB])
    for kc in range(NKC):
        d0 = 0
        while d0 < ncol:
            w_ = min(1024, ncol - d0)
            piece(kc, d0, w_)
            d0 += w_


def proj_phase(R, X_in, w_mix, PT, NT, PTB):
    P = R.P
    load_cols(R, w_mix, 0, 2048, 0)
    load_cols(R, w_mix, 2048, 520, 1)
    ntile = NT // TT
    chunks = [(0, j * 128, 128, j * 128) for j in range(16)]
    chunks += [(1, 8 + j * 128, 128, 2056 + j * 128) for j in range(4)]
    chunks += [(1, 0, 8, 2048)]
    gi = 0
    for s in range(4):
        stage_a_sub(R, X_in, 0, s, 0, 0)
        stage_a_sub(R, X_in, 0, s, 0, 1)

    def out_chunk(t, hb, buf, lc, M, row, gi):
        pp, ppB = R.pB[gi % 4], R.pBB[gi % 4]
        sg, sgB = R.sg[gi % 2], R.sgB[gi % 2]
        wi, sB, hT = R.wi[buf], R.slabB[buf], R.hT[hb]

        def mm(kc):
            P.op("tensor", lambda e: e.matmul(pp[0:M, :], lhsT=wi[:, kc, lc:lc + M], rhs=hT[:, kc, :], start=(kc == 0), stop=(kc == NKC - 1)),
                 reads=[sB] + R.hTB[hb], writes=[ppB], inc=(kc == NKC - 1))
        for kc in range(NKC):
            mm(kc)
        if gi % 2 == 0:
            P.op("vector", lambda e: e.tensor_copy(out=sg[0:M, :], in_=pp[0:M, :]), reads=[ppB], writes=[sgB])
        else:
            P.op("scalar", lambda e: e.copy(out=sg[0:M, :], in_=pp[0:M, :]), reads=[ppB], writes=[sgB])
        P.dma("gpsimd", lambda e: e.dma_start(out=PT[row:row + M, t * TT:(t + 1) * TT], in_=sg[0:M, :]), reads=[sgB], writes=[PTB[t]])
    for t in range(ntile):
        hb = t % 2
        for ci, (buf, lc, M, row) in enumerate(chunks):
            out_chunk(t, hb, buf, lc, M, row, gi)
            gi += 1
            if t + 1 < ntile and ci < 8:
                stage_a_sub(R, X_in, t + 1, ci // 2, 1 - hb, ci % 2)


class MixRes:
    def __init__(self, nc, stack, P, tag=""):
        self.nc, self.P = nc, P
        self._sb = lambda name, shape, dt=F32: stack.enter_context(nc.sbuf_tensor("m_" + name + tag, shape, dt))
        self._ps = lambda name, shape, dt=F32: stack.enter_context(nc.psum_tensor("m_" + name + tag, shape, dt))
        self.bufs = {}

    def sb(self, name, shape, dt=F32):
        t = self._sb(name, shape, dt)
        self.bufs[name] = Buf(name)
        setattr(self, name, t)
        setattr(self, name + "B", self.bufs[name])
        return t

    def ps(self, name, shape, dt=F32):
        t = self._ps(name, shape, dt)
        self.bufs[name] = Buf(name)
        setattr(self, name, t)
        setattr(self, name + "B", self.bufs[name])
        return t


def gdn_phase(M, PT, YT, conv_w, a_log, dt_bias, gnw, cst_dram, NT, PTB, YTB):
    P = M.P
    ntile = NT // TT
    sb, ps = M.sb, M.ps
    G2 = 2
    cst = sb("cst", [128, NCST])
    ident = cst[:, 0:128]
    ones = cst[:, 128:256]
    LE = cst[0:64, 256:320]
    GT = cst[0:64, 320:384]
    P.dma("sync", lambda e: e.dma_start(out=cst[:], in_=cst_dram), writes=[M.cstB])
    sb("LE4", [64, G2, 64]); sb("GT4", [64, G2, 64]); sb("I4", [64, G2, 64])
    for h in range(G2):
        P.op("vector", (lambda e, h=h: e.tensor_copy(out=M.LE4[:, h, :], in_=LE)), reads=[M.cstB], writes=[M.LE4B])
        P.op("vector", (lambda e, h=h: e.tensor_copy(out=M.GT4[:, h, :], in_=GT)), reads=[M.cstB], writes=[M.GT4B])
        P.op("vector", (lambda e, h=h: e.tensor_copy(out=M.I4[:, h, :], in_=cst[0:64, 0:64])), reads=[M.cstB], writes=[M.I4B])
    sb("epsk", [128, 1]); sb("epsq", [128, 1]); sb("epsn", [128, 1]); sb("one1", [128, 1])
    P.op("vector", lambda e: e.memset(M.epsk[:], 1e-6), writes=[M.epskB])
    P.op("vector", lambda e: e.memset(M.epsq[:], 128e-6), writes=[M.epsqB])
    P.op("vector", lambda e: e.memset(M.epsn[:], 1e-6), writes=[M.epsnB])
    P.op("vector", lambda e: e.memset(M.one1[:], 1.0), writes=[M.one1B])

    class Grp:
        pass
    groups = []
    for gi in range(2):
        G = Grp()
        G.h0 = gi * G2
        sfx = "_g%d" % gi

        def gsb(name, shape, G=G, sfx=sfx):
            t = sb(name + sfx, shape)
            setattr(G, name, t)
            setattr(G, name + "B", getattr(M, name + sfx + "B"))

        def gps(name, shape, G=G, sfx=sfx):
            t = ps(name + sfx, shape)
            setattr(G, name, t)
            setattr(G, name + "B", getattr(M, name + sfx + "B"))
        banks = [ps("bk%d" % k + sfx, [128, 512]) for k in range(4)]
        r4 = lambda ap: ap.rearrange("p (a h c) -> p a h c", a=2, h=G2)
        for nm, ap in (("psD", r4(banks[0][0:64, 0:256])), ("psG", r4(banks[0][0:64, 256:512])),
                       ("psI", r4(banks[1][0:64, 0:256])), ("psU", r4(banks[1][0:64, 256:512])),
                       ("psX", banks[2][:, 0:256]), ("psY", banks[2][:, 256:512]),
                       ("psZ", banks[3][:, 0:256]), ("psS", banks[3][:, 256:384])):
            setattr(G, nm, ap)
        bankB = [Buf("bk%d" % k + sfx) for k in range(4)]
        for nm, k in (("psD", 0), ("psG", 0), ("psI", 1), ("psU", 1), ("psX", 2), ("psY", 2), ("psZ", 3), ("psS", 3)):
            setattr(G, nm + "B", bankB[k])
        gsb("S", [128, G2, 128]); gsb("yg", [128, G2, TT])
        gsb("ba", [64, 8]); gsb("bt", [64, G2]); gsb("nbt", [64, G2]); gsb("g", [64, G2]); gsb("gcs", [64, G2]); gsb("egc", [64, G2])
        gsb("egl", [128, G2]); gsb("egd", [64, G2]); gsb("begc", [64, G2])
        gsb("G12", [64, 2, G2, 64]); gsb("eD", [64, 2, G2, 64]); gsb("dec", [64, 2, G2, 64])
        gsb("AA", [64, 2, G2, 64]); gsb("PU", [64, G2, 64]); gsb("attT", [64, G2, 64]); gsb("tL", [64, G2, 64])
        gsb("vb", [64, G2, 128]); gsb("kbg", [64, G2, 128]); gsb("kst", [64, G2, 128]); gsb("wv", [64, G2, 128]); gsb("kcT", [128, G2, 64])
        gsb("vn", [64, G2, 128]); gsb("o1", [64, G2, 128]); gsb("osq", [64, G2, 128]); gsb("on", [64, G2, 128]); gsb("ssq", [64, 2 * G2])
        P.op("vector", (lambda e, G=G: e.memset(G.S[:], 0.0)), writes=[G.SB])
        groups.append(G)
    GA, GB = groups
    psS0, psS0B = GA.psS, GA.psSB
    sb("cwr", [4, 1536]); sb("cw", [128, 12, 4])
    P.dma("sync", lambda e: e.dma_start(out=M.cwr[:], in_=conv_w), writes=[M.cwrB])
    for ct in range(12):
        P.op("tensor", (lambda e, ct=ct: e.matmul(psS0[:, ct * 4:(ct + 1) * 4], lhsT=M.cwr[0:4, ct * 128:(ct + 1) * 128], rhs=cst[0:4, 0:4], start=True, stop=True)),
             reads=[M.cwrB, M.cstB], writes=[psS0B], inc=(ct == 11))
    P.op("vector", lambda e: e.tensor_copy(out=M.cw[:].rearrange("p a b -> p (a b)"), in_=psS0[:, 0:48]), reads=[psS0B], writes=[M.cwB])
    sb("gnr", [1, 128]); sb("gnc", [128, 1])
    P.dma("sync", lambda e: e.dma_start(out=M.gnr[:], in_=gnw.rearrange("(o f) -> o f", o=1)), writes=[M.gnrB])
    P.op("tensor", lambda e: e.matmul(psS0[:, 0:1], lhsT=M.gnr[0:1, :], rhs=cst[0:1, 0:1], start=True, stop=True), reads=[M.gnrB, M.cstB], writes=[psS0B])
    P.op("vector", lambda e: e.tensor_copy(out=M.gnc[:], in_=psS0[:, 0:1]), reads=[psS0B], writes=[M.gncB])
    sb("nA", [64, NH]); sb("dtb", [64, NH]); sb("adr", [1, 8])
    P.dma("sync", lambda e: e.dma_start(out=M.adr[:, 0:4], in_=a_log.rearrange("(o f) -> o f", o=1)), writes=[M.adrB])
    P.dma("sync", lambda e: e.dma_start(out=M.adr[:, 4:8], in_=dt_bias.rearrange("(o f) -> o f", o=1)), writes=[M.adrB])
    P.op("tensor", lambda e: e.matmul(psS0[0:64, 0:8], lhsT=cst[0:1, 128:192], rhs=M.adr[0:1, :], start=True, stop=True), reads=[M.adrB, M.cstB], writes=[psS0B])
    P.op("vector", lambda e: e.tensor_copy(out=M.nA[:], in_=psS0[0:64, 0:4]), reads=[psS0B], writes=[M.nAB])
    P.op("vector", lambda e: e.tensor_copy(out=M.dtb[:], in_=psS0[0:64, 4:8]), reads=[psS0B], writes=[M.dtbB])
    P.op("scalar", lambda e: e.activation(out=M.nA[:], in_=M.nA[:], func=AF.Exp), reads=[M.nAB], writes=[M.nAB])
    P.op("vector", lambda e: e.tensor_scalar(out=M.nA[:], in0=M.nA[:], scalar1=-1.0, scalar2=None, op0=ALU.mult), reads=[M.nAB], writes=[M.nAB])
    sb("qkv", [128, 12, TT]); sb("xin", [128, TT + 3]); sb("acc", [128, TT]); sb("sq", [128, TT]); sb("rn", [128, TT])
    sb("sz", [128, NH, TT]); sb("bar", [8, TT])

    def conv_tile(t, ct):
        t0 = t * TT
        r0 = ct * 128
        if t == 0:
            P.op("gpsimd", lambda e: e.memset(M.xin[:, 0:3], 0.0), writes=[M.xinB])
            P.dma("sync", lambda e: e.dma_start(out=M.xin[:, 3:TT + 3], in_=PT[r0:r0 + 128, 0:TT]), reads=[PTB[0]], writes=[M.xinB])
        else:
            P.dma("sync", lambda e: e.dma_start(out=M.xin[:], in_=PT[r0:r0 + 128, t0 - 3:t0 + TT]), reads=[PTB[t - 1], PTB[t]], writes=[M.xinB])
        P.op("vector", lambda e: e.tensor_scalar(out=M.acc[:], in0=M.xin[:, 0:TT], scalar1=M.cw[:, ct, 0:1], scalar2=None, op0=ALU.mult),
             reads=[M.xinB, M.cwB], writes=[M.accB])
        for j in range(1, 4):
            P.op("vector", (lambda e, j=j: e.scalar_tensor_tensor(out=M.acc[:], in0=M.xin[:, j:j + TT], scalar=M.cw[:, ct, j:j + 1], in1=M.acc[:],
                                                                  op0=ALU.mult, op1=ALU.add)), reads=[M.xinB, M.cwB, M.accB], writes=[M.accB])
        P.op("scalar", lambda e: e.activation(out=M.qkv[:, ct, :], in_=M.acc[:], func=AF.Silu), reads=[M.accB], writes=[M.qkvB])
        if ct < 8:
            P.op("scalar", lambda e: e.activation(out=M.sq[:], in_=M.qkv[:, ct, :], func=AF.Square), reads=[M.qkvB], writes=[M.sqB])
            for hf, Gx in enumerate((GA, GB)):
                def half(hf=hf, Gx=Gx):
                    cs_ = slice(hf * 256, (hf + 1) * 256)
                    P.op("tensor", lambda e: e.matmul(Gx.psX[:], lhsT=ones, rhs=M.sq[:, cs_], start=True, stop=True), reads=[M.sqB, M.cstB], writes=[Gx.psXB])
                    P.op("vector", lambda e: e.tensor_copy(out=M.rn[:, cs_], in_=Gx.psX[:]), reads=[Gx.psXB], writes=[M.rnB])
                    if ct < 4:
                        P.op("scalar", lambda e: e.activation(out=M.rn[:, cs_], in_=M.rn[:, cs_], func=AF.Ln, bias=M.epsq[:, 0:1], scale=128.0),
                             reads=[M.rnB, M.epsqB], writes=[M.rnB])
                    else:
                        P.op("scalar", lambda e: e.activation(out=M.rn[:, cs_], in_=M.rn[:, cs_], func=AF.Ln, bias=M.epsk[:, 0:1], scale=1.0),
                             reads=[M.rnB, M.epskB], writes=[M.rnB])
                half()
            P.op("scalar", lambda e: e.activation(out=M.rn[:], in_=M.rn[:], func=AF.Exp, scale=-0.5), reads=[M.rnB], writes=[M.rnB])
            P.op("gpsimd", lambda e: e.tensor_tensor(out=M.qkv[:, ct, :], in0=M.qkv[:, ct, :], in1=M.rn[:], op=ALU.mult),
                 reads=[M.qkvB, M.rnB], writes=[M.qkvB])

    def z_tile(t, h):
        t0 = t * TT
        P.dma("sync", lambda e: e.dma_start(out=M.sz[:, h, :], in_=PT[1536 + h * 128:1536 + (h + 1) * 128, t0:t0 + TT]), reads=[PTB[t]], writes=[M.szB])
        P.op("scalar", lambda e: e.activation(out=M.sz[:, h, :], in_=M.sz[:, h, :], func=AF.Silu), reads=[M.szB], writes=[M.szB])
        P.op("gpsimd", lambda e: e.tensor_scalar(out=M.sz[:, h, :], in0=M.sz[:, h, :], scalar1=M.gnc[:, 0:1], scalar2=None, op0=ALU.mult),
             reads=[M.szB, M.gncB], writes=[M.szB])

    def mm(out, lhsT, rhs, reads, writes, inc=True):
        P.op("tensor", lambda e: e.matmul(out, lhsT=lhsT, rhs=rhs, start=True, stop=True), reads=reads, writes=writes, inc=inc)

    def chunk(n, G):
        h0 = G.h0
        HR = range(G2)
        c0 = n * 64
        cs = slice(c0, c0 + 64)
        qT = lambda h: M.qkv[:, h0 + h, cs]
        kT = lambda h: M.qkv[:, 4 + h0 + h, cs]
        vT = lambda h: M.qkv[:, 8 + h0 + h, cs]
        mm(G.psS[0:64, 0:8], M.bar[0:8, cs], cst[0:8, 0:8], [M.barB, M.cstB], [G.psSB])
        P.op("vector", lambda e: e.tensor_copy(out=G.ba[:], in_=G.psS[0:64, 0:8]), reads=[G.psSB], writes=[G.baB])
        yield
        P.op("scalar", lambda e: e.activation(out=G.bt[:], in_=G.ba[:, h0:h0 + G2], func=AF.Sigmoid), reads=[G.baB], writes=[G.btB])
        P.op("vector", lambda e: e.tensor_tensor(out=G.g[:], in0=G.ba[:, 4 + h0:4 + h0 + G2], in1=M.dtb[:, h0:h0 + G2], op=ALU.add), reads=[G.baB, M.dtbB], writes=[G.gB])
        yield
        P.op("vector", lambda e: e.tensor_scalar(out=G.nbt[:], in0=G.bt[:], scalar1=-1.0, scalar2=None, op0=ALU.mult), reads=[G.btB], writes=[G.nbtB])
        P.op("scalar", lambda e: e.activation(out=G.g[:], in_=G.g[:], func=AF.Exp), reads=[G.gB], writes=[G.gB])
        yield
        P.op("scalar", lambda e: e.activation(out=G.g[:], in_=G.g[:], func=AF.Ln, bias=M.one1[0:64, 0:1], scale=1.0), reads=[G.gB, M.one1B], writes=[G.gB])
        yield
        P.op("vector", lambda e: e.tensor_tensor(out=G.g[:], in0=G.g[:], in1=M.nA[:, h0:h0 + G2], op=ALU.mult), reads=[G.gB, M.nAB], writes=[G.gB])
        yield
        mm(G.psS[0:64, 8:8 + G2], LE, G.g[:], [G.gB, M.cstB], [G.psSB], inc=False)
        mm(G.psS[:, 12:12 + G2], cst[0:64, 128:256], G.g[:], [G.gB, M.cstB], [G.psSB])
        for h in HR:
            P.op("gpsimd", (lambda e, h=h: e.tensor_scalar(out=G.G12[:, 0, h, :], in0=LE, scalar1=G.g[:, h:h + 1], scalar2=None, op0=ALU.mult)),
                 reads=[G.gB, M.cstB], writes=[G.G12B])
            P.op("gpsimd", (lambda e, h=h: e.tensor_scalar(out=G.G12[:, 1, h, :], in0=GT, scalar1=G.g[:, h:h + 1], scalar2=None, op0=ALU.mult)),
                 reads=[G.gB, M.cstB], writes=[G.G12B])
        yield
        P.op("vector", lambda e: e.tensor_copy(out=G.gcs[:], in_=G.psS[0:64, 8:8 + G2]), reads=[G.psSB], writes=[G.gcsB])
        P.op("vector", lambda e: e.tensor_copy(out=G.egl[:], in_=G.psS[:, 12:12 + G2]), reads=[G.psSB], writes=[G.eglB])
        P.op("scalar", lambda e: e.activation(out=G.egc[:], in_=G.gcs[:], func=AF.Exp), reads=[G.gcsB], writes=[G.egcB])
        P.op("scalar", lambda e: e.activation(out=G.egl[:], in_=G.egl[:], func=AF.Exp), reads=[G.eglB], writes=[G.eglB])
        for h in HR:
            mm(G.psD[:, 0, h, :], G.G12[:, 0, h, :], GT, [G.G12B, M.cstB], [G.psDB], inc=False)
            mm(G.psD[:, 1, h, :], G.G12[:, 1, h, :], LE, [G.G12B, M.cstB], [G.psDB], inc=(h == G2 - 1))
        for h in HR:
            mm(G.psG[:, 0, h, :], kT(h), kT(h), [M.qkvB], [G.psGB], inc=False)
            mm(G.psU[:, 1, h, :], kT(h), qT(h), [M.qkvB], [G.psUB], inc=(h == G2 - 1))
        yield
        P.op("vector", lambda e: e.tensor_tensor(out=G.egd[:], in0=G.psS[0:64, 12:12 + G2], in1=G.gcs[:], op=ALU.subtract), reads=[G.psSB, G.gcsB], writes=[G.egdB])
        P.op("vector", lambda e: e.tensor_tensor(out=G.begc[:], in0=G.bt[:], in1=G.egc[:], op=ALU.mult), reads=[G.btB, G.egcB], writes=[G.begcB])
        P.op("vector", lambda e: e.tensor_copy(out=G.eD[:], in_=G.psD[:]), reads=[G.psDB], writes=[G.eDB])
        P.op("scalar", lambda e: e.activation(out=G.eD[:], in_=G.eD[:], func=AF.Exp), reads=[G.eDB], writes=[G.eDB])
        yield
        P.op("scalar", lambda e: e.activation(out=G.egd[:], in_=G.egd[:], func=AF.Exp), reads=[G.egdB], writes=[G.egdB])
        P.op("gpsimd", lambda e: e.tensor_tensor(out=G.dec[:, 0], in0=G.eD[:, 0], in1=M.GT4[:], op=ALU.mult), reads=[G.eDB, M.GT4B], writes=[G.decB])
        P.op("gpsimd", lambda e: e.tensor_tensor(out=G.dec[:, 1], in0=G.eD[:, 1], in1=M.LE4[:], op=ALU.mult), reads=[G.eDB, M.LE4B], writes=[G.decB])
        yield
        P.op("vector", lambda e: e.tensor_tensor(out=G.tL[:], in0=G.psG[:, 0], in1=G.dec[:, 0], op=ALU.mult), reads=[G.psGB, G.decB], writes=[G.tLB])
        P.op("vector", lambda e: e.tensor_tensor(out=G.attT[:], in0=G.psU[:, 1], in1=G.dec[:, 1], op=ALU.mult), reads=[G.psUB, G.decB], writes=[G.attTB])
        yield
        for h in HR:
            P.op("gpsimd", (lambda e, h=h: e.tensor_scalar(out=G.AA[:, 1, h, :], in0=G.tL[:, h, :], scalar1=G.nbt[:, h:h + 1], scalar2=None, op0=ALU.mult)),
                 reads=[G.tLB, G.nbtB], writes=[G.AAB])
        yield
        for h in HR:
            mm(G.psG[:, 1, h, :], G.AA[:, 1, h, :], cst[0:64, 0:64], [G.AAB, M.cstB], [G.psGB], inc=(h == G2 - 1))
        yield
        P.op("vector", lambda e: e.tensor_copy(out=G.AA[:, 0], in_=G.psG[:, 1]), reads=[G.psGB], writes=[G.AAB])
        yield
        P.op("vector", lambda e: e.tensor_tensor(out=G.PU[:], in0=G.AA[:, 0], in1=M.I4[:], op=ALU.add), reads=[G.AAB, M.I4B], writes=[G.PUB])
        for m in range(5):
            for h in HR:
                mm(G.psI[:, 0, h, :], G.AA[:, 1, h, :], G.AA[:, 0, h, :], [G.AAB], [G.psIB], inc=False)
                mm(G.psI[:, 1, h, :], G.AA[:, 0, h, :], G.AA[:, 1, h, :], [G.AAB], [G.psIB], inc=(h == G2 - 1))
            yield
            P.op("vector", lambda e: e.tensor_copy(out=G.AA[:], in_=G.psI[:]), reads=[G.psIB], writes=[G.AAB])
            yield
            for h in HR:
                mm(G.psU[:, 0, h, :], G.AA[:, 1, h, :], G.PU[:, h, :], [G.AAB, G.PUB], [G.psUB], inc=(h == G2 - 1))
            yield
            P.op("vector", lambda e: e.tensor_tensor(out=G.PU[:], in0=G.PU[:], in1=G.psU[:, 0], op=ALU.add), reads=[G.psUB, G.PUB], writes=[G.PUB])
            yield
        X3 = G.psX[0:64, :].rearrange("p (h d) -> p h d", h=G2)
        Y3 = G.psY[0:64, :].rearrange("p (h d) -> p h d", h=G2)
        Z3 = G.psZ[0:64, :].rearrange("p (h d) -> p h d", h=G2)

        def tr(out, in_, wB, inc):
            P.op("tensor", lambda e: e.transpose(out=out, in_=in_, identity=ident), reads=[M.qkvB, M.cstB], writes=[wB], inc=inc)
        for h in HR:
            tr(X3[:, h, :], kT(h), G.psXB, False)
            tr(Y3[:, h, :], vT(h), G.psYB, h == G2 - 1)
        yield
        for h in HR:
            P.op("vector", (lambda e, h=h: e.tensor_scalar(out=G.vb[:, h, :], in0=Y3[:, h, :], scalar1=G.bt[:, h:h + 1], scalar2=None, op0=ALU.mult)),
                 reads=[G.psYB, G.btB], writes=[G.vbB])
            P.op("vector", (lambda e, h=h: e.tensor_scalar(out=G.kbg[:, h, :], in0=X3[:, h, :], scalar1=G.begc[:, h:h + 1], scalar2=None, op0=ALU.mult)),
                 reads=[G.psXB, G.begcB], writes=[G.kbgB])
            P.op("vector", (lambda e, h=h: e.tensor_scalar(out=G.kst[:, h, :], in0=X3[:, h, :], scalar1=G.egd[:, h:h + 1], scalar2=None, op0=ALU.mult)),
                 reads=[G.psXB, G.egdB], writes=[G.kstB])
        yield
        YK = G.psY[:, 0:G2 * 64].rearrange("p (h d) -> p h d", h=G2)
        for h in HR:
            mm(X3[:, h, :], G.PU[:, h, :], G.vb[:, h, :], [G.PUB, G.vbB], [G.psXB], inc=False)
            mm(YK[:, h, :], G.kbg[:, h, :], G.PU[:, h, :], [G.PUB, G.kbgB], [G.psYB], inc=(h == G2 - 1))
        yield
        P.op("vector", lambda e: e.tensor_copy(out=G.wv[:], in_=X3), reads=[G.psXB], writes=[G.wvB])
        P.op("vector", lambda e: e.tensor_copy(out=G.kcT[:], in_=YK), reads=[G.psYB], writes=[G.kcTB])
        yield
        for h in HR:
            mm(X3[:, h, :], G.kcT[:, h, :], G.S[:, h, :], [G.kcTB, G.SB], [G.psXB], inc=(h == G2 - 1))
        yield
        P.op("vector", lambda e: e.tensor_tensor(out=G.vn[:], in0=G.wv[:], in1=X3, op=ALU.subtract), reads=[G.wvB, G.psXB], writes=[G.vnB])
        yield
        for h in HR:
            mm(Y3[:, h, :], qT(h), G.S[:, h, :], [M.qkvB, G.SB], [G.psYB], inc=False)
            mm(Z3[:, h, :], G.attT[:, h, :], G.vn[:, h, :], [G.attTB, G.vnB], [G.psZB], inc=(h == G2 - 1))
        XS = G.psX[:, :].rearrange("p (h d) -> p h d", h=G2)
        for h in HR:
            mm(XS[:, h, :], G.kst[:, h, :], G.vn[:, h, :], [G.kstB, G.vnB], [G.psXB], inc=(h == G2 - 1))
        yield
        for h in HR:
            P.op("vector", (lambda e, h=h: e.tensor_scalar(out=G.o1[:, h, :], in0=Y3[:, h, :], scalar1=G.egc[:, h:h + 1], scalar2=None, op0=ALU.mult)),
                 reads=[G.psYB, G.egcB], writes=[G.o1B])
            P.op("gpsimd", (lambda e, h=h: e.tensor_scalar(out=G.S[:, h, :], in0=G.S[:, h, :], scalar1=G.egl[:, h:h + 1], scalar2=None, op0=ALU.mult)),
                 reads=[G.SB, G.eglB], writes=[G.SB])
        yield
        P.op("vector", lambda e: e.tensor_tensor(out=G.o1[:], in0=G.o1[:], in1=Z3, op=ALU.add), reads=[G.o1B, G.psZB], writes=[G.o1B])
        P.op("vector", lambda e: e.tensor_tensor(out=G.S[:], in0=G.S[:], in1=XS, op=ALU.add), reads=[G.SB, G.psXB], writes=[G.SB])
        yield
        P.op("gpsimd", lambda e: e.tensor_tensor(out=G.osq[:], in0=G.o1[:], in1=G.o1[:], op=ALU.mult), reads=[G.o1B], writes=[G.osqB])
        yield
        P.op("vector", lambda e: e.reduce_sum(out=G.ssq[:, 0:G2], in_=G.osq[:], axis=mybir.AxisListType.X), reads=[G.osqB], writes=[G.ssqB])
        yield
        P.op("scalar", lambda e: e.activation(out=G.ssq[:, G2:2 * G2], in_=G.ssq[:, 0:G2], func=AF.Sqrt, bias=M.epsn[0:64, 0:1], scale=1.0 / 128),
             reads=[G.ssqB, M.epsnB], writes=[G.ssqB])
        yield
        P.op("vector", lambda e: e.reciprocal(out=G.ssq[:, G2:2 * G2], in_=G.ssq[:, G2:2 * G2]), reads=[G.ssqB], writes=[G.ssqB])
        yield
        for h in HR:
            P.op("gpsimd", (lambda e, h=h: e.tensor_scalar(out=G.on[:, h, :], in0=G.o1[:, h, :], scalar1=G.ssq[:, G2 + h:G2 + h + 1], scalar2=None, op0=ALU.mult)),
                 reads=[G.o1B, G.ssqB], writes=[G.onB])
        yield
        ZT = G.psZ[:, 0:G2 * 64].rearrange("p (h d) -> p h d", h=G2)
        for h in HR:
            mm(ZT[:, h, :], G.on[:, h, :], cst[0:64, 0:64], [G.onB, M.cstB], [G.psZB], inc=(h == G2 - 1))
        yield
        P.op("vector", lambda e: e.tensor_tensor(out=G.yg[:, :, cs], in0=ZT, in1=M.sz[:, h0:h0 + G2, cs], op=ALU.mult), reads=[G.psZB, M.szB], writes=[G.ygB])

    for t in range(ntile):
        t0 = t * TT
        for ct in range(12):
            conv_tile(t, ct)
        for h in range(NH):
            z_tile(t, h)
        P.dma("sync", (lambda e, t0=t0: e.dma_start(out=M.bar[:], in_=PT[2048:2056, t0:t0 + TT])), reads=[PTB[t]], writes=[M.barB])
        for n in range(8):
            gens = [chunk(n, GA), chunk(n, GB)]
            alive = [True, True]
            while any(alive):
                for i, gen in enumerate(gens):
                    if alive[i]:
                        try:
                            next(gen)
                        except StopIteration:
                            alive[i] = False
        for G in (GA, GB):
            for h in range(G2):
                P.dma("gpsimd", (lambda e, G=G, h=h, t0=t0: e.dma_start(out=YT[(G.h0 + h) * 128:(G.h0 + h + 1) * 128, t0:t0 + TT], in_=G.yg[:, h, :])),
                      reads=[G.ygB], writes=[YTB[t]])


NCST2 = 129 + 128 + 16
SEG = 128
TWO_PI = 6.283185307179586


def make_consts2():
    c = np.zeros((128, NCST2), np.float32)
    c[:, 0:129] = np.arange(129)[None, :]
    g = np.arange(32)
    m = np.arange(128)
    c[0:32, 129:257] = (g[:, None] % 2 == (m[None, :] // 64))
    c[0:32, 257:273] = (g[:, None] // 2 == np.arange(16)[None, :])
    return c


def s5_phase(M, PT, YT, prm, cst_dram, cst2_dram, NT, PTB, YTB, tag=""):
    P = M.P
    ntile = NT // TT
    sb, ps = M.sb, M.ps
    c1 = sb("s_c1", [128, NCST]); c2 = sb("s_c2", [128, NCST2])
    c1B, c2B = M.s_c1B, M.s_c2B
    ident = c1[:, 0:128]
    P.dma("sync", lambda e: e.dma_start(out=c1[:], in_=cst_dram), writes=[c1B])
    P.dma("sync", lambda e: e.dma_start(out=c2[:], in_=cst2_dram), writes=[c2B])
    iota = c2[:, 0:129]
    psA = ps("s_psA", [128, 512]); psB = ps("s_psB", [128, 512]); psY = ps("s_psY", [128, 512]); psT = ps("s_psT", [128, 512])
    psAB, psBB, psYB, psTB = M.s_psAB, M.s_psBB, M.s_psYB, M.s_psTB
    NP_ = 16

    def V(eng, fn, reads, writes):
        P.op(eng, fn, reads=reads, writes=writes)

    rows = sb("s_rows", [32, 128]); rowsB = M.s_rowsB
    arc = sb("s_ar", [128, NP_]); aic = sb("s_ai", [128, NP_]); dtc = sb("s_dt", [128, NP_])

    def col_from_rows(dst, dstB, src_ap):
        P.dma("sync", lambda e: e.dma_start(out=rows[0:16, :], in_=src_ap), writes=[rowsB])
        P.op("tensor", lambda e: e.matmul(psT[:, 0:16], lhsT=rows[0:16, :], rhs=c1[0:16, 0:16], start=True, stop=True), reads=[rowsB, c1B], writes=[psTB])
        V("vector", lambda e: e.tensor_copy(out=dst[:], in_=psT[:, 0:16]), [psTB], [dstB])
    col_from_rows(arc, M.s_arB, prm["a_re"].rearrange("(a b) p -> a (b p)", b=2))
    col_from_rows(aic, M.s_aiB, prm["a_im"].rearrange("(a b) p -> a (b p)", b=2))
    ldr = sb("s_ldr", [1, 32]); ldc = sb("s_ldc", [32, 1]); Rm = sb("s_Rm", [32, 16])
    P.dma("sync", lambda e: e.dma_start(out=ldr[:], in_=prm["log_dt"].rearrange("(o f) -> o f", o=1)), writes=[M.s_ldrB])
    P.op("tensor", lambda e: e.matmul(psT[0:32, 16:17], lhsT=ldr[0:1, :], rhs=c1[0:1, 0:1], start=True, stop=True), reads=[M.s_ldrB, c1B], writes=[psTB])
    V("vector", lambda e: e.tensor_copy(out=ldc[:], in_=psT[0:32, 16:17]), [psTB], [M.s_ldcB])
    V("vector", lambda e: e.tensor_scalar(out=Rm[:], in0=c2[0:32, 257:273], scalar1=ldc[:, 0:1], scalar2=None, op0=ALU.mult), [c2B, M.s_ldcB], [M.s_RmB])
    P.op("tensor", lambda e: e.matmul(psT[:, 32:48], lhsT=c2[0:32, 129:257], rhs=Rm[:], start=True, stop=True), reads=[c2B, M.s_RmB], writes=[psTB])
    V("scalar", lambda e: e.activation(out=dtc[:], in_=psT[:, 32:48], func=AF.Exp), [psTB], [M.s_dtB])

    def sincos(x, xB, s_out, sB_, c_out, cB_, F, tmp, tmpB, tmpi, tmpiB):
        V("vector", lambda e: e.tensor_scalar(out=tmp, in0=x, scalar1=1.0 / TWO_PI, scalar2=None, op0=ALU.mult), [xB], [tmpB])
        V("vector", lambda e: e.tensor_copy(out=tmpi, in_=tmp), [tmpB], [tmpiB])
        V("vector", lambda e: e.tensor_copy(out=tmp, in_=tmpi), [tmpiB], [tmpB])
        V("vector", lambda e: e.scalar_tensor_tensor(out=x, in0=tmp, scalar=-TWO_PI, in1=x, op0=ALU.mult, op1=ALU.add), [tmpB, xB], [xB])
        V("scalar", lambda e: e.activation(out=tmp, in_=x, func=AF.Sin, scale=0.25), [xB], [tmpB])
        V("scalar", lambda e: e.activation(out=s_out, in_=x, func=AF.Sin, scale=0.5), [xB], [sB_])
        V("vector", lambda e: e.tensor_tensor(out=tmp, in0=tmp, in1=tmp, op=ALU.mult), [tmpB], [tmpB])
        V("vector", lambda e: e.tensor_scalar(out=tmp, in0=tmp, scalar1=-2.0, scalar2=1.0, op0=ALU.mult, op1=ALU.add), [tmpB], [tmpB])
        V("vector", lambda e: e.tensor_tensor(out=c_out, in0=s_out, in1=s_out, op=ALU.mult), [sB_], [cB_])
        V("vector", lambda e: e.scalar_tensor_tensor(out=s_out, in0=s_out, scalar=2.0, in1=tmp, op0=ALU.mult, op1=ALU.mult), [sB_, tmpB], [sB_])
        V("vector", lambda e: e.tensor_scalar(out=c_out, in0=c_out, scalar1=-2.0, scalar2=1.0, op0=ALU.mult, op1=ALU.add), [cB_], [cB_])

    mag = sb("s_mag", [128, NP_]); th = sb("s_th", [128, NP_]); sn = sb("s_sn", [128, NP_]); cs_ = sb("s_cs", [128, NP_])
    tp = sb("s_tp", [128, NP_]); tpi = sb("s_tpi", [128, NP_], I32); th2 = sb("s_th2", [128, NP_])
    V("vector", lambda e: e.tensor_scalar(out=arc[:], in0=arc[:], scalar1=-1e-4, scalar2=None, op0=ALU.min), [M.s_arB], [M.s_arB])
    V("vector", lambda e: e.tensor_tensor(out=mag[:], in0=dtc[:], in1=arc[:], op=ALU.mult), [M.s_dtB, M.s_arB], [M.s_magB])
    V("scalar", lambda e: e.activation(out=mag[:], in_=mag[:], func=AF.Exp), [M.s_magB], [M.s_magB])
    V("vector", lambda e: e.tensor_tensor(out=th[:], in0=dtc[:], in1=aic[:], op=ALU.mult), [M.s_dtB, M.s_aiB], [M.s_thB])
    V("vector", lambda e: e.tensor_copy(out=th2[:], in_=th[:]), [M.s_thB], [M.s_th2B])
    sincos(th2[:], M.s_th2B, sn[:], M.s_snB, cs_[:], M.s_csB, NP_, tp[:], M.s_tpB, tpi[:], M.s_tpiB)
    zr = sb("s_zr", [128, NP_]); zi = sb("s_zi", [128, NP_]); den = sb("s_den", [128, NP_]); fr = sb("s_fr", [128, NP_]); fi = sb("s_fi", [128, NP_])
    V("vector", lambda e: e.tensor_tensor(out=zr[:], in0=mag[:], in1=cs_[:], op=ALU.mult), [M.s_magB, M.s_csB], [M.s_zrB])
    V("vector", lambda e: e.tensor_scalar(out=zr[:], in0=zr[:], scalar1=-1.0, scalar2=None, op0=ALU.add), [M.s_zrB], [M.s_zrB])
    V("vector", lambda e: e.tensor_tensor(out=zi[:], in0=mag[:], in1=sn[:], op=ALU.mult), [M.s_magB, M.s_snB], [M.s_ziB])
    V("vector", lambda e: e.tensor_tensor(out=den[:], in0=arc[:], in1=arc[:], op=ALU.mult), [M.s_arB], [M.s_denB])
    V("vector", lambda e: e.tensor_tensor(out=tp[:], in0=aic[:], in1=aic[:], op=ALU.mult), [M.s_aiB], [M.s_tpB])
    V("vector", lambda e: e.tensor_tensor(out=den[:], in0=den[:], in1=tp[:], op=ALU.add), [M.s_denB, M.s_tpB], [M.s_denB])
    V("vector", lambda e: e.reciprocal(out=den[:], in_=den[:]), [M.s_denB], [M.s_denB])
    V("vector", lambda e: e.tensor_tensor(out=fr[:], in0=zr[:], in1=arc[:], op=ALU.mult), [M.s_zrB, M.s_arB], [M.s_frB])
    V("vector", lambda e: e.tensor_tensor(out=tp[:], in0=zi[:], in1=aic[:], op=ALU.mult), [M.s_ziB, M.s_aiB], [M.s_tpB])
    V("vector", lambda e: e.tensor_tensor(out=fr[:], in0=fr[:], in1=tp[:], op=ALU.add), [M.s_frB, M.s_tpB], [M.s_frB])
    V("vector", lambda e: e.tensor_tensor(out=fr[:], in0=fr[:], in1=den[:], op=ALU.mult), [M.s_frB, M.s_denB], [M.s_frB])
    V("vector", lambda e: e.tensor_tensor(out=fi[:], in0=zi[:], in1=arc[:], op=ALU.mult), [M.s_ziB, M.s_arB], [M.s_fiB])
    V("vector", lambda e: e.tensor_tensor(out=tp[:], in0=zr[:], in1=aic[:], op=ALU.mult), [M.s_zrB, M.s_aiB], [M.s_tpB])
    V("vector", lambda e: e.tensor_tensor(out=fi[:], in0=fi[:], in1=tp[:], op=ALU.subtract), [M.s_fiB, M.s_tpB], [M.s_fiB])
    V("vector", lambda e: e.tensor_tensor(out=fi[:], in0=fi[:], in1=den[:], op=ALU.mult), [M.s_fiB, M.s_denB], [M.s_fiB])

    CTb = sb("s_CT", [128, NP_, 129]); STb = sb("s_ST", [128, NP_, 129]); XT = sb("s_XT", [128, NP_, 129])
    TT1 = sb("s_TT1", [128, NP_, 129]); TTi = sb("s_TTi", [128, NP_, 129], I32); RM = sb("s_RM", [128, NP_, SEG])
    for gp in range(NP_):
        V("vector", (lambda e, gp=gp: e.tensor_scalar(out=XT[:, gp, :], in0=iota, scalar1=th[:, gp:gp + 1], scalar2=None, op0=ALU.mult)), [c2B, M.s_thB], [M.s_XTB])
        V("gpsimd", (lambda e, gp=gp: e.tensor_scalar(out=RM[:, gp, :], in0=c1[:, 128:256], scalar1=mag[:, gp:gp + 1], scalar2=None, op0=ALU.mult)), [c1B, M.s_magB], [M.s_RMB])
    fl = lambda t_: t_[:].rearrange("p a b -> p (a b)")
    sincos(fl(XT), M.s_XTB, fl(STb), M.s_STB, fl(CTb), M.s_CTB, NP_ * 129, fl(TT1), M.s_TT1B, fl(TTi), M.s_TTiB)

    BnR = sb("s_BnR", [128, 4, 128]); BnI = sb("s_BnI", [128, 4, 128]); bbR = sb("s_bbR", [128, 4, 128]); bbI = sb("s_bbI", [128, 4, 128])
    LBr = sb("s_LBr", [128, NP_, 128]); LBi = sb("s_LBi", [128, NP_, 128]); tmpm = sb("s_tmpm", [128, 128])
    V("vector", lambda e: e.memset(BnR[:], 0.0), [], [M.s_BnRB])
    V("vector", lambda e: e.memset(BnI[:], 0.0), [], [M.s_BnIB])
    V("gpsimd", lambda e: e.memset(bbR[:], 0.0), [], [M.s_bbRB])
    V("gpsimd", lambda e: e.memset(bbI[:], 0.0), [], [M.s_bbIB])
    for g in range(32):
        ct, gl, e_ = g // 8, g % 8, g % 2
        P.dma("sync", (lambda e, g=g, ct=ct, gl=gl, e_=e_: e.dma_start(out=BnR[e_ * 64:(e_ + 1) * 64, ct, gl * 16:(gl + 1) * 16], in_=prm["b_re"][g])), writes=[M.s_BnRB])
        P.dma("sync", (lambda e, g=g, ct=ct, gl=gl, e_=e_: e.dma_start(out=BnI[e_ * 64:(e_ + 1) * 64, ct, gl * 16:(gl + 1) * 16], in_=prm["b_im"][g])), writes=[M.s_BnIB])
    P.barrier()
    for g in range(32):
        ct, gl, e_, gp = g // 8, g % 8, g % 2, g // 2
        rs = slice(e_ * 64, (e_ + 1) * 64)
        csl = slice(gl * 16, (gl + 1) * 16)

        def bb(ct=ct, rs=rs, csl=csl, gp=gp):
            V("vector", lambda e: e.tensor_scalar(out=bbR[rs, ct, csl], in0=BnR[rs, ct, csl], scalar1=fr[rs, gp:gp + 1], scalar2=None, op0=ALU.mult), [M.s_BnRB, M.s_frB], [M.s_bbRB])
            V("vector", lambda e: e.tensor_scalar(out=tmpm[rs, 0:16], in0=BnI[rs, ct, csl], scalar1=fi[rs, gp:gp + 1], scalar2=None, op0=ALU.mult), [M.s_BnIB, M.s_fiB], [M.s_tmpmB])
            V("vector", lambda e: e.tensor_tensor(out=bbR[rs, ct, csl], in0=bbR[rs, ct, csl], in1=tmpm[rs, 0:16], op=ALU.subtract), [M.s_bbRB, M.s_tmpmB], [M.s_bbRB])
            V("vector", lambda e: e.tensor_scalar(out=bbI[rs, ct, csl], in0=BnI[rs, ct, csl], scalar1=fr[rs, gp:gp + 1], scalar2=None, op0=ALU.mult), [M.s_BnIB, M.s_frB], [M.s_bbIB])
            V("vector", lambda e: e.tensor_scalar(out=tmpm[rs, 16:32], in0=BnR[rs, ct, csl], scalar1=fi[rs, gp:gp + 1], scalar2=None, op0=ALU.mult), [M.s_BnRB, M.s_fiB], [M.s_tmpmB])
            V("vector", lambda e: e.tensor_tensor(out=bbI[rs, ct, csl], in0=bbI[rs, ct, csl], in1=tmpm[rs, 16:32], op=ALU.add), [M.s_bbIB, M.s_tmpmB], [M.s_bbIB])
        bb()
    for gp in range(NP_):
        ct, q4 = gp // 4, gp % 4
        csl = slice(q4 * 32, (q4 + 1) * 32)

        def mk(src, srcB, dst, dstB, ct=ct, csl=csl, gp=gp, neg=False):
            V("gpsimd", lambda e: e.memset(tmpm[:], 0.0), [], [M.s_tmpmB])
            V("gpsimd", lambda e: e.tensor_copy(out=tmpm[:, csl], in_=src[:, ct, csl]), [srcB], [M.s_tmpmB])
            P.op("tensor", lambda e: e.matmul(psT[:, 0:128], lhsT=tmpm[:], rhs=ident, start=True, stop=True), reads=[M.s_tmpmB, c1B], writes=[psTB])
            V("vector", lambda e: e.tensor_copy(out=dst[:, gp, :], in_=psT[:, 0:128]), [psTB], [dstB])
        mk(bbR, M.s_bbRB, LBr, M.s_LBrB)
        mk(bbI, M.s_bbIB, LBi, M.s_LBiB)

    CnR = sb("s_CnR", [128, 4, 128]); CnI = sb("s_CnI", [128, 4, 128]); CT2r = sb("s_CT2r", [128, 4, 128]); CT2i = sb("s_CT2i", [128, 4, 128])
    LCr = sb("s_LCr", [128, NP_, 128]); LCi = sb("s_LCi", [128, NP_, 128])
    V("vector", lambda e: e.memset(CnR[:], 0.0), [], [M.s_CnRB])
    V("vector", lambda e: e.memset(CnI[:], 0.0), [], [M.s_CnIB])
    V("gpsimd", lambda e: e.memset(LCr[:], 0.0), [], [M.s_LCrB])
    V("gpsimd", lambda e: e.memset(LCi[:], 0.0), [], [M.s_LCiB])
    for g in range(32):
        ct, gl, e_ = g // 8, g % 8, g % 2
        P.dma("sync", (lambda e, g=g, ct=ct, gl=gl, e_=e_: e.dma_start(out=CnR[gl * 16:(gl + 1) * 16, ct, e_ * 64:(e_ + 1) * 64], in_=prm["c_re"][g])), writes=[M.s_CnRB])
        P.dma("sync", (lambda e, g=g, ct=ct, gl=gl, e_=e_: e.dma_start(out=CnI[gl * 16:(gl + 1) * 16, ct, e_ * 64:(e_ + 1) * 64], in_=prm["c_im"][g])), writes=[M.s_CnIB])
    P.barrier()
    for ct in range(4):
        def trc(src, srcB, dst, dstB, ct=ct, neg=False):
            P.op("tensor", lambda e: e.matmul(psT[:, 0:128], lhsT=src[:, ct, :], rhs=ident, start=True, stop=True), reads=[srcB, c1B], writes=[psTB])
            if neg:
                V("vector", lambda e: e.tensor_scalar(out=dst[:, ct, :], in0=psT[:, 0:128], scalar1=-1.0, scalar2=None, op0=ALU.mult), [psTB], [dstB])
            else:
                V("vector", lambda e: e.tensor_copy(out=dst[:, ct, :], in_=psT[:, 0:128]), [psTB], [dstB])
        trc(CnR, M.s_CnRB, CT2r, M.s_CT2rB)
        trc(CnI, M.s_CnIB, CT2i, M.s_CT2iB, neg=True)
    for gp in range(NP_):
        ct, q4 = gp // 4, gp % 4
        csl = slice(q4 * 32, (q4 + 1) * 32)
        V("gpsimd", (lambda e, gp=gp, ct=ct, csl=csl: e.tensor_copy(out=LCr[:, gp, csl], in_=CT2r[:, ct, csl])), [M.s_CT2rB], [M.s_LCrB])
        V("gpsimd", (lambda e, gp=gp, ct=ct, csl=csl: e.tensor_copy(out=LCi[:, gp, csl], in_=CT2i[:, ct, csl])), [M.s_CT2iB], [M.s_LCiB])
    dcol = sb("s_dcol", [128, 4])
    P.dma("sync", lambda e: e.dma_start(out=rows[0:4, :], in_=prm["d"].rearrange("(a b) h -> a (b h)", b=8)), writes=[rowsB])
    P.op("tensor", lambda e: e.matmul(psT[:, 0:4], lhsT=rows[0:4, :], rhs=c1[0:4, 0:4], start=True, stop=True), reads=[rowsB, c1B], writes=[psTB])
    V("vector", lambda e: e.tensor_copy(out=dcol[:], in_=psT[:, 0:4]), [psTB], [M.s_dcolB])

    uT = sb("s_uT", [128, 4, TT]); ys = sb("s_ys", [128, TT])
    cR = sb("s_cR", [128, NP_]); cI = sb("s_cI", [128, NP_])
    cRB = [Buf("cR%d" % i) for i in range(NP_)]; cIB = [Buf("cI%d" % i) for i in range(NP_)]
    P.op("vector", lambda e: e.memset(cR[:], 0.0), writes=cRB)
    P.op("vector", lambda e: e.memset(cI[:], 0.0), writes=cIB)
    nseg = TT // SEG

    class Lane:
        pass
    lanes = []
    for li in range(2):
        Ln_ = Lane()
        for nm in ("bR", "bI", "t1", "t2", "xR", "xI"):
            setattr(Ln_, nm, sb("s_%s_l%d" % (nm, li), [128, TT]))
            setattr(Ln_, nm + "B", getattr(M, "s_%s_l%dB" % (nm, li)))
        Ln_.cq = sb("s_cq_l%d" % li, [128, 4]); Ln_.cqB = getattr(M, "s_cq_l%dB" % li)
        if li == 0:
            Ln_.psA, Ln_.psAB, Ln_.psB, Ln_.psBB = psA, psAB, psB, psBB
        else:
            Ln_.psA = ps("s_psA2", [128, 512]); Ln_.psAB = M.s_psA2B
            Ln_.psB = ps("s_psB2", [128, 512]); Ln_.psBB = M.s_psB2B
        lanes.append(Ln_)

    def pair_gen(t, gp, L):
        ct = gp // 4
        tabC = CTb[:, gp, 0:SEG].unsqueeze(1).broadcast_to([128, nseg, SEG])
        tabS = STb[:, gp, 0:SEG].unsqueeze(1).broadcast_to([128, nseg, SEG])
        v3 = lambda a: a[:].rearrange("p (s c) -> p s c", s=nseg)
        bR, bI, t1, t2, xR, xI, cq = L.bR, L.bI, L.t1, L.t2, L.xR, L.xI, L.cq
        P.op("tensor", lambda e: e.matmul(L.psA[:], lhsT=LBr[:, gp, :], rhs=uT[:, ct, :], start=True, stop=True), reads=[M.s_LBrB, M.s_uTB], writes=[L.psAB])
        P.op("tensor", lambda e: e.matmul(L.psB[:], lhsT=LBi[:, gp, :], rhs=uT[:, ct, :], start=True, stop=True), reads=[M.s_LBiB, M.s_uTB], writes=[L.psBB])
        pA3 = L.psA[:].rearrange("p (s c) -> p s c", s=nseg)
        pB3 = L.psB[:].rearrange("p (s c) -> p s c", s=nseg)
        yield
        V("vector", lambda e: e.tensor_tensor(out=v3(bR), in0=pA3, in1=tabC, op=ALU.mult), [L.psAB, M.s_CTB], [L.bRB])
        V("vector", lambda e: e.tensor_tensor(out=v3(t1), in0=pB3, in1=tabS, op=ALU.mult), [L.psBB, M.s_STB], [L.t1B])
        yield
        V("gpsimd", lambda e: e.tensor_tensor(out=bR[:], in0=bR[:], in1=t1[:], op=ALU.add), [L.bRB, L.t1B], [L.bRB])
        V("vector", lambda e: e.tensor_tensor(out=v3(bI), in0=pB3, in1=tabC, op=ALU.mult), [L.psBB, M.s_CTB], [L.bIB])
        V("vector", lambda e: e.tensor_tensor(out=v3(t2), in0=pA3, in1=tabS, op=ALU.mult), [L.psAB, M.s_STB], [L.t2B])
        yield
        V("gpsimd", lambda e: e.tensor_tensor(out=bI[:], in0=bI[:], in1=t2[:], op=ALU.subtract), [L.bIB, L.t2B], [L.bIB])
        yield
        c128 = CTb[:, gp, 128:129]
        s128 = STb[:, gp, 128:129]
        for s_ in range(nseg):
            sc = slice(s_ * SEG, (s_ + 1) * SEG)
            lr = xR[:, sc.stop - 1:sc.stop]
            li_ = xI[:, sc.stop - 1:sc.stop]

            def scans(sc=sc):
                V("vector", lambda e: e.tensor_tensor_scan(out=xR[:, sc], data0=RM[:, gp, :], data1=bR[:, sc], initial=cR[:, gp:gp + 1], op0=ALU.mult, op1=ALU.add),
                  [M.s_RMB, L.bRB, cRB[gp]], [L.xRB])
                V("vector", lambda e: e.tensor_tensor_scan(out=xI[:, sc], data0=RM[:, gp, :], data1=bI[:, sc], initial=cI[:, gp:gp + 1], op0=ALU.mult, op1=ALU.add),
                  [M.s_RMB, L.bIB, cIB[gp]], [L.xIB])
            scans()
            yield

            def carry1(lr=lr, li_=li_):
                V("vector", lambda e: e.tensor_tensor(out=cq[:, 0:1], in0=li_, in1=s128, op=ALU.mult), [L.xIB, M.s_STB], [L.cqB])
                V("vector", lambda e: e.tensor_tensor(out=cq[:, 1:2], in0=li_, in1=c128, op=ALU.mult), [L.xIB, M.s_CTB], [L.cqB])
            carry1()
            yield

            def carry2(lr=lr):
                V("vector", lambda e: e.scalar_tensor_tensor(out=cR[:, gp:gp + 1], in0=lr, scalar=c128, in1=cq[:, 0:1], op0=ALU.mult, op1=ALU.subtract),
                  [L.xRB, M.s_CTB, L.cqB], [cRB[gp]])
                V("vector", lambda e: e.scalar_tensor_tensor(out=cI[:, gp:gp + 1], in0=lr, scalar=s128, in1=cq[:, 1:2], op0=ALU.mult, op1=ALU.add),
                  [L.xRB, M.s_STB, L.cqB], [cIB[gp]])
            carry2()
            yield
        V("gpsimd", lambda e: e.tensor_tensor(out=v3(t1), in0=v3(xI), in1=tabS, op=ALU.mult), [L.xIB, M.s_STB], [L.t1B])
        V("gpsimd", lambda e: e.tensor_tensor(out=v3(t2), in0=v3(xR), in1=tabS, op=ALU.mult), [L.xRB, M.s_STB], [L.t2B])
        yield
        V("vector", lambda e: e.tensor_tensor(out=v3(xR), in0=v3(xR), in1=tabC, op=ALU.mult), [L.xRB, M.s_CTB], [L.xRB])
        V("vector", lambda e: e.tensor_tensor(out=v3(xI), in0=v3(xI), in1=tabC, op=ALU.mult), [L.xIB, M.s_CTB], [L.xIB])
        yield
        V("gpsimd", lambda e: e.tensor_tensor(out=xR[:], in0=xR[:], in1=t1[:], op=ALU.subtract), [L.xRB, L.t1B], [L.xRB])
        V("gpsimd", lambda e: e.tensor_tensor(out=xI[:], in0=xI[:], in1=t2[:], op=ALU.add), [L.xIB, L.t2B], [L.xIB])
        yield
        q4 = gp % 4
        P.op("tensor", lambda e: e.matmul(psY[:], lhsT=LCr[:, gp, :], rhs=xR[:], start=(q4 == 0), stop=False), reads=[M.s_LCrB, L.xRB], writes=[psYB], inc=False)
        P.op("tensor", lambda e: e.matmul(psY[:], lhsT=LCi[:, gp, :], rhs=xI[:], start=False, stop=(q4 == 3)), reads=[M.s_LCiB, L.xIB], writes=[psYB])

    for t in range(ntile):
        t0 = t * TT
        for ct in range(4):
            P.dma("sync", (lambda e, ct=ct, t0=t0: e.dma_start(out=uT[:, ct, :], in_=PT[2056 + ct * 128:2056 + (ct + 1) * 128, t0:t0 + TT])), reads=[PTB[t]], writes=[M.s_uTB])
        for gp0 in range(0, NP_, 2):
            gens = [pair_gen(t, gp0, lanes[0]), pair_gen(t, gp0 + 1, lanes[1])]
            alive = [True, True]
            while any(alive):
                for i, gen in enumerate(gens):
                    if alive[i]:
                        try:
                            next(gen)
                        except StopIteration:
                            alive[i] = False
            if gp0 % 4 == 2:
                ct = gp0 // 4

                def fin(ct=ct, t0=t0):
                    V("vector", lambda e: e.scalar_tensor_tensor(out=ys[:], in0=uT[:, ct, :], scalar=dcol[:, ct:ct + 1], in1=psY[:], op0=ALU.mult, op1=ALU.add),
                      [M.s_uTB, M.s_dcolB, psYB], [M.s_ysB])
                    V("scalar", lambda e: e.activation(out=ys[:], in_=ys[:], func=AF.Gelu), [M.s_ysB], [M.s_ysB])
                    P.dma("gpsimd", lambda e: e.dma_start(out=YT[512 + ct * 128:512 + (ct + 1) * 128, t0:t0 + TT], in_=ys[:]), reads=[M.s_ysB], writes=[YTB[t]])
                fin()


def mixpost_phase(R, YT, X_in, X_out, w_glu, w_out, NT):
    P = R.P
    ntile = NT // TT
    wi, wo, sB = R.wi[0], R.wo[0], R.slabB[0]

    def ld(src, dst, w_):
        si = R.stg_i
        R.stg_i = (si + 1) % 3
        st, stB = R.stg[si], R.stgB[si]
        P.dma("sync", lambda e: e.dma_start(out=st[:, 0:w_], in_=src), writes=[stB])
        P.op("gpsimd", lambda e: e.tensor_copy(out=dst, in_=st[:, 0:w_]), reads=[stB], writes=[sB])
    for kc in range(4):
        ld(w_glu[kc * 128:(kc + 1) * 128, 0:512], wi[:, kc, 0:512], 512)
    for kc in range(8):
        ld(w_out[kc * 128:(kc + 1) * 128, 0:1024], wo[:, kc, :], 1024)
    YB = [Buf("Yo%d" % i) for i in range(ntile * 4)]

    def load_y(t, hb):
        hT = R.hT[hb]
        for kc in range(8):
            def one(kc=kc):
                si = R.stg_i
                R.stg_i = (si + 1) % 3
                st, stB = R.stg[si], R.stgB[si]
                P.dma("sync", lambda e: e.dma_start(out=st[:, 0:TT], in_=YT[kc * 128:(kc + 1) * 128, t * TT:(t + 1) * TT]), writes=[stB])
                P.op("gpsimd" if kc % 2 else "vector", lambda e: e.tensor_copy(out=hT[:, kc, :], in_=st[:, 0:TT]), reads=[stB], writes=[R.hTB[hb][0]])
            one()
    load_y(0, 0)
    gi = 0
    for t in range(ntile):
        hb = t % 2
        hT = R.hT[hb]
        hB = R.hTB[hb][0]
        for j in range(4):
            def glu(j=j, gi=gi, hT=hT, hB=hB):
                pp, ppB = R.pB[gi % 4], R.pBB[gi % 4]
                sg, sgB = R.sg[gi % 2], R.sgB[gi % 2]

                def mm(kc):
                    P.op("tensor", lambda e: e.matmul(pp[:], lhsT=wi[:, kc, j * 128:(j + 1) * 128], rhs=hT[:, 4 + kc, :], start=(kc == 0), stop=(kc == 3)),
                         reads=[sB, hB], writes=[ppB], inc=(kc == 3))
                for kc in range(4):
                    mm(kc)
                P.op("scalar", lambda e: e.activation(out=sg[:], in_=pp[:], func=AF.Sigmoid), reads=[ppB], writes=[sgB])
                P.op("vector", lambda e: e.tensor_tensor(out=R.aT[:, j, :], in0=hT[:, 4 + j, :], in1=sg[:], op=ALU.mult), reads=[hB, sgB], writes=[R.aTB[j]])
            glu()
            gi += 1
        if t + 1 < ntile:
            load_y(t + 1, 1 - hb)
        lhs = [(hT, kc, hB) for kc in range(4)] + [(R.aT, kc, R.aTB[kc]) for kc in range(4)]
        for s in range(4):
            stage_c_sub(R, wo, sB, 8, X_in, X_out, None, YB[t * 4 + s], t, s, 0, True, lhs=lhs)
    return YB


DEPTH = 2
NT_CORE = 8192


def mod_phase(R, c_row, w_mod_l, b_mod_l, MOD, sbm):
    P = R.P
    cT, cTB, brow, browB, mrow, mrowB = sbm
    P.dma("sync", lambda e: e.dma_start(out=R.vrows[:, 0, :], in_=c_row.rearrange("(kc p) -> kc p", p=128)), writes=[R.vrowsB])
    pc = R.pC[0]
    P.op("tensor", lambda e: e.matmul(pc[:, 0:8], lhsT=R.vrows[:, 0, :], rhs=R.identf[0:8, 0:8], start=True, stop=True),
         reads=[R.vrowsB, R.identfB], writes=[R.pCB])
    P.op("scalar", lambda e: e.activation(out=cT[:], in_=pc[:, 0:8], func=AF.Silu), reads=[R.pCB], writes=[cTB])
    pm = R.pC[1]
    dummy = Buf("modw")

    def tile(n):
        def kstep(kc):
            si = R.stg_i
            R.stg_i = (si + 1) % 3
            st, stB = R.stg[si], R.stgB[si]
            P.dma("sync", lambda e: e.dma_start(out=st[:, 0:512], in_=w_mod_l[kc * 128:(kc + 1) * 128, n * 512:(n + 1) * 512]), writes=[stB])
            P.op("tensor", lambda e: e.matmul(pm[0:1, :], lhsT=cT[:, kc:kc + 1], rhs=st[:, 0:512], start=(kc == 0), stop=(kc == NKC - 1)),
                 reads=[stB, cTB], writes=[R.pCB])
        for kc in range(NKC):
            kstep(kc)
        P.dma("sync", lambda e: e.dma_start(out=brow[:], in_=b_mod_l[n * 512:(n + 1) * 512].rearrange("(o f) -> o f", o=1)), writes=[browB])
        P.op("vector", lambda e: e.tensor_tensor(out=mrow[:], in0=pm[0:1, :], in1=brow[:], op=ALU.add),
             reads=[R.pCB, browB], writes=[mrowB])
        P.dma("gpsimd", lambda e: e.dma_start(out=MOD[n * 512:(n + 1) * 512].rearrange("(o f) -> o f", o=1), in_=mrow[:]),
              reads=[mrowB], writes=[dummy])
    for n in range(9 * D // 512):
        tile(n)


WNAMES = ["w_mod", "b_mod", "ff1_norm_pre", "ff1_norm_post", "ff1_w_in", "ff1_w_out", "mix_norm_pre", "mix_norm_post", "mix_w_in",
          "conv_w", "a_log", "dt_bias", "gdn_norm_w", "s5_a_re", "s5_a_im", "s5_log_dt", "s5_b_re", "s5_b_im", "s5_c_re", "s5_c_im",
          "s5_d", "s5_w_glu", "mix_w_out", "ff2_norm_pre", "ff2_norm_post", "ff2_w_in", "ff2_w_out"]
WSHAPES = {"w_mod": [D, 9 * D], "b_mod": [9 * D], "ff1_norm_pre": [D], "ff1_norm_post": [D], "ff1_w_in": [D, 2 * DFF], "ff1_w_out": [DFF, D],
           "mix_norm_pre": [D], "mix_norm_post": [D], "mix_w_in": [D, 2568], "conv_w": [4, 1536], "a_log": [4], "dt_bias": [4], "gdn_norm_w": [128],
           "s5_a_re": [32, 64], "s5_a_im": [32, 64], "s5_log_dt": [32], "s5_b_re": [32, 64, 16], "s5_b_im": [32, 64, 16],
           "s5_c_re": [32, 16, 64], "s5_c_im": [32, 16, 64], "s5_d": [32, 16], "s5_w_glu": [512, 512], "mix_w_out": [D, D],
           "ff2_norm_pre": [D], "ff2_norm_post": [D], "ff2_w_in": [D, 2 * DFF], "ff2_w_out": [DFF, D]}


def build_nc(NT, depth=DEPTH):
    nc = bass.Bass("TRN2", target_bir_lowering=False)
    dr = lambda n, s, k="ExternalInput", dt=F32: nc.dram_tensor(n, s, dt, kind=k).ap()
    x = dr("x", [NT, D]); c_row = dr("c_row", [D])
    W = {n: dr(n, [depth] + WSHAPES[n]) for n in WNAMES}
    ident = dr("ident", [128, 128]); cst = dr("cst", [128, NCST]); cst2 = dr("cst2", [128, NCST2])
    y = dr("y", [NT, D], "ExternalOutput")
    scr = lambda n, s: nc.dram_tensor(n, s, F32).ap()
    Xs = [scr("xs0", [NT, D]), scr("xs1", [NT, D])]
    yacc = scr("yacc", [NT, D]); PT = scr("ptscr", [2568, NT]); YT = scr("ytscr", [1024, NT])
    MOD = scr("modscr", [depth, 9 * D])
    ntile = NT // TT
    dB = lambda: [Buf() for _ in range(ntile)]
    with ExitStack() as stack:
        P = Prog(nc, stack)
        phase = [0]

        def ffn_like(fn):
            phase[0] += 1
            with ExitStack() as ph:
                R = FFNRes(nc, ph, P, tag="_p%d" % phase[0])
                R.load_ident(ident)
                fn(R, ph)
            P.barrier()

        def mix_like(fn):
            phase[0] += 1
            with ExitStack() as ph:
                M = MixRes(nc, ph, P, tag="_p%d" % phase[0])
                fn(M)
            P.barrier()

        def do_mod(R, ph):
            sb = lambda name, shape, dt: ph.enter_context(nc.sbuf_tensor(name, shape, dt))
            sbm = (sb("cT", [128, NKC], F32), Buf("cT"), sb("brow", [1, 512], F32), Buf("brow"), sb("mrow", [1, 512], F32), Buf("mrow"))
            for l in range(depth):
                mod_phase(R, c_row, W["w_mod"][l], W["b_mod"][l], MOD[l], sbm)
        ffn_like(do_mod)
        cur = x
        for l in range(depth):
            last_layer = l == depth - 1
            nxt = Xs[0]

            def f1(R, ph, l=l, cur=cur, nxt=nxt):
                prep_vectors(R, W["ff1_norm_pre"][l], W["ff1_norm_post"][l], MOD[l], 0, 0.5)
                ffn_phase(R, cur, nxt, yacc, W["ff1_w_in"][l], W["ff1_w_out"][l], NT)
            ffn_like(f1)
            cur = nxt

            def m1(R, ph, l=l, cur=cur):
                prep_vectors(R, W["mix_norm_pre"][l], W["mix_norm_post"][l], MOD[l], 3, 1.0)
                proj_phase(R, cur, W["mix_w_in"][l], PT, NT, dB())
            ffn_like(m1)

            def g(M, l=l):
                gdn_phase(M, PT, YT, W["conv_w"][l], W["a_log"][l], W["dt_bias"][l], W["gdn_norm_w"][l], cst, NT, dB(), dB())
            mix_like(g)

            def s5(M, l=l):
                prm = {k: W["s5_" + k][l] for k in ("a_re", "a_im", "log_dt", "b_re", "b_im", "c_re", "c_im", "d")}
                s5_phase(M, PT, YT, prm, cst, cst2, NT, dB(), dB())
            mix_like(s5)
            nxt = Xs[1]

            def m3(R, ph, l=l, cur=cur, nxt=nxt):
                prep_vectors(R, W["mix_norm_pre"][l], W["mix_norm_post"][l], MOD[l], 3, 1.0)
                mixpost_phase(R, YT, cur, nxt, W["s5_w_glu"][l], W["mix_w_out"][l], NT)
            ffn_like(m3)
            cur = nxt
            nxt = y if last_layer else Xs[0]
            outB = []

            def f2(R, ph, l=l, cur=cur, nxt=nxt):
                prep_vectors(R, W["ff2_norm_pre"][l], W["ff2_norm_post"][l], MOD[l], 6, 0.5)
                outB.extend(ffn_phase(R, cur, nxt, yacc, W["ff2_w_in"][l], W["ff2_w_out"][l], NT))
            ffn_like(f2)
            cur = nxt
        P.final_wait("gpsimd", outB)
        P.emit()
    return nc


def kernel(**inputs):
    x = np.ascontiguousarray(inputs["x"], dtype=np.float32)
    c = np.ascontiguousarray(inputs["c"], dtype=np.float32)
    B, L, _ = x.shape
    common = {n: np.ascontiguousarray(inputs[n], dtype=np.float32) for n in WNAMES}
    common["ident"] = np.eye(128, dtype=np.float32)
    common["cst"] = make_consts()
    common["cst2"] = make_consts2()
    n_cores = B
    in_maps = []
    for core in range(n_cores):
        m = dict(common)
        m["x"] = np.ascontiguousarray(x[core])
        m["c_row"] = np.ascontiguousarray(c[core])
        in_maps.append(m)
    nc = build_nc(L)
    res = run_bass_kernel_spmd(nc, in_maps, core_ids=list(range(n_cores)))
    out = np.empty((B, L, D), dtype=np.float32)
    for core in range(n_cores):
        out[core] = res.results[core]["y"]
    return out
```
